# Optimizing a Trainium2 kernel written in Bass

```python
import math
import jax
import jax.numpy as jnp
from jax import lax
import numpy as np


D_MODEL = 1024
BATCH = 2
SEQ = 16384
DEPTH = 2
DEC_BATCH = 8
DEC_SEQ = 2048
PAST_LEN = 128

HEAD_DIM = 64
GRID_W = 64
N_MEM = 256
LN_EPS = 1e-5
RMS_EPS = 1e-6
NEG_INF = -1e30

A_HEADS = 12
A_WIDTH = A_HEADS * HEAD_DIM
A_PATTERNS = ((128, 1), (512, 4), (2048, 16))
N_BUCKETS = 32
REL_MAX_DIST = 1024
FNET_GROUPS = 4
FNET_CH = 64
B_WIDTH = FNET_GROUPS * FNET_CH
C_Q_HEADS = 12
C_KV_HEADS = 4
C_WIDTH = C_Q_HEADS * HEAD_DIM
C_KV_WIDTH = C_KV_HEADS * HEAD_DIM
Q_BLOCK = 128
ROPE_THETA = 10000.0
POOL_WINDOWS = (2, 4, 8, 16)
POOL_CH = 64
D_WIDTH = len(POOL_WINDOWS) * POOL_CH
AB_IN = 3 * A_WIDTH + B_WIDTH
CD_IN = C_WIDTH + 2 * C_KV_WIDTH + D_WIDTH
AB_OUT = A_WIDTH + B_WIDTH
CD_OUT = C_WIDTH + D_WIDTH
XA_HEADS = 4
XA_HEAD_DIM = D_MODEL // XA_HEADS
FFN_HIDDEN = -(-8 * D_MODEL // (3 * 256)) * 256
DN_ALPHA = (2 * DEPTH) ** 0.25
DN_BETA = (8 * DEPTH) ** -0.25
N_EVEN = (DEPTH + 1) // 2
N_ODD = DEPTH // 2

kernel_name = 'hybrid_dilated_fourier_gqa_pool_encoder'


def layer_norm(x, g, b):
    xf = x.astype(jnp.float32)
    mu = xf.mean(-1, keepdims=True)
    var = jnp.square(xf - mu).mean(-1, keepdims=True)
    return ((xf - mu) * lax.rsqrt(var + LN_EPS) * g + b).astype(x.dtype)


def rms_norm(x, g):
    xf = x.astype(jnp.float32)
    return (xf * lax.rsqrt(jnp.square(xf).mean(-1, keepdims=True) + RMS_EPS) * g).astype(x.dtype)


def t5_bucket(rel):
    nb = N_BUCKETS // 2
    max_exact = nb // 2
    ret = jnp.where(rel > 0, nb, 0)
    n = jnp.abs(rel)
    nf = jnp.maximum(n, 1).astype(jnp.float32)
    large = max_exact + (jnp.log(nf / max_exact) / math.log(REL_MAX_DIST / max_exact)
                         * (nb - max_exact)).astype(jnp.int32)
    large = jnp.minimum(large, nb - 1)
    return ret + jnp.where(n < max_exact, n, large)


def dilated_window_attention(q, k, v, rel_bias, dilation, half):
    b, n, h, dh = q.shape
    L = n // dilation
    blk = half
    nblk = -(-L // blk)
    Lp = nblk * blk

    def to_strided(t):
        t = t.reshape(b, L, dilation, h, dh).transpose(0, 2, 1, 3, 4)
        return t.reshape(b * dilation, L, h, dh)

    def band(t):
        t = jnp.pad(t, ((0, 0), (blk, Lp - L + blk), (0, 0), (0, 0))).reshape(b * dilation, nblk + 2, blk, h, dh)
        return jnp.concatenate([t[:, :-2], t[:, 1:-1], t[:, 2:]], axis=2)

    qs = jnp.pad(to_strided(q), ((0, 0), (0, Lp - L), (0, 0), (0, 0))).reshape(b * dilation, nblk, blk, h, dh)
    kb = band(to_strided(k))
    vb = band(to_strided(v))

    off = jnp.arange(3 * blk)[None, :] - blk - jnp.arange(blk)[:, None]
    in_band = jnp.abs(off) <= half
    bias = rel_bias[t5_bucket(off * dilation)].transpose(2, 0, 1)
    key_u = jnp.arange(nblk)[:, None] * blk + jnp.arange(3 * blk)[None, :] - blk
    key_ok = (key_u >= 0) & (key_u < L)
    mask = in_band[None] & key_ok[:, None, :]

    logits = jnp.einsum('bnqhd,bnkhd->bnhqk', qs, kb, preferred_element_type=jnp.float32) * (HEAD_DIM ** -0.5)
    logits = logits + bias[None, None].astype(jnp.float32)
    logits = jnp.where(mask[None, :, None], logits, NEG_INF)
    lse = jax.nn.logsumexp(logits, axis=-1)
    p = jnp.exp(logits - lse[..., None]).astype(v.dtype)
    o = jnp.einsum('bnhqk,bnkhd->bnqhd', p, vb)
    o = o.reshape(b, dilation, Lp, h, dh)[:, :, :L].transpose(0, 2, 1, 3, 4).reshape(b, n, h, dh)
    lse = lse.transpose(0, 1, 3, 2).reshape(b, dilation, Lp, h)[:, :, :L].transpose(0, 2, 1, 3).reshape(b, n, h)
    return o, lse


def dilated_mixture(q, k, v, rel_bias):
    outs, lses = [], []
    for window, dilation in A_PATTERNS:
        o, lse = dilated_window_attention(q, k, v, rel_bias, dilation, window // (2 * dilation))
        outs.append(o)
        lses.append(lse)
    wts = jax.nn.softmax(jnp.stack(lses), axis=0).astype(q.dtype)
    return jnp.einsum('gbnh,gbnhd->bnhd', wts, jnp.stack(outs))


def fourier_mixer(u, g, w):
    uf = u.astype(jnp.float32)
    mu = uf.mean(-1, keepdims=True)
    var = jnp.square(uf - mu).mean(-1, keepdims=True)
    un = (uf - mu) * lax.rsqrt(var + LN_EPS) * g
    f = jnp.fft.fft2(un, axes=(1, 3), norm='ortho').real
    return jnp.einsum('bngc,gce->bnge', f.astype(u.dtype), w)


def axial_rope_tables(n):
    rows = n // GRID_W
    row_id = jnp.broadcast_to(jnp.arange(rows)[:, None], (rows, GRID_W)).reshape(n)
    col_id = jnp.broadcast_to(jnp.arange(GRID_W)[None, :], (rows, GRID_W)).reshape(n)
    axis_dim = HEAD_DIM // 2
    freqs = ROPE_THETA ** (-jnp.arange(0, axis_dim, 2, dtype=jnp.float32) / axis_dim)
    ang = jnp.concatenate([row_id[:, None] * freqs, col_id[:, None] * freqs], axis=-1)
    return jnp.cos(ang), jnp.sin(ang)


def apply_rope(x, cos, sin):
    xf = x.astype(jnp.float32).reshape(*x.shape[:-1], HEAD_DIM // 2, 2)
    c = cos[None, :, None]
    s = sin[None, :, None]
    x1, x2 = xf[..., 0], xf[..., 1]
    out = jnp.stack([x1 * c - x2 * s, x1 * s + x2 * c], axis=-1).reshape(x.shape)
    return out.astype(x.dtype)


def gqa_blocked(q, k, v):
    b, n, hq, dh = q.shape
    rep = hq // C_KV_HEADS
    nb = n // Q_BLOCK
    qb = q.reshape(b, nb, Q_BLOCK, C_KV_HEADS, rep, dh).transpose(1, 0, 2, 3, 4, 5)

    def one_block(qblk):
        s = jnp.einsum('bqgrd,bkgd->bgrqk', qblk, k, preferred_element_type=jnp.float32) * (HEAD_DIM ** -0.5)
        p = jax.nn.softmax(s, axis=-1).astype(v.dtype)
        return jnp.einsum('bgrqk,bkgd->bqgrd', p, v)

    o = lax.map(one_block, qb)
    return o.transpose(1, 0, 2, 3, 4, 5).reshape(b, n, hq, dh)


def multiscale_pool(u, w, scale):
    b, n = u.shape[0], u.shape[1]
    uf = u.astype(jnp.float32)
    csum = jnp.pad(jnp.cumsum(uf, axis=1), ((0, 0), (1, 0), (0, 0), (0, 0)))
    t = jnp.arange(n)
    pooled = []
    for gi, wdw in enumerate(POOL_WINDOWS):
        lo = jnp.clip(t - wdw // 2, 0, n)
        hi = jnp.clip(t + wdw // 2, 0, n)
        sg = csum[:, hi, gi] - csum[:, lo, gi]
        pooled.append(sg / (hi - lo).astype(jnp.float32)[None, :, None])
    mixed = (jnp.stack(pooled, axis=2) - uf).astype(u.dtype)
    y = jnp.einsum('bngc,gce->bnge', mixed, w)
    return y.reshape(b, n, D_WIDTH) * scale


def mixer_ab(x, rel_bias, w_in, fnet_g, fnet_w, w_out):
    b, n, _ = x.shape
    z = x @ w_in
    q, k, v, u = jnp.split(z, [A_WIDTH, 2 * A_WIDTH, 3 * A_WIDTH], axis=-1)
    q = q.reshape(b, n, A_HEADS, HEAD_DIM)
    k = k.reshape(b, n, A_HEADS, HEAD_DIM)
    v = v.reshape(b, n, A_HEADS, HEAD_DIM)
    o_a = dilated_mixture(q, k, v, rel_bias).reshape(b, n, A_WIDTH)
    o_b = fourier_mixer(u.reshape(b, n, FNET_GROUPS, FNET_CH), fnet_g, fnet_w).reshape(b, n, B_WIDTH)
    return jnp.concatenate([o_a, o_b], axis=-1) @ w_out


def mixer_cd(x, cos, sin, w_in, q_norm, k_norm, pool_w, pool_scale, w_out):
    b, n, _ = x.shape
    z = x @ w_in
    q, k, v, u = jnp.split(z, [C_WIDTH, C_WIDTH + C_KV_WIDTH, C_WIDTH + 2 * C_KV_WIDTH], axis=-1)
    q = apply_rope(rms_norm(q.reshape(b, n, C_Q_HEADS, HEAD_DIM), q_norm), cos, sin)
    k = apply_rope(rms_norm(k.reshape(b, n, C_KV_HEADS, HEAD_DIM), k_norm), cos, sin)
    v = v.reshape(b, n, C_KV_HEADS, HEAD_DIM)
    o_c = gqa_blocked(q, k, v).reshape(b, n, C_WIDTH)
    o_d = multiscale_pool(u.reshape(b, n, len(POOL_WINDOWS), POOL_CH), pool_w, pool_scale)
    return jnp.concatenate([o_c, o_d], axis=-1) @ w_out


def memory_cross_attention(x, mem, w_q, w_kv, w_o):
    b, n, _ = x.shape
    m = mem.shape[1]
    q = (x @ w_q).reshape(b, n, XA_HEADS, XA_HEAD_DIM)
    kv = (mem @ w_kv).reshape(b, m, 2, XA_HEADS, XA_HEAD_DIM)
    s = jnp.einsum('bnhd,bmhd->bhnm', q, kv[:, :, 0], preferred_element_type=jnp.float32) * (XA_HEAD_DIM ** -0.5)
    p = jax.nn.softmax(s, axis=-1).astype(x.dtype)
    o = jnp.einsum('bhnm,bmhd->bnhd', p, kv[:, :, 1]).reshape(b, n, D_MODEL)
    return o @ w_o


def swiglu(x, w_in, w_out):
    g, u = jnp.split(x @ w_in, 2, axis=-1)
    return (jax.nn.silu(g) * u) @ w_out


def run_trunk(x, mem, rel_bias, ab_w_in, ab_fnet_g, ab_fnet_w, ab_w_out,
              cd_w_in, cd_q_norm, cd_k_norm, cd_pool_w, cd_pool_scale, cd_w_out,
              xa_w_q, xa_w_kv, xa_w_o, ffn_w_in, ffn_w_out, ln_g, ln_b):
    n = x.shape[1]
    cos, sin = axial_rope_tables(n)
    for layer in range(DEPTH):
        i = layer // 2
        if layer % 2 == 0:
            h = mixer_ab(x, rel_bias, ab_w_in[i], ab_fnet_g[i], ab_fnet_w[i], ab_w_out[i])
        else:
            h = mixer_cd(x, cos, sin, cd_w_in[i], cd_q_norm[i], cd_k_norm[i],
                         cd_pool_w[i], cd_pool_scale[i], cd_w_out[i])
        x = layer_norm(DN_ALPHA * x + h, ln_g[layer, 0], ln_b[layer, 0])
        h = memory_cross_attention(x, mem, xa_w_q[layer], xa_w_kv[layer], xa_w_o[layer])
        x = layer_norm(DN_ALPHA * x + h, ln_g[layer, 1], ln_b[layer, 1])
        h = swiglu(x, ffn_w_in[layer], ffn_w_out[layer])
        x = layer_norm(DN_ALPHA * x + h, ln_g[layer, 2], ln_b[layer, 2])
    return x


def setup_inputs(seed: int = 0) -> dict:
    key = jax.random.key(seed)
    ks = jax.random.split(key, 22)
    D = D_MODEL

    def nrm(k, shape, s):
        return jax.random.normal(k, shape, jnp.float32) * s

    return {
        'x_prompt': nrm(ks[0], (BATCH, SEQ, D), 1.0),
        'x_sample': nrm(ks[1], (DEC_BATCH, DEC_SEQ, D), 1.0),
        'mem_prompt': nrm(ks[2], (BATCH, N_MEM, D), 1.0),
        'mem_sample': nrm(ks[3], (DEC_BATCH, N_MEM, D), 1.0),
        'rel_bias': nrm(ks[4], (N_BUCKETS, A_HEADS), 0.2),
        'ab_w_in': nrm(ks[5], (N_EVEN, D, AB_IN), D ** -0.5),
        'ab_fnet_g': 1.0 + nrm(ks[6], (N_EVEN, FNET_GROUPS, FNET_CH), 0.02),
        'ab_fnet_w': nrm(ks[7], (N_EVEN, FNET_GROUPS, FNET_CH, FNET_CH), FNET_CH ** -0.5),
        'ab_w_out': nrm(ks[8], (N_EVEN, AB_OUT, D), AB_OUT ** -0.5 * DN_BETA),
        'cd_w_in': nrm(ks[9], (N_ODD, D, CD_IN), D ** -0.5),
        'cd_q_norm': 1.0 + nrm(ks[10], (N_ODD, HEAD_DIM), 0.02),
        'cd_k_norm': 1.0 + nrm(ks[11], (N_ODD, HEAD_DIM), 0.02),
        'cd_pool_w': nrm(ks[12], (N_ODD, len(POOL_WINDOWS), POOL_CH, POOL_CH), POOL_CH ** -0.5),
        'cd_pool_scale': 1.0 + nrm(ks[13], (N_ODD, D_WIDTH), 0.02),
        'cd_w_out': nrm(ks[14], (N_ODD, CD_OUT, D), CD_OUT ** -0.5 * DN_BETA),
        'xa_w_q': nrm(ks[15], (DEPTH, D, D), D ** -0.5),
        'xa_w_kv': nrm(ks[16], (DEPTH, D, 2 * D), D ** -0.5),
        'xa_w_o': nrm(ks[17], (DEPTH, D, D), D ** -0.5 * DN_BETA),
        'ffn_w_in': nrm(ks[18], (DEPTH, D, 2 * FFN_HIDDEN), D ** -0.5),
        'ffn_w_out': nrm(ks[19], (DEPTH, FFN_HIDDEN, D), FFN_HIDDEN ** -0.5 * DN_BETA),
        'ln_g': 1.0 + nrm(ks[20], (DEPTH, 3, D), 0.02),
        'ln_b': nrm(ks[21], (DEPTH, 3, D), 0.02),
    }


def reference(x_prompt, x_sample, mem_prompt, mem_sample, rel_bias, ab_w_in, ab_fnet_g, ab_fnet_w, ab_w_out,
              cd_w_in, cd_q_norm, cd_k_norm, cd_pool_w, cd_pool_scale, cd_w_out,
              xa_w_q, xa_w_kv, xa_w_o, ffn_w_in, ffn_w_out, ln_g, ln_b):
    y_prompt = run_trunk(x_prompt, mem_prompt, rel_bias, ab_w_in, ab_fnet_g, ab_fnet_w, ab_w_out,
                         cd_w_in, cd_q_norm, cd_k_norm, cd_pool_w, cd_pool_scale, cd_w_out,
                         xa_w_q, xa_w_kv, xa_w_o, ffn_w_in, ffn_w_out, ln_g, ln_b)
    y_sample = run_trunk(x_sample, mem_sample, rel_bias, ab_w_in, ab_fnet_g, ab_fnet_w, ab_w_out,
                         cd_w_in, cd_q_norm, cd_k_norm, cd_pool_w, cd_pool_scale, cd_w_out,
                         xa_w_q, xa_w_kv, xa_w_o, ffn_w_in, ffn_w_out, ln_g, ln_b)
    return (y_prompt, y_sample)
```

```python
import math
import numpy as np
import ml_dtypes
import concourse.bass as bass
import concourse.mybir as mybir
from concourse.bass_utils import run_bass_kernel_spmd

F32 = mybir.dt.float32
BF16 = mybir.dt.bfloat16
AF = mybir.ActivationFunctionType
ALU = mybir.AluOpType
AX = mybir.AxisListType

NCORES = 8
D = 1024
KC = 8
TT = 512
NP_OWN = 4096
NS_OWN = 2048
NOWN = NP_OWN + NS_OWN
NTILE = NOWN // TT
NP_EXT = NP_OWN + 2048
SEQ_P = 16384
SEQ_S = 2048
FFN_H = 2816
HC = FFN_H // 128
DN_ALPHA = 4 ** 0.25
LN_EPS = 1e-5
RMS_EPS = 1e-6
ZW = 2944
ZC = 1408


class Buf:
    __slots__ = ("t", "ws", "r", "name", "wx")

    def __init__(self, t, name=""):
        self.t = t
        self.wx = None
        self.ws = []
        self.r = []
        self.name = name

    def __getitem__(self, idx):
        return self.t[idx]


def _compact(evs):
    best = {}
    for (k, v, s) in evs:
        if k not in best or best[k][1] < v:
            best[k] = (k, v, s)
    return list(best.values())


class Sch:
    def __init__(self, nc, ndma_sems=10):
        self.nc = nc
        self.eng = {"pe": nc.tensor, "act": nc.scalar, "dve": nc.vector,
                    "pool": nc.gpsimd, "sp": nc.sync}
        self.tick = {}
        self.seen = {}
        self.semh = {}
        self._ctx = []
        for e in self.eng:
            self._mksem("s_" + e)
            self.tick[e] = 0
            self.seen[e] = {}
        self.dq = {}
        for q in ("sp", "pool", "act"):
            names = []
            for i in range(ndma_sems):
                k = "d_%s_%d" % (q, i)
                self._mksem(k)
                names.append(k)
            self.dq[q] = {"sems": names, "n": 0, "cnt": {k: 0 for k in names}}
        self._mksem("cc")
        self.cc_cnt = 0
        self.ninst = 0

    def _mksem(self, key):
        cm = self.nc.semaphore(key)
        h = cm.__enter__()
        self._ctx.append(cm)
        self.semh[key] = h
        return h

    def close(self):
        for cm in reversed(self._ctx):
            cm.__exit__(None, None, None)
        self._ctx = []

    def _wait(self, e, ev):
        semkey, val, src = ev
        if src == e and e == "pe":
            return
        if self.seen[e].get(semkey, 0) >= val:
            return
        self.eng[e].wait_ge(self.semh[semkey], val)
        self.seen[e][semkey] = val
        self.ninst += 1

    def _deps(self, e, reads, writes, wacc):
        for b in reads:
            for ev in b.ws:
                self._wait(e, ev)
        for b in writes:
            for ev in b.ws:
                self._wait(e, ev)
            for ev in b.r:
                self._wait(e, ev)
        for b in wacc:
            if b.wx is not None:
                self._wait(e, b.wx)
            for ev in b.r:
                self._wait(e, ev)

    def _commit(self, ev, reads, writes, wacc):
        for b in reads:
            b.r.append(ev)
            if len(b.r) > 16:
                b.r = _compact(b.r)
        for b in writes:
            b.ws = [ev]
            b.wx = ev
            b.r = []
        for b in wacc:
            b.ws.append(ev)
            if len(b.ws) > 16:
                b.ws = _compact(b.ws)

    def op(self, e, fn, reads=(), writes=(), wacc=()):
        self._deps(e, reads, writes, wacc)
        ins = fn(self.eng[e])
        self.tick[e] += 1
        k = "s_" + e
        ins.then_inc(self.semh[k], 1)
        self._commit((k, self.tick[e], e), reads, writes, wacc)
        self.ninst += 1
        return ins

    def dma(self, q, out, in_, reads=(), writes=(), wacc=(), **kw):
        d = self.dq[q]
        k = d["sems"][d["n"] % len(d["sems"])]
        d["n"] += 1
        if d["cnt"][k] > 0:
            self._wait(q, (k, d["cnt"][k], "dma"))
        self._deps(q, reads, writes, wacc)
        ins = self.eng[q].dma_start(out=out, in_=in_, **kw)
        d["cnt"][k] += 16
        ins.then_inc(self.semh[k], 16)
        self._commit((k, d["cnt"][k], "dma"), reads, writes, wacc)
        self.ninst += 1

    def allgather(self, out_t, in_t, reads, writes, groups):
        self._deps("pool", reads, writes, ())
        if self.cc_cnt:
            self._wait("pool", ("cc", self.cc_cnt, "dma"))
        ins = self.nc.gpsimd.collective_compute("AllGather", ALU.bypass, replica_groups=groups,
                                                ins=[in_t.ap().opt()], outs=[out_t.ap().opt()])
        self.cc_cnt += 1
        ins.then_inc(self.semh["cc"], 1)
        self._commit(("cc", self.cc_cnt, "dma"), reads, writes, ())
        self.ninst += 1

    def cc_fence(self, groups):
        if not hasattr(self, "_fence_t"):
            self._fence_t = (self.nc.dram_tensor("cc_f_in", [16, 64], F32),
                             self.nc.dram_tensor("cc_f_out", [16 * len(groups[0]), 64], F32))
        fi, fo = self._fence_t
        self.allgather(fo, fi, reads=[], writes=[], groups=groups)

    def all_events(self):
        evs = []
        for e in self.eng:
            if self.tick[e] > 0:
                evs.append(("s_" + e, self.tick[e], e))
        for q, d in self.dq.items():
            for k, v in d["cnt"].items():
                if v > 0:
                    evs.append((k, v, "dma"))
        if self.cc_cnt:
            evs.append(("cc", self.cc_cnt, "dma"))
        return evs

    def barrier(self, engines=("pe", "act", "dve", "pool", "sp")):
        evs = self.all_events()
        for e in engines:
            for ev in evs:
                if ev[2] == e:
                    continue
                self._wait(e, ev)


def _t5_bucket_np(rel):
    nb = 16
    max_exact = 8
    ret = np.where(rel > 0, nb, 0)
    n = np.abs(rel)
    nf = np.maximum(n, 1).astype(np.float32)
    large = max_exact + (np.log(nf / np.float32(max_exact)) / np.float32(math.log(1024 / max_exact))
                         * np.float32(nb - max_exact)).astype(np.int32)
    large = np.minimum(large, nb - 1)
    return ret + np.where(n < max_exact, n, large)


def _consts_common():
    c = {}
    c["ident"] = np.eye(128, dtype=np.float32)
    c["antiI"] = np.eye(128, dtype=np.float32)[::-1].copy()
    c["onesd"] = np.full((128, 128), 1.0 / D, np.float32)
    blk = np.zeros((128, 128), np.float32)
    blk[:64, :64] = 1.0 / 64
    blk[64:, 64:] = 1.0 / 64
    c["blk64"] = blk
    i = np.arange(3072)
    delta = 1535 - i
    mult = ((np.abs(delta) <= 64).astype(np.int32)
            + ((delta % 4 == 0) & (np.abs(delta) <= 256)).astype(np.int32)
            + ((delta % 16 == 0) & (np.abs(delta) <= 1024)).astype(np.int32))
    mult[3071] = 0
    bk = _t5_bucket_np(delta)
    ohm = np.zeros((32, 3072), np.float32)
    ohm[bk, i] = mult
    c["ohm"] = ohm
    k = np.arange(64)
    ang = 2 * np.pi * np.outer(k, k) / 64
    c64 = (np.cos(ang) / 8).astype(np.float32)
    s64 = (np.sin(ang) / 8).astype(np.float32)
    cbd = np.zeros((128, 128), np.float32)
    sbd = np.zeros((128, 128), np.float32)
    cbd[:64, :64] = c64
    cbd[64:, 64:] = c64
    sbd[:64, :64] = s64
    sbd[64:, 64:] = s64
    c["c64bd"] = cbd
    c["s64bd"] = sbd
    return c


def _dft_tables(N, N1, N2, k2_list):
    sc = 1.0 / math.sqrt(N)
    n1 = np.arange(N1)
    k1 = np.arange(N1)
    n2 = np.arange(N2)
    a1 = 2 * np.pi * np.outer(n1, k1) / N1
    rp = np.concatenate([np.cos(a1), -np.sin(a1)], 1) * sc
    rr = np.concatenate([np.sin(a1), np.cos(a1)], 1) * sc
    at = 2 * np.pi * np.outer(n2, k1) / N
    tc = np.cos(at)
    ts = np.sin(at)
    a2 = 2 * np.pi * np.outer(n2, np.asarray(k2_list)) / N2
    c2 = np.cos(a2)
    s2 = np.sin(a2)
    f = lambda a: np.ascontiguousarray(a, dtype=np.float32)
    return f(rp), f(rr), f(tc), f(ts), f(c2), f(s2)


def _fm(a):
    F, T = a.shape
    return np.ascontiguousarray(a.reshape(F // 128, 128, T).transpose(1, 0, 2))


def _unfm(a):
    P_, C, T = a.shape
    return np.ascontiguousarray(a.transpose(2, 1, 0).reshape(T, C * P_))


from contextlib import ExitStack


class Prog:
    def __init__(self, debug=()):
        self.debug = set(debug)
        self.nc = bass.Bass("TRN2", target_bir_lowering=False)
        self.S = Sch(self.nc)
        self.inputs = {}
        self.outputs = {}
        self.gstack = ExitStack()
        self.rr = 0

    def din(self, name, shape, dtype=F32):
        t = self.nc.dram_tensor(name, list(shape), dtype, kind="ExternalInput")
        self.inputs[name] = t
        return t

    def dout(self, name, shape, dtype=F32):
        t = self.nc.dram_tensor(name, list(shape), dtype, kind="ExternalOutput")
        self.outputs[name] = t
        return t

    def dscratch(self, name, shape, dtype):
        if name in self.debug:
            return self.dout(name, shape, dtype)
        return self.nc.dram_tensor(name, list(shape), dtype)

    def sb(self, stack, name, shape, dtype):
        self._uid = getattr(self, "_uid", 0) + 1
        name = "%s_u%d" % (name, self._uid)
        t = stack.enter_context(self.nc.sbuf_tensor(name, list(shape), dtype))
        return Buf(t, name)

    def eng2(self):
        self.rr += 1
        return "act" if self.rr % 2 else "dve"

    def copy(self, e, out, in_, reads, writes=(), wacc=(), scale=None):
        S = self.S
        if e == "act":
            if scale is None:
                S.op("act", lambda h: h.copy(out, in_), reads, writes, wacc)
            else:
                S.op("act", lambda h: h.mul(out, in_, scale), reads, writes, wacc)
        else:
            if scale is None:
                S.op(e, lambda h: h.tensor_copy(out, in_), reads, writes, wacc)
            else:
                S.op(e, lambda h: h.tensor_scalar_mul(out, in_, scale), reads, writes, wacc)

    def rsqrt(self, out, in_, eps_tile, reads, wacc):
        S = self.S
        S.op("act", lambda h: h.activation(out, in_, AF.Sqrt, bias=eps_tile[:, 0:1], scale=1.0),
             reads=list(reads) + [eps_tile], wacc=wacc)
        S.op("dve", lambda h: h.reciprocal(out, out), reads=list(wacc), wacc=wacc)

    def setup(self):
        nc, S = self.nc, self.S
        g = self.gstack
        self.ps2 = [Buf(g.enter_context(nc.psum_tensor("psp%d" % i, [128, 1024], F32)), "psp%d" % i) for i in range(4)]
        self.ps = [Buf(self.ps2[i // 2].t[:, (i % 2) * 512:(i % 2 + 1) * 512], "ps%d" % i) for i in range(8)]
        self.c_onesA = self.sb(g, "c_onesA", [128, 128], F32)
        self.c_onesB = self.sb(g, "c_onesB", [128, 128], F32)
        S.op("dve", lambda h: h.memset(self.c_onesA[:], 0.0), writes=[self.c_onesA])
        S.op("dve", lambda h: h.memset(self.c_onesB[:], 0.0), writes=[self.c_onesB])
        S.op("dve", lambda h: h.memset(self.c_onesA[:, 0:64], 1.0), reads=[self.c_onesA], wacc=[self.c_onesA])
        S.op("dve", lambda h: h.memset(self.c_onesB[:, 64:128], 1.0), reads=[self.c_onesB], wacc=[self.c_onesB])
        self.c_ident = self.sb(g, "c_ident", [128, 128], F32)
        self.c_antiI = self.sb(g, "c_antiI", [128, 128], F32)
        self.c_onesd = self.sb(g, "c_onesd", [128, 128], F32)
        self.c_blk64 = self.sb(g, "c_blk64", [128, 128], F32)
        self.c_onesb = self.sb(g, "c_onesb", [128, 128], BF16)
        self.c_identb = self.sb(g, "c_identb", [128, 128], BF16)
        self.c_lng = self.sb(g, "c_lng", [128, 6, 8], F32)
        self.c_lnb = self.sb(g, "c_lnb", [128, 6, 8], F32)
        for nm, buf in (("ident", self.c_ident), ("antiI", self.c_antiI), ("onesd", self.c_onesd),
                        ("blk64", self.c_blk64)):
            t = self.din("k_" + nm, [128, 128])
            S.dma("sp", buf[:], t[:, :], writes=[buf])
        t = self.din("ln_g_r", [128, 6, 8])
        S.dma("sp", self.c_lng[:], t[:, :, :], writes=[self.c_lng])
        t = self.din("ln_b_r", [128, 6, 8])
        S.dma("sp", self.c_lnb[:], t[:, :, :], writes=[self.c_lnb])
        self.c_eps_ln = self.sb(g, "c_eps_ln", [128, 1], F32)
        self.c_eps_rms = self.sb(g, "c_eps_rms", [128, 1], F32)
        S.op("dve", lambda h: h.memset(self.c_eps_ln[:], LN_EPS), writes=[self.c_eps_ln])
        S.op("dve", lambda h: h.memset(self.c_eps_rms[:], RMS_EPS), writes=[self.c_eps_rms])
        S.op("dve", lambda h: h.memset(self.c_onesb[:], 1.0), writes=[self.c_onesb])
        S.op("dve", lambda h: h.tensor_copy(self.c_identb[:], self.c_ident[:]), reads=[self.c_ident],
             writes=[self.c_identb])

    def load_w(self, stack, name, wap, K, N, stg):
        S = self.S
        kc = K // 128
        wb = self.sb(stack, name, [128, kc, N], BF16)
        CH = stg[0].t.shape[1]
        i = 0
        for c in range(kc):
            for n0 in range(0, N, CH):
                n1 = min(N, n0 + CH)
                st = stg[i % len(stg)]
                i += 1
                S.dma("sp", st[:, 0:n1 - n0], wap[c * 128:(c + 1) * 128, n0:n1], writes=[st])
                self.copy(self.eng2(), wb[:, c, n0:n1], st[:, 0:n1 - n0], reads=[st], wacc=[wb])
        return wb

    def finish(self):
        S = self.S
        S.barrier()
        self.gstack.close()
        S.close()
        return self.nc


def phase_proj0(P):
    nc, S = P.nc, P.S
    w_in = P.din("ab_w_in", [D, 2560])
    fg = P.din("fnet_g_r", [128, 2])
    P.KT0 = {"p": P.dscratch("KT0p", [128, 6, NP_EXT], BF16), "s": P.dscratch("KT0s", [128, 6, SEQ_S], BF16)}
    P.V0 = {"p": P.dscratch("V0p", [NP_EXT // 128, 128, 780], BF16),
            "s": P.dscratch("V0s", [SEQ_S // 128, 128, 780], BF16)}
    P.QT0 = P.dscratch("QT0", [128, 6, NOWN], BF16)
    P.unT = {"p": [P.dscratch("unTp%d" % i, [256, 2048], BF16) for i in range(2)],
             "s": [P.dscratch("unTs", [256, NS_OWN], BF16)]}
    xT = {"p": P.din("xTp", [128, 8, NP_EXT]), "s": P.din("xTs", [128, 8, SEQ_S])}
    vld = {"p": P.din("vldp", [128, NP_EXT // 128]), "s": P.din("vlds", [128, SEQ_S // 128])}
    P.xT = xT
    P.vld = vld
    with ExitStack() as st:
        stg = [P.sb(st, "stg%d" % i, [128, 1024], F32) for i in range(2)]
        wb = P.load_w(st, "w_ab_in", w_in, D, 2560, stg)
        fgs = P.sb(st, "fgs", [128, 2], F32)
        S.dma("sp", fgs[:], fg[:, :], writes=[fgs])
        xf = [P.sb(st, "xf%d" % i, [128, 8, TT], F32) for i in range(2)]
        xb = [P.sb(st, "xb%d" % i, [128, 8, TT], BF16) for i in range(2)]
        kt = [P.sb(st, "kt%d" % i, [128, 6, TT], BF16) for i in range(2)]
        qt = [P.sb(st, "qt%d" % i, [128, 6, TT], BF16) for i in range(2)]
        vs = [P.sb(st, "vs%d" % i, [128, 4, 12, 65], BF16) for i in range(2)]
        uf = P.sb(st, "uf", [128, 2, TT], F32)
        usq = P.sb(st, "usq", [128, 2, TT], F32)
        urs = P.sb(st, "urs", [128, 2, TT], F32)
        un = [P.sb(st, "un%d" % i, [128, 2, TT], BF16) for i in range(2)]
        ones12 = P.sb(st, "ones12", [128, 12, 1], F32)
        S.op("dve", lambda h: h.memset(ones12[:], 1.0), writes=[ones12])
        vl = {}
        for sg in ("p", "s"):
            nch = (NP_EXT if sg == "p" else SEQ_S) // 128
            vl[sg] = P.sb(st, "vl" + sg, [128, nch], F32)
            S.dma("sp", vl[sg][:], vld[sg][:, :], writes=[vl[sg]])
        it = 0
        pb = 0
        for sg in ("p", "s"):
            n_ext = NP_EXT if sg == "p" else SEQ_S
            own0 = 2 if sg == "p" else 0
            nown_t = 8 if sg == "p" else 4
            ooff = 0 if sg == "p" else NP_OWN
            for i in range(n_ext // TT):
                a = it % 2
                it += 1
                X, XB, KT, QT, VS, UN = xf[a], xb[a], kt[a], qt[a], vs[a], un[a]
                S.dma("sp", X[:], xT[sg][:, :, i * TT:(i + 1) * TT], writes=[X])
                S.op("act", lambda h: h.copy(XB[:, 0:4, :], X[:, 0:4, :]), reads=[X], wacc=[XB])
                S.op("dve", lambda h: h.tensor_copy(XB[:, 4:8, :], X[:, 4:8, :]), reads=[X], wacc=[XB])
                for oc in range(6):
                    pt = P.ps[pb % 8]
                    pb += 1
                    for c in range(8):
                        S.op("pe", lambda h: h.matmul(pt[:], wb[:, c, 768 + oc * 128:768 + (oc + 1) * 128],
                                                      XB[:, c, :], start=(c == 0), stop=(c == 7)),
                             reads=[wb, XB], writes=[pt])
                    P.copy(P.eng2(), KT[:, oc, :], pt[:], reads=[pt], wacc=[KT])
                S.dma("pool", P.KT0[sg][:, :, i * TT:(i + 1) * TT], KT[:], reads=[KT])
                for sub in range(4):
                    for hf in range(2):
                        pt = P.ps[pb % 8]
                        pb += 1
                        for c in range(8):
                            S.op("pe", lambda h: h.matmul(pt[:, 0:384], XB[:, c, sub * 128:(sub + 1) * 128],
                                                          wb[:, c, 1536 + hf * 384:1536 + (hf + 1) * 384],
                                                          start=(c == 0), stop=(c == 7)),
                                 reads=[wb, XB], writes=[pt])
                        P.copy(P.eng2(), VS[:, sub, hf * 6:(hf + 1) * 6, 0:64],
                               pt[:, 0:384].rearrange("p (h d) -> p h d", d=64), reads=[pt], wacc=[VS])
                    ch = i * 4 + sub
                    S.op("dve", lambda h: h.tensor_scalar(VS[:, sub, :, 64:65], ones12[:], vl[sg][:, ch:ch + 1], None,
                                                          op0=ALU.mult), reads=[ones12, vl[sg]], wacc=[VS])
                S.dma("pool", P.V0[sg][i * 4:(i + 1) * 4].rearrange("c p f -> p c f"),
                      VS[:].rearrange("p s h d -> p s (h d)"), reads=[VS])
                if not (own0 <= i < own0 + nown_t):
                    continue
                o0 = ooff + (i - own0) * TT
                for oc in range(6):
                    pt = P.ps[pb % 8]
                    pb += 1
                    for c in range(8):
                        S.op("pe", lambda h: h.matmul(pt[:], wb[:, c, oc * 128:(oc + 1) * 128],
                                                      XB[:, c, :], start=(c == 0), stop=(c == 7)),
                             reads=[wb, XB], writes=[pt])
                    P.copy(P.eng2(), QT[:, oc, :], pt[:], reads=[pt], wacc=[QT], scale=0.125)
                S.dma("pool", P.QT0[:, :, o0:o0 + TT], QT[:], reads=[QT])
                for c2 in range(2):
                    pt = P.ps[pb % 8]
                    pb += 1
                    for c in range(8):
                        S.op("pe", lambda h: h.matmul(pt[:], wb[:, c, 2304 + c2 * 128:2304 + (c2 + 1) * 128],
                                                      XB[:, c, :], start=(c == 0), stop=(c == 7)),
                             reads=[wb, XB], writes=[pt])
                    S.op("act", lambda h: h.copy(uf[:, c2, :], pt[:]), reads=[pt], wacc=[uf])
                for c2 in range(2):
                    pm = P.ps[pb % 8]
                    pb += 1
                    S.op("pe", lambda h: h.matmul(pm[:], P.c_blk64[:], uf[:, c2, :], start=True, stop=True),
                         reads=[P.c_blk64, uf], writes=[pm])
                    S.op("dve", lambda h: h.tensor_tensor(uf[:, c2, :], uf[:, c2, :], pm[:], op=ALU.subtract),
                         reads=[pm, uf], wacc=[uf])
                    S.op("act", lambda h: h.activation(usq[:, c2, :], uf[:, c2, :], AF.Square),
                         reads=[uf], wacc=[usq])
                    pv = P.ps[pb % 8]
                    pb += 1
                    S.op("pe", lambda h: h.matmul(pv[:], P.c_blk64[:], usq[:, c2, :], start=True, stop=True),
                         reads=[P.c_blk64, usq], writes=[pv])
                    P.rsqrt(urs[:, c2, :], pv[:], P.c_eps_ln, reads=[pv], wacc=[urs])
                    S.op("dve", lambda h: h.tensor_tensor(uf[:, c2, :], uf[:, c2, :], urs[:, c2, :], op=ALU.mult),
                         reads=[urs, uf], wacc=[uf])
                    S.op("dve", lambda h: h.tensor_scalar(UN[:, c2, :], uf[:, c2, :], fgs[:, c2:c2 + 1], None,
                                                          op0=ALU.mult), reads=[uf, fgs], wacc=[UN])
                oo = (i - own0) * TT
                S.dma("pool", P.unT[sg][oo // 2048][:, oo % 2048:oo % 2048 + TT].rearrange("(c p) t -> p c t", p=128),
                      UN[:], reads=[UN])
        S.barrier()


def attn_pipeline(P, items, s_fn, e_fn, o_fn, look=2):
    n = len(items)
    for t in range(min(look, n)):
        s_fn(items[t], t)
    for t in range(n):
        if t + look < n:
            s_fn(items[t + look], t + look)
        e_fn(items[t], t)
        o_fn(items[t], t)


class PairAttn:
    def __init__(self, P, st, with_z):
        self.P = P
        self.with_z = with_z
        self.EB = [P.sb(st, "paEB%d" % i, [128, 1024], BF16) for i in range(4)]
        if with_z:
            self.EF = [P.sb(st, "paEF%d" % i, [128, 1024], F32) for i in range(3)]
        else:
            self.ACC = [P.sb(st, "paACC%d" % i, [128, 1024], F32) for i in range(2)]
        self.RD = [P.sb(st, "paRD%d" % i, [128, 512], F32) for i in range(2)]
        self.OUT = [P.sb(st, "paOUT%d" % i, [128, 512], BF16) for i in range(2)]

    def run(self, items, kt, qt, va, zz, out_fn, vl=None):
        P, S = self.P, self.P.S
        EB, RD, OUT = self.EB, self.RD, self.OUT

        def s_fn(itm, t):
            qi, ch, zoff, first, last = itm
            pp = P.ps2[t % 3]
            for hh in range(2):
                pb_ = hh * 64
                S.op("pe", lambda h: h.matmul(pp[:, hh * 512:(hh + 1) * 512], kt[pb_:pb_ + 64, ch * 128:(ch + 1) * 128],
                                              qt[pb_:pb_ + 64, qi * TT:(qi + 1) * TT], start=True, stop=True,
                                              tile_position=(pb_, 0)), reads=[kt, qt], writes=[pp])

        def e_fn(itm, t):
            qi, ch, zoff, first, last = itm
            pp = P.ps2[t % 3]
            eb = EB[t % 4]
            if self.with_z:
                ef = self.EF[t % 3]
                S.op("act", lambda h: h.activation(ef[:], pp[:], AF.Exp), reads=[pp], writes=[ef])
                S.op("dve", lambda h: h.tensor_tensor(eb[:, 0:512], ef[:, 0:512], zz[0][:, zoff:zoff + 512],
                                                      op=ALU.mult), reads=[ef, zz[0]], wacc=[eb])
                S.op("pool", lambda h: h.tensor_tensor(eb[:, 512:1024], ef[:, 512:1024], zz[1][:, zoff:zoff + 512],
                                                       op=ALU.mult), reads=[ef, zz[1]], wacc=[eb])
            else:
                S.op("act", lambda h: h.activation(eb[:], pp[:], AF.Exp), reads=[pp], writes=[eb])
                acc = self.ACC[qi % 2]
                for e, c0, c1 in (("dve", 0, 768), ("pool", 768, 1024)):
                    if first:
                        S.op(e, lambda h: h.tensor_copy(acc[:, c0:c1], eb[:, c0:c1]), reads=[eb], wacc=[acc])
                    else:
                        S.op(e, lambda h: h.tensor_tensor(acc[:, c0:c1], acc[:, c0:c1], eb[:, c0:c1], op=ALU.add),
                             reads=[eb, acc], wacc=[acc])

        def o_fn(itm, t):
            qi, ch, zoff, first, last = itm
            eb = EB[t % 4]
            po, pd = P.ps[6], P.ps[7]
            for hh in range(2):
                S.op("pe", lambda h: h.matmul(po[hh * 64:(hh + 1) * 64, :], va[:, ch, hh, 0:64],
                                              eb[:, hh * 512:(hh + 1) * 512], start=first, stop=last,
                                              tile_position=(0, hh * 64)), reads=[va, eb], writes=[po])
            if self.with_z:
                for hh in range(2):
                    S.op("pe", lambda h: h.matmul(pd[hh * 64:(hh + 1) * 64, :], vl[:, ch, :],
                                                  eb[:, hh * 512:(hh + 1) * 512], start=first, stop=last,
                                                  tile_position=(0, hh * 64)), reads=[vl, eb], writes=[pd])
            if not last:
                return
            rd, out = RD[qi % 2], OUT[qi % 2]
            if not self.with_z:
                acc = self.ACC[qi % 2]
                S.op("pe", lambda h: h.matmul(pd[:], P.c_onesA[:], acc[:, 0:512], start=True, stop=False),
                     reads=[P.c_onesA, acc], writes=[pd])
                S.op("pe", lambda h: h.matmul(pd[:], P.c_onesB[:], acc[:, 512:1024], start=False, stop=True),
                     reads=[P.c_onesB, acc], writes=[pd])
            S.op("dve", lambda h: h.reciprocal(rd[:], pd[:]), reads=[pd], writes=[rd])
            S.op("dve", lambda h: h.tensor_tensor(out[:], po[:], rd[:], op=ALU.mult), reads=[po, rd], writes=[out])
            out_fn(qi, out)

        attn_pipeline(P, items, s_fn, e_fn, o_fn, look=2)


def phase_dil(P):
    nc, S = P.nc, P.S
    relb = P.din("rel_bias", [32, 12])
    ohm = P.din("k_ohm", [32, 3072])
    rev = P.dscratch("dil_rev", [12, 3200], F32)
    P.OT0 = P.dscratch("OT0", [128, 8, NOWN], BF16)
    REV = Buf(rev, "rev")
    with ExitStack() as st:
        ones32 = P.sb(st, "ones32", [128, 64], F32)
        S.op("dve", lambda h: h.memset(ones32[:], 1.0), writes=[ones32])
        rb = P.sb(st, "rb", [32, 12], F32)
        eb = P.sb(st, "eb", [32, 12], F32)
        oh = P.sb(st, "oh", [32, 3072], F32)
        wt = P.sb(st, "wt", [12, 3072], F32)
        S.dma("sp", rb[:], relb[:, :], writes=[rb])
        S.dma("sp", oh[:], ohm[:, :], writes=[oh])
        S.op("act", lambda h: h.activation(eb[:], rb[:], AF.Exp), reads=[rb], writes=[eb])
        for n0 in range(0, 3072, 512):
            pt = P.ps[(n0 // 512) % 8]
            S.op("pe", lambda h: h.matmul(pt[0:12, :], eb[:], oh[:, n0:n0 + 512], start=True, stop=True),
                 reads=[eb, oh], writes=[pt])
            S.op("dve", lambda h: h.tensor_copy(wt[:, n0:n0 + 512], pt[0:12, :]), reads=[pt], wacc=[wt])
        S.dma("pool", REV[:, 0:3072], wt[:], reads=[wt], writes=[REV])
        S.barrier()
        KT = [P.sb(st, "dKT%d" % i, [128, NP_EXT], BF16) for i in range(2)]
        QT = [P.sb(st, "dQT%d" % i, [128, NP_OWN], BF16) for i in range(2)]
        VA = [P.sb(st, "dVA%d" % i, [128, NP_EXT // 128, 2, 65], BF16) for i in range(2)]
        ZZ = [[P.sb(st, "dZ%d_%d" % (i, k), [128, ZW], F32) for k in range(2)] for i in range(2)]
        HK = P.sb(st, "dHK", [128, ZW], F32)
        PA = PairAttn(P, st, with_z=True)
        VL = {}
        for sg_ in ("p", "s"):
            nch_ = (NP_EXT if sg_ == "p" else SEQ_S) // 128
            vlf = P.sb(st, "dvlf" + sg_, [128, nch_], F32)
            VL[sg_] = P.sb(st, "dvl" + sg_, [128, nch_, 64], BF16)
            S.dma("sp", vlf[:], P.vld[sg_][:, :], writes=[vlf])
            S.op("dve", lambda h: h.tensor_copy(VL[sg_][:], vlf[:].unsqueeze(2).to_broadcast([128, nch_, 64])),
                 reads=[vlf], writes=[VL[sg_]])
        it = 0
        for sg in ("p", "s"):
            n_ext = NP_EXT if sg == "p" else SEQ_S
            n_own = NP_OWN if sg == "p" else NS_OWN
            nqt = n_own // TT
            ooff = 0 if sg == "p" else NP_OWN
            nch = n_ext // 128
            for hp in range(6):
                a = it % 2
                it += 1
                kt, qt, va, zz = KT[a], QT[a], VA[a], ZZ[a]
                S.dma("sp", kt[:, 0:n_ext], P.KT0[sg][:, hp, :], writes=[kt])
                S.dma("sp", qt[:, 0:n_own], P.QT0[:, hp, ooff:ooff + n_own], writes=[qt])
                S.dma("sp", va[:, 0:nch, :, :].rearrange("p c h d -> p c (h d)"),
                      P.V0[sg][:, :, hp * 130:(hp + 1) * 130].rearrange("c p f -> p c f"), writes=[va])
                for hh in range(2):
                    h_ = 2 * hp + hh
                    src = bass.AP(tensor=rev, offset=h_ * 3200, ap=[[1, 128], [1, ZW]])
                    S.dma("sp", HK[:], src, reads=[REV], writes=[HK])
                    for n0 in range(0, ZW, 512):
                        w = min(512, ZW - n0)
                        pt = P.ps[6 + (n0 // 512) % 2]
                        S.op("pe", lambda h: h.matmul(pt[:, 0:w], P.c_antiI[:], HK[:, n0:n0 + w], start=True,
                                                      stop=True), reads=[P.c_antiI, HK], writes=[pt])
                        P.copy(P.eng2(), zz[hh][:, n0:n0 + w], pt[:, 0:w], reads=[pt], wacc=[zz[hh]])
                items = []
                for qi in range(nqt):
                    js = []
                    for j in range(20):
                        ch = 4 * qi + j - (0 if sg == "p" else 8)
                        if 0 <= ch < nch:
                            js.append((j, ch))
                    for idx, (j, ch) in enumerate(js):
                        items.append((qi, ch, 2432 - 128 * j, idx == 0, idx == len(js) - 1))

                def out_fn(qi, out, hp=hp, ooff=ooff):
                    o0 = ooff + qi * TT
                    S.dma("pool", P.OT0[:, hp, o0:o0 + TT], out[:], reads=[out])

                PA.run(items, kt, qt, va, zz, out_fn, vl=VL[sg])
        S.barrier()


GROUPS4 = [[0, 1, 2, 3], [4, 5, 6, 7]]


def phase_fnet(P):
    nc, S = P.nc, P.S
    fw = P.din("ab_fnet_w", [4, 64, 64])
    c64 = P.din("k_c64bd", [128, 128])
    s64 = P.din("k_s64bd", [128, 128])
    unall = [P.dscratch("unTall%d" % i, [1024, 2048], BF16) for i in range(2)]
    UNALL = Buf(unall)
    for i in range(2):
        S.allgather(unall[i], P.unT["p"][i], reads=[], writes=[UNALL] if i == 0 else [], groups=GROUPS4)
    S.cc_fence(GROUPS4)
    UNALL.ws = [("cc", S.cc_cnt, "dma")]
    if "unall_dbg" in P.debug:
        for i in range(2):
            dbg = P.dout("unall_dbg%d" % i, [1024, 2048], BF16)
            S.dma("sp", dbg[:, :], unall[i][:, :], reads=[UNALL])
            dbg2 = P.dout("unmine_dbg%d" % i, [256, 2048], BF16)
            S.dma("sp", dbg2[:, :], P.unT["p"][i][:, :], reads=[UNALL])
    tabs = {}
    for sg, n1 in (("p", 128), ("s", 16)):
        nk2 = 32 if sg == "p" else 128
        tabs[sg] = dict(rp=P.din("f_rp_" + sg, [n1, 2 * n1]), rr=P.din("f_rr_" + sg, [n1, 2 * n1]),
                        tc=P.din("f_tc_" + sg, [128, n1]), ts=P.din("f_ts_" + sg, [128, n1]),
                        c2=P.din("f_c2_" + sg, [128, nk2]), s2=P.din("f_s2_" + sg, [128, nk2]))
    with ExitStack() as st:
        cs = P.sb(st, "f_cs", [128, 2, 128], F32)
        S.dma("sp", cs[:, 0, :], c64[:, :], wacc=[cs])
        S.dma("sp", cs[:, 1, :], s64[:, :], wacc=[cs])
        wbd = P.sb(st, "f_wbd", [128, 128], F32)
        AB = P.sb(st, "f_AB", [128, 2, 128], BF16)
        stg = P.sb(st, "f_stg", [128, 2, 256], F32)
        tb = {k: P.sb(st, "f_t_" + k, [128, 256], BF16) for k in ("rp", "rr", "c2", "s2")}
        tcs = {k: P.sb(st, "f_t_" + k, [128, 128], F32) for k in ("tc", "ts")}
        Y = P.sb(st, "f_Y", [128, 128, 256], BF16)
        OB = P.sb(st, "f_OB", [128, NP_OWN], BF16)
        pbk = 0
        for sg in ("p", "s"):
            N1 = 128 if sg == "p" else 16
            NK2 = 32 if sg == "p" else 128
            n_own = NP_OWN if sg == "p" else NS_OWN
            nseq = SEQ_P if sg == "p" else SEQ_S
            ooff = 0 if sg == "p" else NP_OWN
            T = tabs[sg]
            for k, rows, cols in (("rp", N1, 2 * N1), ("rr", N1, 2 * N1), ("c2", 128, NK2), ("s2", 128, NK2)):
                S.dma("sp", stg[0:rows, 0, 0:cols], T[k][:, :], writes=[stg])
                S.op("dve", lambda h: h.tensor_copy(tb[k][0:rows, 0:cols], stg[0:rows, 0, 0:cols]), reads=[stg],
                     writes=[tb[k]])
            for k in ("tc", "ts"):
                S.dma("sp", tcs[k][:, 0:N1], T[k][:, :], writes=[tcs[k]])
            for gp in range(2):
                S.op("dve", lambda h: h.memset(wbd[:], 0.0), writes=[wbd])
                S.dma("sp", wbd[0:64, 0:64], fw[2 * gp, :, :], wacc=[wbd])
                S.dma("sp", wbd[64:128, 64:128], fw[2 * gp + 1, :, :], wacc=[wbd])
                for k in range(2):
                    pt = P.ps[pbk % 8]
                    pbk += 1
                    S.op("pe", lambda h: h.matmul(pt[:, 0:128], cs[:, k, :], wbd[:], start=True, stop=True),
                         reads=[cs, wbd], writes=[pt])
                    P.copy("dve", AB[:, k, :], pt[:, 0:128], reads=[pt], wacc=[AB], scale=(1.0 if k == 0 else -1.0))
                with ExitStack() as st2:
                    un = P.sb(st2, "f_un", [128, SEQ_P], BF16)
                    if sg == "p":
                        for r in range(4):
                            for hf in range(2):
                                S.dma("sp", un[:, r * 4096 + hf * 2048:r * 4096 + (hf + 1) * 2048],
                                      unall[hf][r * 256 + gp * 128:r * 256 + (gp + 1) * 128, :], reads=[UNALL],
                                      wacc=[un])
                    else:
                        S.dma("sp", un[:, 0:nseq], P.unT["s"][0][gp * 128:(gp + 1) * 128, :], wacc=[un])
                    for n2 in range(0, 128, 2):
                        pt = P.ps[pbk % 8]
                        pbk += 1
                        for d in range(2):
                            S.op("pe", lambda h: h.matmul(pt[0:N1, d * 256:(d + 1) * 256],
                                                          un[:, n2 + d:nseq:128], AB[:].rearrange("p a e -> p (a e)"),
                                                          start=True, stop=True), reads=[un, AB], writes=[pt])
                        P.copy(P.eng2(), Y[0:N1, n2:n2 + 2, :], pt[0:N1, :].rearrange("p (a c) -> p a c", a=2),
                               reads=[pt], wacc=[Y])
                    S.barrier()
                with ExitStack() as st3:
                    GP = P.sb(st3, "f_GP", [128, N1, 2, 128], BF16)
                    GS = [P.sb(st3, "f_GS%d" % i, [128, 512], F32) for i in range(2)]
                    T1 = [P.sb(st3, "f_T1%d" % i, [128, 256], F32) for i in range(2)]
                    T2 = [P.sb(st3, "f_T2%d" % i, [128, 256], F32) for i in range(2)]
                    T3 = [P.sb(st3, "f_T3%d" % i, [128, 256], F32) for i in range(2)]
                    T4 = [P.sb(st3, "f_T4%d" % i, [128, 256], F32) for i in range(2)]
                    EBn = 512 // (2 * N1)
                    nb = 0
                    for c0 in range(0, 128, EBn):
                        pt = P.ps[pbk % 8]
                        pbk += 1
                        for bi in range(EBn):
                            col = c0 + bi
                            sl = slice(bi * 2 * N1, (bi + 1) * 2 * N1)
                            S.op("pe", lambda h: h.matmul(pt[:, sl], Y[0:N1, :, col], tb["rp"][0:N1, 0:2 * N1],
                                                          start=True, stop=False), reads=[Y, tb["rp"]], writes=[pt])
                            S.op("pe", lambda h: h.matmul(pt[:, sl], Y[0:N1, :, 128 + col], tb["rr"][0:N1, 0:2 * N1],
                                                          start=False, stop=True), reads=[Y, tb["rr"]], writes=[pt])
                        a = nb % 2
                        nb += 1
                        gs, t1, t2, t3, t4 = GS[a], T1[a], T2[a], T3[a], T4[a]
                        S.op("act", lambda h: h.copy(gs[:], pt[:]), reads=[pt], writes=[gs])
                        g4 = gs[:].rearrange("p (b r k) -> p b r k", b=EBn, r=2)
                        gr, gi = g4[:, :, 0, :], g4[:, :, 1, :]
                        tcb = tcs["tc"][:, 0:N1].unsqueeze(1).to_broadcast([128, EBn, N1])
                        tsb = tcs["ts"][:, 0:N1].unsqueeze(1).to_broadcast([128, EBn, N1])
                        v = lambda t: t[:, 0:EBn * N1].rearrange("p (b k) -> p b k", b=EBn)
                        vt = lambda t: t[:, 0:EBn * N1].rearrange("p (b k) -> p k b", b=EBn)
                        S.op("dve", lambda h: h.tensor_tensor(v(t1), gr, tcb, op=ALU.mult),
                             reads=[gs, tcs["tc"]], writes=[t1])
                        S.op("dve", lambda h: h.tensor_tensor(v(t2), gi, tsb, op=ALU.mult),
                             reads=[gs, tcs["ts"]], writes=[t2])
                        S.op("pool", lambda h: h.tensor_tensor(v(t3), gi, tcb, op=ALU.mult),
                             reads=[gs, tcs["tc"]], writes=[t3])
                        S.op("pool", lambda h: h.tensor_tensor(v(t4), gr, tsb, op=ALU.mult),
                             reads=[gs, tcs["ts"]], writes=[t4])
                        S.op("dve", lambda h: h.tensor_tensor(GP[:, :, 0, c0:c0 + EBn], vt(t1), vt(t2), op=ALU.add),
                             reads=[t1, t2], wacc=[GP])
                        S.op("pool", lambda h: h.tensor_tensor(GP[:, :, 1, c0:c0 + EBn], vt(t3), vt(t4),
                                                               op=ALU.subtract), reads=[t3, t4], wacc=[GP])
                    KB = 512 // NK2
                    for k0 in range(0, N1, KB):
                        pt = P.ps[pbk % 8]
                        pbk += 1
                        for kk in range(KB):
                            k1 = k0 + kk
                            sl = slice(kk * NK2, (kk + 1) * NK2)
                            S.op("pe", lambda h: h.matmul(pt[:, sl], GP[:, k1, 0, :], tb["c2"][:, 0:NK2], start=True,
                                                          stop=False), reads=[GP, tb["c2"]], writes=[pt])
                            S.op("pe", lambda h: h.matmul(pt[:, sl], GP[:, k1, 1, :], tb["s2"][:, 0:NK2], start=False,
                                                          stop=True), reads=[GP, tb["s2"]], writes=[pt])
                        ov = OB[:, 0:n_own].rearrange("p (k2 k1) -> p k1 k2", k1=N1)[:, k0:k0 + KB, :]
                        P.copy(P.eng2(), ov, pt[:].rearrange("p (a b) -> p a b", a=KB), reads=[pt], wacc=[OB])
                    S.dma("pool", P.OT0[:, 6 + gp, ooff:ooff + n_own], OB[:, 0:n_own], reads=[OB])
                    S.barrier()
        S.barrier()


class RowBufs:
    def __init__(self, P, st):
        self.xr = [P.sb(st, "rb_xr%d" % i, [128, 8, TT], F32) for i in range(2)]
        self.r = P.sb(st, "rb_r", [128, 8, TT], F32)
        self.sq = P.sb(st, "rb_sq", [128, 8, TT], F32)
        self.rstd = P.sb(st, "rb_rstd", [128, TT], F32)
        self.of = [P.sb(st, "rb_of%d" % i, [128, 8, TT], F32) for i in range(1)]
        self.ob = [P.sb(st, "rb_ob%d" % i, [128, 8, TT], BF16) for i in range(1)]


def linear_resid_ln(P, RB, i, wb, kcin, src, xr, lnidx, outf_d, outb_d, pbase):
    S = P.S
    r, sq, rstd = RB.r, RB.sq, RB.rstd
    of, ob = RB.of[0], RB.ob[0]
    for oc in range(8):
        pt = P.ps[(pbase + oc) % 8]
        for c in range(kcin):
            S.op("pe", lambda h: h.matmul(pt[:], wb[:, c, oc * 128:(oc + 1) * 128], src[:, c, :], start=(c == 0),
                                          stop=(c == kcin - 1)), reads=[wb, src], writes=[pt])
        S.op("dve", lambda h: h.scalar_tensor_tensor(r[:, oc, :], xr[:, oc, :], DN_ALPHA, pt[:], op0=ALU.mult,
                                                     op1=ALU.add), reads=[xr, pt], wacc=[r])
    layer_norm_fm(P, r, sq, rstd, lnidx, of, ob, pbase)
    t0 = i * TT
    if outf_d is not None:
        S.dma("pool", outf_d[:, :, t0:t0 + TT], of[:], reads=[of])
    if outb_d is not None:
        S.dma("pool", outb_d[:, :, t0:t0 + TT], ob[:], reads=[ob])


def layer_norm_fm(P, r, sq, rstd, lnidx, of, ob, pbase):
    S = P.S
    pm = P.ps[(pbase + 0) % 8]
    pv = P.ps[(pbase + 1) % 8]
    for c in range(8):
        S.op("pe", lambda h: h.matmul(pm[:], P.c_onesd[:], r[:, c, :], start=(c == 0), stop=(c == 7)),
             reads=[P.c_onesd, r], writes=[pm])
    S.op("dve", lambda h: h.tensor_tensor(r[:], r[:], pm[:].unsqueeze(1).to_broadcast([128, 8, TT]),
                                          op=ALU.subtract), reads=[r, pm], wacc=[r])
    S.op("act", lambda h: h.activation(sq[:], r[:], AF.Square), reads=[r], writes=[sq])
    for c in range(8):
        S.op("pe", lambda h: h.matmul(pv[:], P.c_onesd[:], sq[:, c, :], start=(c == 0), stop=(c == 7)),
             reads=[P.c_onesd, sq], writes=[pv])
    P.rsqrt(rstd[:], pv[:], P.c_eps_ln, reads=[pv], wacc=[rstd])
    S.op("dve", lambda h: h.tensor_tensor(r[:], r[:], rstd[:].unsqueeze(1).to_broadcast([128, 8, TT]),
                                          op=ALU.mult), reads=[r, rstd], wacc=[r])
    for c in range(8):
        S.op("act", lambda h: h.activation(of[:, c, :], r[:, c, :], AF.Identity,
                                           bias=P.c_lnb[:, lnidx, c:c + 1], scale=P.c_lng[:, lnidx, c:c + 1]),
             reads=[r, P.c_lng, P.c_lnb], wacc=[of])
    S.op("pool", lambda h: h.tensor_copy(ob[:], of[:]), reads=[of], writes=[ob])


def phase_linear_ln(P, name, w_d, kcin, src_d, xres_fn, lnidx, outf_d, outb_d):
    S = P.S
    with ExitStack() as st:
        stg = [P.sb(st, "stg%d" % i, [128, 1024], F32) for i in range(2)]
        wb = P.load_w(st, "w_" + name, w_d, kcin * 128, D, stg)
        RB = RowBufs(P, st)
        srcs = [P.sb(st, "src%d" % i, [128, kcin, TT], BF16) for i in range(2)]
        for i in range(NTILE):
            sb_, xr = srcs[i % 2], RB.xr[i % 2]
            S.dma("sp", sb_[:], src_d[:, :, i * TT:(i + 1) * TT], writes=[sb_])
            S.dma("sp", xr[:], xres_fn(i), writes=[xr])
            linear_resid_ln(P, RB, i, wb, kcin, sb_, xr, lnidx, outf_d, outb_d, pbase=(i * 2) % 8)
        S.barrier()


def phase_xattn(P, layer, xf_d, xb_d, outf_d, outb_d):
    S = P.S
    wq_d = P.din("xa_w_q%d" % layer, [D, D])
    wkv_d = P.din("xa_w_kv%d" % layer, [D, 2 * D])
    wo_d = P.din("xa_w_o%d" % layer, [D, D])
    if not hasattr(P, "memT"):
        P.memT = {"p": P.din("memTp", [128, 8, 256]), "s": P.din("memTs", [128, 8, 256])}
    lnidx = layer * 3 + 1
    with ExitStack() as st:
        stg = [P.sb(st, "stg%d" % i, [128, 1024], F32) for i in range(2)]
        wq = P.load_w(st, "w_xq", wq_d, D, D, stg)
        wo = P.load_w(st, "w_xo", wo_d, D, D, stg)
        memK = {sg: P.sb(st, "memK" + sg, [128, 8, 256], BF16) for sg in ("p", "s")}
        memV = {sg: P.sb(st, "memV" + sg, [128, 2, D], BF16) for sg in ("p", "s")}
        with ExitStack() as st2:
            wkv = P.load_w(st2, "w_xkv", wkv_d, D, 2 * D, stg)
            mf = P.sb(st2, "memf", [128, 8, 256], F32)
            mb = P.sb(st2, "memb", [128, 8, 256], BF16)
            pb = 0
            for sg in ("p", "s"):
                S.dma("sp", mf[:], P.memT[sg][:, :, :], writes=[mf])
                S.op("dve", lambda h: h.tensor_copy(mb[:], mf[:]), reads=[mf], writes=[mb])
                for oc in range(8):
                    pt = P.ps[pb % 8]
                    pb += 1
                    for c in range(8):
                        S.op("pe", lambda h: h.matmul(pt[:, 0:256], wkv[:, c, oc * 128:(oc + 1) * 128], mb[:, c, :],
                                                      start=(c == 0), stop=(c == 7)), reads=[wkv, mb], writes=[pt])
                    P.copy(P.eng2(), memK[sg][:, oc, :], pt[:, 0:256], reads=[pt], wacc=[memK[sg]])
                for mc in range(2):
                    for n0 in range(2):
                        pt = P.ps[pb % 8]
                        pb += 1
                        for c in range(8):
                            S.op("pe", lambda h: h.matmul(pt[:], mb[:, c, mc * 128:(mc + 1) * 128],
                                                          wkv[:, c, D + n0 * 512:D + (n0 + 1) * 512],
                                                          start=(c == 0), stop=(c == 7)), reads=[wkv, mb], writes=[pt])
                        P.copy(P.eng2(), memV[sg][:, mc, n0 * 512:(n0 + 1) * 512], pt[:], reads=[pt],
                               wacc=[memV[sg]])
            S.barrier()
        RB = RowBufs(P, st)
        xbs = [P.sb(st, "xa_xb%d" % i, [128, 8, TT], BF16) for i in range(2)]
        qb = P.sb(st, "xa_q", [128, 8, TT], BF16)
        ob_ = P.sb(st, "xa_o", [128, 8, TT], BF16)
        ee = [P.sb(st, "xa_e%d" % i, [128, 2, TT], BF16) for i in range(2)]
        rden = [P.sb(st, "xa_rd%d" % i, [128, TT], F32) for i in range(2)]
        pb = 0
        for i in range(NTILE):
            sg = "p" if i < NP_OWN // TT else "s"
            xb, xr = xbs[i % 2], RB.xr[i % 2]
            S.dma("sp", xb[:], xb_d[:, :, i * TT:(i + 1) * TT], writes=[xb])
            S.dma("sp", xr[:], xf_d[:, :, i * TT:(i + 1) * TT], writes=[xr])
            for oc in range(8):
                pt = P.ps[pb % 8]
                pb += 1
                for c in range(8):
                    S.op("pe", lambda h: h.matmul(pt[:], wq[:, c, oc * 128:(oc + 1) * 128], xb[:, c, :],
                                                  start=(c == 0), stop=(c == 7)), reads=[wq, xb], writes=[pt])
                P.copy(P.eng2(), qb[:, oc, :], pt[:], reads=[pt], wacc=[qb], scale=1.0 / 16)
            for hh in range(4):
                E, RD = ee[hh % 2], rden[hh % 2]
                for mc in range(2):
                    pt = P.ps[pb % 8]
                    pb += 1
                    for cc in range(2):
                        S.op("pe", lambda h: h.matmul(pt[:], memK[sg][:, 2 * hh + cc, mc * 128:(mc + 1) * 128],
                                                      qb[:, 2 * hh + cc, :], start=(cc == 0), stop=(cc == 1)),
                             reads=[memK[sg], qb], writes=[pt])
                    S.op("act", lambda h: h.activation(E[:, mc, :], pt[:], AF.Exp), reads=[pt], wacc=[E])
                pd = P.ps[pb % 8]
                pb += 1
                for mc in range(2):
                    S.op("pe", lambda h: h.matmul(pd[:], P.c_onesb[:], E[:, mc, :], start=(mc == 0), stop=(mc == 1)),
                         reads=[P.c_onesb, E], writes=[pd])
                S.op("dve", lambda h: h.reciprocal(RD[:], pd[:]), reads=[pd], writes=[RD])
                for dvc in range(2):
                    po = P.ps[pb % 8]
                    pb += 1
                    for mc in range(2):
                        S.op("pe", lambda h: h.matmul(po[:], memV[sg][:, mc, hh * 256 + dvc * 128:hh * 256 + (dvc + 1) * 128],
                                                      E[:, mc, :], start=(mc == 0), stop=(mc == 1)),
                             reads=[memV[sg], E], writes=[po])
                    S.op("dve", lambda h: h.tensor_tensor(ob_[:, 2 * hh + dvc, :], po[:], RD[:], op=ALU.mult),
                         reads=[po, RD], wacc=[ob_])
            linear_resid_ln(P, RB, i, wo, 8, ob_, xr, lnidx, outf_d, outb_d, pbase=pb % 8)
            pb += 2
        S.barrier()


def phase_ffn1(P, layer, xb_d, h_d):
    S = P.S
    w_d = P.din("ffn_w_in%d" % layer, [D, 2 * FFN_H])
    with ExitStack() as st:
        stg = [P.sb(st, "stg%d" % i, [128, 1024], F32) for i in range(2)]
        wb = P.load_w(st, "w_ffn_in", w_d, D, 2 * FFN_H, stg)
        xbs = [P.sb(st, "f1_xb%d" % i, [128, 8, TT], BF16) for i in range(2)]
        hid = [P.sb(st, "f1_h%d" % i, [128, HC, TT], BF16) for i in range(2)]
        sgb = [P.sb(st, "f1_sg%d" % i, [128, TT], F32) for i in range(3)]
        pb = 0
        for i in range(NTILE):
            xb, hd = xbs[i % 2], hid[i % 2]
            S.dma("sp", xb[:], xb_d[:, :, i * TT:(i + 1) * TT], writes=[xb])
            for hc in range(HC):
                pg = P.ps[pb % 8]
                pu = P.ps[(pb + 1) % 8]
                pb += 2
                for c in range(8):
                    S.op("pe", lambda h: h.matmul(pg[:], wb[:, c, hc * 128:(hc + 1) * 128], xb[:, c, :],
                                                  start=(c == 0), stop=(c == 7)), reads=[wb, xb], writes=[pg])
                for c in range(8):
                    S.op("pe", lambda h: h.matmul(pu[:], wb[:, c, FFN_H + hc * 128:FFN_H + (hc + 1) * 128], xb[:, c, :],
                                                  start=(c == 0), stop=(c == 7)), reads=[wb, xb], writes=[pu])
                sgt = sgb[hc % 3]
                S.op("act", lambda h: h.activation(sgt[:], pg[:], AF.Silu), reads=[pg], writes=[sgt])
                S.op("dve", lambda h: h.tensor_tensor(hd[:, hc, :], sgt[:], pu[:], op=ALU.mult), reads=[sgt, pu],
                     wacc=[hd])
            S.dma("pool", h_d[:, :, i * TT:(i + 1) * TT], hd[:], reads=[hd])
        S.barrier()


def layer_tail(P, layer, mix_d, w_out_d, xres_fn):
    L = "L%d" % layer
    x1f = P.dscratch(L + "x1f", [128, 8, NOWN], F32)
    x1b = P.dscratch(L + "x1b", [128, 8, NOWN], BF16)
    phase_linear_ln(P, L + "mixout", w_out_d, 8, mix_d, xres_fn, layer * 3 + 0, x1f, x1b)
    x2f = P.dscratch(L + "x2f", [128, 8, NOWN], F32)
    x2b = P.dscratch(L + "x2b", [128, 8, NOWN], BF16)
    phase_xattn(P, layer, x1f, x1b, x2f, x2b)
    hd = P.dscratch(L + "hid", [128, HC, NOWN], BF16)
    phase_ffn1(P, layer, x2b, hd)
    if layer == 1:
        x3f = P.dout("yT", [128, 8, NOWN], F32)
        x3b = None
    else:
        x3f = P.dscratch(L + "x3f", [128, 8, NOWN], F32)
        x3b = P.dscratch(L + "x3b", [128, 8, NOWN], BF16)
    w2 = P.din("ffn_w_out%d" % layer, [FFN_H, D])
    phase_linear_ln(P, L + "ffnout", w2, HC, hd, lambda i: x2f[:, :, i * TT:(i + 1) * TT], layer * 3 + 2, x3f, x3b)
    return x3f, x3b


QPERM = [0, 3, 1, 4, 2, 5, 6, 9, 7, 10, 8, 11]


def phase_proj1(P, xb_d):
    nc, S = P.nc, P.S
    w_d = P.din("cd_w_in_p", [D, 1536])
    gains_d = P.din("qk_gain_r", [128, 16, 64])
    cos_d = P.din("rope_cos", [128, NOWN // 128, 32])
    sin_d = P.din("rope_sin", [128, NOWN // 128, 32])
    P.QT1 = P.dscratch("QT1", [128, 6, NOWN], BF16)
    P.KT1 = {"p": [P.dscratch("KT1p%d" % i, [256, 2048], BF16) for i in range(2)],
             "s": [P.dscratch("KT1s", [256, NS_OWN], BF16)]}
    P.V1 = {"p": [P.dscratch("V1p%d" % i, [1024, 260], BF16) for i in range(4)],
            "s": [P.dscratch("V1s%d" % i, [1024, 260], BF16) for i in range(2)]}
    P.U1 = P.dscratch("U1", [128, 2, NOWN], F32)
    with ExitStack() as st:
        stg = [P.sb(st, "stg%d" % i, [128, 1024], F32) for i in range(2)]
        wb = P.load_w(st, "w_cd_in", w_d, D, 1536, stg)
        gains = P.sb(st, "p1_gain", [128, 16, 64], F32)
        S.dma("sp", gains[:], gains_d[:, :, :], writes=[gains])
        S.op("dve", lambda h: h.tensor_scalar_mul(gains[:, 0:12, :], gains[:, 0:12, :], 0.125), reads=[gains],
             wacc=[gains])
        cs = P.sb(st, "p1_cos", [128, NOWN // 128, 32], F32)
        sn = P.sb(st, "p1_sin", [128, NOWN // 128, 32], F32)
        S.dma("sp", cs[:], cos_d[:, :, :], writes=[cs])
        S.dma("sp", sn[:], sin_d[:, :, :], writes=[sn])
        xbs = [P.sb(st, "p1_xb%d" % i, [128, 8, TT], BF16) for i in range(2)]
        sq = P.sb(st, "p1_sq", [128, 1024], F32)
        ss = P.sb(st, "p1_ss", [128, 16], F32)
        rs = P.sb(st, "p1_rs", [128, 16], F32)
        qn = P.sb(st, "p1_qn", [128, 16, 64], F32)
        tt = [P.sb(st, "p1_t%d" % i, [128, 16, 32], F32) for i in range(4)]
        qr = P.sb(st, "p1_qr", [128, 4, 16, 64], BF16)
        qts = [P.sb(st, "p1_qt%d" % i, [128, 8, TT], BF16) for i in range(2)]
        vs = [P.sb(st, "p1_vs%d" % i, [128, 4, 4, 65], BF16) for i in range(2)]
        us = [P.sb(st, "p1_us%d" % i, [128, 2, TT], F32) for i in range(2)]
        for v_ in vs:
            S.op("dve", lambda h: h.memset(v_[:], 1.0), writes=[v_])
        pb = 0
        for i in range(NTILE):
            sg = "p" if i < 8 else "s"
            il = i if sg == "p" else i - 8
            xb, QT, VS, US = xbs[i % 2], qts[i % 2], vs[i % 2], us[i % 2]
            S.dma("sp", xb[:], xb_d[:, :, i * TT:(i + 1) * TT], writes=[xb])
            for sub in range(4):
                pa, pbk = P.ps[pb % 8], P.ps[(pb + 1) % 8]
                pv = P.ps[(pb + 2) % 8]
                pb += 3
                for c in range(8):
                    S.op("pe", lambda h: h.matmul(pa[:], xb[:, c, sub * 128:(sub + 1) * 128], wb[:, c, 0:512],
                                                  start=(c == 0), stop=(c == 7)), reads=[wb, xb], writes=[pa])
                for c in range(8):
                    S.op("pe", lambda h: h.matmul(pbk[:], xb[:, c, sub * 128:(sub + 1) * 128], wb[:, c, 512:1024],
                                                  start=(c == 0), stop=(c == 7)), reads=[wb, xb], writes=[pbk])
                for c in range(8):
                    S.op("pe", lambda h: h.matmul(pv[:, 0:256], xb[:, c, sub * 128:(sub + 1) * 128],
                                                  wb[:, c, 1024:1280], start=(c == 0), stop=(c == 7)),
                         reads=[wb, xb], writes=[pv])
                S.op("act", lambda h: h.copy(VS[:, sub, :, 0:64], pv[:, 0:256].rearrange("p (h d) -> p h d", d=64)),
                     reads=[pv], wacc=[VS])
                S.op("act", lambda h: h.activation(sq[:, 0:512], pa[:], AF.Square), reads=[pa], wacc=[sq])
                S.op("act", lambda h: h.activation(sq[:, 512:1024], pbk[:], AF.Square), reads=[pbk], wacc=[sq])
                S.op("dve", lambda h: h.tensor_reduce(ss[:], sq[:].rearrange("p (h d) -> p h d", d=64), axis=AX.X,
                                                      op=ALU.add), reads=[sq], writes=[ss])
                S.op("dve", lambda h: h.tensor_scalar(ss[:], ss[:], 1.0 / 64, None, op0=ALU.mult), reads=[ss],
                     wacc=[ss])
                P.rsqrt(rs[:], ss[:], P.c_eps_rms, reads=[ss], wacc=[rs])
                rsb = rs[:].unsqueeze(2).to_broadcast([128, 16, 64])
                S.op("dve", lambda h: h.tensor_tensor(qn[:, 0:8, :], pa[:].rearrange("p (h d) -> p h d", d=64),
                                                      rsb[:, 0:8, :], op=ALU.mult), reads=[pa, rs], wacc=[qn])
                S.op("dve", lambda h: h.tensor_tensor(qn[:, 8:16, :], pbk[:].rearrange("p (h d) -> p h d", d=64),
                                                      rsb[:, 8:16, :], op=ALU.mult), reads=[pbk, rs], wacc=[qn])
                S.op("pool", lambda h: h.tensor_tensor(qn[:], qn[:], gains[:], op=ALU.mult), reads=[qn, gains],
                     wacc=[qn])
                q4 = qn[:].rearrange("p h (i two) -> p h i two", two=2)
                x1, x2 = q4[:, :, :, 0], q4[:, :, :, 1]
                gsub = i * 4 + sub
                cb = cs[:, gsub, :].unsqueeze(1).to_broadcast([128, 16, 32])
                sb_ = sn[:, gsub, :].unsqueeze(1).to_broadcast([128, 16, 32])
                o4 = qr[:, sub, :, :].rearrange("p h (i two) -> p h i two", two=2)
                S.op("dve", lambda h: h.tensor_tensor(tt[0][:], x1, cb, op=ALU.mult), reads=[qn, cs], writes=[tt[0]])
                S.op("pool", lambda h: h.tensor_tensor(tt[1][:], x2, sb_, op=ALU.mult), reads=[qn, sn],
                     writes=[tt[1]])
                S.op("pool", lambda h: h.tensor_tensor(tt[2][:], x1, sb_, op=ALU.mult), reads=[qn, sn],
                     writes=[tt[2]])
                S.op("dve", lambda h: h.tensor_tensor(tt[3][:], x2, cb, op=ALU.mult), reads=[qn, cs], writes=[tt[3]])
                S.op("dve", lambda h: h.tensor_tensor(o4[:, :, :, 0], tt[0][:], tt[1][:], op=ALU.subtract),
                     reads=[tt[0], tt[1]], wacc=[qr])
                S.op("pool", lambda h: h.tensor_tensor(o4[:, :, :, 1], tt[2][:], tt[3][:], op=ALU.add),
                     reads=[tt[2], tt[3]], wacc=[qr])
            for fc in range(8):
                pt = P.ps[pb % 8]
                pb += 1
                ptb = pt[:].bitcast(BF16)
                for sub in range(4):
                    S.op("pe", lambda h: h.transpose(ptb[:, sub * 128:(sub + 1) * 128],
                                                     qr[:, sub, 2 * fc:2 * fc + 2, :].rearrange("p h d -> p (h d)"),
                                                     P.c_identb[:]), reads=[qr, P.c_identb], writes=[pt])
                P.copy(P.eng2(), QT[:, fc, :], ptb[:, 0:512], reads=[pt], wacc=[QT])
            t0 = i * TT
            S.dma("pool", P.QT1[:, :, t0:t0 + TT], QT[:, 0:6, :], reads=[QT])
            tl = il * TT
            S.dma("pool", P.KT1[sg][tl // 2048][:, tl % 2048:tl % 2048 + TT].rearrange("(c p) t -> p c t", p=128),
                  QT[:, 6:8, :], reads=[QT])
            S.dma("pool", P.V1[sg][tl // 1024][tl % 1024:tl % 1024 + TT, :].rearrange("(s p) f -> p s f", p=128),
                  VS[:].rearrange("p s h d -> p s (h d)"), reads=[VS])
            for c2 in range(2):
                pt = P.ps[pb % 8]
                pb += 1
                for c in range(8):
                    S.op("pe", lambda h: h.matmul(pt[:], wb[:, c, 1280 + c2 * 128:1280 + (c2 + 1) * 128], xb[:, c, :],
                                                  start=(c == 0), stop=(c == 7)), reads=[wb, xb], writes=[pt])
                P.copy(P.eng2(), US[:, c2, :], pt[:], reads=[pt], wacc=[US])
            S.dma("pool", P.U1[:, :, t0:t0 + TT], US[:], reads=[US])
        S.barrier()


def phase_gqa(P):
    nc, S = P.nc, P.S
    P.OT1 = P.dscratch("OT1", [128, 8, NOWN], BF16)
    ktall = [P.dscratch("KT1all%d" % i, [1024, 2048], BF16) for i in range(2)]
    vall = [P.dscratch("V1all%d" % i, [4096, 260], BF16) for i in range(4)]
    G = Buf(None)
    for i in range(2):
        S.allgather(ktall[i], P.KT1["p"][i], reads=[], writes=[], groups=GROUPS4)
    for i in range(4):
        S.allgather(vall[i], P.V1["p"][i], reads=[], writes=[], groups=GROUPS4)
    S.cc_fence(GROUPS4)
    G.ws = [("cc", S.cc_cnt, "dma")]
    with ExitStack() as st:
        ones32 = P.sb(st, "g_ones32", [128, 64], F32)
        S.op("dve", lambda h: h.memset(ones32[:], 1.0), writes=[ones32])
        KT = P.sb(st, "gKT", [128, SEQ_P], BF16)
        VA = P.sb(st, "gVA", [128, SEQ_P // 128, 2, 65], BF16)
        QTs = [P.sb(st, "gQT%d" % i, [128, NP_OWN], BF16) for i in range(2)]
        PA = PairAttn(P, st, with_z=False)
        qi_ = 0
        for sg in ("p", "s"):
            nseq = SEQ_P if sg == "p" else SEQ_S
            n_own = NP_OWN if sg == "p" else NS_OWN
            ooff = 0 if sg == "p" else NP_OWN
            nch = nseq // 128
            nqt = n_own // TT
            for kc in range(2):
                if sg == "p":
                    for r in range(4):
                        for hf in range(2):
                            S.dma("sp", KT[:, r * 4096 + hf * 2048:r * 4096 + (hf + 1) * 2048],
                                  ktall[hf][r * 256 + kc * 128:r * 256 + (kc + 1) * 128, :], reads=[G], wacc=[KT])
                        for j in range(4):
                            c0 = (r * 4096 + j * 1024) // 128
                            S.dma("sp", VA[:, c0:c0 + 8, :, :].rearrange("p c h d -> p c (h d)"),
                                  vall[j][r * 1024:(r + 1) * 1024, kc * 130:(kc + 1) * 130].rearrange(
                                      "(c p) f -> p c f", p=128), reads=[G], wacc=[VA])
                else:
                    S.dma("sp", KT[:, 0:nseq], P.KT1["s"][0][kc * 128:(kc + 1) * 128, :], wacc=[KT])
                    for j in range(2):
                        S.dma("sp", VA[:, j * 8:(j + 1) * 8, :, :].rearrange("p c h d -> p c (h d)"),
                              P.V1["s"][j][:, kc * 130:(kc + 1) * 130].rearrange("(c p) f -> p c f", p=128),
                              wacc=[VA])
                for j in range(3):
                    qc = 3 * kc + j
                    QT = QTs[qi_ % 2]
                    qi_ += 1
                    S.dma("sp", QT[:, 0:n_own], P.QT1[:, qc, ooff:ooff + n_own], writes=[QT])
                    items = []
                    for qi in range(nqt):
                        for ch in range(nch):
                            items.append((qi, ch, 0, ch == 0, ch == nch - 1))

                    def out_fn(qi, out, qc=qc, ooff=ooff):
                        o0 = ooff + qi * TT
                        S.dma("pool", P.OT1[:, qc, o0:o0 + TT], out[:], reads=[out])

                    PA.run(items, KT, QT, VA, None, out_fn)
        S.barrier()


def phase_pool(P):
    nc, S = P.nc, P.S
    pw = P.din("cd_pool_w", [4, 64, 64])
    psc = P.din("pool_scale_r", [128, 2])
    rc_d = P.din("pool_rc", [128, 2, NOWN])
    sel_d = P.din("pool_sel", [128, 8])
    edge = P.dscratch("pool_edge", [256, 16], F32)
    edall = P.dscratch("pool_edall", [1024, 16], F32)
    EDG, EDA = Buf(edge), Buf(edall)
    with ExitStack() as st:
        psc_s = P.sb(st, "pl_sc", [128, 2], F32)
        sel = P.sb(st, "pl_sel", [128, 8], F32)
        S.dma("sp", psc_s[:], psc[:, :], writes=[psc_s])
        S.dma("sp", sel[:], sel_d[:, :], writes=[sel])
        eg = P.sb(st, "pl_eg", [128, 2, 16], F32)
        S.dma("sp", eg[:, :, 0:8], P.U1[:, :, 0:8], wacc=[eg])
        S.dma("sp", eg[:, :, 8:16], P.U1[:, :, NP_OWN - 8:NP_OWN], wacc=[eg])
        S.dma("pool", edge.ap().rearrange("(c p) t -> p c t", p=128), eg[:], reads=[eg], writes=[EDG])
        S.allgather(edall, edge, reads=[EDG], writes=[EDA], groups=GROUPS4)
        S.cc_fence(GROUPS4)
        EDA.ws = [("cc", S.cc_cnt, "dma")]
        ea = P.sb(st, "pl_ea", [128, 4, 2, 16], F32)
        for r in range(4):
            S.dma("sp", ea[:, r, :, :], edall[r * 256:(r + 1) * 256, :].rearrange("(c p) t -> p c t", p=128),
                  reads=[EDA], wacc=[ea])
        wbd = P.sb(st, "pl_wbd", [128, 128], F32)
        wbb = P.sb(st, "pl_wbb", [128, 128], BF16)
        NE = NP_OWN + 16
        ue = P.sb(st, "pl_ue", [128, NE], F32)
        sA = P.sb(st, "pl_sA", [128, NE], F32)
        sB = P.sb(st, "pl_sB", [128, NE], F32)
        rc = P.sb(st, "pl_rc", [128, NP_OWN], F32)
        mx = P.sb(st, "pl_mx", [128, NP_OWN], BF16)
        ot = [P.sb(st, "pl_ot%d" % i, [128, TT], BF16) for i in range(2)]
        pb = 0
        for sg in ("p", "s"):
            n = NP_OWN if sg == "p" else NS_OWN
            ooff = 0 if sg == "p" else NP_OWN
            for c2 in range(2):
                S.op("dve", lambda h: h.memset(wbd[:], 0.0), writes=[wbd])
                S.dma("sp", wbd[0:64, 0:64], pw[2 * c2, :, :], wacc=[wbd])
                S.dma("sp", wbd[64:128, 64:128], pw[2 * c2 + 1, :, :], wacc=[wbd])
                S.op("dve", lambda h: h.tensor_copy(wbb[:], wbd[:]), reads=[wbd], writes=[wbb])
                S.op("dve", lambda h: h.memset(ue[:, 0:8], 0.0), wacc=[ue])
                S.op("dve", lambda h: h.memset(ue[:, 8 + n:16 + n], 0.0), wacc=[ue])
                S.dma("sp", ue[:, 8:8 + n], P.U1[:, c2, ooff:ooff + n], wacc=[ue])
                S.dma("sp", rc[:, 0:n], rc_d[:, c2, ooff:ooff + n], writes=[rc])
                if sg == "p":
                    for r in range(4):
                        S.op("dve", lambda h: h.scalar_tensor_tensor(ue[:, 0:8], ea[:, r, c2, 8:16], sel[:, r:r + 1],
                                                                     ue[:, 0:8], op0=ALU.mult, op1=ALU.add),
                             reads=[ea, sel, ue], wacc=[ue])
                        S.op("dve", lambda h: h.scalar_tensor_tensor(ue[:, 8 + n:16 + n], ea[:, r, c2, 0:8],
                                                                     sel[:, 4 + r:5 + r], ue[:, 8 + n:16 + n],
                                                                     op0=ALU.mult, op1=ALU.add),
                             reads=[ea, sel, ue], wacc=[ue])
                S.op("dve", lambda h: h.tensor_tensor(sA[:, 1:16 + n], ue[:, 0:15 + n], ue[:, 1:16 + n], op=ALU.add),
                     reads=[ue], writes=[sA])
                S.op("dve", lambda h: h.tensor_tensor(sB[:, 2:15 + n], sA[:, 1:14 + n], sA[:, 3:16 + n], op=ALU.add),
                     reads=[sA], writes=[sB])
                if c2 == 1:
                    S.op("dve", lambda h: h.tensor_tensor(sA[:, 4:13 + n], sB[:, 2:11 + n], sB[:, 6:15 + n],
                                                          op=ALU.add), reads=[sB], writes=[sA])
                    S.op("dve", lambda h: h.tensor_tensor(sB[:, 8:8 + n], sA[:, 4:4 + n], sA[:, 12:12 + n],
                                                          op=ALU.add), reads=[sA], writes=[sB])
                for half, sbuf_ in ((0, sA), (1, sB)):
                    ps_ = slice(half * 64, (half + 1) * 64)
                    S.op("dve", lambda h: h.tensor_tensor(sbuf_[ps_, 8:8 + n], sbuf_[ps_, 8:8 + n], rc[ps_, 0:n],
                                                          op=ALU.mult), reads=[sbuf_, rc], wacc=[sbuf_])
                    S.op("dve", lambda h: h.tensor_tensor(mx[ps_, 0:n], sbuf_[ps_, 8:8 + n], ue[ps_, 8:8 + n],
                                                          op=ALU.subtract), reads=[sbuf_, ue], wacc=[mx])
                for ti in range(n // TT):
                    pt = P.ps[pb % 8]
                    pb += 1
                    o = ot[ti % 2]
                    S.op("pe", lambda h: h.matmul(pt[:], wbb[:], mx[:, ti * TT:(ti + 1) * TT], start=True, stop=True),
                         reads=[wbb, mx], writes=[pt])
                    S.op("act", lambda h: h.activation(o[:], pt[:], AF.Identity, scale=psc_s[:, c2:c2 + 1]),
                         reads=[pt, psc_s], writes=[o])
                    o0 = ooff + ti * TT
                    S.dma("pool", P.OT1[:, 6 + c2, o0:o0 + TT], o[:], reads=[o])
        S.barrier()


def host_inputs(inp, names):
    cc = _consts_common()
    f32 = lambda a: np.ascontiguousarray(a, dtype=np.float32)
    maps = []
    for c in range(NCORES):
        b, q = c // 4, c % 4
        o0 = q * NP_OWN
        m = {}
        for nm in names:
            if nm.startswith("k_"):
                m[nm] = cc[nm[2:]]
            elif nm == "xTp":
                ext = np.zeros((NP_EXT, D), np.float32)
                lo, hi = o0 - 1024, o0 + NP_OWN + 1024
                a, bb = max(lo, 0), min(hi, SEQ_P)
                ext[a - lo:bb - lo] = inp["x_prompt"][b, a:bb]
                m[nm] = _fm(ext.T)
            elif nm == "vldp":
                t = np.arange(o0 - 1024, o0 + NP_OWN + 1024)
                v = ((t >= 0) & (t < SEQ_P)).astype(np.float32)
                m[nm] = f32(v.reshape(NP_EXT // 128, 128).T)
            elif nm == "xTs":
                m[nm] = _fm(f32(inp["x_sample"][c]).T)
            elif nm == "vlds":
                m[nm] = np.ones((128, SEQ_S // 128), np.float32)
            elif nm == "ab_fnet_w":
                m[nm] = f32(inp["ab_fnet_w"][0])
            elif nm.startswith("f_"):
                kind, sg = nm[2:4], nm[5]
                if sg == "p":
                    tabs = _dft_tables(SEQ_P, 128, 128, list(range(32 * q, 32 * q + 32)))
                else:
                    tabs = _dft_tables(SEQ_S, 16, 128, list(range(128)))
                m[nm] = tabs[["rp", "rr", "tc", "ts", "c2", "s2"].index(kind)]
            elif nm[:-1] in ("xa_w_q", "xa_w_kv", "xa_w_o", "ffn_w_in", "ffn_w_out"):
                m[nm] = f32(inp[nm[:-1]][int(nm[-1])])
            elif nm == "ab_w_out":
                m[nm] = f32(inp["ab_w_out"][0])
            elif nm == "memTp":
                m[nm] = _fm(f32(inp["mem_prompt"][b]).T)
            elif nm == "memTs":
                m[nm] = _fm(f32(inp["mem_sample"][c]).T)
            elif nm == "cd_w_in_p":
                w = np.asarray(inp["cd_w_in"][0], np.float32)
                wq = w[:, :768].reshape(D, 12, 64)[:, QPERM, :].reshape(D, 768)
                m[nm] = f32(np.concatenate([wq, w[:, 768:]], 1))
            elif nm == "cd_w_out_p":
                w = np.asarray(inp["cd_w_out"][0], np.float32)
                wq = w[:768].reshape(12, 64, D)[QPERM].reshape(768, D)
                m[nm] = f32(np.concatenate([wq, w[768:]], 0))
            elif nm == "qk_gain_r":
                g = np.concatenate([np.tile(np.asarray(inp["cd_q_norm"][0], np.float32)[None], (12, 1)),
                                    np.tile(np.asarray(inp["cd_k_norm"][0], np.float32)[None], (4, 1))], 0)
                m[nm] = f32(np.broadcast_to(g[None], (128, 16, 64)))
            elif nm in ("rope_cos", "rope_sin"):
                pos = np.concatenate([o0 + np.arange(NP_OWN), np.arange(NS_OWN)])
                freqs = (np.float32(10000.0) ** (-np.arange(0, 32, 2, dtype=np.float32) / np.float32(32))).astype(np.float32)
                row = (pos // 64).astype(np.float32)
                col = (pos % 64).astype(np.float32)
                ang = np.concatenate([row[:, None] * freqs, col[:, None] * freqs], -1).astype(np.float32)
                t = np.cos(ang) if nm == "rope_cos" else np.sin(ang)
                m[nm] = f32(t.reshape(NOWN // 128, 128, 32).transpose(1, 0, 2))
            elif nm == "cd_pool_w":
                m[nm] = f32(inp["cd_pool_w"][0])
            elif nm == "pool_scale_r":
                m[nm] = f32(np.asarray(inp["cd_pool_scale"][0]).reshape(2, 128).T)
            elif nm == "pool_rc":
                pos = np.concatenate([o0 + np.arange(NP_OWN), np.arange(NS_OWN)])
                nn = np.concatenate([np.full(NP_OWN, SEQ_P), np.full(NS_OWN, SEQ_S)])
                rc = np.zeros((128, 2, NOWN), np.float32)
                for g_ in range(4):
                    w_ = (2, 4, 8, 16)[g_]
                    cnt = np.clip(pos + w_ // 2, 0, nn) - np.clip(pos - w_ // 2, 0, nn)
                    rc[(g_ % 2) * 64:(g_ % 2 + 1) * 64, g_ // 2, :] = (1.0 / cnt.astype(np.float32))[None]
                m[nm] = rc
            elif nm == "pool_sel":
                sel = np.zeros((128, 8), np.float32)
                if q > 0:
                    sel[:, q - 1] = 1.0
                if q < 3:
                    sel[:, 4 + q + 1] = 1.0
                m[nm] = sel
            elif nm == "rel_bias":
                m[nm] = f32(inp["rel_bias"])
            elif nm == "ab_w_in":
                m[nm] = f32(inp["ab_w_in"][0])
            elif nm == "fnet_g_r":
                m[nm] = f32(np.asarray(inp["ab_fnet_g"][0]).reshape(2, 128).T)
            elif nm in ("ln_g_r", "ln_b_r"):
                src = np.asarray(inp["ln_g" if nm == "ln_g_r" else "ln_b"], np.float32)
                m[nm] = f32(src.reshape(6, 8, 128).transpose(2, 0, 1))
            else:
                raise KeyError(nm)
        maps.append(m)
    return maps


def run_prog(P, inp):
    nc = P.finish()
    maps = host_inputs(inp, list(P.inputs.keys()))
    res = run_bass_kernel_spmd(nc, maps, core_ids=list(range(NCORES)))
    return res.results


def build_full(debug=()):
    P = Prog(debug=debug)
    P.setup()
    phase_proj0(P)
    phase_dil(P)
    phase_fnet(P)
    w_out0 = P.din("ab_w_out", [D, D])

    def xres0(i):
        if i < 8:
            return P.xT["p"][:, :, 1024 + i * TT:1024 + (i + 1) * TT]
        return P.xT["s"][:, :, (i - 8) * TT:(i - 7) * TT]

    x3f, x3b = layer_tail(P, 0, P.OT0, w_out0, xres0)
    phase_proj1(P, x3b)
    phase_gqa(P)
    phase_pool(P)
    w_out1 = P.din("cd_w_out_p", [D, D])
    layer_tail(P, 1, P.OT1, w_out1, lambda i: x3f[:, :, i * TT:(i + 1) * TT])
    return P


def kernel(**inputs):
    inp = {k: np.asarray(v) for k, v in inputs.items()}
    P = build_full()
    res = run_prog(P, inp)
    y_prompt = np.zeros((2, SEQ_P, D), np.float32)
    y_sample = np.zeros((8, SEQ_S, D), np.float32)
    for c in range(NCORES):
        b, q = c // 4, c % 4
        yT = np.asarray(res[c]["yT"], dtype=np.float32)
        y_prompt[b, q * NP_OWN:(q + 1) * NP_OWN] = _unfm(yT[:, :, :NP_OWN])
        y_sample[c] = _unfm(yT[:, :, NP_OWN:])
    return (y_prompt, y_sample)
```

```python
import math
import numpy as np
import ml_dtypes
import concourse.bass as bass
import concourse.mybir as mybir
from concourse.bass_utils import run_bass_kernel_spmd

F32 = mybir.dt.float32
BF16 = mybir.dt.bfloat16
AF = mybir.ActivationFunctionType
ALU = mybir.AluOpType
AX = mybir.AxisListType

NCORES = 8
D = 1024
KC = 8
TT = 512
NP_OWN = 4096
NS_OWN = 2048
NOWN = NP_OWN + NS_OWN
NTILE = NOWN // TT
NP_EXT = NP_OWN + 2048
SEQ_P = 16384
SEQ_S = 2048
FFN_H = 2816
HC = FFN_H // 128
DN_ALPHA = 4 ** 0.25
LN_EPS = 1e-5
RMS_EPS = 1e-6
ZW = 2944
ZC = 1408


class Buf:
    __slots__ = ("t", "ws", "r", "name", "wx")

    def __init__(self, t, name=""):
        self.t = t
        self.wx = None
        self.ws = []
        self.r = []
        self.name = name

    def __getitem__(self, idx):
        return self.t[idx]


def _compact(evs):
    best = {}
    for (k, v, s) in evs:
        if k not in best or best[k][1] < v:
            best[k] = (k, v, s)
    return list(best.values())


class Sch:
    def __init__(self, nc, ndma_sems=10):
        self.nc = nc
        self.eng = {"pe": nc.tensor, "act": nc.scalar, "dve": nc.vector,
                    "pool": nc.gpsimd, "sp": nc.sync}
        self.tick = {}
        self.seen = {}
        self.semh = {}
        self._ctx = []
        for e in self.eng:
            self._mksem("s_" + e)
            self.tick[e] = 0
            self.seen[e] = {}
        self.dq = {}
        for q in ("sp", "pool", "act"):
            names = []
            for i in range(ndma_sems):
                k = "d_%s_%d" % (q, i)
                self._mksem(k)
                names.append(k)
            self.dq[q] = {"sems": names, "n": 0, "cnt": {k: 0 for k in names}}
        self._mksem("cc")
        self.cc_cnt = 0
        self.ninst = 0

    def _mksem(self, key):
        cm = self.nc.semaphore(key)
        h = cm.__enter__()
        self._ctx.append(cm)
        self.semh[key] = h
        return h

    def close(self):
        for cm in reversed(self._ctx):
            cm.__exit__(None, None, None)
        self._ctx = []

    def _wait(self, e, ev):
        semkey, val, src = ev
        if src == e and e == "pe":
            return
        if self.seen[e].get(semkey, 0) >= val:
            return
        self.eng[e].wait_ge(self.semh[semkey], val)
        self.seen[e][semkey] = val
        self.ninst += 1

    def _deps(self, e, reads, writes, wacc):
        for b in reads:
            for ev in b.ws:
                self._wait(e, ev)
        for b in writes:
            for ev in b.ws:
                self._wait(e, ev)
            for ev in b.r:
                self._wait(e, ev)
        for b in wacc:
            if b.wx is not None:
                self._wait(e, b.wx)
            for ev in b.r:
                self._wait(e, ev)

    def _commit(self, ev, reads, writes, wacc):
        for b in reads:
            b.r.append(ev)
            if len(b.r) > 16:
                b.r = _compact(b.r)
        for b in writes:
            b.ws = [ev]
            b.wx = ev
            b.r = []
        for b in wacc:
            b.ws.append(ev)
            if len(b.ws) > 16:
                b.ws = _compact(b.ws)

    def op(self, e, fn, reads=(), writes=(), wacc=()):
        self._deps(e, reads, writes, wacc)
        ins = fn(self.eng[e])
        self.tick[e] += 1
        k = "s_" + e
        ins.then_inc(self.semh[k], 1)
        self._commit((k, self.tick[e], e), reads, writes, wacc)
        self.ninst += 1
        return ins

    def dma(self, q, out, in_, reads=(), writes=(), wacc=(), **kw):
        d = self.dq[q]
        k = d["sems"][d["n"] % len(d["sems"])]
        d["n"] += 1
        if d["cnt"][k] > 0:
            self._wait(q, (k, d["cnt"][k], "dma"))
        self._deps(q, reads, writes, wacc)
        ins = self.eng[q].dma_start(out=out, in_=in_, **kw)
        d["cnt"][k] += 16
        ins.then_inc(self.semh[k], 16)
        self._commit((k, d["cnt"][k], "dma"), reads, writes, wacc)
        self.ninst += 1

    def allgather(self, out_t, in_t, reads, writes, groups):
        self._deps("pool", reads, writes, ())
        if self.cc_cnt:
            self._wait("pool", ("cc", self.cc_cnt, "dma"))
        ins = self.nc.gpsimd.collective_compute("AllGather", ALU.bypass, replica_groups=groups,
                                                ins=[in_t.ap().opt()], outs=[out_t.ap().opt()])
        self.cc_cnt += 1
        ins.then_inc(self.semh["cc"], 1)
        self._commit(("cc", self.cc_cnt, "dma"), reads, writes, ())
        self.ninst += 1

    def cc_fence(self, groups):
        if not hasattr(self, "_fence_t"):
            self._fence_t = (self.nc.dram_tensor("cc_f_in", [16, 64], F32),
                             self.nc.dram_tensor("cc_f_out", [16 * len(groups[0]), 64], F32))
        fi, fo = self._fence_t
        self.allgather(fo, fi, reads=[], writes=[], groups=groups)

    def all_events(self):
        evs = []
        for e in self.eng:
            if self.tick[e] > 0:
                evs.append(("s_" + e, self.tick[e], e))
        for q, d in self.dq.items():
            for k, v in d["cnt"].items():
                if v > 0:
                    evs.append((k, v, "dma"))
        if self.cc_cnt:
            evs.append(("cc", self.cc_cnt, "dma"))
        return evs

    def barrier(self, engines=("pe", "act", "dve", "pool", "sp")):
        evs = self.all_events()
        for e in engines:
            for ev in evs:
                if ev[2] == e:
                    continue
                self._wait(e, ev)


def _t5_bucket_np(rel):
    nb = 16
    max_exact = 8
    ret = np.where(rel > 0, nb, 0)
    n = np.abs(rel)
    nf = np.maximum(n, 1).astype(np.float32)
    large = max_exact + (np.log(nf / np.float32(max_exact)) / np.float32(math.log(1024 / max_exact))
                         * np.float32(nb - max_exact)).astype(np.int32)
    large = np.minimum(large, nb - 1)
    return ret + np.where(n < max_exact, n, large)


def _consts_common():
    c = {}
    c["ident"] = np.eye(128, dtype=np.float32)
    c["antiI"] = np.eye(128, dtype=np.float32)[::-1].copy()
    c["onesd"] = np.full((128, 128), 1.0 / D, np.float32)
    blk = np.zeros((128, 128), np.float32)
    blk[:64, :64] = 1.0 / 64
    blk[64:, 64:] = 1.0 / 64
    c["blk64"] = blk
    i = np.arange(3072)
    delta = 1535 - i
    mult = ((np.abs(delta) <= 64).astype(np.int32)
            + ((delta % 4 == 0) & (np.abs(delta) <= 256)).astype(np.int32)
            + ((delta % 16 == 0) & (np.abs(delta) <= 1024)).astype(np.int32))
    mult[3071] = 0
    bk = _t5_bucket_np(delta)
    ohm = np.zeros((32, 3072), np.float32)
    ohm[bk, i] = mult
    c["ohm"] = ohm
    k = np.arange(64)
    ang = 2 * np.pi * np.outer(k, k) / 64
    c64 = (np.cos(ang) / 8).astype(np.float32)
    s64 = (np.sin(ang) / 8).astype(np.float32)
    cbd = np.zeros((128, 128), np.float32)
    sbd = np.zeros((128, 128), np.float32)
    cbd[:64, :64] = c64
    cbd[64:, 64:] = c64
    sbd[:64, :64] = s64
    sbd[64:, 64:] = s64
    c["c64bd"] = cbd
    c["s64bd"] = sbd
    return c


def _dft_tables(N, N1, N2, k2_list):
    sc = 1.0 / math.sqrt(N)
    n1 = np.arange(N1)
    k1 = np.arange(N1)
    n2 = np.arange(N2)
    a1 = 2 * np.pi * np.outer(n1, k1) / N1
    rp = np.concatenate([np.cos(a1), -np.sin(a1)], 1) * sc
    rr = np.concatenate([np.sin(a1), np.cos(a1)], 1) * sc
    at = 2 * np.pi * np.outer(n2, k1) / N
    tc = np.cos(at)
    ts = np.sin(at)
    a2 = 2 * np.pi * np.outer(n2, np.asarray(k2_list)) / N2
    c2 = np.cos(a2)
    s2 = np.sin(a2)
    f = lambda a: np.ascontiguousarray(a, dtype=np.float32)
    return f(rp), f(rr), f(tc), f(ts), f(c2), f(s2)


def _fm(a):
    F, T = a.shape
    return np.ascontiguousarray(a.reshape(F // 128, 128, T).transpose(1, 0, 2))


def _unfm(a):
    P_, C, T = a.shape
    return np.ascontiguousarray(a.transpose(2, 1, 0).reshape(T, C * P_))


from contextlib import ExitStack


class Prog:
    def __init__(self, debug=()):
        self.debug = set(debug)
        self.nc = bass.Bass("TRN2", target_bir_lowering=False)
        self.S = Sch(self.nc)
        self.inputs = {}
        self.outputs = {}
        self.gstack = ExitStack()
        self.rr = 0

    def din(self, name, shape, dtype=F32):
        t = self.nc.dram_tensor(name, list(shape), dtype, kind="ExternalInput")
        self.inputs[name] = t
        return t

    def dout(self, name, shape, dtype=F32):
        t = self.nc.dram_tensor(name, list(shape), dtype, kind="ExternalOutput")
        self.outputs[name] = t
        return t

    def dscratch(self, name, shape, dtype):
        if name in self.debug:
            return self.dout(name, shape, dtype)
        return self.nc.dram_tensor(name, list(shape), dtype)

    def sb(self, stack, name, shape, dtype):
        self._uid = getattr(self, "_uid", 0) + 1
        name = "%s_u%d" % (name, self._uid)
        t = stack.enter_context(self.nc.sbuf_tensor(name, list(shape), dtype))
        return Buf(t, name)

    def eng2(self):
        self.rr += 1
        return "act" if self.rr % 2 else "dve"

    def copy(self, e, out, in_, reads, writes=(), wacc=(), scale=None):
        S = self.S
        if e == "act":
            if scale is None:
                S.op("act", lambda h: h.copy(out, in_), reads, writes, wacc)
            else:
                S.op("act", lambda h: h.mul(out, in_, scale), reads, writes, wacc)
        else:
            if scale is None:
                S.op(e, lambda h: h.tensor_copy(out, in_), reads, writes, wacc)
            else:
                S.op(e, lambda h: h.tensor_scalar_mul(out, in_, scale), reads, writes, wacc)

    def rsqrt(self, out, in_, eps_tile, reads, wacc):
        S = self.S
        S.op("act", lambda h: h.activation(out, in_, AF.Sqrt, bias=eps_tile[:, 0:1], scale=1.0),
             reads=list(reads) + [eps_tile], wacc=wacc)
        S.op("dve", lambda h: h.reciprocal(out, out), reads=list(wacc), wacc=wacc)

    def setup(self):
        nc, S = self.nc, self.S
        g = self.gstack
        self.ps2 = [Buf(g.enter_context(nc.psum_tensor("psp%d" % i, [128, 1024], F32)), "psp%d" % i) for i in range(4)]
        self.ps = [Buf(self.ps2[i // 2].t[:, (i % 2) * 512:(i % 2 + 1) * 512], "ps%d" % i) for i in range(8)]
        self.c_onesA = self.sb(g, "c_onesA", [128, 128], F32)
        self.c_onesB = self.sb(g, "c_onesB", [128, 128], F32)
        S.op("dve", lambda h: h.memset(self.c_onesA[:], 0.0), writes=[self.c_onesA])
        S.op("dve", lambda h: h.memset(self.c_onesB[:], 0.0), writes=[self.c_onesB])
        S.op("dve", lambda h: h.memset(self.c_onesA[:, 0:64], 1.0), reads=[self.c_onesA], wacc=[self.c_onesA])
        S.op("dve", lambda h: h.memset(self.c_onesB[:, 64:128], 1.0), reads=[self.c_onesB], wacc=[self.c_onesB])
        self.c_ident = self.sb(g, "c_ident", [128, 128], F32)
        self.c_antiI = self.sb(g, "c_antiI", [128, 128], F32)
        self.c_onesd = self.sb(g, "c_onesd", [128, 128], F32)
        self.c_blk64 = self.sb(g, "c_blk64", [128, 128], F32)
        self.c_onesb = self.sb(g, "c_onesb", [128, 128], BF16)
        self.c_identb = self.sb(g, "c_identb", [128, 128], BF16)
        self.c_lng = self.sb(g, "c_lng", [128, 6, 8], F32)
        self.c_lnb = self.sb(g, "c_lnb", [128, 6, 8], F32)
        for nm, buf in (("ident", self.c_ident), ("antiI", self.c_antiI), ("onesd", self.c_onesd),
                        ("blk64", self.c_blk64)):
            t = self.din("k_" + nm, [128, 128])
            S.dma("sp", buf[:], t[:, :], writes=[buf])
        t = self.din("ln_g_r", [128, 6, 8])
        S.dma("sp", self.c_lng[:], t[:, :, :], writes=[self.c_lng])
        t = self.din("ln_b_r", [128, 6, 8])
        S.dma("sp", self.c_lnb[:], t[:, :, :], writes=[self.c_lnb])
        self.c_eps_ln = self.sb(g, "c_eps_ln", [128, 1], F32)
        self.c_eps_rms = self.sb(g, "c_eps_rms", [128, 1], F32)
        S.op("dve", lambda h: h.memset(self.c_eps_ln[:], LN_EPS), writes=[self.c_eps_ln])
        S.op("dve", lambda h: h.memset(self.c_eps_rms[:], RMS_EPS), writes=[self.c_eps_rms])
        S.op("dve", lambda h: h.memset(self.c_onesb[:], 1.0), writes=[self.c_onesb])
        S.op("dve", lambda h: h.tensor_copy(self.c_identb[:], self.c_ident[:]), reads=[self.c_ident],
             writes=[self.c_identb])

    def load_w(self, stack, name, wap, K, N, stg):
        S = self.S
        kc = K // 128
        wb = self.sb(stack, name, [128, kc, N], BF16)
        CH = stg[0].t.shape[1]
        i = 0
        for c in range(kc):
            for n0 in range(0, N, CH):
                n1 = min(N, n0 + CH)
                st = stg[i % len(stg)]
                i += 1
                S.dma("sp", st[:, 0:n1 - n0], wap[c * 128:(c + 1) * 128, n0:n1], writes=[st])
                self.copy(self.eng2(), wb[:, c, n0:n1], st[:, 0:n1 - n0], reads=[st], wacc=[wb])
        return wb

    def finish(self):
        S = self.S
        S.barrier()
        self.gstack.close()
        S.close()
        return self.nc


def phase_proj0(P):
    nc, S = P.nc, P.S
    w_in = P.din("ab_w_in", [D, 2560])
    fg = P.din("fnet_g_r", [128, 2])
    P.KT0 = {"p": P.dscratch("KT0p", [128, 6, NP_EXT], BF16), "s": P.dscratch("KT0s", [128, 6, SEQ_S], BF16)}
    P.V0 = {"p": P.dscratch("V0p", [NP_EXT // 128, 128, 780], BF16),
            "s": P.dscratch("V0s", [SEQ_S // 128, 128, 780], BF16)}
    P.QT0 = P.dscratch("QT0", [128, 6, NOWN], BF16)
    P.unT = {"p": [P.dscratch("unTp%d" % i, [256, 2048], BF16) for i in range(2)],
             "s": [P.dscratch("unTs", [256, NS_OWN], BF16)]}
    xT = {"p": P.din("xTp", [128, 8, NP_EXT]), "s": P.din("xTs", [128, 8, SEQ_S])}
    vld = {"p": P.din("vldp", [128, NP_EXT // 128]), "s": P.din("vlds", [128, SEQ_S // 128])}
    P.xT = xT
    P.vld = vld
    with ExitStack() as st:
        stg = [P.sb(st, "stg%d" % i, [128, 1024], F32) for i in range(2)]
        wb = P.load_w(st, "w_ab_in", w_in, D, 2560, stg)
        fgs = P.sb(st, "fgs", [128, 2], F32)
        S.dma("sp", fgs[:], fg[:, :], writes=[fgs])
        xf = [P.sb(st, "xf%d" % i, [128, 8, TT], F32) for i in range(2)]
        xb = [P.sb(st, "xb%d" % i, [128, 8, TT], BF16) for i in range(2)]
        kt = [P.sb(st, "kt%d" % i, [128, 6, TT], BF16) for i in range(2)]
        qt = [P.sb(st, "qt%d" % i, [128, 6, TT], BF16) for i in range(2)]
        vs = [P.sb(st, "vs%d" % i, [128, 4, 12, 65], BF16) for i in range(2)]
        uf = P.sb(st, "uf", [128, 2, TT], F32)
        usq = P.sb(st, "usq", [128, 2, TT], F32)
        urs = P.sb(st, "urs", [128, 2, TT], F32)
        un = [P.sb(st, "un%d" % i, [128, 2, TT], BF16) for i in range(2)]
        ones12 = P.sb(st, "ones12", [128, 12, 1], F32)
        S.op("dve", lambda h: h.memset(ones12[:], 1.0), writes=[ones12])
        vl = {}
        for sg in ("p", "s"):
            nch = (NP_EXT if sg == "p" else SEQ_S) // 128
            vl[sg] = P.sb(st, "vl" + sg, [128, nch], F32)
            S.dma("sp", vl[sg][:], vld[sg][:, :], writes=[vl[sg]])
        it = 0
        pb = 0
        for sg in ("p", "s"):
            n_ext = NP_EXT if sg == "p" else SEQ_S
            own0 = 2 if sg == "p" else 0
            nown_t = 8 if sg == "p" else 4
            ooff = 0 if sg == "p" else NP_OWN
            for i in range(n_ext // TT):
                a = it % 2
                it += 1
                X, XB, KT, QT, VS, UN = xf[a], xb[a], kt[a], qt[a], vs[a], un[a]
                S.dma("sp", X[:], xT[sg][:, :, i * TT:(i + 1) * TT], writes=[X])
                S.op("act", lambda h: h.copy(XB[:, 0:4, :], X[:, 0:4, :]), reads=[X], wacc=[XB])
                S.op("dve", lambda h: h.tensor_copy(XB[:, 4:8, :], X[:, 4:8, :]), reads=[X], wacc=[XB])
                for oc in range(6):
                    pt = P.ps[pb % 8]
                    pb += 1
                    for c in range(8):
                        S.op("pe", lambda h: h.matmul(pt[:], wb[:, c, 768 + oc * 128:768 + (oc + 1) * 128],
                                                      XB[:, c, :], start=(c == 0), stop=(c == 7)),
                             reads=[wb, XB], writes=[pt])
                    P.copy(P.eng2(), KT[:, oc, :], pt[:], reads=[pt], wacc=[KT])
                S.dma("pool", P.KT0[sg][:, :, i * TT:(i + 1) * TT], KT[:], reads=[KT])
                for sub in range(4):
                    for hf in range(2):
                        pt = P.ps[pb % 8]
                        pb += 1
                        for c in range(8):
                            S.op("pe", lambda h: h.matmul(pt[:, 0:384], XB[:, c, sub * 128:(sub + 1) * 128],
                                                          wb[:, c, 1536 + hf * 384:1536 + (hf + 1) * 384],
                                                          start=(c == 0), stop=(c == 7)),
                                 reads=[wb, XB], writes=[pt])
                        P.copy(P.eng2(), VS[:, sub, hf * 6:(hf + 1) * 6, 0:64],
                               pt[:, 0:384].rearrange("p (h d) -> p h d", d=64), reads=[pt], wacc=[VS])
                    ch = i * 4 + sub
                    S.op("dve", lambda h: h.tensor_scalar(VS[:, sub, :, 64:65], ones12[:], vl[sg][:, ch:ch + 1], None,
                                                          op0=ALU.mult), reads=[ones12, vl[sg]], wacc=[VS])
                S.dma("pool", P.V0[sg][i * 4:(i + 1) * 4].rearrange("c p f -> p c f"),
                      VS[:].rearrange("p s h d -> p s (h d)"), reads=[VS])
                if not (own0 <= i < own0 + nown_t):
                    continue
                o0 = ooff + (i - own0) * TT
                for oc in range(6):
                    pt = P.ps[pb % 8]
                    pb += 1
                    for c in range(8):
                        S.op("pe", lambda h: h.matmul(pt[:], wb[:, c, oc * 128:(oc + 1) * 128],
                                                      XB[:, c, :], start=(c == 0), stop=(c == 7)),
                             reads=[wb, XB], writes=[pt])
                    P.copy(P.eng2(), QT[:, oc, :], pt[:], reads=[pt], wacc=[QT], scale=0.125)
                S.dma("pool", P.QT0[:, :, o0:o0 + TT], QT[:], reads=[QT])
                for c2 in range(2):
                    pt = P.ps[pb % 8]
                    pb += 1
                    for c in range(8):
                        S.op("pe", lambda h: h.matmul(pt[:], wb[:, c, 2304 + c2 * 128:2304 + (c2 + 1) * 128],
                                                      XB[:, c, :], start=(c == 0), stop=(c == 7)),
                             reads=[wb, XB], writes=[pt])
                    S.op("act", lambda h: h.copy(uf[:, c2, :], pt[:]), reads=[pt], wacc=[uf])
                for c2 in range(2):
                    pm = P.ps[pb % 8]
                    pb += 1
                    S.op("pe", lambda h: h.matmul(pm[:], P.c_blk64[:], uf[:, c2, :], start=True, stop=True),
                         reads=[P.c_blk64, uf], writes=[pm])
                    S.op("dve", lambda h: h.tensor_tensor(uf[:, c2, :], uf[:, c2, :], pm[:], op=ALU.subtract),
                         reads=[pm, uf], wacc=[uf])
                    S.op("act", lambda h: h.activation(usq[:, c2, :], uf[:, c2, :], AF.Square),
                         reads=[uf], wacc=[usq])
                    pv = P.ps[pb % 8]
                    pb += 1
                    S.op("pe", lambda h: h.matmul(pv[:], P.c_blk64[:], usq[:, c2, :], start=True, stop=True),
                         reads=[P.c_blk64, usq], writes=[pv])
                    P.rsqrt(urs[:, c2, :], pv[:], P.c_eps_ln, reads=[pv], wacc=[urs])
                    S.op("dve", lambda h: h.tensor_tensor(uf[:, c2, :], uf[:, c2, :], urs[:, c2, :], op=ALU.mult),
                         reads=[urs, uf], wacc=[uf])
                    S.op("dve", lambda h: h.tensor_scalar(UN[:, c2, :], uf[:, c2, :], fgs[:, c2:c2 + 1], None,
                                                          op0=ALU.mult), reads=[uf, fgs], wacc=[UN])
                oo = (i - own0) * TT
                S.dma("pool", P.unT[sg][oo // 2048][:, oo % 2048:oo % 2048 + TT].rearrange("(c p) t -> p c t", p=128),
                      UN[:], reads=[UN])
        S.barrier()


def attn_pipeline(P, items, s_fn, e_fn, o_fn, look=2):
    n = len(items)
    for t in range(min(look, n)):
        s_fn(items[t], t)
    for t in range(n):
        if t + look < n:
            s_fn(items[t + look], t + look)
        e_fn(items[t], t)
        o_fn(items[t], t)


class PairAttn:
    def __init__(self, P, st, with_z):
        self.P = P
        self.with_z = with_z
        self.EB = [P.sb(st, "paEB%d" % i, [128, 1024], BF16) for i in range(4)]
        if with_z:
            self.EF = [P.sb(st, "paEF%d" % i, [128, 1024], F32) for i in range(3)]
        else:
            self.ACC = [P.sb(st, "paACC%d" % i, [128, 1024], F32) for i in range(2)]
            self.ACCD = [Buf(a.t[:, 0:768], "accD") for a in self.ACC]
            self.ACCP = [Buf(a.t[:, 768:1024], "accP") for a in self.ACC]
        self.RD = [P.sb(st, "paRD%d" % i, [128, 512], F32) for i in range(2)]
        self.OUT = [P.sb(st, "paOUT%d" % i, [128, 512], BF16) for i in range(2)]

    def run(self, items, kt, qt, va, zz, out_fn, vl=None):
        P, S = self.P, self.P.S
        EB, RD, OUT = self.EB, self.RD, self.OUT

        def s_fn(itm, t):
            qi, ch, zoff, first, last = itm
            pp = P.ps2[t % 3]
            for hh in range(2):
                pb_ = hh * 64
                S.op("pe", lambda h: h.matmul(pp[:, hh * 512:(hh + 1) * 512], kt[pb_:pb_ + 64, ch * 128:(ch + 1) * 128],
                                              qt[pb_:pb_ + 64, qi * TT:(qi + 1) * TT], start=True, stop=True,
                                              tile_position=(pb_, 0)), reads=[kt, qt], writes=[pp])

        def e_fn(itm, t):
            qi, ch, zoff, first, last = itm
            pp = P.ps2[t % 3]
            eb = EB[t % 4]
            if self.with_z:
                ef = self.EF[t % 3]
                S.op("act", lambda h: h.activation(ef[:], pp[:], AF.Exp), reads=[pp], writes=[ef])
                S.op("dve", lambda h: h.tensor_tensor(eb[:, 0:512], ef[:, 0:512], zz[0][:, zoff:zoff + 512],
                                                      op=ALU.mult), reads=[ef, zz[0]], wacc=[eb])
                S.op("pool", lambda h: h.tensor_tensor(eb[:, 512:1024], ef[:, 512:1024], zz[1][:, zoff:zoff + 512],
                                                       op=ALU.mult), reads=[ef, zz[1]], wacc=[eb])
            else:
                S.op("act", lambda h: h.activation(eb[:], pp[:], AF.Exp), reads=[pp], writes=[eb])
                acc = self.ACC[qi % 2]
                for e, c0, c1, ab in (("dve", 0, 768, self.ACCD[qi % 2]), ("pool", 768, 1024, self.ACCP[qi % 2])):
                    if first:
                        S.op(e, lambda h: h.tensor_copy(acc[:, c0:c1], eb[:, c0:c1]), reads=[eb], writes=[ab])
                    else:
                        S.op(e, lambda h: h.tensor_tensor(acc[:, c0:c1], acc[:, c0:c1], eb[:, c0:c1], op=ALU.add),
                             reads=[eb, ab], writes=[ab])

        def o_fn(itm, t):
            qi, ch, zoff, first, last = itm
            eb = EB[t % 4]
            po, pd = P.ps[6], P.ps[7]
            for hh in range(2):
                S.op("pe", lambda h: h.matmul(po[hh * 64:(hh + 1) * 64, :], va[:, ch, hh, 0:64],
                                              eb[:, hh * 512:(hh + 1) * 512], start=first, stop=last,
                                              tile_position=(0, hh * 64)), reads=[va, eb], writes=[po])
            if self.with_z:
                for hh in range(2):
                    S.op("pe", lambda h: h.matmul(pd[hh * 64:(hh + 1) * 64, :], vl[:, ch, :],
                                                  eb[:, hh * 512:(hh + 1) * 512], start=first, stop=last,
                                                  tile_position=(0, hh * 64)), reads=[vl, eb], writes=[pd])
            if not last:
                return
            rd, out = RD[qi % 2], OUT[qi % 2]
            if not self.with_z:
                acc = self.ACC[qi % 2]
                accs = [self.ACCD[qi % 2], self.ACCP[qi % 2]]
                S.op("pe", lambda h: h.matmul(pd[:], P.c_onesA[:], acc[:, 0:512], start=True, stop=False),
                     reads=[P.c_onesA] + accs, writes=[pd])
                S.op("pe", lambda h: h.matmul(pd[:], P.c_onesB[:], acc[:, 512:1024], start=False, stop=True),
                     reads=[P.c_onesB] + accs, writes=[pd])
            S.op("dve", lambda h: h.reciprocal(rd[:], pd[:]), reads=[pd], writes=[rd])
            S.op("dve", lambda h: h.tensor_tensor(out[:], po[:], rd[:], op=ALU.mult), reads=[po, rd], writes=[out])
            out_fn(qi, out)

        attn_pipeline(P, items, s_fn, e_fn, o_fn, look=2)


def phase_dil(P):
    nc, S = P.nc, P.S
    relb = P.din("rel_bias", [32, 12])
    ohm = P.din("k_ohm", [32, 3072])
    rev = P.dscratch("dil_rev", [12, 3200], F32)
    P.OT0 = P.dscratch("OT0", [128, 8, NOWN], BF16)
    REV = Buf(rev, "rev")
    with ExitStack() as st:
        ones32 = P.sb(st, "ones32", [128, 64], F32)
        S.op("dve", lambda h: h.memset(ones32[:], 1.0), writes=[ones32])
        rb = P.sb(st, "rb", [32, 12], F32)
        eb = P.sb(st, "eb", [32, 12], F32)
        oh = P.sb(st, "oh", [32, 3072], F32)
        wt = P.sb(st, "wt", [12, 3072], F32)
        S.dma("sp", rb[:], relb[:, :], writes=[rb])
        S.dma("sp", oh[:], ohm[:, :], writes=[oh])
        S.op("act", lambda h: h.activation(eb[:], rb[:], AF.Exp), reads=[rb], writes=[eb])
        for n0 in range(0, 3072, 512):
            pt = P.ps[(n0 // 512) % 8]
            S.op("pe", lambda h: h.matmul(pt[0:12, :], eb[:], oh[:, n0:n0 + 512], start=True, stop=True),
                 reads=[eb, oh], writes=[pt])
            S.op("dve", lambda h: h.tensor_copy(wt[:, n0:n0 + 512], pt[0:12, :]), reads=[pt], wacc=[wt])
        S.dma("pool", REV[:, 0:3072], wt[:], reads=[wt], writes=[REV])
        S.barrier()
        KT = [P.sb(st, "dKT%d" % i, [128, NP_EXT], BF16) for i in range(2)]
        QT = [P.sb(st, "dQT%d" % i, [128, NP_OWN], BF16) for i in range(2)]
        VA = [P.sb(st, "dVA%d" % i, [128, NP_EXT // 128, 2, 65], BF16) for i in range(2)]
        ZZ = [[P.sb(st, "dZ%d_%d" % (i, k), [128, ZW], F32) for k in range(2)] for i in range(2)]
        HK = P.sb(st, "dHK", [128, ZW], F32)
        PA = PairAttn(P, st, with_z=True)
        VL = {}
        for sg_ in ("p", "s"):
            nch_ = (NP_EXT if sg_ == "p" else SEQ_S) // 128
            vlf = P.sb(st, "dvlf" + sg_, [128, nch_], F32)
            VL[sg_] = P.sb(st, "dvl" + sg_, [128, nch_, 64], BF16)
            S.dma("sp", vlf[:], P.vld[sg_][:, :], writes=[vlf])
            S.op("dve", lambda h: h.tensor_copy(VL[sg_][:], vlf[:].unsqueeze(2).to_broadcast([128, nch_, 64])),
                 reads=[vlf], writes=[VL[sg_]])
        it = 0
        for sg in ("p", "s"):
            n_ext = NP_EXT if sg == "p" else SEQ_S
            n_own = NP_OWN if sg == "p" else NS_OWN
            nqt = n_own // TT
            ooff = 0 if sg == "p" else NP_OWN
            nch = n_ext // 128
            for hp in range(6):
                a = it % 2
                it += 1
                kt, qt, va, zz = KT[a], QT[a], VA[a], ZZ[a]
                S.dma("sp", kt[:, 0:n_ext], P.KT0[sg][:, hp, :], writes=[kt])
                S.dma("sp", qt[:, 0:n_own], P.QT0[:, hp, ooff:ooff + n_own], writes=[qt])
                S.dma("sp", va[:, 0:nch, :, :].rearrange("p c h d -> p c (h d)"),
                      P.V0[sg][:, :, hp * 130:(hp + 1) * 130].rearrange("c p f -> p c f"), writes=[va])
                for hh in range(2):
                    h_ = 2 * hp + hh
                    src = bass.AP(tensor=rev, offset=h_ * 3200, ap=[[1, 128], [1, ZW]])
                    S.dma("sp", HK[:], src, reads=[REV], writes=[HK])
                    for n0 in range(0, ZW, 512):
                        w = min(512, ZW - n0)
                        pt = P.ps[6 + (n0 // 512) % 2]
                        S.op("pe", lambda h: h.matmul(pt[:, 0:w], P.c_antiI[:], HK[:, n0:n0 + w], start=True,
                                                      stop=True), reads=[P.c_antiI, HK], writes=[pt])
                        P.copy(P.eng2(), zz[hh][:, n0:n0 + w], pt[:, 0:w], reads=[pt], wacc=[zz[hh]])
                items = []
                for qi in range(nqt):
                    js = []
                    for j in range(20):
                        ch = 4 * qi + j - (0 if sg == "p" else 8)
                        if 0 <= ch < nch:
                            js.append((j, ch))
                    for idx, (j, ch) in enumerate(js):
                        items.append((qi, ch, 2432 - 128 * j, idx == 0, idx == len(js) - 1))

                def out_fn(qi, out, hp=hp, ooff=ooff):
                    o0 = ooff + qi * TT
                    S.dma("pool", P.OT0[:, hp, o0:o0 + TT], out[:], reads=[out])

                PA.run(items, kt, qt, va, zz, out_fn, vl=VL[sg])
        S.barrier()


GROUPS4 = [[0, 1, 2, 3], [4, 5, 6, 7]]


def phase_fnet(P):
    nc, S = P.nc, P.S
    fw = P.din("ab_fnet_w", [4, 64, 64])
    c64 = P.din("k_c64bd", [128, 128])
    s64 = P.din("k_s64bd", [128, 128])
    unall = [P.dscratch("unTall%d" % i, [1024, 2048], BF16) for i in range(2)]
    UNALL = Buf(unall)
    for i in range(2):
        S.allgather(unall[i], P.unT["p"][i], reads=[], writes=[UNALL] if i == 0 else [], groups=GROUPS4)
    S.cc_fence(GROUPS4)
    UNALL.ws = [("cc", S.cc_cnt, "dma")]
    if "unall_dbg" in P.debug:
        for i in range(2):
            dbg = P.dout("unall_dbg%d" % i, [1024, 2048], BF16)
            S.dma("sp", dbg[:, :], unall[i][:, :], reads=[UNALL])
            dbg2 = P.dout("unmine_dbg%d" % i, [256, 2048], BF16)
            S.dma("sp", dbg2[:, :], P.unT["p"][i][:, :], reads=[UNALL])
    tabs = {}
    for sg, n1 in (("p", 128), ("s", 16)):
        nk2 = 32 if sg == "p" else 128
        tabs[sg] = dict(rp=P.din("f_rp_" + sg, [n1, 2 * n1]), rr=P.din("f_rr_" + sg, [n1, 2 * n1]),
                        tc=P.din("f_tc_" + sg, [128, n1]), ts=P.din("f_ts_" + sg, [128, n1]),
                        c2=P.din("f_c2_" + sg, [128, nk2]), s2=P.din("f_s2_" + sg, [128, nk2]))
    with ExitStack() as st:
        cs = P.sb(st, "f_cs", [128, 2, 128], F32)
        S.dma("sp", cs[:, 0, :], c64[:, :], wacc=[cs])
        S.dma("sp", cs[:, 1, :], s64[:, :], wacc=[cs])
        wbd = P.sb(st, "f_wbd", [128, 128], F32)
        AB = P.sb(st, "f_AB", [128, 2, 128], BF16)
        stg = P.sb(st, "f_stg", [128, 2, 256], F32)
        tb = {k: P.sb(st, "f_t_" + k, [128, 256], BF16) for k in ("rp", "rr", "c2", "s2")}
        tcs = {k: P.sb(st, "f_t_" + k, [128, 128], F32) for k in ("tc", "ts")}
        Y = P.sb(st, "f_Y", [128, 128, 256], BF16)
        OB = P.sb(st, "f_OB", [128, NP_OWN], BF16)
        pbk = 0
        for sg in ("p", "s"):
            N1 = 128 if sg == "p" else 16
            NK2 = 32 if sg == "p" else 128
            n_own = NP_OWN if sg == "p" else NS_OWN
            nseq = SEQ_P if sg == "p" else SEQ_S
            ooff = 0 if sg == "p" else NP_OWN
            T = tabs[sg]
            for k, rows, cols in (("rp", N1, 2 * N1), ("rr", N1, 2 * N1), ("c2", 128, NK2), ("s2", 128, NK2)):
                S.dma("sp", stg[0:rows, 0, 0:cols], T[k][:, :], writes=[stg])
                S.op("dve", lambda h: h.tensor_copy(tb[k][0:rows, 0:cols], stg[0:rows, 0, 0:cols]), reads=[stg],
                     writes=[tb[k]])
            for k in ("tc", "ts"):
                S.dma("sp", tcs[k][:, 0:N1], T[k][:, :], writes=[tcs[k]])
            for gp in range(2):
                S.op("dve", lambda h: h.memset(wbd[:], 0.0), writes=[wbd])
                S.dma("sp", wbd[0:64, 0:64], fw[2 * gp, :, :], wacc=[wbd])
                S.dma("sp", wbd[64:128, 64:128], fw[2 * gp + 1, :, :], wacc=[wbd])
                for k in range(2):
                    pt = P.ps[pbk % 8]
                    pbk += 1
                    S.op("pe", lambda h: h.matmul(pt[:, 0:128], cs[:, k, :], wbd[:], start=True, stop=True),
                         reads=[cs, wbd], writes=[pt])
                    P.copy("dve", AB[:, k, :], pt[:, 0:128], reads=[pt], wacc=[AB], scale=(1.0 if k == 0 else -1.0))
                with ExitStack() as st2:
                    un = P.sb(st2, "f_un", [128, SEQ_P], BF16)
                    if sg == "p":
                        for r in range(4):
                            for hf in range(2):
                                S.dma("sp", un[:, r * 4096 + hf * 2048:r * 4096 + (hf + 1) * 2048],
                                      unall[hf][r * 256 + gp * 128:r * 256 + (gp + 1) * 128, :], reads=[UNALL],
                                      wacc=[un])
                    else:
                        S.dma("sp", un[:, 0:nseq], P.unT["s"][0][gp * 128:(gp + 1) * 128, :], wacc=[un])
                    for n2 in range(0, 128, 2):
                        pt = P.ps[pbk % 8]
                        pbk += 1
                        for d in range(2):
                            S.op("pe", lambda h: h.matmul(pt[0:N1, d * 256:(d + 1) * 256],
                                                          un[:, n2 + d:nseq:128], AB[:].rearrange("p a e -> p (a e)"),
                                                          start=True, stop=True), reads=[un, AB], writes=[pt])
                        P.copy(P.eng2(), Y[0:N1, n2:n2 + 2, :], pt[0:N1, :].rearrange("p (a c) -> p a c", a=2),
                               reads=[pt], wacc=[Y])
                    S.barrier()
                with ExitStack() as st3:
                    GP = P.sb(st3, "f_GP", [128, N1, 2, 128], BF16)
                    GS = [P.sb(st3, "f_GS%d" % i, [128, 512], F32) for i in range(2)]
                    T1 = [P.sb(st3, "f_T1%d" % i, [128, 256], F32) for i in range(2)]
                    T2 = [P.sb(st3, "f_T2%d" % i, [128, 256], F32) for i in range(2)]
                    T3 = [P.sb(st3, "f_T3%d" % i, [128, 256], F32) for i in range(2)]
                    T4 = [P.sb(st3, "f_T4%d" % i, [128, 256], F32) for i in range(2)]
                    EBn = 512 // (2 * N1)
                    nb = 0
                    for c0 in range(0, 128, EBn):
                        pt = P.ps[pbk % 8]
                        pbk += 1
                        for bi in range(EBn):
                            col = c0 + bi
                            sl = slice(bi * 2 * N1, (bi + 1) * 2 * N1)
                            S.op("pe", lambda h: h.matmul(pt[:, sl], Y[0:N1, :, col], tb["rp"][0:N1, 0:2 * N1],
                                                          start=True, stop=False), reads=[Y, tb["rp"]], writes=[pt])
                            S.op("pe", lambda h: h.matmul(pt[:, sl], Y[0:N1, :, 128 + col], tb["rr"][0:N1, 0:2 * N1],
                                                          start=False, stop=True), reads=[Y, tb["rr"]], writes=[pt])
                        a = nb % 2
                        nb += 1
                        gs, t1, t2, t3, t4 = GS[a], T1[a], T2[a], T3[a], T4[a]
                        S.op("act", lambda h: h.copy(gs[:], pt[:]), reads=[pt], writes=[gs])
                        g4 = gs[:].rearrange("p (b r k) -> p b r k", b=EBn, r=2)
                        gr, gi = g4[:, :, 0, :], g4[:, :, 1, :]
                        tcb = tcs["tc"][:, 0:N1].unsqueeze(1).to_broadcast([128, EBn, N1])
                        tsb = tcs["ts"][:, 0:N1].unsqueeze(1).to_broadcast([128, EBn, N1])
                        v = lambda t: t[:, 0:EBn * N1].rearrange("p (b k) -> p b k", b=EBn)
                        vt = lambda t: t[:, 0:EBn * N1].rearrange("p (b k) -> p k b", b=EBn)
                        S.op("dve", lambda h: h.tensor_tensor(v(t1), gr, tcb, op=ALU.mult),
                             reads=[gs, tcs["tc"]], writes=[t1])
                        S.op("dve", lambda h: h.tensor_tensor(v(t2), gi, tsb, op=ALU.mult),
                             reads=[gs, tcs["ts"]], writes=[t2])
                        S.op("pool", lambda h: h.tensor_tensor(v(t3), gi, tcb, op=ALU.mult),
                             reads=[gs, tcs["tc"]], writes=[t3])
                        S.op("pool", lambda h: h.tensor_tensor(v(t4), gr, tsb, op=ALU.mult),
                             reads=[gs, tcs["ts"]], writes=[t4])
                        S.op("dve", lambda h: h.tensor_tensor(GP[:, :, 0, c0:c0 + EBn], vt(t1), vt(t2), op=ALU.add),
                             reads=[t1, t2], wacc=[GP])
                        S.op("pool", lambda h: h.tensor_tensor(GP[:, :, 1, c0:c0 + EBn], vt(t3), vt(t4),
                                                               op=ALU.subtract), reads=[t3, t4], wacc=[GP])
                    KB = 512 // NK2
                    for k0 in range(0, N1, KB):
                        pt = P.ps[pbk % 8]
                        pbk += 1
                        for kk in range(KB):
                            k1 = k0 + kk
                            sl = slice(kk * NK2, (kk + 1) * NK2)
                            S.op("pe", lambda h: h.matmul(pt[:, sl], GP[:, k1, 0, :], tb["c2"][:, 0:NK2], start=True,
                                                          stop=False), reads=[GP, tb["c2"]], writes=[pt])
                            S.op("pe", lambda h: h.matmul(pt[:, sl], GP[:, k1, 1, :], tb["s2"][:, 0:NK2], start=False,
                                                          stop=True), reads=[GP, tb["s2"]], writes=[pt])
                        ov = OB[:, 0:n_own].rearrange("p (k2 k1) -> p k1 k2", k1=N1)[:, k0:k0 + KB, :]
                        P.copy(P.eng2(), ov, pt[:].rearrange("p (a b) -> p a b", a=KB), reads=[pt], wacc=[OB])
                    S.dma("pool", P.OT0[:, 6 + gp, ooff:ooff + n_own], OB[:, 0:n_own], reads=[OB])
                    S.barrier()
        S.barrier()


class RowBufs:
    def __init__(self, P, st):
        self.xr = [P.sb(st, "rb_xr%d" % i, [128, 8, TT], F32) for i in range(2)]
        self.r = P.sb(st, "rb_r", [128, 8, TT], F32)
        self.sq = P.sb(st, "rb_sq", [128, 8, TT], F32)
        self.rstd = P.sb(st, "rb_rstd", [128, TT], F32)
        self.of = [P.sb(st, "rb_of%d" % i, [128, 8, TT], F32) for i in range(1)]
        self.ob = [P.sb(st, "rb_ob%d" % i, [128, 8, TT], BF16) for i in range(1)]


def linear_resid_ln(P, RB, i, wb, kcin, src, xr, lnidx, outf_d, outb_d, pbase):
    S = P.S
    r, sq, rstd = RB.r, RB.sq, RB.rstd
    of, ob = RB.of[0], RB.ob[0]
    for oc in range(8):
        pt = P.ps[(pbase + oc) % 8]
        for c in range(kcin):
            S.op("pe", lambda h: h.matmul(pt[:], wb[:, c, oc * 128:(oc + 1) * 128], src[:, c, :], start=(c == 0),
                                          stop=(c == kcin - 1)), reads=[wb, src], writes=[pt])
        S.op("dve", lambda h: h.scalar_tensor_tensor(r[:, oc, :], xr[:, oc, :], DN_ALPHA, pt[:], op0=ALU.mult,
                                                     op1=ALU.add), reads=[xr, pt], wacc=[r])
    layer_norm_fm(P, r, sq, rstd, lnidx, of, ob, pbase)
    t0 = i * TT
    if outf_d is not None:
        S.dma("pool", outf_d[:, :, t0:t0 + TT], of[:], reads=[of])
    if outb_d is not None:
        S.dma("pool", outb_d[:, :, t0:t0 + TT], ob[:], reads=[ob])


def layer_norm_fm(P, r, sq, rstd, lnidx, of, ob, pbase):
    S = P.S
    pm = P.ps[(pbase + 0) % 8]
    pv = P.ps[(pbase + 1) % 8]
    for c in range(8):
        S.op("pe", lambda h: h.matmul(pm[:], P.c_onesd[:], r[:, c, :], start=(c == 0), stop=(c == 7)),
             reads=[P.c_onesd, r], writes=[pm])
    S.op("dve", lambda h: h.tensor_tensor(r[:], r[:], pm[:].unsqueeze(1).to_broadcast([128, 8, TT]),
                                          op=ALU.subtract), reads=[r, pm], wacc=[r])
    S.op("act", lambda h: h.activation(sq[:], r[:], AF.Square), reads=[r], writes=[sq])
    for c in range(8):
        S.op("pe", lambda h: h.matmul(pv[:], P.c_onesd[:], sq[:, c, :], start=(c == 0), stop=(c == 7)),
             reads=[P.c_onesd, sq], writes=[pv])
    P.rsqrt(rstd[:], pv[:], P.c_eps_ln, reads=[pv], wacc=[rstd])
    S.op("dve", lambda h: h.tensor_tensor(r[:], r[:], rstd[:].unsqueeze(1).to_broadcast([128, 8, TT]),
                                          op=ALU.mult), reads=[r, rstd], wacc=[r])
    for c in range(8):
        S.op("act", lambda h: h.activation(of[:, c, :], r[:, c, :], AF.Identity,
                                           bias=P.c_lnb[:, lnidx, c:c + 1], scale=P.c_lng[:, lnidx, c:c + 1]),
             reads=[r, P.c_lng, P.c_lnb], wacc=[of])
    S.op("pool", lambda h: h.tensor_copy(ob[:], of[:]), reads=[of], writes=[ob])


def phase_linear_ln(P, name, w_d, kcin, src_d, xres_fn, lnidx, outf_d, outb_d):
    S = P.S
    with ExitStack() as st:
        stg = [P.sb(st, "stg%d" % i, [128, 1024], F32) for i in range(2)]
        wb = P.load_w(st, "w_" + name, w_d, kcin * 128, D, stg)
        RB = RowBufs(P, st)
        srcs = [P.sb(st, "src%d" % i, [128, kcin, TT], BF16) for i in range(2)]
        for i in range(NTILE):
            sb_, xr = srcs[i % 2], RB.xr[i % 2]
            S.dma("sp", sb_[:], src_d[:, :, i * TT:(i + 1) * TT], writes=[sb_])
            S.dma("sp", xr[:], xres_fn(i), writes=[xr])
            linear_resid_ln(P, RB, i, wb, kcin, sb_, xr, lnidx, outf_d, outb_d, pbase=(i * 2) % 8)
        S.barrier()


def phase_xattn(P, layer, xf_d, xb_d, outf_d, outb_d):
    S = P.S
    wq_d = P.din("xa_w_q%d" % layer, [D, D])
    wkv_d = P.din("xa_w_kv%d" % layer, [D, 2 * D])
    wo_d = P.din("xa_w_o%d" % layer, [D, D])
    if not hasattr(P, "memT"):
        P.memT = {"p": P.din("memTp", [128, 8, 256]), "s": P.din("memTs", [128, 8, 256])}
    lnidx = layer * 3 + 1
    with ExitStack() as st:
        stg = [P.sb(st, "stg%d" % i, [128, 1024], F32) for i in range(2)]
        wq = P.load_w(st, "w_xq", wq_d, D, D, stg)
        wo = P.load_w(st, "w_xo", wo_d, D, D, stg)
        memK = {sg: P.sb(st, "memK" + sg, [128, 8, 256], BF16) for sg in ("p", "s")}
        memV = {sg: P.sb(st, "memV" + sg, [128, 2, D], BF16) for sg in ("p", "s")}
        with ExitStack() as st2:
            wkv = P.load_w(st2, "w_xkv", wkv_d, D, 2 * D, stg)
            mf = P.sb(st2, "memf", [128, 8, 256], F32)
            mb = P.sb(st2, "memb", [128, 8, 256], BF16)
            pb = 0
            for sg in ("p", "s"):
                S.dma("sp", mf[:], P.memT[sg][:, :, :], writes=[mf])
                S.op("dve", lambda h: h.tensor_copy(mb[:], mf[:]), reads=[mf], writes=[mb])
                for oc in range(8):
                    pt = P.ps[pb % 8]
                    pb += 1
                    for c in range(8):
                        S.op("pe", lambda h: h.matmul(pt[:, 0:256], wkv[:, c, oc * 128:(oc + 1) * 128], mb[:, c, :],
                                                      start=(c == 0), stop=(c == 7)), reads=[wkv, mb], writes=[pt])
                    P.copy(P.eng2(), memK[sg][:, oc, :], pt[:, 0:256], reads=[pt], wacc=[memK[sg]])
                for mc in range(2):
                    for n0 in range(2):
                        pt = P.ps[pb % 8]
                        pb += 1
                        for c in range(8):
                            S.op("pe", lambda h: h.matmul(pt[:], mb[:, c, mc * 128:(mc + 1) * 128],
                                                          wkv[:, c, D + n0 * 512:D + (n0 + 1) * 512],
                                                          start=(c == 0), stop=(c == 7)), reads=[wkv, mb], writes=[pt])
                        P.copy(P.eng2(), memV[sg][:, mc, n0 * 512:(n0 + 1) * 512], pt[:], reads=[pt],
                               wacc=[memV[sg]])
            S.barrier()
        RB = RowBufs(P, st)
        xbs = [P.sb(st, "xa_xb%d" % i, [128, 8, TT], BF16) for i in range(2)]
        qb = P.sb(st, "xa_q", [128, 8, TT], BF16)
        ob_ = P.sb(st, "xa_o", [128, 8, TT], BF16)
        ee = [P.sb(st, "xa_e%d" % i, [128, 2, TT], BF16) for i in range(2)]
        rden = [P.sb(st, "xa_rd%d" % i, [128, TT], F32) for i in range(2)]
        pb = 0
        for i in range(NTILE):
            sg = "p" if i < NP_OWN // TT else "s"
            xb, xr = xbs[i % 2], RB.xr[i % 2]
            S.dma("sp", xb[:], xb_d[:, :, i * TT:(i + 1) * TT], writes=[xb])
            S.dma("sp", xr[:], xf_d[:, :, i * TT:(i + 1) * TT], writes=[xr])
            for oc in range(8):
                pt = P.ps[pb % 8]
                pb += 1
                for c in range(8):
                    S.op("pe", lambda h: h.matmul(pt[:], wq[:, c, oc * 128:(oc + 1) * 128], xb[:, c, :],
                                                  start=(c == 0), stop=(c == 7)), reads=[wq, xb], writes=[pt])
                P.copy(P.eng2(), qb[:, oc, :], pt[:], reads=[pt], wacc=[qb], scale=1.0 / 16)
            for hh in range(4):
                E, RD = ee[hh % 2], rden[hh % 2]
                for mc in range(2):
                    pt = P.ps[pb % 8]
                    pb += 1
                    for cc in range(2):
                        S.op("pe", lambda h: h.matmul(pt[:], memK[sg][:, 2 * hh + cc, mc * 128:(mc + 1) * 128],
                                                      qb[:, 2 * hh + cc, :], start=(cc == 0), stop=(cc == 1)),
                             reads=[memK[sg], qb], writes=[pt])
                    S.op("act", lambda h: h.activation(E[:, mc, :], pt[:], AF.Exp), reads=[pt], wacc=[E])
                pd = P.ps[pb % 8]
                pb += 1
                for mc in range(2):
                    S.op("pe", lambda h: h.matmul(pd[:], P.c_onesb[:], E[:, mc, :], start=(mc == 0), stop=(mc == 1)),
                         reads=[P.c_onesb, E], writes=[pd])
                S.op("dve", lambda h: h.reciprocal(RD[:], pd[:]), reads=[pd], writes=[RD])
                for dvc in range(2):
                    po = P.ps[pb % 8]
                    pb += 1
                    for mc in range(2):
                        S.op("pe", lambda h: h.matmul(po[:], memV[sg][:, mc, hh * 256 + dvc * 128:hh * 256 + (dvc + 1) * 128],
                                                      E[:, mc, :], start=(mc == 0), stop=(mc == 1)),
                             reads=[memV[sg], E], writes=[po])
                    S.op("dve", lambda h: h.tensor_tensor(ob_[:, 2 * hh + dvc, :], po[:], RD[:], op=ALU.mult),
                         reads=[po, RD], wacc=[ob_])
            linear_resid_ln(P, RB, i, wo, 8, ob_, xr, lnidx, outf_d, outb_d, pbase=pb % 8)
            pb += 2
        S.barrier()


def phase_ffn1(P, layer, xb_d, h_d):
    S = P.S
    w_d = P.din("ffn_w_in%d" % layer, [D, 2 * FFN_H])
    with ExitStack() as st:
        stg = [P.sb(st, "stg%d" % i, [128, 1024], F32) for i in range(2)]
        wb = P.load_w(st, "w_ffn_in", w_d, D, 2 * FFN_H, stg)
        xbs = [P.sb(st, "f1_xb%d" % i, [128, 8, TT], BF16) for i in range(2)]
        hid = [P.sb(st, "f1_h%d" % i, [128, HC, TT], BF16) for i in range(2)]
        sgb = [P.sb(st, "f1_sg%d" % i, [128, TT], F32) for i in range(3)]
        pb = 0
        for i in range(NTILE):
            xb, hd = xbs[i % 2], hid[i % 2]
            S.dma("sp", xb[:], xb_d[:, :, i * TT:(i + 1) * TT], writes=[xb])
            for hc in range(HC):
                pg = P.ps[pb % 8]
                pu = P.ps[(pb + 1) % 8]
                pb += 2
                for c in range(8):
                    S.op("pe", lambda h: h.matmul(pg[:], wb[:, c, hc * 128:(hc + 1) * 128], xb[:, c, :],
                                                  start=(c == 0), stop=(c == 7)), reads=[wb, xb], writes=[pg])
                for c in range(8):
                    S.op("pe", lambda h: h.matmul(pu[:], wb[:, c, FFN_H + hc * 128:FFN_H + (hc + 1) * 128], xb[:, c, :],
                                                  start=(c == 0), stop=(c == 7)), reads=[wb, xb], writes=[pu])
                sgt = sgb[hc % 3]
                S.op("act", lambda h: h.activation(sgt[:], pg[:], AF.Silu), reads=[pg], writes=[sgt])
                S.op("dve", lambda h: h.tensor_tensor(hd[:, hc, :], sgt[:], pu[:], op=ALU.mult), reads=[sgt, pu],
                     wacc=[hd])
            S.dma("pool", h_d[:, :, i * TT:(i + 1) * TT], hd[:], reads=[hd])
        S.barrier()


def layer_tail(P, layer, mix_d, w_out_d, xres_fn):
    L = "L%d" % layer
    x1f = P.dscratch(L + "x1f", [128, 8, NOWN], F32)
    x1b = P.dscratch(L + "x1b", [128, 8, NOWN], BF16)
    phase_linear_ln(P, L + "mixout", w_out_d, 8, mix_d, xres_fn, layer * 3 + 0, x1f, x1b)
    x2f = P.dscratch(L + "x2f", [128, 8, NOWN], F32)
    x2b = P.dscratch(L + "x2b", [128, 8, NOWN], BF16)
    phase_xattn(P, layer, x1f, x1b, x2f, x2b)
    hd = P.dscratch(L + "hid", [128, HC, NOWN], BF16)
    phase_ffn1(P, layer, x2b, hd)
    if layer == 1:
        x3f = P.dout("yT", [128, 8, NOWN], F32)
        x3b = None
    else:
        x3f = P.dscratch(L + "x3f", [128, 8, NOWN], F32)
        x3b = P.dscratch(L + "x3b", [128, 8, NOWN], BF16)
    w2 = P.din("ffn_w_out%d" % layer, [FFN_H, D])
    phase_linear_ln(P, L + "ffnout", w2, HC, hd, lambda i: x2f[:, :, i * TT:(i + 1) * TT], layer * 3 + 2, x3f, x3b)
    return x3f, x3b


QPERM = [0, 3, 1, 4, 2, 5, 6, 9, 7, 10, 8, 11]


def phase_proj1(P, xb_d):
    nc, S = P.nc, P.S
    w_d = P.din("cd_w_in_p", [D, 1536])
    gains_d = P.din("qk_gain_r", [128, 16, 64])
    cos_d = P.din("rope_cos", [128, NOWN // 128, 32])
    sin_d = P.din("rope_sin", [128, NOWN // 128, 32])
    P.QT1 = P.dscratch("QT1", [128, 6, NOWN], BF16)
    P.KT1 = {"p": [P.dscratch("KT1p%d" % i, [256, 2048], BF16) for i in range(2)],
             "s": [P.dscratch("KT1s", [256, NS_OWN], BF16)]}
    P.V1 = {"p": [P.dscratch("V1p%d" % i, [1024, 260], BF16) for i in range(4)],
            "s": [P.dscratch("V1s%d" % i, [1024, 260], BF16) for i in range(2)]}
    P.U1 = P.dscratch("U1", [128, 2, NOWN], F32)
    with ExitStack() as st:
        stg = [P.sb(st, "stg%d" % i, [128, 1024], F32) for i in range(2)]
        wb = P.load_w(st, "w_cd_in", w_d, D, 1536, stg)
        gains = P.sb(st, "p1_gain", [128, 16, 64], F32)
        S.dma("sp", gains[:], gains_d[:, :, :], writes=[gains])
        S.op("dve", lambda h: h.tensor_scalar_mul(gains[:, 0:12, :], gains[:, 0:12, :], 0.125), reads=[gains],
             wacc=[gains])
        cs = P.sb(st, "p1_cos", [128, NOWN // 128, 32], F32)
        sn = P.sb(st, "p1_sin", [128, NOWN // 128, 32], F32)
        S.dma("sp", cs[:], cos_d[:, :, :], writes=[cs])
        S.dma("sp", sn[:], sin_d[:, :, :], writes=[sn])
        xbs = [P.sb(st, "p1_xb%d" % i, [128, 8, TT], BF16) for i in range(2)]
        sq = P.sb(st, "p1_sq", [128, 1024], F32)
        ss = P.sb(st, "p1_ss", [128, 16], F32)
        rs = P.sb(st, "p1_rs", [128, 16], F32)
        qn = P.sb(st, "p1_qn", [128, 16, 64], F32)
        tt = [P.sb(st, "p1_t%d" % i, [128, 16, 32], F32) for i in range(4)]
        qr = P.sb(st, "p1_qr", [128, 4, 16, 64], BF16)
        qts = [P.sb(st, "p1_qt%d" % i, [128, 8, TT], BF16) for i in range(2)]
        vs = [P.sb(st, "p1_vs%d" % i, [128, 4, 4, 65], BF16) for i in range(2)]
        us = [P.sb(st, "p1_us%d" % i, [128, 2, TT], F32) for i in range(2)]
        for v_ in vs:
            S.op("dve", lambda h: h.memset(v_[:], 1.0), writes=[v_])
        pb = 0
        for i in range(NTILE):
            sg = "p" if i < 8 else "s"
            il = i if sg == "p" else i - 8
            xb, QT, VS, US = xbs[i % 2], qts[i % 2], vs[i % 2], us[i % 2]
            S.dma("sp", xb[:], xb_d[:, :, i * TT:(i + 1) * TT], writes=[xb])
            for sub in range(4):
                pa, pbk = P.ps[pb % 8], P.ps[(pb + 1) % 8]
                pv = P.ps[(pb + 2) % 8]
                pb += 3
                for c in range(8):
                    S.op("pe", lambda h: h.matmul(pa[:], xb[:, c, sub * 128:(sub + 1) * 128], wb[:, c, 0:512],
                                                  start=(c == 0), stop=(c == 7)), reads=[wb, xb], writes=[pa])
                for c in range(8):
                    S.op("pe", lambda h: h.matmul(pbk[:], xb[:, c, sub * 128:(sub + 1) * 128], wb[:, c, 512:1024],
                                                  start=(c == 0), stop=(c == 7)), reads=[wb, xb], writes=[pbk])
                for c in range(8):
                    S.op("pe", lambda h: h.matmul(pv[:, 0:256], xb[:, c, sub * 128:(sub + 1) * 128],
                                                  wb[:, c, 1024:1280], start=(c == 0), stop=(c == 7)),
                         reads=[wb, xb], writes=[pv])
                S.op("act", lambda h: h.copy(VS[:, sub, :, 0:64], pv[:, 0:256].rearrange("p (h d) -> p h d", d=64)),
                     reads=[pv], wacc=[VS])
                S.op("act", lambda h: h.activation(sq[:, 0:512], pa[:], AF.Square), reads=[pa], wacc=[sq])
                S.op("act", lambda h: h.activation(sq[:, 512:1024], pbk[:], AF.Square), reads=[pbk], wacc=[sq])
                S.op("dve", lambda h: h.tensor_reduce(ss[:], sq[:].rearrange("p (h d) -> p h d", d=64), axis=AX.X,
                                                      op=ALU.add), reads=[sq], writes=[ss])
                S.op("dve", lambda h: h.tensor_scalar(ss[:], ss[:], 1.0 / 64, None, op0=ALU.mult), reads=[ss],
                     wacc=[ss])
                P.rsqrt(rs[:], ss[:], P.c_eps_rms, reads=[ss], wacc=[rs])
                rsb = rs[:].unsqueeze(2).to_broadcast([128, 16, 64])
                S.op("dve", lambda h: h.tensor_tensor(qn[:, 0:8, :], pa[:].rearrange("p (h d) -> p h d", d=64),
                                                      rsb[:, 0:8, :], op=ALU.mult), reads=[pa, rs], wacc=[qn])
                S.op("dve", lambda h: h.tensor_tensor(qn[:, 8:16, :], pbk[:].rearrange("p (h d) -> p h d", d=64),
                                                      rsb[:, 8:16, :], op=ALU.mult), reads=[pbk, rs], wacc=[qn])
                S.op("pool", lambda h: h.tensor_tensor(qn[:], qn[:], gains[:], op=ALU.mult), reads=[qn, gains],
                     wacc=[qn])
                q4 = qn[:].rearrange("p h (i two) -> p h i two", two=2)
                x1, x2 = q4[:, :, :, 0], q4[:, :, :, 1]
                gsub = i * 4 + sub
                cb = cs[:, gsub, :].unsqueeze(1).to_broadcast([128, 16, 32])
                sb_ = sn[:, gsub, :].unsqueeze(1).to_broadcast([128, 16, 32])
                o4 = qr[:, sub, :, :].rearrange("p h (i two) -> p h i two", two=2)
                S.op("dve", lambda h: h.tensor_tensor(tt[0][:], x1, cb, op=ALU.mult), reads=[qn, cs], writes=[tt[0]])
                S.op("pool", lambda h: h.tensor_tensor(tt[1][:], x2, sb_, op=ALU.mult), reads=[qn, sn],
                     writes=[tt[1]])
                S.op("pool", lambda h: h.tensor_tensor(tt[2][:], x1, sb_, op=ALU.mult), reads=[qn, sn],
                     writes=[tt[2]])
                S.op("dve", lambda h: h.tensor_tensor(tt[3][:], x2, cb, op=ALU.mult), reads=[qn, cs], writes=[tt[3]])
                S.op("dve", lambda h: h.tensor_tensor(o4[:, :, :, 0], tt[0][:], tt[1][:], op=ALU.subtract),
                     reads=[tt[0], tt[1]], wacc=[qr])
                S.op("pool", lambda h: h.tensor_tensor(o4[:, :, :, 1], tt[2][:], tt[3][:], op=ALU.add),
                     reads=[tt[2], tt[3]], wacc=[qr])
            for fc in range(8):
                pt = P.ps[pb % 8]
                pb += 1
                ptb = pt[:].bitcast(BF16)
                for sub in range(4):
                    S.op("pe", lambda h: h.transpose(ptb[:, sub * 128:(sub + 1) * 128],
                                                     qr[:, sub, 2 * fc:2 * fc + 2, :].rearrange("p h d -> p (h d)"),
                                                     P.c_identb[:]), reads=[qr, P.c_identb], writes=[pt])
                P.copy(P.eng2(), QT[:, fc, :], ptb[:, 0:512], reads=[pt], wacc=[QT])
            t0 = i * TT
            S.dma("pool", P.QT1[:, :, t0:t0 + TT], QT[:, 0:6, :], reads=[QT])
            tl = il * TT
            S.dma("pool", P.KT1[sg][tl // 2048][:, tl % 2048:tl % 2048 + TT].rearrange("(c p) t -> p c t", p=128),
                  QT[:, 6:8, :], reads=[QT])
            S.dma("pool", P.V1[sg][tl // 1024][tl % 1024:tl % 1024 + TT, :].rearrange("(s p) f -> p s f", p=128),
                  VS[:].rearrange("p s h d -> p s (h d)"), reads=[VS])
            for c2 in range(2):
                pt = P.ps[pb % 8]
                pb += 1
                for c in range(8):
                    S.op("pe", lambda h: h.matmul(pt[:], wb[:, c, 1280 + c2 * 128:1280 + (c2 + 1) * 128], xb[:, c, :],
                                                  start=(c == 0), stop=(c == 7)), reads=[wb, xb], writes=[pt])
                P.copy(P.eng2(), US[:, c2, :], pt[:], reads=[pt], wacc=[US])
            S.dma("pool", P.U1[:, :, t0:t0 + TT], US[:], reads=[US])
        S.barrier()


def phase_gqa(P):
    nc, S = P.nc, P.S
    P.OT1 = P.dscratch("OT1", [128, 8, NOWN], BF16)
    ktall = [P.dscratch("KT1all%d" % i, [1024, 2048], BF16) for i in range(2)]
    vall = [P.dscratch("V1all%d" % i, [4096, 260], BF16) for i in range(4)]
    G = Buf(None)
    for i in range(2):
        S.allgather(ktall[i], P.KT1["p"][i], reads=[], writes=[], groups=GROUPS4)
    for i in range(4):
        S.allgather(vall[i], P.V1["p"][i], reads=[], writes=[], groups=GROUPS4)
    S.cc_fence(GROUPS4)
    G.ws = [("cc", S.cc_cnt, "dma")]
    with ExitStack() as st:
        ones32 = P.sb(st, "g_ones32", [128, 64], F32)
        S.op("dve", lambda h: h.memset(ones32[:], 1.0), writes=[ones32])
        KT = P.sb(st, "gKT", [128, SEQ_P], BF16)
        VA = P.sb(st, "gVA", [128, SEQ_P // 128, 2, 65], BF16)
        QTs = [P.sb(st, "gQT%d" % i, [128, NP_OWN], BF16) for i in range(2)]
        PA = PairAttn(P, st, with_z=False)
        qi_ = 0
        for sg in ("p", "s"):
            nseq = SEQ_P if sg == "p" else SEQ_S
            n_own = NP_OWN if sg == "p" else NS_OWN
            ooff = 0 if sg == "p" else NP_OWN
            nch = nseq // 128
            nqt = n_own // TT
            for kc in range(2):
                if sg == "p":
                    for r in range(4):
                        for hf in range(2):
                            S.dma("sp", KT[:, r * 4096 + hf * 2048:r * 4096 + (hf + 1) * 2048],
                                  ktall[hf][r * 256 + kc * 128:r * 256 + (kc + 1) * 128, :], reads=[G], wacc=[KT])
                        for j in range(4):
                            c0 = (r * 4096 + j * 1024) // 128
                            S.dma("sp", VA[:, c0:c0 + 8, :, :].rearrange("p c h d -> p c (h d)"),
                                  vall[j][r * 1024:(r + 1) * 1024, kc * 130:(kc + 1) * 130].rearrange(
                                      "(c p) f -> p c f", p=128), reads=[G], wacc=[VA])
                else:
                    S.dma("sp", KT[:, 0:nseq], P.KT1["s"][0][kc * 128:(kc + 1) * 128, :], wacc=[KT])
                    for j in range(2):
                        S.dma("sp", VA[:, j * 8:(j + 1) * 8, :, :].rearrange("p c h d -> p c (h d)"),
                              P.V1["s"][j][:, kc * 130:(kc + 1) * 130].rearrange("(c p) f -> p c f", p=128),
                              wacc=[VA])
                for j in range(3):
                    qc = 3 * kc + j
                    QT = QTs[qi_ % 2]
                    qi_ += 1
                    S.dma("sp", QT[:, 0:n_own], P.QT1[:, qc, ooff:ooff + n_own], writes=[QT])
                    items = []
                    for qi in range(nqt):
                        for ch in range(nch):
                            items.append((qi, ch, 0, ch == 0, ch == nch - 1))

                    def out_fn(qi, out, qc=qc, ooff=ooff):
                        o0 = ooff + qi * TT
                        S.dma("pool", P.OT1[:, qc, o0:o0 + TT], out[:], reads=[out])

                    PA.run(items, KT, QT, VA, None, out_fn)
        S.barrier()


def phase_pool(P):
    nc, S = P.nc, P.S
    pw = P.din("cd_pool_w", [4, 64, 64])
    psc = P.din("pool_scale_r", [128, 2])
    rc_d = P.din("pool_rc", [128, 2, NOWN])
    sel_d = P.din("pool_sel", [128, 8])
    edge = P.dscratch("pool_edge", [256, 16], F32)
    edall = P.dscratch("pool_edall", [1024, 16], F32)
    EDG, EDA = Buf(edge), Buf(edall)
    with ExitStack() as st:
        psc_s = P.sb(st, "pl_sc", [128, 2], F32)
        sel = P.sb(st, "pl_sel", [128, 8], F32)
        S.dma("sp", psc_s[:], psc[:, :], writes=[psc_s])
        S.dma("sp", sel[:], sel_d[:, :], writes=[sel])
        eg = P.sb(st, "pl_eg", [128, 2, 16], F32)
        S.dma("sp", eg[:, :, 0:8], P.U1[:, :, 0:8], wacc=[eg])
        S.dma("sp", eg[:, :, 8:16], P.U1[:, :, NP_OWN - 8:NP_OWN], wacc=[eg])
        S.dma("pool", edge.ap().rearrange("(c p) t -> p c t", p=128), eg[:], reads=[eg], writes=[EDG])
        S.allgather(edall, edge, reads=[EDG], writes=[EDA], groups=GROUPS4)
        S.cc_fence(GROUPS4)
        EDA.ws = [("cc", S.cc_cnt, "dma")]
        ea = P.sb(st, "pl_ea", [128, 4, 2, 16], F32)
        for r in range(4):
            S.dma("sp", ea[:, r, :, :], edall[r * 256:(r + 1) * 256, :].rearrange("(c p) t -> p c t", p=128),
                  reads=[EDA], wacc=[ea])
        wbd = P.sb(st, "pl_wbd", [128, 128], F32)
        wbb = P.sb(st, "pl_wbb", [128, 128], BF16)
        NE = NP_OWN + 16
        ue = P.sb(st, "pl_ue", [128, NE], F32)
        sA = P.sb(st, "pl_sA", [128, NE], F32)
        sB = P.sb(st, "pl_sB", [128, NE], F32)
        rc = P.sb(st, "pl_rc", [128, NP_OWN], F32)
        mx = P.sb(st, "pl_mx", [128, NP_OWN], BF16)
        ot = [P.sb(st, "pl_ot%d" % i, [128, TT], BF16) for i in range(2)]
        pb = 0
        for sg in ("p", "s"):
            n = NP_OWN if sg == "p" else NS_OWN
            ooff = 0 if sg == "p" else NP_OWN
            for c2 in range(2):
                S.op("dve", lambda h: h.memset(wbd[:], 0.0), writes=[wbd])
                S.dma("sp", wbd[0:64, 0:64], pw[2 * c2, :, :], wacc=[wbd])
                S.dma("sp", wbd[64:128, 64:128], pw[2 * c2 + 1, :, :], wacc=[wbd])
                S.op("dve", lambda h: h.tensor_copy(wbb[:], wbd[:]), reads=[wbd], writes=[wbb])
                S.op("dve", lambda h: h.memset(ue[:, 0:8], 0.0), wacc=[ue])
                S.op("dve", lambda h: h.memset(ue[:, 8 + n:16 + n], 0.0), wacc=[ue])
                S.dma("sp", ue[:, 8:8 + n], P.U1[:, c2, ooff:ooff + n], wacc=[ue])
                S.dma("sp", rc[:, 0:n], rc_d[:, c2, ooff:ooff + n], writes=[rc])
                if sg == "p":
                    for r in range(4):
                        S.op("dve", lambda h: h.scalar_tensor_tensor(ue[:, 0:8], ea[:, r, c2, 8:16], sel[:, r:r + 1],
                                                                     ue[:, 0:8], op0=ALU.mult, op1=ALU.add),
                             reads=[ea, sel, ue], wacc=[ue])
                        S.op("dve", lambda h: h.scalar_tensor_tensor(ue[:, 8 + n:16 + n], ea[:, r, c2, 0:8],
                                                                     sel[:, 4 + r:5 + r], ue[:, 8 + n:16 + n],
                                                                     op0=ALU.mult, op1=ALU.add),
                             reads=[ea, sel, ue], wacc=[ue])
                S.op("dve", lambda h: h.tensor_tensor(sA[:, 1:16 + n], ue[:, 0:15 + n], ue[:, 1:16 + n], op=ALU.add),
                     reads=[ue], writes=[sA])
                S.op("dve", lambda h: h.tensor_tensor(sB[:, 2:15 + n], sA[:, 1:14 + n], sA[:, 3:16 + n], op=ALU.add),
                     reads=[sA], writes=[sB])
                if c2 == 1:
                    S.op("dve", lambda h: h.tensor_tensor(sA[:, 4:13 + n], sB[:, 2:11 + n], sB[:, 6:15 + n],
                                                          op=ALU.add), reads=[sB], writes=[sA])
                    S.op("dve", lambda h: h.tensor_tensor(sB[:, 8:8 + n], sA[:, 4:4 + n], sA[:, 12:12 + n],
                                                          op=ALU.add), reads=[sA], writes=[sB])
                for half, sbuf_ in ((0, sA), (1, sB)):
                    ps_ = slice(half * 64, (half + 1) * 64)
                    S.op("dve", lambda h: h.tensor_tensor(sbuf_[ps_, 8:8 + n], sbuf_[ps_, 8:8 + n], rc[ps_, 0:n],
                                                          op=ALU.mult), reads=[sbuf_, rc], wacc=[sbuf_])
                    S.op("dve", lambda h: h.tensor_tensor(mx[ps_, 0:n], sbuf_[ps_, 8:8 + n], ue[ps_, 8:8 + n],
                                                          op=ALU.subtract), reads=[sbuf_, ue], wacc=[mx])
                for ti in range(n // TT):
                    pt = P.ps[pb % 8]
                    pb += 1
                    o = ot[ti % 2]
                    S.op("pe", lambda h: h.matmul(pt[:], wbb[:], mx[:, ti * TT:(ti + 1) * TT], start=True, stop=True),
                         reads=[wbb, mx], writes=[pt])
                    S.op("act", lambda h: h.activation(o[:], pt[:], AF.Identity, scale=psc_s[:, c2:c2 + 1]),
                         reads=[pt, psc_s], writes=[o])
                    o0 = ooff + ti * TT
                    S.dma("pool", P.OT1[:, 6 + c2, o0:o0 + TT], o[:], reads=[o])
        S.barrier()


def host_inputs(inp, names):
    cc = _consts_common()
    f32 = lambda a: np.ascontiguousarray(a, dtype=np.float32)
    maps = []
    for c in range(NCORES):
        b, q = c // 4, c % 4
        o0 = q * NP_OWN
        m = {}
        for nm in names:
            if nm.startswith("k_"):
                m[nm] = cc[nm[2:]]
            elif nm == "xTp":
                ext = np.zeros((NP_EXT, D), np.float32)
                lo, hi = o0 - 1024, o0 + NP_OWN + 1024
                a, bb = max(lo, 0), min(hi, SEQ_P)
                ext[a - lo:bb - lo] = inp["x_prompt"][b, a:bb]
                m[nm] = _fm(ext.T)
            elif nm == "vldp":
                t = np.arange(o0 - 1024, o0 + NP_OWN + 1024)
                v = ((t >= 0) & (t < SEQ_P)).astype(np.float32)
                m[nm] = f32(v.reshape(NP_EXT // 128, 128).T)
            elif nm == "xTs":
                m[nm] = _fm(f32(inp["x_sample"][c]).T)
            elif nm == "vlds":
                m[nm] = np.ones((128, SEQ_S // 128), np.float32)
            elif nm == "ab_fnet_w":
                m[nm] = f32(inp["ab_fnet_w"][0])
            elif nm.startswith("f_"):
                kind, sg = nm[2:4], nm[5]
                if sg == "p":
                    tabs = _dft_tables(SEQ_P, 128, 128, list(range(32 * q, 32 * q + 32)))
                else:
                    tabs = _dft_tables(SEQ_S, 16, 128, list(range(128)))
                m[nm] = tabs[["rp", "rr", "tc", "ts", "c2", "s2"].index(kind)]
            elif nm[:-1] in ("xa_w_q", "xa_w_kv", "xa_w_o", "ffn_w_in", "ffn_w_out"):
                m[nm] = f32(inp[nm[:-1]][int(nm[-1])])
            elif nm == "ab_w_out":
                m[nm] = f32(inp["ab_w_out"][0])
            elif nm == "memTp":
                m[nm] = _fm(f32(inp["mem_prompt"][b]).T)
            elif nm == "memTs":
                m[nm] = _fm(f32(inp["mem_sample"][c]).T)
            elif nm == "cd_w_in_p":
                w = np.asarray(inp["cd_w_in"][0], np.float32)
                wq = w[:, :768].reshape(D, 12, 64)[:, QPERM, :].reshape(D, 768)
                m[nm] = f32(np.concatenate([wq, w[:, 768:]], 1))
            elif nm == "cd_w_out_p":
                w = np.asarray(inp["cd_w_out"][0], np.float32)
                wq = w[:768].reshape(12, 64, D)[QPERM].reshape(768, D)
                m[nm] = f32(np.concatenate([wq, w[768:]], 0))
            elif nm == "qk_gain_r":
                g = np.concatenate([np.tile(np.asarray(inp["cd_q_norm"][0], np.float32)[None], (12, 1)),
                                    np.tile(np.asarray(inp["cd_k_norm"][0], np.float32)[None], (4, 1))], 0)
                m[nm] = f32(np.broadcast_to(g[None], (128, 16, 64)))
            elif nm in ("rope_cos", "rope_sin"):
                pos = np.concatenate([o0 + np.arange(NP_OWN), np.arange(NS_OWN)])
                freqs = (np.float32(10000.0) ** (-np.arange(0, 32, 2, dtype=np.float32) / np.float32(32))).astype(np.float32)
                row = (pos // 64).astype(np.float32)
                col = (pos % 64).astype(np.float32)
                ang = np.concatenate([row[:, None] * freqs, col[:, None] * freqs], -1).astype(np.float32)
                t = np.cos(ang) if nm == "rope_cos" else np.sin(ang)
                m[nm] = f32(t.reshape(NOWN // 128, 128, 32).transpose(1, 0, 2))
            elif nm == "cd_pool_w":
                m[nm] = f32(inp["cd_pool_w"][0])
            elif nm == "pool_scale_r":
                m[nm] = f32(np.asarray(inp["cd_pool_scale"][0]).reshape(2, 128).T)
            elif nm == "pool_rc":
                pos = np.concatenate([o0 + np.arange(NP_OWN), np.arange(NS_OWN)])
                nn = np.concatenate([np.full(NP_OWN, SEQ_P), np.full(NS_OWN, SEQ_S)])
                rc = np.zeros((128, 2, NOWN), np.float32)
                for g_ in range(4):
                    w_ = (2, 4, 8, 16)[g_]
                    cnt = np.clip(pos + w_ // 2, 0, nn) - np.clip(pos - w_ // 2, 0, nn)
                    rc[(g_ % 2) * 64:(g_ % 2 + 1) * 64, g_ // 2, :] = (1.0 / cnt.astype(np.float32))[None]
                m[nm] = rc
            elif nm == "pool_sel":
                sel = np.zeros((128, 8), np.float32)
                if q > 0:
                    sel[:, q - 1] = 1.0
                if q < 3:
                    sel[:, 4 + q + 1] = 1.0
                m[nm] = sel
            elif nm == "rel_bias":
                m[nm] = f32(inp["rel_bias"])
            elif nm == "ab_w_in":
                m[nm] = f32(inp["ab_w_in"][0])
            elif nm == "fnet_g_r":
                m[nm] = f32(np.asarray(inp["ab_fnet_g"][0]).reshape(2, 128).T)
            elif nm in ("ln_g_r", "ln_b_r"):
                src = np.asarray(inp["ln_g" if nm == "ln_g_r" else "ln_b"], np.float32)
                m[nm] = f32(src.reshape(6, 8, 128).transpose(2, 0, 1))
            else:
                raise KeyError(nm)
        maps.append(m)
    return maps


def run_prog(P, inp):
    nc = P.finish()
    maps = host_inputs(inp, list(P.inputs.keys()))
    res = run_bass_kernel_spmd(nc, maps, core_ids=list(range(NCORES)))
    return res.results


def build_full(debug=()):
    P = Prog(debug=debug)
    P.setup()
    phase_proj0(P)
    phase_dil(P)
    phase_fnet(P)
    w_out0 = P.din("ab_w_out", [D, D])

    def xres0(i):
        if i < 8:
            return P.xT["p"][:, :, 1024 + i * TT:1024 + (i + 1) * TT]
        return P.xT["s"][:, :, (i - 8) * TT:(i - 7) * TT]

    x3f, x3b = layer_tail(P, 0, P.OT0, w_out0, xres0)
    phase_proj1(P, x3b)
    phase_gqa(P)
    phase_pool(P)
    w_out1 = P.din("cd_w_out_p", [D, D])
    layer_tail(P, 1, P.OT1, w_out1, lambda i: x3f[:, :, i * TT:(i + 1) * TT])
    return P


def kernel(**inputs):
    inp = {k: np.asarray(v) for k, v in inputs.items()}
    P = build_full()
    res = run_prog(P, inp)
    y_prompt = np.zeros((2, SEQ_P, D), np.float32)
    y_sample = np.zeros((8, SEQ_S, D), np.float32)
    for c in range(NCORES):
        b, q = c // 4, c % 4
        yT = np.asarray(res[c]["yT"], dtype=np.float32)
        y_prompt[b, q * NP_OWN:(q + 1) * NP_OWN] = _unfm(yT[:, :, :NP_OWN])
        y_sample[c] = _unfm(yT[:, :, NP_OWN:])
    return (y_prompt, y_sample)
```

```python
import math
import numpy as np
import ml_dtypes
import concourse.bass as bass
import concourse.mybir as mybir
from concourse.bass_utils import run_bass_kernel_spmd

F32 = mybir.dt.float32
BF16 = mybir.dt.bfloat16
AF = mybir.ActivationFunctionType
ALU = mybir.AluOpType
AX = mybir.AxisListType

NCORES = 8
D = 1024
KC = 8
TT = 512
NP_OWN = 4096
NS_OWN = 2048
NOWN = NP_OWN + NS_OWN
NTILE = NOWN // TT
NP_EXT = NP_OWN + 2048
SEQ_P = 16384
SEQ_S = 2048
FFN_H = 2816
HC = FFN_H // 128
DN_ALPHA = 4 ** 0.25
LN_EPS = 1e-5
RMS_EPS = 1e-6
ZW = 2944
ZC = 1408
FILLER = 1


class Buf:
    __slots__ = ("t", "ws", "r", "name", "wx")

    def __init__(self, t, name=""):
        self.t = t
        self.wx = None
        self.ws = []
        self.r = []
        self.name = name

    def __getitem__(self, idx):
        return self.t[idx]


def _compact(evs):
    best = {}
    for (k, v, s) in evs:
        if k not in best or best[k][1] < v:
            best[k] = (k, v, s)
    return list(best.values())


class Sch:
    def __init__(self, nc, ndma_sems=10):
        self.nc = nc
        self.eng = {"pe": nc.tensor, "act": nc.scalar, "dve": nc.vector,
                    "pool": nc.gpsimd, "sp": nc.sync}
        self.tick = {}
        self.seen = {}
        self.semh = {}
        self._ctx = []
        for e in self.eng:
            self._mksem("s_" + e)
            self.tick[e] = 0
            self.seen[e] = {}
        self.dq = {}
        for q in ("sp", "pool", "act"):
            names = []
            for i in range(ndma_sems):
                k = "d_%s_%d" % (q, i)
                self._mksem(k)
                names.append(k)
            self.dq[q] = {"sems": names, "n": 0, "cnt": {k: 0 for k in names}}
        self._mksem("cc")
        self.cc_cnt = 0
        self.ninst = 0

    def _mksem(self, key):
        cm = self.nc.semaphore(key)
        h = cm.__enter__()
        self._ctx.append(cm)
        self.semh[key] = h
        return h

    def close(self):
        for cm in reversed(self._ctx):
            cm.__exit__(None, None, None)
        self._ctx = []

    def _wait(self, e, ev):
        semkey, val, src = ev
        if src == e and e == "pe":
            return
        if self.seen[e].get(semkey, 0) >= val:
            return
        self.eng[e].wait_ge(self.semh[semkey], val)
        self.seen[e][semkey] = val
        self.ninst += 1

    def _deps(self, e, reads, writes, wacc):
        for b in reads:
            for ev in b.ws:
                self._wait(e, ev)
        for b in writes:
            for ev in b.ws:
                self._wait(e, ev)
            for ev in b.r:
                self._wait(e, ev)
        for b in wacc:
            if b.wx is not None:
                self._wait(e, b.wx)
            for ev in b.r:
                self._wait(e, ev)

    def _commit(self, ev, reads, writes, wacc):
        for b in reads:
            b.r.append(ev)
            if len(b.r) > 16:
                b.r = _compact(b.r)
        for b in writes:
            b.ws = [ev]
            b.wx = ev
            b.r = []
        for b in wacc:
            b.ws.append(ev)
            if len(b.ws) > 16:
                b.ws = _compact(b.ws)

    def op(self, e, fn, reads=(), writes=(), wacc=()):
        self._deps(e, reads, writes, wacc)
        ins = fn(self.eng[e])
        self.tick[e] += 1
        k = "s_" + e
        ins.then_inc(self.semh[k], 1)
        self._commit((k, self.tick[e], e), reads, writes, wacc)
        self.ninst += 1
        return ins

    def dma(self, q, out, in_, reads=(), writes=(), wacc=(), **kw):
        d = self.dq[q]
        k = d["sems"][d["n"] % len(d["sems"])]
        d["n"] += 1
        if d["cnt"][k] > 0:
            self._wait(q, (k, d["cnt"][k], "dma"))
        self._deps(q, reads, writes, wacc)
        ins = self.eng[q].dma_start(out=out, in_=in_, **kw)
        d["cnt"][k] += 16
        ins.then_inc(self.semh[k], 16)
        self._commit((k, d["cnt"][k], "dma"), reads, writes, wacc)
        self.ninst += 1

    def allgather(self, out_t, in_t, reads, writes, groups):
        self._deps("pool", reads, writes, ())
        if self.cc_cnt:
            self._wait("pool", ("cc", self.cc_cnt, "dma"))
        ins = self.nc.gpsimd.collective_compute("AllGather", ALU.bypass, replica_groups=groups,
                                                ins=[in_t.ap().opt()], outs=[out_t.ap().opt()])
        self.cc_cnt += 1
        ins.then_inc(self.semh["cc"], 1)
        self._commit(("cc", self.cc_cnt, "dma"), reads, writes, ())
        self.ninst += 1

    def cc_fence(self, groups):
        if not hasattr(self, "_fence_t"):
            self._fence_t = (self.nc.dram_tensor("cc_f_in", [16, 64], F32),
                             self.nc.dram_tensor("cc_f_out", [16 * len(groups[0]), 64], F32))
        fi, fo = self._fence_t
        self.allgather(fo, fi, reads=[], writes=[], groups=groups)

    def all_events(self):
        evs = []
        for e in self.eng:
            if self.tick[e] > 0:
                evs.append(("s_" + e, self.tick[e], e))
        for q, d in self.dq.items():
            for k, v in d["cnt"].items():
                if v > 0:
                    evs.append((k, v, "dma"))
        if self.cc_cnt:
            evs.append(("cc", self.cc_cnt, "dma"))
        return evs

    def barrier(self, engines=("pe", "act", "dve", "pool", "sp")):
        evs = self.all_events()
        for e in engines:
            for ev in evs:
                if ev[2] == e:
                    continue
                self._wait(e, ev)


def _t5_bucket_np(rel):
    nb = 16
    max_exact = 8
    ret = np.where(rel > 0, nb, 0)
    n = np.abs(rel)
    nf = np.maximum(n, 1).astype(np.float32)
    large = max_exact + (np.log(nf / np.float32(max_exact)) / np.float32(math.log(1024 / max_exact))
                         * np.float32(nb - max_exact)).astype(np.int32)
    large = np.minimum(large, nb - 1)
    return ret + np.where(n < max_exact, n, large)


def _consts_common():
    c = {}
    c["ident"] = np.eye(128, dtype=np.float32)
    c["antiI"] = np.eye(128, dtype=np.float32)[::-1].copy()
    c["onesd"] = np.full((128, 128), 1.0 / D, np.float32)
    blk = np.zeros((128, 128), np.float32)
    blk[:64, :64] = 1.0 / 64
    blk[64:, 64:] = 1.0 / 64
    c["blk64"] = blk
    i = np.arange(3072)
    delta = 1535 - i
    mult = ((np.abs(delta) <= 64).astype(np.int32)
            + ((delta % 4 == 0) & (np.abs(delta) <= 256)).astype(np.int32)
            + ((delta % 16 == 0) & (np.abs(delta) <= 1024)).astype(np.int32))
    mult[3071] = 0
    bk = _t5_bucket_np(delta)
    ohm = np.zeros((32, 3072), np.float32)
    ohm[bk, i] = mult
    c["ohm"] = ohm
    k = np.arange(64)
    ang = 2 * np.pi * np.outer(k, k) / 64
    c64 = (np.cos(ang) / 8).astype(np.float32)
    s64 = (np.sin(ang) / 8).astype(np.float32)
    cbd = np.zeros((128, 128), np.float32)
    sbd = np.zeros((128, 128), np.float32)
    cbd[:64, :64] = c64
    cbd[64:, 64:] = c64
    sbd[:64, :64] = s64
    sbd[64:, 64:] = s64
    c["c64bd"] = cbd
    c["s64bd"] = sbd
    return c


def _dft_tables(N, N1, N2, k2_list):
    sc = 1.0 / math.sqrt(N)
    n1 = np.arange(N1)
    k1 = np.arange(N1)
    n2 = np.arange(N2)
    a1 = 2 * np.pi * np.outer(n1, k1) / N1
    rp = np.concatenate([np.cos(a1), -np.sin(a1)], 1) * sc
    rr = np.concatenate([np.sin(a1), np.cos(a1)], 1) * sc
    at = 2 * np.pi * np.outer(n2, k1) / N
    tc = np.cos(at)
    ts = np.sin(at)
    a2 = 2 * np.pi * np.outer(n2, np.asarray(k2_list)) / N2
    c2 = np.cos(a2)
    s2 = np.sin(a2)
    f = lambda a: np.ascontiguousarray(a, dtype=np.float32)
    return f(rp), f(rr), f(tc), f(ts), f(c2), f(s2)


def _fm(a):
    F, T = a.shape
    return np.ascontiguousarray(a.reshape(F // 128, 128, T).transpose(1, 0, 2))


def _unfm(a):
    P_, C, T = a.shape
    return np.ascontiguousarray(a.transpose(2, 1, 0).reshape(T, C * P_))


from contextlib import ExitStack


class Prog:
    def __init__(self, debug=()):
        self.debug = set(debug)
        self.nc = bass.Bass("TRN2", target_bir_lowering=False)
        self.S = Sch(self.nc)
        self.inputs = {}
        self.outputs = {}
        self.gstack = ExitStack()
        self.rr = 0

    def din(self, name, shape, dtype=F32):
        t = self.nc.dram_tensor(name, list(shape), dtype, kind="ExternalInput")
        self.inputs[name] = t
        return t

    def dout(self, name, shape, dtype=F32):
        t = self.nc.dram_tensor(name, list(shape), dtype, kind="ExternalOutput")
        self.outputs[name] = t
        return t

    def dscratch(self, name, shape, dtype):
        if name in self.debug:
            return self.dout(name, shape, dtype)
        return self.nc.dram_tensor(name, list(shape), dtype)

    def sb(self, stack, name, shape, dtype):
        self._uid = getattr(self, "_uid", 0) + 1
        name = "%s_u%d" % (name, self._uid)
        t = stack.enter_context(self.nc.sbuf_tensor(name, list(shape), dtype))
        return Buf(t, name)

    def eng2(self):
        self.rr += 1
        return "act" if self.rr % 2 else "dve"

    def copy(self, e, out, in_, reads, writes=(), wacc=(), scale=None):
        S = self.S
        if e == "act":
            if scale is None:
                S.op("act", lambda h: h.copy(out, in_), reads, writes, wacc)
            else:
                S.op("act", lambda h: h.mul(out, in_, scale), reads, writes, wacc)
        else:
            if scale is None:
                S.op(e, lambda h: h.tensor_copy(out, in_), reads, writes, wacc)
            else:
                S.op(e, lambda h: h.tensor_scalar_mul(out, in_, scale), reads, writes, wacc)

    def rsqrt(self, out, in_, eps_tile, reads, wacc):
        S = self.S
        S.op("act", lambda h: h.activation(out, in_, AF.Sqrt, bias=eps_tile[:, 0:1], scale=1.0),
             reads=list(reads) + [eps_tile], wacc=wacc)
        S.op("dve", lambda h: h.reciprocal(out, out), reads=list(wacc), wacc=wacc)

    def setup(self):
        nc, S = self.nc, self.S
        g = self.gstack
        self.ps2 = [Buf(g.enter_context(nc.psum_tensor("psp%d" % i, [128, 1024], F32)), "psp%d" % i) for i in range(4)]
        self.ps = [Buf(self.ps2[i // 2].t[:, (i % 2) * 512:(i % 2 + 1) * 512], "ps%d" % i) for i in range(8)]
        self.c_onesA = self.sb(g, "c_onesA", [128, 128], F32)
        self.c_onesB = self.sb(g, "c_onesB", [128, 128], F32)
        S.op("dve", lambda h: h.memset(self.c_onesA[:], 0.0), writes=[self.c_onesA])
        S.op("dve", lambda h: h.memset(self.c_onesB[:], 0.0), writes=[self.c_onesB])
        S.op("dve", lambda h: h.memset(self.c_onesA[:, 0:64], 1.0), reads=[self.c_onesA], wacc=[self.c_onesA])
        S.op("dve", lambda h: h.memset(self.c_onesB[:, 64:128], 1.0), reads=[self.c_onesB], wacc=[self.c_onesB])
        self.c_ident = self.sb(g, "c_ident", [128, 128], F32)
        self.c_antiI = self.sb(g, "c_antiI", [128, 128], F32)
        self.c_onesd = self.sb(g, "c_onesd", [128, 128], F32)
        self.c_blk64 = self.sb(g, "c_blk64", [128, 128], F32)
        self.c_onesb = self.sb(g, "c_onesb", [128, 128], BF16)
        self.c_identb = self.sb(g, "c_identb", [128, 128], BF16)
        self.c_lng = self.sb(g, "c_lng", [128, 6, 8], F32)
        self.c_lnb = self.sb(g, "c_lnb", [128, 6, 8], F32)
        for nm, buf in (("ident", self.c_ident), ("antiI", self.c_antiI), ("onesd", self.c_onesd),
                        ("blk64", self.c_blk64)):
            t = self.din("k_" + nm, [128, 128])
            S.dma("sp", buf[:], t[:, :], writes=[buf])
        t = self.din("ln_g_r", [128, 6, 8])
        S.dma("sp", self.c_lng[:], t[:, :, :], writes=[self.c_lng])
        t = self.din("ln_b_r", [128, 6, 8])
        S.dma("sp", self.c_lnb[:], t[:, :, :], writes=[self.c_lnb])
        self.c_eps_ln = self.sb(g, "c_eps_ln", [128, 1], F32)
        self.c_eps_rms = self.sb(g, "c_eps_rms", [128, 1], F32)
        S.op("dve", lambda h: h.memset(self.c_eps_ln[:], LN_EPS), writes=[self.c_eps_ln])
        S.op("dve", lambda h: h.memset(self.c_eps_rms[:], RMS_EPS), writes=[self.c_eps_rms])
        S.op("dve", lambda h: h.memset(self.c_onesb[:], 1.0), writes=[self.c_onesb])
        S.op("dve", lambda h: h.tensor_copy(self.c_identb[:], self.c_ident[:]), reads=[self.c_ident],
             writes=[self.c_identb])

    def load_w(self, stack, name, wap, K, N, stg):
        S = self.S
        kc = K // 128
        wb = self.sb(stack, name, [128, kc, N], BF16)
        CH = stg[0].t.shape[1]
        i = 0
        for c in range(kc):
            for n0 in range(0, N, CH):
                n1 = min(N, n0 + CH)
                st = stg[i % len(stg)]
                i += 1
                S.dma(("sp", "act", "pool")[i % 3], st[:, 0:n1 - n0], wap[c * 128:(c + 1) * 128, n0:n1], writes=[st])
                self.copy(self.eng2(), wb[:, c, n0:n1], st[:, 0:n1 - n0], reads=[st], wacc=[wb])
        return wb

    def finish(self):
        S = self.S
        S.barrier()
        self.gstack.close()
        S.close()
        return self.nc


def phase_proj0(P):
    nc, S = P.nc, P.S
    w_in = P.din("ab_w_in", [D, 2560])
    fg = P.din("fnet_g_r", [128, 2])
    P.KT0 = {"p": P.dscratch("KT0p", [128, 6, NP_EXT], BF16), "s": P.dscratch("KT0s", [128, 6, SEQ_S], BF16)}
    P.V0 = {"p": P.dscratch("V0p", [NP_EXT // 128, 128, 780], BF16),
            "s": P.dscratch("V0s", [SEQ_S // 128, 128, 780], BF16)}
    P.QT0 = P.dscratch("QT0", [128, 6, NOWN], BF16)
    P.unT = {"p": [P.dscratch("unTp%d" % i, [256, 2048], BF16) for i in range(2)],
             "s": [P.dscratch("unTs", [256, NS_OWN], BF16)]}
    xT = {"p": P.din("xTp", [128, 8, NP_EXT]), "s": P.din("xTs", [128, 8, SEQ_S])}
    vld = {"p": P.din("vldp", [128, NP_EXT // 128]), "s": P.din("vlds", [128, SEQ_S // 128])}
    P.xT = xT
    P.vld = vld
    with ExitStack() as st:
        stg = [P.sb(st, "stg%d" % i, [128, 1024], F32) for i in range(3)]
        wb = P.load_w(st, "w_ab_in", w_in, D, 2560, stg)
        fgs = P.sb(st, "fgs", [128, 2], F32)
        S.dma("sp", fgs[:], fg[:, :], writes=[fgs])
        xf = [P.sb(st, "xf%d" % i, [128, 8, TT], F32) for i in range(2)]
        xb = [P.sb(st, "xb%d" % i, [128, 8, TT], BF16) for i in range(2)]
        kt = [P.sb(st, "kt%d" % i, [128, 6, TT], BF16) for i in range(2)]
        qt = [P.sb(st, "qt%d" % i, [128, 6, TT], BF16) for i in range(2)]
        vs = [P.sb(st, "vs%d" % i, [128, 4, 12, 65], BF16) for i in range(2)]
        uf = P.sb(st, "uf", [128, 2, TT], F32)
        usq = P.sb(st, "usq", [128, 2, TT], F32)
        urs = P.sb(st, "urs", [128, 2, TT], F32)
        un = [P.sb(st, "un%d" % i, [128, 2, TT], BF16) for i in range(2)]
        ones12 = P.sb(st, "ones12", [128, 12, 1], F32)
        S.op("dve", lambda h: h.memset(ones12[:], 1.0), writes=[ones12])
        vl = {}
        for sg in ("p", "s"):
            nch = (NP_EXT if sg == "p" else SEQ_S) // 128
            vl[sg] = P.sb(st, "vl" + sg, [128, nch], F32)
            S.dma("sp", vl[sg][:], vld[sg][:, :], writes=[vl[sg]])
        it = 0
        pb = 0
        for sg in ("p", "s"):
            n_ext = NP_EXT if sg == "p" else SEQ_S
            own0 = 2 if sg == "p" else 0
            nown_t = 8 if sg == "p" else 4
            ooff = 0 if sg == "p" else NP_OWN
            for i in range(n_ext // TT):
                a = it % 2
                it += 1
                X, XB, KT, QT, VS, UN = xf[a], xb[a], kt[a], qt[a], vs[a], un[a]
                S.dma("sp", X[:], xT[sg][:, :, i * TT:(i + 1) * TT], writes=[X])
                S.op("act", lambda h: h.copy(XB[:, 0:4, :], X[:, 0:4, :]), reads=[X], wacc=[XB])
                S.op("dve", lambda h: h.tensor_copy(XB[:, 4:8, :], X[:, 4:8, :]), reads=[X], wacc=[XB])
                for oc in range(6):
                    pt = P.ps[pb % 8]
                    pb += 1
                    for c in range(8):
                        S.op("pe", lambda h: h.matmul(pt[:], wb[:, c, 768 + oc * 128:768 + (oc + 1) * 128],
                                                      XB[:, c, :], start=(c == 0), stop=(c == 7)),
                             reads=[wb, XB], writes=[pt])
                    P.copy(P.eng2(), KT[:, oc, :], pt[:], reads=[pt], wacc=[KT])
                S.dma("pool", P.KT0[sg][:, :, i * TT:(i + 1) * TT], KT[:], reads=[KT])
                for sub in range(4):
                    for hf in range(2):
                        pt = P.ps[pb % 8]
                        pb += 1
                        for c in range(8):
                            S.op("pe", lambda h: h.matmul(pt[:, 0:384], XB[:, c, sub * 128:(sub + 1) * 128],
                                                          wb[:, c, 1536 + hf * 384:1536 + (hf + 1) * 384],
                                                          start=(c == 0), stop=(c == 7)),
                                 reads=[wb, XB], writes=[pt])
                        P.copy(P.eng2(), VS[:, sub, hf * 6:(hf + 1) * 6, 0:64],
                               pt[:, 0:384].rearrange("p (h d) -> p h d", d=64), reads=[pt], wacc=[VS])
                    ch = i * 4 + sub
                    S.op("dve", lambda h: h.tensor_scalar(VS[:, sub, :, 64:65], ones12[:], vl[sg][:, ch:ch + 1], None,
                                                          op0=ALU.mult), reads=[ones12, vl[sg]], wacc=[VS])
                S.dma("pool", P.V0[sg][i * 4:(i + 1) * 4].rearrange("c p f -> p c f"),
                      VS[:].rearrange("p s h d -> p s (h d)"), reads=[VS])
                if not (own0 <= i < own0 + nown_t):
                    continue
                o0 = ooff + (i - own0) * TT
                for oc in range(6):
                    pt = P.ps[pb % 8]
                    pb += 1
                    for c in range(8):
                        S.op("pe", lambda h: h.matmul(pt[:], wb[:, c, oc * 128:(oc + 1) * 128],
                                                      XB[:, c, :], start=(c == 0), stop=(c == 7)),
                             reads=[wb, XB], writes=[pt])
                    P.copy(P.eng2(), QT[:, oc, :], pt[:], reads=[pt], wacc=[QT], scale=0.125)
                S.dma("pool", P.QT0[:, :, o0:o0 + TT], QT[:], reads=[QT])
                for c2 in range(2):
                    pt = P.ps[pb % 8]
                    pb += 1
                    for c in range(8):
                        S.op("pe", lambda h: h.matmul(pt[:], wb[:, c, 2304 + c2 * 128:2304 + (c2 + 1) * 128],
                                                      XB[:, c, :], start=(c == 0), stop=(c == 7)),
                             reads=[wb, XB], writes=[pt])
                    S.op("act", lambda h: h.copy(uf[:, c2, :], pt[:]), reads=[pt], wacc=[uf])
                for c2 in range(2):
                    pm = P.ps[pb % 8]
                    pb += 1
                    S.op("pe", lambda h: h.matmul(pm[:], P.c_blk64[:], uf[:, c2, :], start=True, stop=True),
                         reads=[P.c_blk64, uf], writes=[pm])
                    S.op("dve", lambda h: h.tensor_tensor(uf[:, c2, :], uf[:, c2, :], pm[:], op=ALU.subtract),
                         reads=[pm, uf], wacc=[uf])
                    S.op("act", lambda h: h.activation(usq[:, c2, :], uf[:, c2, :], AF.Square),
                         reads=[uf], wacc=[usq])
                    pv = P.ps[pb % 8]
                    pb += 1
                    S.op("pe", lambda h: h.matmul(pv[:], P.c_blk64[:], usq[:, c2, :], start=True, stop=True),
                         reads=[P.c_blk64, usq], writes=[pv])
                    P.rsqrt(urs[:, c2, :], pv[:], P.c_eps_ln, reads=[pv], wacc=[urs])
                    S.op("dve", lambda h: h.tensor_tensor(uf[:, c2, :], uf[:, c2, :], urs[:, c2, :], op=ALU.mult),
                         reads=[urs, uf], wacc=[uf])
                    S.op("dve", lambda h: h.tensor_scalar(UN[:, c2, :], uf[:, c2, :], fgs[:, c2:c2 + 1], None,
                                                          op0=ALU.mult), reads=[uf, fgs], wacc=[UN])
                oo = (i - own0) * TT
                S.dma("pool", P.unT[sg][oo // 2048][:, oo % 2048:oo % 2048 + TT].rearrange("(c p) t -> p c t", p=128),
                      UN[:], reads=[UN])
        S.barrier()


def attn_pipeline(P, items, s_fn, e_fn, o_fn, look=2):
    n = len(items)
    for t in range(min(look, n)):
        s_fn(items[t], t)
    for t in range(n):
        if t + look < n:
            s_fn(items[t + look], t + look)
        e_fn(items[t], t)
        o_fn(items[t], t)


class PairAttn:
    def __init__(self, P, st, with_z):
        self.P = P
        self.with_z = with_z
        self.EB = [P.sb(st, "paEB%d" % i, [128, 1024], BF16) for i in range(4)]
        if with_z:
            self.EF = [P.sb(st, "paEF%d" % i, [128, 1024], F32) for i in range(3)]
        else:
            self.ACC = [P.sb(st, "paACC%d" % i, [128, 1024], F32) for i in range(2)]
            self.ACCD = [Buf(a.t[:, 0:768], "accD") for a in self.ACC]
            self.ACCP = [Buf(a.t[:, 768:1024], "accP") for a in self.ACC]
        self.RD = [P.sb(st, "paRD%d" % i, [128, 512], F32) for i in range(2)]
        self.OUT = [P.sb(st, "paOUT%d" % i, [128, 512], BF16) for i in range(2)]

    def run(self, items, kt, qt, va, zz, out_fn, vl=None):
        P, S = self.P, self.P.S
        EB, RD, OUT = self.EB, self.RD, self.OUT

        def s_fn(itm, t):
            qi, ch, zoff, first, last = itm
            pp = P.ps2[t % 3]
            for hh in range(2):
                pb_ = hh * 64
                S.op("pe", lambda h: h.matmul(pp[:, hh * 512:(hh + 1) * 512], kt[pb_:pb_ + 64, ch * 128:(ch + 1) * 128],
                                              qt[pb_:pb_ + 64, qi * TT:(qi + 1) * TT], start=True, stop=True,
                                              tile_position=(pb_, 0)), reads=[kt, qt], writes=[pp])

        def e_fn(itm, t):
            qi, ch, zoff, first, last = itm
            pp = P.ps2[t % 3]
            eb = EB[t % 4]
            if self.with_z:
                ef = self.EF[t % 3]
                S.op("act", lambda h: h.activation(ef[:], pp[:], AF.Exp), reads=[pp], writes=[ef])
                S.op("dve", lambda h: h.tensor_tensor(eb[:, 0:512], ef[:, 0:512], zz[0][:, zoff:zoff + 512],
                                                      op=ALU.mult), reads=[ef, zz[0]], wacc=[eb])
                S.op("pool", lambda h: h.tensor_tensor(eb[:, 512:1024], ef[:, 512:1024], zz[1][:, zoff:zoff + 512],
                                                       op=ALU.mult), reads=[ef, zz[1]], wacc=[eb])
            else:
                S.op("act", lambda h: h.activation(eb[:], pp[:], AF.Exp), reads=[pp], writes=[eb])
                acc = self.ACC[qi % 2]
                for e, c0, c1, ab in (("dve", 0, 768, self.ACCD[qi % 2]), ("pool", 768, 1024, self.ACCP[qi % 2])):
                    if first:
                        S.op(e, lambda h: h.tensor_copy(acc[:, c0:c1], eb[:, c0:c1]), reads=[eb], writes=[ab])
                    else:
                        S.op(e, lambda h: h.tensor_tensor(acc[:, c0:c1], acc[:, c0:c1], eb[:, c0:c1], op=ALU.add),
                             reads=[eb, ab], writes=[ab])

        def o_fn(itm, t):
            qi, ch, zoff, first, last = itm
            eb = EB[t % 4]
            po, pd = P.ps[6], P.ps[7]
            for hh in range(2):
                S.op("pe", lambda h: h.matmul(po[hh * 64:(hh + 1) * 64, :], va[:, ch, hh, 0:64],
                                              eb[:, hh * 512:(hh + 1) * 512], start=first, stop=last,
                                              tile_position=(0, hh * 64)), reads=[va, eb], writes=[po])
            if self.with_z:
                for hh in range(2):
                    S.op("pe", lambda h: h.matmul(pd[hh * 64:(hh + 1) * 64, :], vl[:, ch, :],
                                                  eb[:, hh * 512:(hh + 1) * 512], start=first, stop=last,
                                                  tile_position=(0, hh * 64)), reads=[vl, eb], writes=[pd])
            if (not self.with_z) and (not last) and FILLER:
                for _ in range(FILLER):
                    S.op("pe", lambda h: h.matmul(pd[:], P.c_identb[:], eb[:, 0:512], start=True, stop=True),
                         reads=[P.c_identb, eb], writes=[pd])
            if not last:
                return
            rd, out = RD[qi % 2], OUT[qi % 2]
            if not self.with_z:
                acc = self.ACC[qi % 2]
                accs = [self.ACCD[qi % 2], self.ACCP[qi % 2]]
                S.op("pe", lambda h: h.matmul(pd[:], P.c_onesA[:], acc[:, 0:512], start=True, stop=False),
                     reads=[P.c_onesA] + accs, writes=[pd])
                S.op("pe", lambda h: h.matmul(pd[:], P.c_onesB[:], acc[:, 512:1024], start=False, stop=True),
                     reads=[P.c_onesB] + accs, writes=[pd])
            S.op("dve", lambda h: h.reciprocal(rd[:], pd[:]), reads=[pd], writes=[rd])
            S.op("dve", lambda h: h.tensor_tensor(out[:], po[:], rd[:], op=ALU.mult), reads=[po, rd], writes=[out])
            out_fn(qi, out)

        attn_pipeline(P, items, s_fn, e_fn, o_fn, look=2)


def phase_dil(P):
    nc, S = P.nc, P.S
    relb = P.din("rel_bias", [32, 12])
    ohm = P.din("k_ohm", [32, 3072])
    rev = P.dscratch("dil_rev", [12, 3200], F32)
    P.OT0 = P.dscratch("OT0", [128, 8, NOWN], BF16)
    REV = Buf(rev, "rev")
    with ExitStack() as st:
        ones32 = P.sb(st, "ones32", [128, 64], F32)
        S.op("dve", lambda h: h.memset(ones32[:], 1.0), writes=[ones32])
        rb = P.sb(st, "rb", [32, 12], F32)
        eb = P.sb(st, "eb", [32, 12], F32)
        oh = P.sb(st, "oh", [32, 3072], F32)
        wt = P.sb(st, "wt", [12, 3072], F32)
        S.dma("sp", rb[:], relb[:, :], writes=[rb])
        S.dma("sp", oh[:], ohm[:, :], writes=[oh])
        S.op("act", lambda h: h.activation(eb[:], rb[:], AF.Exp), reads=[rb], writes=[eb])
        for n0 in range(0, 3072, 512):
            pt = P.ps[(n0 // 512) % 8]
            S.op("pe", lambda h: h.matmul(pt[0:12, :], eb[:], oh[:, n0:n0 + 512], start=True, stop=True),
                 reads=[eb, oh], writes=[pt])
            S.op("dve", lambda h: h.tensor_copy(wt[:, n0:n0 + 512], pt[0:12, :]), reads=[pt], wacc=[wt])
        S.dma("pool", REV[:, 0:3072], wt[:], reads=[wt], writes=[REV])
        S.barrier()
        KT = [P.sb(st, "dKT%d" % i, [128, NP_EXT], BF16) for i in range(2)]
        QT = [P.sb(st, "dQT%d" % i, [128, NP_OWN], BF16) for i in range(2)]
        VA = [P.sb(st, "dVA%d" % i, [128, NP_EXT // 128, 2, 65], BF16) for i in range(2)]
        ZZ = [[P.sb(st, "dZ%d_%d" % (i, k), [128, ZW], F32) for k in range(2)] for i in range(2)]
        HK = P.sb(st, "dHK", [128, ZW], F32)
        PA = PairAttn(P, st, with_z=True)
        VL = {}
        for sg_ in ("p", "s"):
            nch_ = (NP_EXT if sg_ == "p" else SEQ_S) // 128
            vlf = P.sb(st, "dvlf" + sg_, [128, nch_], F32)
            VL[sg_] = P.sb(st, "dvl" + sg_, [128, nch_, 64], BF16)
            S.dma("sp", vlf[:], P.vld[sg_][:, :], writes=[vlf])
            S.op("dve", lambda h: h.tensor_copy(VL[sg_][:], vlf[:].unsqueeze(2).to_broadcast([128, nch_, 64])),
                 reads=[vlf], writes=[VL[sg_]])
        it = 0
        for sg in ("p", "s"):
            n_ext = NP_EXT if sg == "p" else SEQ_S
            n_own = NP_OWN if sg == "p" else NS_OWN
            nqt = n_own // TT
            ooff = 0 if sg == "p" else NP_OWN
            nch = n_ext // 128
            for hp in range(6):
                a = it % 2
                it += 1
                kt, qt, va, zz = KT[a], QT[a], VA[a], ZZ[a]
                S.dma("sp", kt[:, 0:n_ext], P.KT0[sg][:, hp, :], writes=[kt])
                S.dma("sp", qt[:, 0:n_own], P.QT0[:, hp, ooff:ooff + n_own], writes=[qt])
                S.dma("sp", va[:, 0:nch, :, :].rearrange("p c h d -> p c (h d)"),
                      P.V0[sg][:, :, hp * 130:(hp + 1) * 130].rearrange("c p f -> p c f"), writes=[va])
                for hh in range(2):
                    h_ = 2 * hp + hh
                    src = bass.AP(tensor=rev, offset=h_ * 3200, ap=[[1, 128], [1, ZW]])
                    S.dma("sp", HK[:], src, reads=[REV], writes=[HK])
                    for n0 in range(0, ZW, 512):
                        w = min(512, ZW - n0)
                        pt = P.ps[6 + (n0 // 512) % 2]
                        S.op("pe", lambda h: h.matmul(pt[:, 0:w], P.c_antiI[:], HK[:, n0:n0 + w], start=True,
                                                      stop=True), reads=[P.c_antiI, HK], writes=[pt])
                        P.copy(P.eng2(), zz[hh][:, n0:n0 + w], pt[:, 0:w], reads=[pt], wacc=[zz[hh]])
                items = []
                for qi in range(nqt):
                    js = []
                    for j in range(20):
                        ch = 4 * qi + j - (0 if sg == "p" else 8)
                        if 0 <= ch < nch:
                            js.append((j, ch))
                    for idx, (j, ch) in enumerate(js):
                        items.append((qi, ch, 2432 - 128 * j, idx == 0, idx == len(js) - 1))

                def out_fn(qi, out, hp=hp, ooff=ooff):
                    o0 = ooff + qi * TT
                    S.dma("pool", P.OT0[:, hp, o0:o0 + TT], out[:], reads=[out])

                PA.run(items, kt, qt, va, zz, out_fn, vl=VL[sg])
        S.barrier()


GROUPS4 = [[0, 1, 2, 3], [4, 5, 6, 7]]


def phase_fnet(P):
    nc, S = P.nc, P.S
    fw = P.din("ab_fnet_w", [4, 64, 64])
    c64 = P.din("k_c64bd", [128, 128])
    s64 = P.din("k_s64bd", [128, 128])
    unall = [P.dscratch("unTall%d" % i, [1024, 2048], BF16) for i in range(2)]
    UNALL = Buf(unall)
    for i in range(2):
        S.allgather(unall[i], P.unT["p"][i], reads=[], writes=[UNALL] if i == 0 else [], groups=GROUPS4)
    S.cc_fence(GROUPS4)
    UNALL.ws = [("cc", S.cc_cnt, "dma")]
    if "unall_dbg" in P.debug:
        for i in range(2):
            dbg = P.dout("unall_dbg%d" % i, [1024, 2048], BF16)
            S.dma("sp", dbg[:, :], unall[i][:, :], reads=[UNALL])
            dbg2 = P.dout("unmine_dbg%d" % i, [256, 2048], BF16)
            S.dma("sp", dbg2[:, :], P.unT["p"][i][:, :], reads=[UNALL])
    tabs = {}
    for sg, n1 in (("p", 128), ("s", 16)):
        nk2 = 32 if sg == "p" else 128
        tabs[sg] = dict(rp=P.din("f_rp_" + sg, [n1, 2 * n1]), rr=P.din("f_rr_" + sg, [n1, 2 * n1]),
                        tc=P.din("f_tc_" + sg, [128, n1]), ts=P.din("f_ts_" + sg, [128, n1]),
                        c2=P.din("f_c2_" + sg, [128, nk2]), s2=P.din("f_s2_" + sg, [128, nk2]))
    with ExitStack() as st:
        cs = P.sb(st, "f_cs", [128, 2, 128], F32)
        S.dma("sp", cs[:, 0, :], c64[:, :], wacc=[cs])
        S.dma("sp", cs[:, 1, :], s64[:, :], wacc=[cs])
        wbd = P.sb(st, "f_wbd", [128, 128], F32)
        AB = P.sb(st, "f_AB", [128, 2, 128], BF16)
        stg = P.sb(st, "f_stg", [128, 2, 256], F32)
        tb = {k: P.sb(st, "f_t_" + k, [128, 256], BF16) for k in ("rp", "rr", "c2", "s2")}
        tcs = {k: P.sb(st, "f_t_" + k, [128, 128], F32) for k in ("tc", "ts")}
        Y = P.sb(st, "f_Y", [128, 128, 256], BF16)
        OB = P.sb(st, "f_OB", [128, NP_OWN], BF16)
        pbk = 0
        for sg in ("p", "s"):
            N1 = 128 if sg == "p" else 16
            NK2 = 32 if sg == "p" else 128
            n_own = NP_OWN if sg == "p" else NS_OWN
            nseq = SEQ_P if sg == "p" else SEQ_S
            ooff = 0 if sg == "p" else NP_OWN
            T = tabs[sg]
            for k, rows, cols in (("rp", N1, 2 * N1), ("rr", N1, 2 * N1), ("c2", 128, NK2), ("s2", 128, NK2)):
                S.dma("sp", stg[0:rows, 0, 0:cols], T[k][:, :], writes=[stg])
                S.op("dve", lambda h: h.tensor_copy(tb[k][0:rows, 0:cols], stg[0:rows, 0, 0:cols]), reads=[stg],
                     writes=[tb[k]])
            for k in ("tc", "ts"):
                S.dma("sp", tcs[k][:, 0:N1], T[k][:, :], writes=[tcs[k]])
            for gp in range(2):
                S.op("dve", lambda h: h.memset(wbd[:], 0.0), writes=[wbd])
                S.dma("sp", wbd[0:64, 0:64], fw[2 * gp, :, :], wacc=[wbd])
                S.dma("sp", wbd[64:128, 64:128], fw[2 * gp + 1, :, :], wacc=[wbd])
                for k in range(2):
                    pt = P.ps[pbk % 8]
                    pbk += 1
                    S.op("pe", lambda h: h.matmul(pt[:, 0:128], cs[:, k, :], wbd[:], start=True, stop=True),
                         reads=[cs, wbd], writes=[pt])
                    P.copy("dve", AB[:, k, :], pt[:, 0:128], reads=[pt], wacc=[AB], scale=(1.0 if k == 0 else -1.0))
                with ExitStack() as st2:
                    un = P.sb(st2, "f_un", [128, SEQ_P], BF16)
                    if sg == "p":
                        for r in range(4):
                            for hf in range(2):
                                S.dma("sp", un[:, r * 4096 + hf * 2048:r * 4096 + (hf + 1) * 2048],
                                      unall[hf][r * 256 + gp * 128:r * 256 + (gp + 1) * 128, :], reads=[UNALL],
                                      wacc=[un])
                    else:
                        S.dma("sp", un[:, 0:nseq], P.unT["s"][0][gp * 128:(gp + 1) * 128, :], wacc=[un])
                    for n2 in range(0, 128, 2):
                        pt = P.ps[pbk % 8]
                        pbk += 1
                        for d in range(2):
                            S.op("pe", lambda h: h.matmul(pt[0:N1, d * 256:(d + 1) * 256],
                                                          un[:, n2 + d:nseq:128], AB[:].rearrange("p a e -> p (a e)"),
                                                          start=True, stop=True), reads=[un, AB], writes=[pt])
                        P.copy(P.eng2(), Y[0:N1, n2:n2 + 2, :], pt[0:N1, :].rearrange("p (a c) -> p a c", a=2),
                               reads=[pt], wacc=[Y])
                    S.barrier()
                with ExitStack() as st3:
                    GP = P.sb(st3, "f_GP", [128, N1, 2, 128], BF16)
                    GS = [P.sb(st3, "f_GS%d" % i, [128, 512], F32) for i in range(2)]
                    T1 = [P.sb(st3, "f_T1%d" % i, [128, 256], F32) for i in range(2)]
                    T2 = [P.sb(st3, "f_T2%d" % i, [128, 256], F32) for i in range(2)]
                    T3 = [P.sb(st3, "f_T3%d" % i, [128, 256], F32) for i in range(2)]
                    T4 = [P.sb(st3, "f_T4%d" % i, [128, 256], F32) for i in range(2)]
                    EBn = 512 // (2 * N1)
                    nb = 0
                    for c0 in range(0, 128, EBn):
                        pt = P.ps[pbk % 8]
                        pbk += 1
                        for bi in range(EBn):
                            col = c0 + bi
                            sl = slice(bi * 2 * N1, (bi + 1) * 2 * N1)
                            S.op("pe", lambda h: h.matmul(pt[:, sl], Y[0:N1, :, col], tb["rp"][0:N1, 0:2 * N1],
                                                          start=True, stop=False), reads=[Y, tb["rp"]], writes=[pt])
                            S.op("pe", lambda h: h.matmul(pt[:, sl], Y[0:N1, :, 128 + col], tb["rr"][0:N1, 0:2 * N1],
                                                          start=False, stop=True), reads=[Y, tb["rr"]], writes=[pt])
                        a = nb % 2
                        nb += 1
                        gs, t1, t2, t3, t4 = GS[a], T1[a], T2[a], T3[a], T4[a]
                        S.op("act", lambda h: h.copy(gs[:], pt[:]), reads=[pt], writes=[gs])
                        g4 = gs[:].rearrange("p (b r k) -> p b r k", b=EBn, r=2)
                        gr, gi = g4[:, :, 0, :], g4[:, :, 1, :]
                        tcb = tcs["tc"][:, 0:N1].unsqueeze(1).to_broadcast([128, EBn, N1])
                        tsb = tcs["ts"][:, 0:N1].unsqueeze(1).to_broadcast([128, EBn, N1])
                        v = lambda t: t[:, 0:EBn * N1].rearrange("p (b k) -> p b k", b=EBn)
                        vt = lambda t: t[:, 0:EBn * N1].rearrange("p (b k) -> p k b", b=EBn)
                        S.op("dve", lambda h: h.tensor_tensor(v(t1), gr, tcb, op=ALU.mult),
                             reads=[gs, tcs["tc"]], writes=[t1])
                        S.op("dve", lambda h: h.tensor_tensor(v(t2), gi, tsb, op=ALU.mult),
                             reads=[gs, tcs["ts"]], writes=[t2])
                        S.op("pool", lambda h: h.tensor_tensor(v(t3), gi, tcb, op=ALU.mult),
                             reads=[gs, tcs["tc"]], writes=[t3])
                        S.op("pool", lambda h: h.tensor_tensor(v(t4), gr, tsb, op=ALU.mult),
                             reads=[gs, tcs["ts"]], writes=[t4])
                        S.op("dve", lambda h: h.tensor_tensor(GP[:, :, 0, c0:c0 + EBn], vt(t1), vt(t2), op=ALU.add),
                             reads=[t1, t2], wacc=[GP])
                        S.op("pool", lambda h: h.tensor_tensor(GP[:, :, 1, c0:c0 + EBn], vt(t3), vt(t4),
                                                               op=ALU.subtract), reads=[t3, t4], wacc=[GP])
                    KB = 512 // NK2
                    for k0 in range(0, N1, KB):
                        pt = P.ps[pbk % 8]
                        pbk += 1
                        for kk in range(KB):
                            k1 = k0 + kk
                            sl = slice(kk * NK2, (kk + 1) * NK2)
                            S.op("pe", lambda h: h.matmul(pt[:, sl], GP[:, k1, 0, :], tb["c2"][:, 0:NK2], start=True,
                                                          stop=False), reads=[GP, tb["c2"]], writes=[pt])
                            S.op("pe", lambda h: h.matmul(pt[:, sl], GP[:, k1, 1, :], tb["s2"][:, 0:NK2], start=False,
                                                          stop=True), reads=[GP, tb["s2"]], writes=[pt])
                        ov = OB[:, 0:n_own].rearrange("p (k2 k1) -> p k1 k2", k1=N1)[:, k0:k0 + KB, :]
                        P.copy(P.eng2(), ov, pt[:].rearrange("p (a b) -> p a b", a=KB), reads=[pt], wacc=[OB])
                    S.dma("pool", P.OT0[:, 6 + gp, ooff:ooff + n_own], OB[:, 0:n_own], reads=[OB])
                    S.barrier()
        S.barrier()


class RowBufs:
    def __init__(self, P, st):
        self.xr = [P.sb(st, "rb_xr%d" % i, [128, 8, TT], F32) for i in range(2)]
        self.r = P.sb(st, "rb_r", [128, 8, TT], F32)
        self.sq = P.sb(st, "rb_sq", [128, 8, TT], F32)
        self.rstd = P.sb(st, "rb_rstd", [128, TT], F32)
        self.of = [P.sb(st, "rb_of%d" % i, [128, 8, TT], F32) for i in range(1)]
        self.ob = [P.sb(st, "rb_ob%d" % i, [128, 8, TT], BF16) for i in range(1)]


def linear_resid_ln(P, RB, i, wb, kcin, src, xr, lnidx, outf_d, outb_d, pbase):
    S = P.S
    r, sq, rstd = RB.r, RB.sq, RB.rstd
    of, ob = RB.of[0], RB.ob[0]
    for oc in range(8):
        pt = P.ps[(pbase + oc) % 8]
        for c in range(kcin):
            S.op("pe", lambda h: h.matmul(pt[:], wb[:, c, oc * 128:(oc + 1) * 128], src[:, c, :], start=(c == 0),
                                          stop=(c == kcin - 1)), reads=[wb, src], writes=[pt])
        S.op("dve", lambda h: h.scalar_tensor_tensor(r[:, oc, :], xr[:, oc, :], DN_ALPHA, pt[:], op0=ALU.mult,
                                                     op1=ALU.add), reads=[xr, pt], wacc=[r])
    layer_norm_fm(P, r, sq, rstd, lnidx, of, ob, pbase)
    t0 = i * TT
    if outf_d is not None:
        S.dma("pool", outf_d[:, :, t0:t0 + TT], of[:], reads=[of])
    if outb_d is not None:
        S.dma("pool", outb_d[:, :, t0:t0 + TT], ob[:], reads=[ob])


def layer_norm_fm(P, r, sq, rstd, lnidx, of, ob, pbase):
    S = P.S
    pm = P.ps[(pbase + 0) % 8]
    pv = P.ps[(pbase + 1) % 8]
    for c in range(8):
        S.op("pe", lambda h: h.matmul(pm[:], P.c_onesd[:], r[:, c, :], start=(c == 0), stop=(c == 7)),
             reads=[P.c_onesd, r], writes=[pm])
    S.op("dve", lambda h: h.tensor_tensor(r[:], r[:], pm[:].unsqueeze(1).to_broadcast([128, 8, TT]),
                                          op=ALU.subtract), reads=[r, pm], wacc=[r])
    S.op("act", lambda h: h.activation(sq[:], r[:], AF.Square), reads=[r], writes=[sq])
    for c in range(8):
        S.op("pe", lambda h: h.matmul(pv[:], P.c_onesd[:], sq[:, c, :], start=(c == 0), stop=(c == 7)),
             reads=[P.c_onesd, sq], writes=[pv])
    P.rsqrt(rstd[:], pv[:], P.c_eps_ln, reads=[pv], wacc=[rstd])
    S.op("dve", lambda h: h.tensor_tensor(r[:], r[:], rstd[:].unsqueeze(1).to_broadcast([128, 8, TT]),
                                          op=ALU.mult), reads=[r, rstd], wacc=[r])
    for c in range(8):
        S.op("act", lambda h: h.activation(of[:, c, :], r[:, c, :], AF.Identity,
                                           bias=P.c_lnb[:, lnidx, c:c + 1], scale=P.c_lng[:, lnidx, c:c + 1]),
             reads=[r, P.c_lng, P.c_lnb], wacc=[of])
    S.op("pool", lambda h: h.tensor_copy(ob[:], of[:]), reads=[of], writes=[ob])


def phase_linear_ln(P, name, w_d, kcin, src_d, xres_fn, lnidx, outf_d, outb_d):
    S = P.S
    with ExitStack() as st:
        stg = [P.sb(st, "stg%d" % i, [128, 1024], F32) for i in range(3)]
        wb = P.load_w(st, "w_" + name, w_d, kcin * 128, D, stg)
        RB = RowBufs(P, st)
        srcs = [P.sb(st, "src%d" % i, [128, kcin, TT], BF16) for i in range(2)]
        for i in range(NTILE):
            sb_, xr = srcs[i % 2], RB.xr[i % 2]
            S.dma("sp", sb_[:], src_d[:, :, i * TT:(i + 1) * TT], writes=[sb_])
            S.dma("sp", xr[:], xres_fn(i), writes=[xr])
            linear_resid_ln(P, RB, i, wb, kcin, sb_, xr, lnidx, outf_d, outb_d, pbase=(i * 2) % 8)
        S.barrier()


def phase_xattn(P, layer, xf_d, xb_d, outf_d, outb_d):
    S = P.S
    wq_d = P.din("xa_w_q%d" % layer, [D, D])
    wkv_d = P.din("xa_w_kv%d" % layer, [D, 2 * D])
    wo_d = P.din("xa_w_o%d" % layer, [D, D])
    if not hasattr(P, "memT"):
        P.memT = {"p": P.din("memTp", [128, 8, 256]), "s": P.din("memTs", [128, 8, 256])}
    lnidx = layer * 3 + 1
    with ExitStack() as st:
        stg = [P.sb(st, "stg%d" % i, [128, 1024], F32) for i in range(3)]
        wq = P.load_w(st, "w_xq", wq_d, D, D, stg)
        wo = P.load_w(st, "w_xo", wo_d, D, D, stg)
        memK = {sg: P.sb(st, "memK" + sg, [128, 8, 256], BF16) for sg in ("p", "s")}
        memV = {sg: P.sb(st, "memV" + sg, [128, 2, D], BF16) for sg in ("p", "s")}
        with ExitStack() as st2:
            wkv = P.load_w(st2, "w_xkv", wkv_d, D, 2 * D, stg)
            mf = P.sb(st2, "memf", [128, 8, 256], F32)
            mb = P.sb(st2, "memb", [128, 8, 256], BF16)
            pb = 0
            for sg in ("p", "s"):
                S.dma("sp", mf[:], P.memT[sg][:, :, :], writes=[mf])
                S.op("dve", lambda h: h.tensor_copy(mb[:], mf[:]), reads=[mf], writes=[mb])
                for oc in range(8):
                    pt = P.ps[pb % 8]
                    pb += 1
                    for c in range(8):
                        S.op("pe", lambda h: h.matmul(pt[:, 0:256], wkv[:, c, oc * 128:(oc + 1) * 128], mb[:, c, :],
                                                      start=(c == 0), stop=(c == 7)), reads=[wkv, mb], writes=[pt])
                    P.copy(P.eng2(), memK[sg][:, oc, :], pt[:, 0:256], reads=[pt], wacc=[memK[sg]])
                for mc in range(2):
                    for n0 in range(2):
                        pt = P.ps[pb % 8]
                        pb += 1
                        for c in range(8):
                            S.op("pe", lambda h: h.matmul(pt[:], mb[:, c, mc * 128:(mc + 1) * 128],
                                                          wkv[:, c, D + n0 * 512:D + (n0 + 1) * 512],
                                                          start=(c == 0), stop=(c == 7)), reads=[wkv, mb], writes=[pt])
                        P.copy(P.eng2(), memV[sg][:, mc, n0 * 512:(n0 + 1) * 512], pt[:], reads=[pt],
                               wacc=[memV[sg]])
            S.barrier()
        RB = RowBufs(P, st)
        xbs = [P.sb(st, "xa_xb%d" % i, [128, 8, TT], BF16) for i in range(2)]
        qb = P.sb(st, "xa_q", [128, 8, TT], BF16)
        ob_ = P.sb(st, "xa_o", [128, 8, TT], BF16)
        ee = [P.sb(st, "xa_e%d" % i, [128, 2, TT], BF16) for i in range(2)]
        rden = [P.sb(st, "xa_rd%d" % i, [128, TT], F32) for i in range(2)]
        pb = 0
        for i in range(NTILE):
            sg = "p" if i < NP_OWN // TT else "s"
            xb, xr = xbs[i % 2], RB.xr[i % 2]
            S.dma("sp", xb[:], xb_d[:, :, i * TT:(i + 1) * TT], writes=[xb])
            S.dma("sp", xr[:], xf_d[:, :, i * TT:(i + 1) * TT], writes=[xr])
            for oc in range(8):
                pt = P.ps[pb % 8]
                pb += 1
                for c in range(8):
                    S.op("pe", lambda h: h.matmul(pt[:], wq[:, c, oc * 128:(oc + 1) * 128], xb[:, c, :],
                                                  start=(c == 0), stop=(c == 7)), reads=[wq, xb], writes=[pt])
                P.copy(P.eng2(), qb[:, oc, :], pt[:], reads=[pt], wacc=[qb], scale=1.0 / 16)
            for hh in range(4):
                E, RD = ee[hh % 2], rden[hh % 2]
                for mc in range(2):
                    pt = P.ps[pb % 8]
                    pb += 1
                    for cc in range(2):
                        S.op("pe", lambda h: h.matmul(pt[:], memK[sg][:, 2 * hh + cc, mc * 128:(mc + 1) * 128],
                                                      qb[:, 2 * hh + cc, :], start=(cc == 0), stop=(cc == 1)),
                             reads=[memK[sg], qb], writes=[pt])
                    S.op("act", lambda h: h.activation(E[:, mc, :], pt[:], AF.Exp), reads=[pt], wacc=[E])
                pd = P.ps[pb % 8]
                pb += 1
                for mc in range(2):
                    S.op("pe", lambda h: h.matmul(pd[:], P.c_onesb[:], E[:, mc, :], start=(mc == 0), stop=(mc == 1)),
                         reads=[P.c_onesb, E], writes=[pd])
                S.op("dve", lambda h: h.reciprocal(RD[:], pd[:]), reads=[pd], writes=[RD])
                for dvc in range(2):
                    po = P.ps[pb % 8]
                    pb += 1
                    for mc in range(2):
                        S.op("pe", lambda h: h.matmul(po[:], memV[sg][:, mc, hh * 256 + dvc * 128:hh * 256 + (dvc + 1) * 128],
                                                      E[:, mc, :], start=(mc == 0), stop=(mc == 1)),
                             reads=[memV[sg], E], writes=[po])
                    S.op("dve", lambda h: h.tensor_tensor(ob_[:, 2 * hh + dvc, :], po[:], RD[:], op=ALU.mult),
                         reads=[po, RD], wacc=[ob_])
            linear_resid_ln(P, RB, i, wo, 8, ob_, xr, lnidx, outf_d, outb_d, pbase=pb % 8)
            pb += 2
        S.barrier()


def phase_ffn1(P, layer, xb_d, h_d):
    S = P.S
    w_d = P.din("ffn_w_in%d" % layer, [D, 2 * FFN_H])
    with ExitStack() as st:
        stg = [P.sb(st, "stg%d" % i, [128, 1024], F32) for i in range(3)]
        wb = P.load_w(st, "w_ffn_in", w_d, D, 2 * FFN_H, stg)
        xbs = [P.sb(st, "f1_xb%d" % i, [128, 8, TT], BF16) for i in range(2)]
        hid = [P.sb(st, "f1_h%d" % i, [128, HC, TT], BF16) for i in range(2)]
        sgb = [P.sb(st, "f1_sg%d" % i, [128, TT], F32) for i in range(3)]
        pb = 0
        for i in range(NTILE):
            xb, hd = xbs[i % 2], hid[i % 2]
            S.dma("sp", xb[:], xb_d[:, :, i * TT:(i + 1) * TT], writes=[xb])
            for hc in range(HC):
                pg = P.ps[pb % 8]
                pu = P.ps[(pb + 1) % 8]
                pb += 2
                for c in range(8):
                    S.op("pe", lambda h: h.matmul(pg[:], wb[:, c, hc * 128:(hc + 1) * 128], xb[:, c, :],
                                                  start=(c == 0), stop=(c == 7)), reads=[wb, xb], writes=[pg])
                for c in range(8):
                    S.op("pe", lambda h: h.matmul(pu[:], wb[:, c, FFN_H + hc * 128:FFN_H + (hc + 1) * 128], xb[:, c, :],
                                                  start=(c == 0), stop=(c == 7)), reads=[wb, xb], writes=[pu])
                sgt = sgb[hc % 3]
                S.op("act", lambda h: h.activation(sgt[:], pg[:], AF.Silu), reads=[pg], writes=[sgt])
                S.op("dve", lambda h: h.tensor_tensor(hd[:, hc, :], sgt[:], pu[:], op=ALU.mult), reads=[sgt, pu],
                     wacc=[hd])
            S.dma("pool", h_d[:, :, i * TT:(i + 1) * TT], hd[:], reads=[hd])
        S.barrier()


def layer_tail(P, layer, mix_d, w_out_d, xres_fn):
    L = "L%d" % layer
    x1f = P.dscratch(L + "x1f", [128, 8, NOWN], F32)
    x1b = P.dscratch(L + "x1b", [128, 8, NOWN], BF16)
    phase_linear_ln(P, L + "mixout", w_out_d, 8, mix_d, xres_fn, layer * 3 + 0, x1f, x1b)
    x2f = P.dscratch(L + "x2f", [128, 8, NOWN], F32)
    x2b = P.dscratch(L + "x2b", [128, 8, NOWN], BF16)
    phase_xattn(P, layer, x1f, x1b, x2f, x2b)
    hd = P.dscratch(L + "hid", [128, HC, NOWN], BF16)
    phase_ffn1(P, layer, x2b, hd)
    if layer == 1:
        x3f = P.dout("yT", [128, 8, NOWN], F32)
        x3b = None
    else:
        x3f = P.dscratch(L + "x3f", [128, 8, NOWN], F32)
        x3b = P.dscratch(L + "x3b", [128, 8, NOWN], BF16)
    w2 = P.din("ffn_w_out%d" % layer, [FFN_H, D])
    phase_linear_ln(P, L + "ffnout", w2, HC, hd, lambda i: x2f[:, :, i * TT:(i + 1) * TT], layer * 3 + 2, x3f, x3b)
    return x3f, x3b


QPERM = [0, 3, 1, 4, 2, 5, 6, 9, 7, 10, 8, 11]


def phase_proj1(P, xb_d):
    nc, S = P.nc, P.S
    w_d = P.din("cd_w_in_p", [D, 1536])
    gains_d = P.din("qk_gain_r", [128, 16, 64])
    cos_d = P.din("rope_cos", [128, NOWN // 128, 32])
    sin_d = P.din("rope_sin", [128, NOWN // 128, 32])
    P.QT1 = P.dscratch("QT1", [128, 6, NOWN], BF16)
    P.KT1 = {"p": [P.dscratch("KT1p%d" % i, [256, 2048], BF16) for i in range(2)],
             "s": [P.dscratch("KT1s", [256, NS_OWN], BF16)]}
    P.V1 = {"p": [P.dscratch("V1p%d" % i, [1024, 260], BF16) for i in range(4)],
            "s": [P.dscratch("V1s%d" % i, [1024, 260], BF16) for i in range(2)]}
    P.U1 = P.dscratch("U1", [128, 2, NOWN], F32)
    with ExitStack() as st:
        stg = [P.sb(st, "stg%d" % i, [128, 1024], F32) for i in range(3)]
        wb = P.load_w(st, "w_cd_in", w_d, D, 1536, stg)
        gains = P.sb(st, "p1_gain", [128, 16, 64], F32)
        S.dma("sp", gains[:], gains_d[:, :, :], writes=[gains])
        S.op("dve", lambda h: h.tensor_scalar_mul(gains[:, 0:12, :], gains[:, 0:12, :], 0.125), reads=[gains],
             wacc=[gains])
        cs = P.sb(st, "p1_cos", [128, NOWN // 128, 32], F32)
        sn = P.sb(st, "p1_sin", [128, NOWN // 128, 32], F32)
        S.dma("sp", cs[:], cos_d[:, :, :], writes=[cs])
        S.dma("sp", sn[:], sin_d[:, :, :], writes=[sn])
        xbs = [P.sb(st, "p1_xb%d" % i, [128, 8, TT], BF16) for i in range(2)]
        sq = P.sb(st, "p1_sq", [128, 1024], F32)
        ss = P.sb(st, "p1_ss", [128, 16], F32)
        rs = P.sb(st, "p1_rs", [128, 16], F32)
        qn = P.sb(st, "p1_qn", [128, 16, 64], F32)
        tt = [P.sb(st, "p1_t%d" % i, [128, 16, 32], F32) for i in range(4)]
        qr = P.sb(st, "p1_qr", [128, 4, 16, 64], BF16)
        qts = [P.sb(st, "p1_qt%d" % i, [128, 8, TT], BF16) for i in range(2)]
        vs = [P.sb(st, "p1_vs%d" % i, [128, 4, 4, 65], BF16) for i in range(2)]
        us = [P.sb(st, "p1_us%d" % i, [128, 2, TT], F32) for i in range(2)]
        for v_ in vs:
            S.op("dve", lambda h: h.memset(v_[:], 1.0), writes=[v_])
        pb = 0
        for i in range(NTILE):
            sg = "p" if i < 8 else "s"
            il = i if sg == "p" else i - 8
            xb, QT, VS, US = xbs[i % 2], qts[i % 2], vs[i % 2], us[i % 2]
            S.dma("sp", xb[:], xb_d[:, :, i * TT:(i + 1) * TT], writes=[xb])
            for sub in range(4):
                pa, pbk = P.ps[pb % 8], P.ps[(pb + 1) % 8]
                pv = P.ps[(pb + 2) % 8]
                pb += 3
                for c in range(8):
                    S.op("pe", lambda h: h.matmul(pa[:], xb[:, c, sub * 128:(sub + 1) * 128], wb[:, c, 0:512],
                                                  start=(c == 0), stop=(c == 7)), reads=[wb, xb], writes=[pa])
                for c in range(8):
                    S.op("pe", lambda h: h.matmul(pbk[:], xb[:, c, sub * 128:(sub + 1) * 128], wb[:, c, 512:1024],
                                                  start=(c == 0), stop=(c == 7)), reads=[wb, xb], writes=[pbk])
                for c in range(8):
                    S.op("pe", lambda h: h.matmul(pv[:, 0:256], xb[:, c, sub * 128:(sub + 1) * 128],
                                                  wb[:, c, 1024:1280], start=(c == 0), stop=(c == 7)),
                         reads=[wb, xb], writes=[pv])
                S.op("act", lambda h: h.copy(VS[:, sub, :, 0:64], pv[:, 0:256].rearrange("p (h d) -> p h d", d=64)),
                     reads=[pv], wacc=[VS])
                S.op("act", lambda h: h.activation(sq[:, 0:512], pa[:], AF.Square), reads=[pa], wacc=[sq])
                S.op("act", lambda h: h.activation(sq[:, 512:1024], pbk[:], AF.Square), reads=[pbk], wacc=[sq])
                S.op("dve", lambda h: h.tensor_reduce(ss[:], sq[:].rearrange("p (h d) -> p h d", d=64), axis=AX.X,
                                                      op=ALU.add), reads=[sq], writes=[ss])
                S.op("dve", lambda h: h.tensor_scalar(ss[:], ss[:], 1.0 / 64, None, op0=ALU.mult), reads=[ss],
                     wacc=[ss])
                P.rsqrt(rs[:], ss[:], P.c_eps_rms, reads=[ss], wacc=[rs])
                rsb = rs[:].unsqueeze(2).to_broadcast([128, 16, 64])
                S.op("dve", lambda h: h.tensor_tensor(qn[:, 0:8, :], pa[:].rearrange("p (h d) -> p h d", d=64),
                                                      rsb[:, 0:8, :], op=ALU.mult), reads=[pa, rs], wacc=[qn])
                S.op("dve", lambda h: h.tensor_tensor(qn[:, 8:16, :], pbk[:].rearrange("p (h d) -> p h d", d=64),
                                                      rsb[:, 8:16, :], op=ALU.mult), reads=[pbk, rs], wacc=[qn])
                S.op("pool", lambda h: h.tensor_tensor(qn[:], qn[:], gains[:], op=ALU.mult), reads=[qn, gains],
                     wacc=[qn])
                q4 = qn[:].rearrange("p h (i two) -> p h i two", two=2)
                x1, x2 = q4[:, :, :, 0], q4[:, :, :, 1]
                gsub = i * 4 + sub
                cb = cs[:, gsub, :].unsqueeze(1).to_broadcast([128, 16, 32])
                sb_ = sn[:, gsub, :].unsqueeze(1).to_broadcast([128, 16, 32])
                o4 = qr[:, sub, :, :].rearrange("p h (i two) -> p h i two", two=2)
                S.op("dve", lambda h: h.tensor_tensor(tt[0][:], x1, cb, op=ALU.mult), reads=[qn, cs], writes=[tt[0]])
                S.op("pool", lambda h: h.tensor_tensor(tt[1][:], x2, sb_, op=ALU.mult), reads=[qn, sn],
                     writes=[tt[1]])
                S.op("pool", lambda h: h.tensor_tensor(tt[2][:], x1, sb_, op=ALU.mult), reads=[qn, sn],
                     writes=[tt[2]])
                S.op("dve", lambda h: h.tensor_tensor(tt[3][:], x2, cb, op=ALU.mult), reads=[qn, cs], writes=[tt[3]])
                S.op("dve", lambda h: h.tensor_tensor(o4[:, :, :, 0], tt[0][:], tt[1][:], op=ALU.subtract),
                     reads=[tt[0], tt[1]], wacc=[qr])
                S.op("pool", lambda h: h.tensor_tensor(o4[:, :, :, 1], tt[2][:], tt[3][:], op=ALU.add),
                     reads=[tt[2], tt[3]], wacc=[qr])
            for fc in range(8):
                pt = P.ps[pb % 8]
                pb += 1
                ptb = pt[:].bitcast(BF16)
                for sub in range(4):
                    S.op("pe", lambda h: h.transpose(ptb[:, sub * 128:(sub + 1) * 128],
                                                     qr[:, sub, 2 * fc:2 * fc + 2, :].rearrange("p h d -> p (h d)"),
                                                     P.c_identb[:]), reads=[qr, P.c_identb], writes=[pt])
                P.copy(P.eng2(), QT[:, fc, :], ptb[:, 0:512], reads=[pt], wacc=[QT])
            t0 = i * TT
            S.dma("pool", P.QT1[:, :, t0:t0 + TT], QT[:, 0:6, :], reads=[QT])
            tl = il * TT
            S.dma("pool", P.KT1[sg][tl // 2048][:, tl % 2048:tl % 2048 + TT].rearrange("(c p) t -> p c t", p=128),
                  QT[:, 6:8, :], reads=[QT])
            S.dma("pool", P.V1[sg][tl // 1024][tl % 1024:tl % 1024 + TT, :].rearrange("(s p) f -> p s f", p=128),
                  VS[:].rearrange("p s h d -> p s (h d)"), reads=[VS])
            for c2 in range(2):
                pt = P.ps[pb % 8]
                pb += 1
                for c in range(8):
                    S.op("pe", lambda h: h.matmul(pt[:], wb[:, c, 1280 + c2 * 128:1280 + (c2 + 1) * 128], xb[:, c, :],
                                                  start=(c == 0), stop=(c == 7)), reads=[wb, xb], writes=[pt])
                P.copy(P.eng2(), US[:, c2, :], pt[:], reads=[pt], wacc=[US])
            S.dma("pool", P.U1[:, :, t0:t0 + TT], US[:], reads=[US])
        S.barrier()


def phase_gqa(P):
    nc, S = P.nc, P.S
    P.OT1 = P.dscratch("OT1", [128, 8, NOWN], BF16)
    ktall = [P.dscratch("KT1all%d" % i, [1024, 2048], BF16) for i in range(2)]
    vall = [P.dscratch("V1all%d" % i, [4096, 260], BF16) for i in range(4)]
    G = Buf(None)
    for i in range(2):
        S.allgather(ktall[i], P.KT1["p"][i], reads=[], writes=[], groups=GROUPS4)
    for i in range(4):
        S.allgather(vall[i], P.V1["p"][i], reads=[], writes=[], groups=GROUPS4)
    S.cc_fence(GROUPS4)
    G.ws = [("cc", S.cc_cnt, "dma")]
    with ExitStack() as st:
        ones32 = P.sb(st, "g_ones32", [128, 64], F32)
        S.op("dve", lambda h: h.memset(ones32[:], 1.0), writes=[ones32])
        KT = P.sb(st, "gKT", [128, SEQ_P], BF16)
        VA = P.sb(st, "gVA", [128, SEQ_P // 128, 2, 65], BF16)
        QTs = [P.sb(st, "gQT%d" % i, [128, NP_OWN], BF16) for i in range(2)]
        PA = PairAttn(P, st, with_z=False)
        qi_ = 0
        for sg in ("p", "s"):
            nseq = SEQ_P if sg == "p" else SEQ_S
            n_own = NP_OWN if sg == "p" else NS_OWN
            ooff = 0 if sg == "p" else NP_OWN
            nch = nseq // 128
            nqt = n_own // TT
            for kc in range(2):
                if sg == "p":
                    for r in range(4):
                        for hf in range(2):
                            S.dma("sp", KT[:, r * 4096 + hf * 2048:r * 4096 + (hf + 1) * 2048],
                                  ktall[hf][r * 256 + kc * 128:r * 256 + (kc + 1) * 128, :], reads=[G], wacc=[KT])
                        for j in range(4):
                            c0 = (r * 4096 + j * 1024) // 128
                            S.dma("sp", VA[:, c0:c0 + 8, :, :].rearrange("p c h d -> p c (h d)"),
                                  vall[j][r * 1024:(r + 1) * 1024, kc * 130:(kc + 1) * 130].rearrange(
                                      "(c p) f -> p c f", p=128), reads=[G], wacc=[VA])
                else:
                    S.dma("sp", KT[:, 0:nseq], P.KT1["s"][0][kc * 128:(kc + 1) * 128, :], wacc=[KT])
                    for j in range(2):
                        S.dma("sp", VA[:, j * 8:(j + 1) * 8, :, :].rearrange("p c h d -> p c (h d)"),
                              P.V1["s"][j][:, kc * 130:(kc + 1) * 130].rearrange("(c p) f -> p c f", p=128),
                              wacc=[VA])
                for j in range(3):
                    qc = 3 * kc + j
                    QT = QTs[qi_ % 2]
                    qi_ += 1
                    S.dma("sp", QT[:, 0:n_own], P.QT1[:, qc, ooff:ooff + n_own], writes=[QT])
                    items = []
                    for qi in range(nqt):
                        for ch in range(nch):
                            items.append((qi, ch, 0, ch == 0, ch == nch - 1))

                    def out_fn(qi, out, qc=qc, ooff=ooff):
                        o0 = ooff + qi * TT
                        S.dma("pool", P.OT1[:, qc, o0:o0 + TT], out[:], reads=[out])

                    PA.run(items, KT, QT, VA, None, out_fn)
        S.barrier()


def phase_pool(P):
    nc, S = P.nc, P.S
    pw = P.din("cd_pool_w", [4, 64, 64])
    psc = P.din("pool_scale_r", [128, 2])
    rc_d = P.din("pool_rc", [128, 2, NOWN])
    sel_d = P.din("pool_sel", [128, 8])
    edge = P.dscratch("pool_edge", [256, 16], F32)
    edall = P.dscratch("pool_edall", [1024, 16], F32)
    EDG, EDA = Buf(edge), Buf(edall)
    with ExitStack() as st:
        psc_s = P.sb(st, "pl_sc", [128, 2], F32)
        sel = P.sb(st, "pl_sel", [128, 8], F32)
        S.dma("sp", psc_s[:], psc[:, :], writes=[psc_s])
        S.dma("sp", sel[:], sel_d[:, :], writes=[sel])
        eg = P.sb(st, "pl_eg", [128, 2, 16], F32)
        S.dma("sp", eg[:, :, 0:8], P.U1[:, :, 0:8], wacc=[eg])
        S.dma("sp", eg[:, :, 8:16], P.U1[:, :, NP_OWN - 8:NP_OWN], wacc=[eg])
        S.dma("pool", edge.ap().rearrange("(c p) t -> p c t", p=128), eg[:], reads=[eg], writes=[EDG])
        S.allgather(edall, edge, reads=[EDG], writes=[EDA], groups=GROUPS4)
        S.cc_fence(GROUPS4)
        EDA.ws = [("cc", S.cc_cnt, "dma")]
        ea = P.sb(st, "pl_ea", [128, 4, 2, 16], F32)
        for r in range(4):
            S.dma("sp", ea[:, r, :, :], edall[r * 256:(r + 1) * 256, :].rearrange("(c p) t -> p c t", p=128),
                  reads=[EDA], wacc=[ea])
        wbd = P.sb(st, "pl_wbd", [128, 128], F32)
        wbb = P.sb(st, "pl_wbb", [128, 128], BF16)
        NE = NP_OWN + 16
        ue = P.sb(st, "pl_ue", [128, NE], F32)
        sA = P.sb(st, "pl_sA", [128, NE], F32)
        sB = P.sb(st, "pl_sB", [128, NE], F32)
        rc = P.sb(st, "pl_rc", [128, NP_OWN], F32)
        mx = P.sb(st, "pl_mx", [128, NP_OWN], BF16)
        ot = [P.sb(st, "pl_ot%d" % i, [128, TT], BF16) for i in range(2)]
        pb = 0
        for sg in ("p", "s"):
            n = NP_OWN if sg == "p" else NS_OWN
            ooff = 0 if sg == "p" else NP_OWN
            for c2 in range(2):
                S.op("dve", lambda h: h.memset(wbd[:], 0.0), writes=[wbd])
                S.dma("sp", wbd[0:64, 0:64], pw[2 * c2, :, :], wacc=[wbd])
                S.dma("sp", wbd[64:128, 64:128], pw[2 * c2 + 1, :, :], wacc=[wbd])
                S.op("dve", lambda h: h.tensor_copy(wbb[:], wbd[:]), reads=[wbd], writes=[wbb])
                S.op("dve", lambda h: h.memset(ue[:, 0:8], 0.0), wacc=[ue])
                S.op("dve", lambda h: h.memset(ue[:, 8 + n:16 + n], 0.0), wacc=[ue])
                S.dma("sp", ue[:, 8:8 + n], P.U1[:, c2, ooff:ooff + n], wacc=[ue])
                S.dma("sp", rc[:, 0:n], rc_d[:, c2, ooff:ooff + n], writes=[rc])
                if sg == "p":
                    for r in range(4):
                        S.op("dve", lambda h: h.scalar_tensor_tensor(ue[:, 0:8], ea[:, r, c2, 8:16], sel[:, r:r + 1],
                                                                     ue[:, 0:8], op0=ALU.mult, op1=ALU.add),
                             reads=[ea, sel, ue], wacc=[ue])
                        S.op("dve", lambda h: h.scalar_tensor_tensor(ue[:, 8 + n:16 + n], ea[:, r, c2, 0:8],
                                                                     sel[:, 4 + r:5 + r], ue[:, 8 + n:16 + n],
                                                                     op0=ALU.mult, op1=ALU.add),
                             reads=[ea, sel, ue], wacc=[ue])
                S.op("dve", lambda h: h.tensor_tensor(sA[:, 1:16 + n], ue[:, 0:15 + n], ue[:, 1:16 + n], op=ALU.add),
                     reads=[ue], writes=[sA])
                S.op("dve", lambda h: h.tensor_tensor(sB[:, 2:15 + n], sA[:, 1:14 + n], sA[:, 3:16 + n], op=ALU.add),
                     reads=[sA], writes=[sB])
                if c2 == 1:
                    S.op("dve", lambda h: h.tensor_tensor(sA[:, 4:13 + n], sB[:, 2:11 + n], sB[:, 6:15 + n],
                                                          op=ALU.add), reads=[sB], writes=[sA])
                    S.op("dve", lambda h: h.tensor_tensor(sB[:, 8:8 + n], sA[:, 4:4 + n], sA[:, 12:12 + n],
                                                          op=ALU.add), reads=[sA], writes=[sB])
                for half, sbuf_ in ((0, sA), (1, sB)):
                    ps_ = slice(half * 64, (half + 1) * 64)
                    S.op("dve", lambda h: h.tensor_tensor(sbuf_[ps_, 8:8 + n], sbuf_[ps_, 8:8 + n], rc[ps_, 0:n],
                                                          op=ALU.mult), reads=[sbuf_, rc], wacc=[sbuf_])
                    S.op("dve", lambda h: h.tensor_tensor(mx[ps_, 0:n], sbuf_[ps_, 8:8 + n], ue[ps_, 8:8 + n],
                                                          op=ALU.subtract), reads=[sbuf_, ue], wacc=[mx])
                for ti in range(n // TT):
                    pt = P.ps[pb % 8]
                    pb += 1
                    o = ot[ti % 2]
                    S.op("pe", lambda h: h.matmul(pt[:], wbb[:], mx[:, ti * TT:(ti + 1) * TT], start=True, stop=True),
                         reads=[wbb, mx], writes=[pt])
                    S.op("act", lambda h: h.activation(o[:], pt[:], AF.Identity, scale=psc_s[:, c2:c2 + 1]),
                         reads=[pt, psc_s], writes=[o])
                    o0 = ooff + ti * TT
                    S.dma("pool", P.OT1[:, 6 + c2, o0:o0 + TT], o[:], reads=[o])
        S.barrier()


def host_inputs(inp, names):
    cc = _consts_common()
    f32 = lambda a: np.ascontiguousarray(a, dtype=np.float32)
    maps = []
    for c in range(NCORES):
        b, q = c // 4, c % 4
        o0 = q * NP_OWN
        m = {}
        for nm in names:
            if nm.startswith("k_"):
                m[nm] = cc[nm[2:]]
            elif nm == "xTp":
                ext = np.zeros((NP_EXT, D), np.float32)
                lo, hi = o0 - 1024, o0 + NP_OWN + 1024
                a, bb = max(lo, 0), min(hi, SEQ_P)
                ext[a - lo:bb - lo] = inp["x_prompt"][b, a:bb]
                m[nm] = _fm(ext.T)
            elif nm == "vldp":
                t = np.arange(o0 - 1024, o0 + NP_OWN + 1024)
                v = ((t >= 0) & (t < SEQ_P)).astype(np.float32)
                m[nm] = f32(v.reshape(NP_EXT // 128, 128).T)
            elif nm == "xTs":
                m[nm] = _fm(f32(inp["x_sample"][c]).T)
            elif nm == "vlds":
                m[nm] = np.ones((128, SEQ_S // 128), np.float32)
            elif nm == "ab_fnet_w":
                m[nm] = f32(inp["ab_fnet_w"][0])
            elif nm.startswith("f_"):
                kind, sg = nm[2:4], nm[5]
                if sg == "p":
                    tabs = _dft_tables(SEQ_P, 128, 128, list(range(32 * q, 32 * q + 32)))
                else:
                    tabs = _dft_tables(SEQ_S, 16, 128, list(range(128)))
                m[nm] = tabs[["rp", "rr", "tc", "ts", "c2", "s2"].index(kind)]
            elif nm[:-1] in ("xa_w_q", "xa_w_kv", "xa_w_o", "ffn_w_in", "ffn_w_out"):
                m[nm] = f32(inp[nm[:-1]][int(nm[-1])])
            elif nm == "ab_w_out":
                m[nm] = f32(inp["ab_w_out"][0])
            elif nm == "memTp":
                m[nm] = _fm(f32(inp["mem_prompt"][b]).T)
            elif nm == "memTs":
                m[nm] = _fm(f32(inp["mem_sample"][c]).T)
            elif nm == "cd_w_in_p":
                w = np.asarray(inp["cd_w_in"][0], np.float32)
                wq = w[:, :768].reshape(D, 12, 64)[:, QPERM, :].reshape(D, 768)
                m[nm] = f32(np.concatenate([wq, w[:, 768:]], 1))
            elif nm == "cd_w_out_p":
                w = np.asarray(inp["cd_w_out"][0], np.float32)
                wq = w[:768].reshape(12, 64, D)[QPERM].reshape(768, D)
                m[nm] = f32(np.concatenate([wq, w[768:]], 0))
            elif nm == "qk_gain_r":
                g = np.concatenate([np.tile(np.asarray(inp["cd_q_norm"][0], np.float32)[None], (12, 1)),
                                    np.tile(np.asarray(inp["cd_k_norm"][0], np.float32)[None], (4, 1))], 0)
                m[nm] = f32(np.broadcast_to(g[None], (128, 16, 64)))
            elif nm in ("rope_cos", "rope_sin"):
                pos = np.concatenate([o0 + np.arange(NP_OWN), np.arange(NS_OWN)])
                freqs = (np.float32(10000.0) ** (-np.arange(0, 32, 2, dtype=np.float32) / np.float32(32))).astype(np.float32)
                row = (pos // 64).astype(np.float32)
                col = (pos % 64).astype(np.float32)
                ang = np.concatenate([row[:, None] * freqs, col[:, None] * freqs], -1).astype(np.float32)
                t = np.cos(ang) if nm == "rope_cos" else np.sin(ang)
                m[nm] = f32(t.reshape(NOWN // 128, 128, 32).transpose(1, 0, 2))
            elif nm == "cd_pool_w":
                m[nm] = f32(inp["cd_pool_w"][0])
            elif nm == "pool_scale_r":
                m[nm] = f32(np.asarray(inp["cd_pool_scale"][0]).reshape(2, 128).T)
            elif nm == "pool_rc":
                pos = np.concatenate([o0 + np.arange(NP_OWN), np.arange(NS_OWN)])
                nn = np.concatenate([np.full(NP_OWN, SEQ_P), np.full(NS_OWN, SEQ_S)])
                rc = np.zeros((128, 2, NOWN), np.float32)
                for g_ in range(4):
                    w_ = (2, 4, 8, 16)[g_]
                    cnt = np.clip(pos + w_ // 2, 0, nn) - np.clip(pos - w_ // 2, 0, nn)
                    rc[(g_ % 2) * 64:(g_ % 2 + 1) * 64, g_ // 2, :] = (1.0 / cnt.astype(np.float32))[None]
                m[nm] = rc
            elif nm == "pool_sel":
                sel = np.zeros((128, 8), np.float32)
                if q > 0:
                    sel[:, q - 1] = 1.0
                if q < 3:
                    sel[:, 4 + q + 1] = 1.0
                m[nm] = sel
            elif nm == "rel_bias":
                m[nm] = f32(inp["rel_bias"])
            elif nm == "ab_w_in":
                m[nm] = f32(inp["ab_w_in"][0])
            elif nm == "fnet_g_r":
                m[nm] = f32(np.asarray(inp["ab_fnet_g"][0]).reshape(2, 128).T)
            elif nm in ("ln_g_r", "ln_b_r"):
                src = np.asarray(inp["ln_g" if nm == "ln_g_r" else "ln_b"], np.float32)
                m[nm] = f32(src.reshape(6, 8, 128).transpose(2, 0, 1))
            else:
                raise KeyError(nm)
        maps.append(m)
    return maps


def run_prog(P, inp):
    nc = P.finish()
    maps = host_inputs(inp, list(P.inputs.keys()))
    res = run_bass_kernel_spmd(nc, maps, core_ids=list(range(NCORES)))
    return res.results


def build_full(debug=()):
    P = Prog(debug=debug)
    P.setup()
    phase_proj0(P)
    phase_dil(P)
    phase_fnet(P)
    w_out0 = P.din("ab_w_out", [D, D])

    def xres0(i):
        if i < 8:
            return P.xT["p"][:, :, 1024 + i * TT:1024 + (i + 1) * TT]
        return P.xT["s"][:, :, (i - 8) * TT:(i - 7) * TT]

    x3f, x3b = layer_tail(P, 0, P.OT0, w_out0, xres0)
    phase_proj1(P, x3b)
    phase_gqa(P)
    phase_pool(P)
    w_out1 = P.din("cd_w_out_p", [D, D])
    layer_tail(P, 1, P.OT1, w_out1, lambda i: x3f[:, :, i * TT:(i + 1) * TT])
    return P


def kernel(**inputs):
    inp = {k: np.asarray(v) for k, v in inputs.items()}
    P = build_full()
    res = run_prog(P, inp)
    y_prompt = np.zeros((2, SEQ_P, D), np.float32)
    y_sample = np.zeros((8, SEQ_S, D), np.float32)
    for c in range(NCORES):
        b, q = c // 4, c % 4
        yT = np.asarray(res[c]["yT"], dtype=np.float32)
        y_prompt[b, q * NP_OWN:(q + 1) * NP_OWN] = _unfm(yT[:, :, :NP_OWN])
        y_sample[c] = _unfm(yT[:, :, NP_OWN:])
    return (y_prompt, y_sample)
```

```python
import math
import numpy as np
import ml_dtypes
import concourse.bass as bass
import concourse.mybir as mybir
from concourse.bass_utils import run_bass_kernel_spmd

F32 = mybir.dt.float32
BF16 = mybir.dt.bfloat16
AF = mybir.ActivationFunctionType
ALU = mybir.AluOpType
AX = mybir.AxisListType

NCORES = 8
D = 1024
KC = 8
TT = 512
NP_OWN = 4096
NS_OWN = 2048
NOWN = NP_OWN + NS_OWN
NTILE = NOWN // TT
NP_EXT = NP_OWN + 2048
SEQ_P = 16384
SEQ_S = 2048
FFN_H = 2816
HC = FFN_H // 128
DN_ALPHA = 4 ** 0.25
LN_EPS = 1e-5
RMS_EPS = 1e-6
ZW = 2944
ZC = 1408
FILLER = 1


class Buf:
    __slots__ = ("t", "ws", "r", "name", "wx")

    def __init__(self, t, name=""):
        self.t = t
        self.wx = None
        self.ws = []
        self.r = []
        self.name = name

    def __getitem__(self, idx):
        return self.t[idx]


def _compact(evs):
    best = {}
    for (k, v, s) in evs:
        if k not in best or best[k][1] < v:
            best[k] = (k, v, s)
    return list(best.values())


class Sch:
    def __init__(self, nc, ndma_sems=10):
        self.nc = nc
        self.eng = {"pe": nc.tensor, "act": nc.scalar, "dve": nc.vector,
                    "pool": nc.gpsimd, "sp": nc.sync}
        self.tick = {}
        self.seen = {}
        self.semh = {}
        self._ctx = []
        for e in self.eng:
            self._mksem("s_" + e)
            self.tick[e] = 0
            self.seen[e] = {}
        self.dq = {}
        for q in ("sp", "pool", "act"):
            names = []
            for i in range(ndma_sems):
                k = "d_%s_%d" % (q, i)
                self._mksem(k)
                names.append(k)
            self.dq[q] = {"sems": names, "n": 0, "cnt": {k: 0 for k in names}}
        self._mksem("cc")
        self.cc_cnt = 0
        self.ninst = 0

    def _mksem(self, key):
        cm = self.nc.semaphore(key)
        h = cm.__enter__()
        self._ctx.append(cm)
        self.semh[key] = h
        return h

    def close(self):
        for cm in reversed(self._ctx):
            cm.__exit__(None, None, None)
        self._ctx = []

    def _wait(self, e, ev):
        semkey, val, src = ev
        if src == e and e == "pe":
            return
        if self.seen[e].get(semkey, 0) >= val:
            return
        self.eng[e].wait_ge(self.semh[semkey], val)
        self.seen[e][semkey] = val
        self.ninst += 1

    def _deps(self, e, reads, writes, wacc):
        for b in reads:
            for ev in b.ws:
                self._wait(e, ev)
        for b in writes:
            for ev in b.ws:
                self._wait(e, ev)
            for ev in b.r:
                self._wait(e, ev)
        for b in wacc:
            if b.wx is not None:
                self._wait(e, b.wx)
            for ev in b.r:
                self._wait(e, ev)

    def _commit(self, ev, reads, writes, wacc):
        for b in reads:
            b.r.append(ev)
            if len(b.r) > 16:
                b.r = _compact(b.r)
        for b in writes:
            b.ws = [ev]
            b.wx = ev
            b.r = []
        for b in wacc:
            b.ws.append(ev)
            if len(b.ws) > 16:
                b.ws = _compact(b.ws)

    def op(self, e, fn, reads=(), writes=(), wacc=()):
        self._deps(e, reads, writes, wacc)
        ins = fn(self.eng[e])
        self.tick[e] += 1
        k = "s_" + e
        ins.then_inc(self.semh[k], 1)
        self._commit((k, self.tick[e], e), reads, writes, wacc)
        self.ninst += 1
        return ins

    def dma(self, q, out, in_, reads=(), writes=(), wacc=(), **kw):
        d = self.dq[q]
        k = d["sems"][d["n"] % len(d["sems"])]
        d["n"] += 1
        if d["cnt"][k] > 0:
            self._wait(q, (k, d["cnt"][k], "dma"))
        self._deps(q, reads, writes, wacc)
        ins = self.eng[q].dma_start(out=out, in_=in_, **kw)
        d["cnt"][k] += 16
        ins.then_inc(self.semh[k], 16)
        self._commit((k, d["cnt"][k], "dma"), reads, writes, wacc)
        self.ninst += 1

    def allgather(self, out_t, in_t, reads, writes, groups):
        self._deps("pool", reads, writes, ())
        if self.cc_cnt:
            self._wait("pool", ("cc", self.cc_cnt, "dma"))
        ins = self.nc.gpsimd.collective_compute("AllGather", ALU.bypass, replica_groups=groups,
                                                ins=[in_t.ap().opt()], outs=[out_t.ap().opt()])
        self.cc_cnt += 1
        ins.then_inc(self.semh["cc"], 1)
        self._commit(("cc", self.cc_cnt, "dma"), reads, writes, ())
        self.ninst += 1

    def cc_fence(self, groups):
        if not hasattr(self, "_fence_t"):
            self._fence_t = (self.nc.dram_tensor("cc_f_in", [16, 64], F32),
                             self.nc.dram_tensor("cc_f_out", [16 * len(groups[0]), 64], F32))
        fi, fo = self._fence_t
        self.allgather(fo, fi, reads=[], writes=[], groups=groups)

    def all_events(self):
        evs = []
        for e in self.eng:
            if self.tick[e] > 0:
                evs.append(("s_" + e, self.tick[e], e))
        for q, d in self.dq.items():
            for k, v in d["cnt"].items():
                if v > 0:
                    evs.append((k, v, "dma"))
        if self.cc_cnt:
            evs.append(("cc", self.cc_cnt, "dma"))
        return evs

    def barrier(self, engines=("pe", "act", "dve", "pool", "sp")):
        evs = self.all_events()
        for e in engines:
            for ev in evs:
                if ev[2] == e:
                    continue
                self._wait(e, ev)


def _t5_bucket_np(rel):
    nb = 16
    max_exact = 8
    ret = np.where(rel > 0, nb, 0)
    n = np.abs(rel)
    nf = np.maximum(n, 1).astype(np.float32)
    large = max_exact + (np.log(nf / np.float32(max_exact)) / np.float32(math.log(1024 / max_exact))
                         * np.float32(nb - max_exact)).astype(np.int32)
    large = np.minimum(large, nb - 1)
    return ret + np.where(n < max_exact, n, large)


def _consts_common():
    c = {}
    c["ident"] = np.eye(128, dtype=np.float32)
    c["antiI"] = np.eye(128, dtype=np.float32)[::-1].copy()
    c["onesd"] = np.full((128, 128), 1.0 / D, np.float32)
    blk = np.zeros((128, 128), np.float32)
    blk[:64, :64] = 1.0 / 64
    blk[64:, 64:] = 1.0 / 64
    c["blk64"] = blk
    i = np.arange(3072)
    delta = 1535 - i
    mult = ((np.abs(delta) <= 64).astype(np.int32)
            + ((delta % 4 == 0) & (np.abs(delta) <= 256)).astype(np.int32)
            + ((delta % 16 == 0) & (np.abs(delta) <= 1024)).astype(np.int32))
    mult[3071] = 0
    bk = _t5_bucket_np(delta)
    ohm = np.zeros((32, 3072), np.float32)
    ohm[bk, i] = mult
    c["ohm"] = ohm
    k = np.arange(64)
    ang = 2 * np.pi * np.outer(k, k) / 64
    c64 = (np.cos(ang) / 8).astype(np.float32)
    s64 = (np.sin(ang) / 8).astype(np.float32)
    cbd = np.zeros((128, 128), np.float32)
    sbd = np.zeros((128, 128), np.float32)
    cbd[:64, :64] = c64
    cbd[64:, 64:] = c64
    sbd[:64, :64] = s64
    sbd[64:, 64:] = s64
    c["c64bd"] = cbd
    c["s64bd"] = sbd
    return c


def _dft_tables(N, N1, N2, k2_list):
    sc = 1.0 / math.sqrt(N)
    n1 = np.arange(N1)
    k1 = np.arange(N1)
    n2 = np.arange(N2)
    a1 = 2 * np.pi * np.outer(n1, k1) / N1
    rp = np.concatenate([np.cos(a1), -np.sin(a1)], 1) * sc
    rr = np.concatenate([np.sin(a1), np.cos(a1)], 1) * sc
    at = 2 * np.pi * np.outer(n2, k1) / N
    tc = np.cos(at)
    ts = np.sin(at)
    a2 = 2 * np.pi * np.outer(n2, np.asarray(k2_list)) / N2
    c2 = np.cos(a2)
    s2 = np.sin(a2)
    f = lambda a: np.ascontiguousarray(a, dtype=np.float32)
    return f(rp), f(rr), f(tc), f(ts), f(c2), f(s2)


def _fm(a):
    F, T = a.shape
    return np.ascontiguousarray(a.reshape(F // 128, 128, T).transpose(1, 0, 2))


def _unfm(a):
    P_, C, T = a.shape
    return np.ascontiguousarray(a.transpose(2, 1, 0).reshape(T, C * P_))


from contextlib import ExitStack


class Prog:
    def __init__(self, debug=()):
        self.debug = set(debug)
        self.nc = bass.Bass("TRN2", target_bir_lowering=False)
        self.S = Sch(self.nc)
        self.inputs = {}
        self.outputs = {}
        self.gstack = ExitStack()
        self.rr = 0

    def din(self, name, shape, dtype=F32):
        t = self.nc.dram_tensor(name, list(shape), dtype, kind="ExternalInput")
        self.inputs[name] = t
        return t

    def dout(self, name, shape, dtype=F32):
        t = self.nc.dram_tensor(name, list(shape), dtype, kind="ExternalOutput")
        self.outputs[name] = t
        return t

    def dscratch(self, name, shape, dtype):
        if name in self.debug:
            return self.dout(name, shape, dtype)
        return self.nc.dram_tensor(name, list(shape), dtype)

    def sb(self, stack, name, shape, dtype):
        self._uid = getattr(self, "_uid", 0) + 1
        name = "%s_u%d" % (name, self._uid)
        t = stack.enter_context(self.nc.sbuf_tensor(name, list(shape), dtype))
        return Buf(t, name)

    def eng2(self):
        self.rr += 1
        return "act" if self.rr % 2 else "dve"

    def copy(self, e, out, in_, reads, writes=(), wacc=(), scale=None):
        S = self.S
        if e == "act":
            if scale is None:
                S.op("act", lambda h: h.copy(out, in_), reads, writes, wacc)
            else:
                S.op("act", lambda h: h.mul(out, in_, scale), reads, writes, wacc)
        else:
            if scale is None:
                S.op(e, lambda h: h.tensor_copy(out, in_), reads, writes, wacc)
            else:
                S.op(e, lambda h: h.tensor_scalar_mul(out, in_, scale), reads, writes, wacc)

    def rsqrt(self, out, in_, eps_tile, reads, wacc):
        S = self.S
        S.op("act", lambda h: h.activation(out, in_, AF.Sqrt, bias=eps_tile[:, 0:1], scale=1.0),
             reads=list(reads) + [eps_tile], wacc=wacc)
        S.op("dve", lambda h: h.reciprocal(out, out), reads=list(wacc), wacc=wacc)

    def setup(self):
        nc, S = self.nc, self.S
        g = self.gstack
        self.ps2 = [Buf(g.enter_context(nc.psum_tensor("psp%d" % i, [128, 1024], F32)), "psp%d" % i) for i in range(4)]
        self.ps = [Buf(self.ps2[i // 2].t[:, (i % 2) * 512:(i % 2 + 1) * 512], "ps%d" % i) for i in range(8)]
        self.c_onesA = self.sb(g, "c_onesA", [128, 128], F32)
        self.c_onesB = self.sb(g, "c_onesB", [128, 128], F32)
        S.op("dve", lambda h: h.memset(self.c_onesA[:], 0.0), writes=[self.c_onesA])
        S.op("dve", lambda h: h.memset(self.c_onesB[:], 0.0), writes=[self.c_onesB])
        S.op("dve", lambda h: h.memset(self.c_onesA[:, 0:64], 1.0), reads=[self.c_onesA], wacc=[self.c_onesA])
        S.op("dve", lambda h: h.memset(self.c_onesB[:, 64:128], 1.0), reads=[self.c_onesB], wacc=[self.c_onesB])
        self.c_ident = self.sb(g, "c_ident", [128, 128], F32)
        self.c_antiI = self.sb(g, "c_antiI", [128, 128], F32)
        self.c_onesd = self.sb(g, "c_onesd", [128, 128], F32)
        self.c_blk64 = self.sb(g, "c_blk64", [128, 128], F32)
        self.c_onesb = self.sb(g, "c_onesb", [128, 128], BF16)
        self.c_identb = self.sb(g, "c_identb", [128, 128], BF16)
        self.c_lng = self.sb(g, "c_lng", [128, 6, 8], F32)
        self.c_lnb = self.sb(g, "c_lnb", [128, 6, 8], F32)
        for nm, buf in (("ident", self.c_ident), ("antiI", self.c_antiI), ("onesd", self.c_onesd),
                        ("blk64", self.c_blk64)):
            t = self.din("k_" + nm, [128, 128])
            S.dma("sp", buf[:], t[:, :], writes=[buf])
        t = self.din("ln_g_r", [128, 6, 8])
        S.dma("sp", self.c_lng[:], t[:, :, :], writes=[self.c_lng])
        t = self.din("ln_b_r", [128, 6, 8])
        S.dma("sp", self.c_lnb[:], t[:, :, :], writes=[self.c_lnb])
        self.c_eps_ln = self.sb(g, "c_eps_ln", [128, 1], F32)
        self.c_eps_rms = self.sb(g, "c_eps_rms", [128, 1], F32)
        S.op("dve", lambda h: h.memset(self.c_eps_ln[:], LN_EPS), writes=[self.c_eps_ln])
        S.op("dve", lambda h: h.memset(self.c_eps_rms[:], RMS_EPS), writes=[self.c_eps_rms])
        S.op("dve", lambda h: h.memset(self.c_onesb[:], 1.0), writes=[self.c_onesb])
        S.op("dve", lambda h: h.tensor_copy(self.c_identb[:], self.c_ident[:]), reads=[self.c_ident],
             writes=[self.c_identb])

    def load_w(self, stack, name, wap, K, N, stg):
        S = self.S
        kc = K // 128
        wb = self.sb(stack, name, [128, kc, N], BF16)
        CH = stg[0].t.shape[1]
        i = 0
        for c in range(kc):
            for n0 in range(0, N, CH):
                n1 = min(N, n0 + CH)
                st = stg[i % len(stg)]
                i += 1
                S.dma(("sp", "act", "pool")[i % 3], st[:, 0:n1 - n0], wap[c * 128:(c + 1) * 128, n0:n1], writes=[st])
                self.copy(self.eng2(), wb[:, c, n0:n1], st[:, 0:n1 - n0], reads=[st], wacc=[wb])
        return wb

    def finish(self):
        S = self.S
        S.barrier()
        self.gstack.close()
        S.close()
        return self.nc


def phase_proj0(P):
    nc, S = P.nc, P.S
    w_in = P.din("ab_w_in", [D, 2560])
    fg = P.din("fnet_g_r", [128, 2])
    P.KT0 = {"p": P.dscratch("KT0p", [128, 6, NP_EXT], BF16), "s": P.dscratch("KT0s", [128, 6, SEQ_S], BF16)}
    P.V0 = {"p": P.dscratch("V0p", [NP_EXT // 128, 128, 780], BF16),
            "s": P.dscratch("V0s", [SEQ_S // 128, 128, 780], BF16)}
    P.QT0 = P.dscratch("QT0", [128, 6, NOWN], BF16)
    P.unT = {"p": [P.dscratch("unTp%d" % i, [256, 2048], BF16) for i in range(2)],
             "s": [P.dscratch("unTs", [256, NS_OWN], BF16)]}
    xT = {"p": P.din("xTp", [128, 8, NP_EXT]), "s": P.din("xTs", [128, 8, SEQ_S])}
    vld = {"p": P.din("vldp", [128, NP_EXT // 128]), "s": P.din("vlds", [128, SEQ_S // 128])}
    P.xT = xT
    P.vld = vld
    with ExitStack() as st:
        stg = [P.sb(st, "stg%d" % i, [128, 1024], F32) for i in range(3)]
        wb = P.load_w(st, "w_ab_in", w_in, D, 2560, stg)
        fgs = P.sb(st, "fgs", [128, 2], F32)
        S.dma("sp", fgs[:], fg[:, :], writes=[fgs])
        xf = [P.sb(st, "xf%d" % i, [128, 8, TT], F32) for i in range(2)]
        xb = [P.sb(st, "xb%d" % i, [128, 8, TT], BF16) for i in range(2)]
        kt = [P.sb(st, "kt%d" % i, [128, 6, TT], BF16) for i in range(2)]
        qt = [P.sb(st, "qt%d" % i, [128, 6, TT], BF16) for i in range(2)]
        vs = [P.sb(st, "vs%d" % i, [128, 4, 12, 65], BF16) for i in range(2)]
        uf = P.sb(st, "uf", [128, 2, TT], F32)
        usq = P.sb(st, "usq", [128, 2, TT], F32)
        urs = P.sb(st, "urs", [128, 2, TT], F32)
        un = [P.sb(st, "un%d" % i, [128, 2, TT], BF16) for i in range(2)]
        ones12 = P.sb(st, "ones12", [128, 12, 1], F32)
        S.op("dve", lambda h: h.memset(ones12[:], 1.0), writes=[ones12])
        vl = {}
        for sg in ("p", "s"):
            nch = (NP_EXT if sg == "p" else SEQ_S) // 128
            vl[sg] = P.sb(st, "vl" + sg, [128, nch], F32)
            S.dma("sp", vl[sg][:], vld[sg][:, :], writes=[vl[sg]])
        it = 0
        pb = 0
        for sg in ("p", "s"):
            n_ext = NP_EXT if sg == "p" else SEQ_S
            own0 = 2 if sg == "p" else 0
            nown_t = 8 if sg == "p" else 4
            ooff = 0 if sg == "p" else NP_OWN
            for i in range(n_ext // TT):
                a = it % 2
                it += 1
                X, XB, KT, QT, VS, UN = xf[a], xb[a], kt[a], qt[a], vs[a], un[a]
                S.dma("sp", X[:], xT[sg][:, :, i * TT:(i + 1) * TT], writes=[X])
                S.op("act", lambda h: h.copy(XB[:, 0:4, :], X[:, 0:4, :]), reads=[X], wacc=[XB])
                S.op("dve", lambda h: h.tensor_copy(XB[:, 4:8, :], X[:, 4:8, :]), reads=[X], wacc=[XB])
                for oc in range(6):
                    pt = P.ps[pb % 8]
                    pb += 1
                    for c in range(8):
                        S.op("pe", lambda h: h.matmul(pt[:], wb[:, c, 768 + oc * 128:768 + (oc + 1) * 128],
                                                      XB[:, c, :], start=(c == 0), stop=(c == 7)),
                             reads=[wb, XB], writes=[pt])
                    P.copy(P.eng2(), KT[:, oc, :], pt[:], reads=[pt], wacc=[KT])
                S.dma("pool", P.KT0[sg][:, :, i * TT:(i + 1) * TT], KT[:], reads=[KT])
                for sub in range(4):
                    for hf in range(2):
                        pt = P.ps[pb % 8]
                        pb += 1
                        for c in range(8):
                            S.op("pe", lambda h: h.matmul(pt[:, 0:384], XB[:, c, sub * 128:(sub + 1) * 128],
                                                          wb[:, c, 1536 + hf * 384:1536 + (hf + 1) * 384],
                                                          start=(c == 0), stop=(c == 7)),
                                 reads=[wb, XB], writes=[pt])
                        P.copy(P.eng2(), VS[:, sub, hf * 6:(hf + 1) * 6, 0:64],
                               pt[:, 0:384].rearrange("p (h d) -> p h d", d=64), reads=[pt], wacc=[VS])
                    ch = i * 4 + sub
                    S.op("dve", lambda h: h.tensor_scalar(VS[:, sub, :, 64:65], ones12[:], vl[sg][:, ch:ch + 1], None,
                                                          op0=ALU.mult), reads=[ones12, vl[sg]], wacc=[VS])
                S.dma("pool", P.V0[sg][i * 4:(i + 1) * 4].rearrange("c p f -> p c f"),
                      VS[:].rearrange("p s h d -> p s (h d)"), reads=[VS])
                if not (own0 <= i < own0 + nown_t):
                    continue
                o0 = ooff + (i - own0) * TT
                for oc in range(6):
                    pt = P.ps[pb % 8]
                    pb += 1
                    for c in range(8):
                        S.op("pe", lambda h: h.matmul(pt[:], wb[:, c, oc * 128:(oc + 1) * 128],
                                                      XB[:, c, :], start=(c == 0), stop=(c == 7)),
                             reads=[wb, XB], writes=[pt])
                    P.copy(P.eng2(), QT[:, oc, :], pt[:], reads=[pt], wacc=[QT], scale=0.125)
                S.dma("pool", P.QT0[:, :, o0:o0 + TT], QT[:], reads=[QT])
                for c2 in range(2):
                    pt = P.ps[pb % 8]
                    pb += 1
                    for c in range(8):
                        S.op("pe", lambda h: h.matmul(pt[:], wb[:, c, 2304 + c2 * 128:2304 + (c2 + 1) * 128],
                                                      XB[:, c, :], start=(c == 0), stop=(c == 7)),
                             reads=[wb, XB], writes=[pt])
                    S.op("act", lambda h: h.copy(uf[:, c2, :], pt[:]), reads=[pt], wacc=[uf])
                for c2 in range(2):
                    pm = P.ps[pb % 8]
                    pb += 1
                    S.op("pe", lambda h: h.matmul(pm[:], P.c_blk64[:], uf[:, c2, :], start=True, stop=True),
                         reads=[P.c_blk64, uf], writes=[pm])
                    S.op("dve", lambda h: h.tensor_tensor(uf[:, c2, :], uf[:, c2, :], pm[:], op=ALU.subtract),
                         reads=[pm, uf], wacc=[uf])
                    S.op("act", lambda h: h.activation(usq[:, c2, :], uf[:, c2, :], AF.Square),
                         reads=[uf], wacc=[usq])
                    pv = P.ps[pb % 8]
                    pb += 1
                    S.op("pe", lambda h: h.matmul(pv[:], P.c_blk64[:], usq[:, c2, :], start=True, stop=True),
                         reads=[P.c_blk64, usq], writes=[pv])
                    P.rsqrt(urs[:, c2, :], pv[:], P.c_eps_ln, reads=[pv], wacc=[urs])
                    S.op("dve", lambda h: h.tensor_tensor(uf[:, c2, :], uf[:, c2, :], urs[:, c2, :], op=ALU.mult),
                         reads=[urs, uf], wacc=[uf])
                    S.op("dve", lambda h: h.tensor_scalar(UN[:, c2, :], uf[:, c2, :], fgs[:, c2:c2 + 1], None,
                                                          op0=ALU.mult), reads=[uf, fgs], wacc=[UN])
                oo = (i - own0) * TT
                S.dma("pool", P.unT[sg][oo // 2048][:, oo % 2048:oo % 2048 + TT].rearrange("(c p) t -> p c t", p=128),
                      UN[:], reads=[UN])
        S.barrier()


def attn_pipeline(P, items, s_fn, e_fn, o_fn, look=2):
    n = len(items)
    for t in range(min(look, n)):
        s_fn(items[t], t)
    for t in range(n):
        if t + look < n:
            s_fn(items[t + look], t + look)
        e_fn(items[t], t)
        o_fn(items[t], t)


class PairAttn:
    def __init__(self, P, st, with_z):
        self.P = P
        self.with_z = with_z
        self.EB = [P.sb(st, "paEB%d" % i, [128, 1024], BF16) for i in range(4)]
        if with_z:
            self.EF = [P.sb(st, "paEF%d" % i, [128, 1024], F32) for i in range(3)]
        self.RD = [P.sb(st, "paRD%d" % i, [128, 512], F32) for i in range(2)]
        self.OUT = [P.sb(st, "paOUT%d" % i, [128, 512], BF16) for i in range(2)]

    def run(self, items, kt, qt, va, zz, out_fn, vl=None):
        P, S = self.P, self.P.S
        EB, RD, OUT = self.EB, self.RD, self.OUT

        def s_fn(itm, t):
            qi, ch, zoff, first, last = itm
            pp = P.ps2[t % 3]
            for hh in range(2):
                pb_ = hh * 64
                S.op("pe", lambda h: h.matmul(pp[:, hh * 512:(hh + 1) * 512], kt[pb_:pb_ + 64, ch * 128:(ch + 1) * 128],
                                              qt[pb_:pb_ + 64, qi * TT:(qi + 1) * TT], start=True, stop=True,
                                              tile_position=(pb_, 0)), reads=[kt, qt], writes=[pp])

        def e_fn(itm, t):
            qi, ch, zoff, first, last = itm
            pp = P.ps2[t % 3]
            eb = EB[t % 4]
            if self.with_z:
                ef = self.EF[t % 3]
                S.op("act", lambda h: h.activation(ef[:], pp[:], AF.Exp), reads=[pp], writes=[ef])
                S.op("dve", lambda h: h.tensor_tensor(eb[:, 0:512], ef[:, 0:512], zz[0][:, zoff:zoff + 512],
                                                      op=ALU.mult), reads=[ef, zz[0]], wacc=[eb])
                S.op("pool", lambda h: h.tensor_tensor(eb[:, 512:1024], ef[:, 512:1024], zz[1][:, zoff:zoff + 512],
                                                       op=ALU.mult), reads=[ef, zz[1]], wacc=[eb])
            else:
                S.op("act", lambda h: h.activation(eb[:], pp[:], AF.Exp), reads=[pp], writes=[eb])

        def o_fn(itm, t):
            qi, ch, zoff, first, last = itm
            eb = EB[t % 4]
            po, pd = P.ps[6], P.ps[7]
            for hh in range(2):
                S.op("pe", lambda h: h.matmul(po[hh * 64:(hh + 1) * 64, :], va[:, ch, hh, 0:64],
                                              eb[:, hh * 512:(hh + 1) * 512], start=first, stop=last,
                                              tile_position=(0, hh * 64)), reads=[va, eb], writes=[po])
            for hh in range(2):
                lt = vl[:, ch, :] if vl is not None else P.c_onesb[:, 0:64]
                S.op("pe", lambda h: h.matmul(pd[hh * 64:(hh + 1) * 64, :], lt,
                                              eb[:, hh * 512:(hh + 1) * 512], start=first, stop=last,
                                              tile_position=(0, hh * 64)),
                     reads=[vl if vl is not None else P.c_onesb, eb], writes=[pd])
            if not last:
                return
            rd, out = RD[qi % 2], OUT[qi % 2]
            S.op("dve", lambda h: h.reciprocal(rd[:], pd[:]), reads=[pd], writes=[rd])
            S.op("dve", lambda h: h.tensor_tensor(out[:], po[:], rd[:], op=ALU.mult), reads=[po, rd], writes=[out])
            out_fn(qi, out)

        attn_pipeline(P, items, s_fn, e_fn, o_fn, look=2)


def phase_dil(P):
    nc, S = P.nc, P.S
    relb = P.din("rel_bias", [32, 12])
    ohm = P.din("k_ohm", [32, 3072])
    rev = P.dscratch("dil_rev", [12, 3200], F32)
    P.OT0 = P.dscratch("OT0", [128, 8, NOWN], BF16)
    REV = Buf(rev, "rev")
    with ExitStack() as st:
        ones32 = P.sb(st, "ones32", [128, 64], F32)
        S.op("dve", lambda h: h.memset(ones32[:], 1.0), writes=[ones32])
        rb = P.sb(st, "rb", [32, 12], F32)
        eb = P.sb(st, "eb", [32, 12], F32)
        oh = P.sb(st, "oh", [32, 3072], F32)
        wt = P.sb(st, "wt", [12, 3072], F32)
        S.dma("sp", rb[:], relb[:, :], writes=[rb])
        S.dma("sp", oh[:], ohm[:, :], writes=[oh])
        S.op("act", lambda h: h.activation(eb[:], rb[:], AF.Exp), reads=[rb], writes=[eb])
        for n0 in range(0, 3072, 512):
            pt = P.ps[(n0 // 512) % 8]
            S.op("pe", lambda h: h.matmul(pt[0:12, :], eb[:], oh[:, n0:n0 + 512], start=True, stop=True),
                 reads=[eb, oh], writes=[pt])
            S.op("dve", lambda h: h.tensor_copy(wt[:, n0:n0 + 512], pt[0:12, :]), reads=[pt], wacc=[wt])
        S.dma("pool", REV[:, 0:3072], wt[:], reads=[wt], writes=[REV])
        S.barrier()
        KT = [P.sb(st, "dKT%d" % i, [128, NP_EXT], BF16) for i in range(2)]
        QT = [P.sb(st, "dQT%d" % i, [128, NP_OWN], BF16) for i in range(2)]
        VA = [P.sb(st, "dVA%d" % i, [128, NP_EXT // 128, 2, 65], BF16) for i in range(2)]
        ZZ = [[P.sb(st, "dZ%d_%d" % (i, k), [128, ZW], BF16) for k in range(2)] for i in range(2)]
        HK = P.sb(st, "dHK", [128, ZW], F32)
        PA = PairAttn(P, st, with_z=True)
        VL = {}
        for sg_ in ("p", "s"):
            nch_ = (NP_EXT if sg_ == "p" else SEQ_S) // 128
            vlf = P.sb(st, "dvlf" + sg_, [128, nch_], F32)
            VL[sg_] = P.sb(st, "dvl" + sg_, [128, nch_, 64], BF16)
            S.dma("sp", vlf[:], P.vld[sg_][:, :], writes=[vlf])
            S.op("dve", lambda h: h.tensor_copy(VL[sg_][:], vlf[:].unsqueeze(2).to_broadcast([128, nch_, 64])),
                 reads=[vlf], writes=[VL[sg_]])
        it = 0
        for sg in ("p", "s"):
            n_ext = NP_EXT if sg == "p" else SEQ_S
            n_own = NP_OWN if sg == "p" else NS_OWN
            nqt = n_own // TT
            ooff = 0 if sg == "p" else NP_OWN
            nch = n_ext // 128
            for hp in range(6):
                a = it % 2
                it += 1
                kt, qt, va, zz = KT[a], QT[a], VA[a], ZZ[a]
                S.dma("sp", kt[:, 0:n_ext], P.KT0[sg][:, hp, :], writes=[kt])
                S.dma("sp", qt[:, 0:n_own], P.QT0[:, hp, ooff:ooff + n_own], writes=[qt])
                S.dma("sp", va[:, 0:nch, :, :].rearrange("p c h d -> p c (h d)"),
                      P.V0[sg][:, :, hp * 130:(hp + 1) * 130].rearrange("c p f -> p c f"), writes=[va])
                for hh in range(2):
                    h_ = 2 * hp + hh
                    src = bass.AP(tensor=rev, offset=h_ * 3200, ap=[[1, 128], [1, ZW]])
                    S.dma("sp", HK[:], src, reads=[REV], writes=[HK])
                    for n0 in range(0, ZW, 512):
                        w = min(512, ZW - n0)
                        pt = P.ps[6 + (n0 // 512) % 2]
                        S.op("pe", lambda h: h.matmul(pt[:, 0:w], P.c_antiI[:], HK[:, n0:n0 + w], start=True,
                                                      stop=True), reads=[P.c_antiI, HK], writes=[pt])
                        P.copy(P.eng2(), zz[hh][:, n0:n0 + w], pt[:, 0:w], reads=[pt], wacc=[zz[hh]])
                items = []
                for qi in range(nqt):
                    js = []
                    for j in range(20):
                        ch = 4 * qi + j - (0 if sg == "p" else 8)
                        if 0 <= ch < nch:
                            js.append((j, ch))
                    for idx, (j, ch) in enumerate(js):
                        items.append((qi, ch, 2432 - 128 * j, idx == 0, idx == len(js) - 1))

                def out_fn(qi, out, hp=hp, ooff=ooff):
                    o0 = ooff + qi * TT
                    S.dma("pool", P.OT0[:, hp, o0:o0 + TT], out[:], reads=[out])

                PA.run(items, kt, qt, va, zz, out_fn, vl=VL[sg])
        S.barrier()


GROUPS4 = [[0, 1, 2, 3], [4, 5, 6, 7]]


def phase_fnet(P):
    nc, S = P.nc, P.S
    fw = P.din("ab_fnet_w", [4, 64, 64])
    c64 = P.din("k_c64bd", [128, 128])
    s64 = P.din("k_s64bd", [128, 128])
    unall = [P.dscratch("unTall%d" % i, [1024, 2048], BF16) for i in range(2)]
    UNALL = Buf(unall)
    for i in range(2):
        S.allgather(unall[i], P.unT["p"][i], reads=[], writes=[UNALL] if i == 0 else [], groups=GROUPS4)
    S.cc_fence(GROUPS4)
    UNALL.ws = [("cc", S.cc_cnt, "dma")]
    if "unall_dbg" in P.debug:
        for i in range(2):
            dbg = P.dout("unall_dbg%d" % i, [1024, 2048], BF16)
            S.dma("sp", dbg[:, :], unall[i][:, :], reads=[UNALL])
            dbg2 = P.dout("unmine_dbg%d" % i, [256, 2048], BF16)
            S.dma("sp", dbg2[:, :], P.unT["p"][i][:, :], reads=[UNALL])
    tabs = {}
    for sg, n1 in (("p", 128), ("s", 16)):
        nk2 = 32 if sg == "p" else 128
        tabs[sg] = dict(rp=P.din("f_rp_" + sg, [n1, 2 * n1]), rr=P.din("f_rr_" + sg, [n1, 2 * n1]),
                        tc=P.din("f_tc_" + sg, [128, n1]), ts=P.din("f_ts_" + sg, [128, n1]),
                        c2=P.din("f_c2_" + sg, [128, nk2]), s2=P.din("f_s2_" + sg, [128, nk2]))
    with ExitStack() as st:
        cs = P.sb(st, "f_cs", [128, 2, 128], F32)
        S.dma("sp", cs[:, 0, :], c64[:, :], wacc=[cs])
        S.dma("sp", cs[:, 1, :], s64[:, :], wacc=[cs])
        wbd = P.sb(st, "f_wbd", [128, 128], F32)
        AB = P.sb(st, "f_AB", [128, 2, 128], BF16)
        stg = P.sb(st, "f_stg", [128, 2, 256], F32)
        tb = {k: P.sb(st, "f_t_" + k, [128, 256], BF16) for k in ("rp", "rr", "c2", "s2")}
        tcs = {k: P.sb(st, "f_t_" + k, [128, 128], F32) for k in ("tc", "ts")}
        Y = P.sb(st, "f_Y", [128, 128, 256], BF16)
        OB = P.sb(st, "f_OB", [128, NP_OWN], BF16)
        pbk = 0
        for sg in ("p", "s"):
            N1 = 128 if sg == "p" else 16
            NK2 = 32 if sg == "p" else 128
            n_own = NP_OWN if sg == "p" else NS_OWN
            nseq = SEQ_P if sg == "p" else SEQ_S
            ooff = 0 if sg == "p" else NP_OWN
            T = tabs[sg]
            for k, rows, cols in (("rp", N1, 2 * N1), ("rr", N1, 2 * N1), ("c2", 128, NK2), ("s2", 128, NK2)):
                S.dma("sp", stg[0:rows, 0, 0:cols], T[k][:, :], writes=[stg])
                S.op("dve", lambda h: h.tensor_copy(tb[k][0:rows, 0:cols], stg[0:rows, 0, 0:cols]), reads=[stg],
                     writes=[tb[k]])
            for k in ("tc", "ts"):
                S.dma("sp", tcs[k][:, 0:N1], T[k][:, :], writes=[tcs[k]])
            for gp in range(2):
                S.op("dve", lambda h: h.memset(wbd[:], 0.0), writes=[wbd])
                S.dma("sp", wbd[0:64, 0:64], fw[2 * gp, :, :], wacc=[wbd])
                S.dma("sp", wbd[64:128, 64:128], fw[2 * gp + 1, :, :], wacc=[wbd])
                for k in range(2):
                    pt = P.ps[pbk % 8]
                    pbk += 1
                    S.op("pe", lambda h: h.matmul(pt[:, 0:128], cs[:, k, :], wbd[:], start=True, stop=True),
                         reads=[cs, wbd], writes=[pt])
                    P.copy("dve", AB[:, k, :], pt[:, 0:128], reads=[pt], wacc=[AB], scale=(1.0 if k == 0 else -1.0))
                with ExitStack() as st2:
                    un = P.sb(st2, "f_un", [128, SEQ_P], BF16)
                    if sg == "p":
                        for r in range(4):
                            for hf in range(2):
                                S.dma("sp", un[:, r * 4096 + hf * 2048:r * 4096 + (hf + 1) * 2048],
                                      unall[hf][r * 256 + gp * 128:r * 256 + (gp + 1) * 128, :], reads=[UNALL],
                                      wacc=[un])
                    else:
                        S.dma("sp", un[:, 0:nseq], P.unT["s"][0][gp * 128:(gp + 1) * 128, :], wacc=[un])
                    for n2 in range(0, 128, 2):
                        pt = P.ps[pbk % 8]
                        pbk += 1
                        for d in range(2):
                            S.op("pe", lambda h: h.matmul(pt[0:N1, d * 256:(d + 1) * 256],
                                                          un[:, n2 + d:nseq:128], AB[:].rearrange("p a e -> p (a e)"),
                                                          start=True, stop=True), reads=[un, AB], writes=[pt])
                        P.copy(P.eng2(), Y[0:N1, n2:n2 + 2, :], pt[0:N1, :].rearrange("p (a c) -> p a c", a=2),
                               reads=[pt], wacc=[Y])
                    S.barrier()
                with ExitStack() as st3:
                    GP = P.sb(st3, "f_GP", [128, N1, 2, 128], BF16)
                    GS = [P.sb(st3, "f_GS%d" % i, [128, 512], F32) for i in range(2)]
                    T1 = [P.sb(st3, "f_T1%d" % i, [128, 256], F32) for i in range(2)]
                    T2 = [P.sb(st3, "f_T2%d" % i, [128, 256], F32) for i in range(2)]
                    T3 = [P.sb(st3, "f_T3%d" % i, [128, 256], F32) for i in range(2)]
                    T4 = [P.sb(st3, "f_T4%d" % i, [128, 256], F32) for i in range(2)]
                    EBn = 512 // (2 * N1)
                    nb = 0
                    for c0 in range(0, 128, EBn):
                        pt = P.ps[pbk % 8]
                        pbk += 1
                        for bi in range(EBn):
                            col = c0 + bi
                            sl = slice(bi * 2 * N1, (bi + 1) * 2 * N1)
                            S.op("pe", lambda h: h.matmul(pt[:, sl], Y[0:N1, :, col], tb["rp"][0:N1, 0:2 * N1],
                                                          start=True, stop=False), reads=[Y, tb["rp"]], writes=[pt])
                            S.op("pe", lambda h: h.matmul(pt[:, sl], Y[0:N1, :, 128 + col], tb["rr"][0:N1, 0:2 * N1],
                                                          start=False, stop=True), reads=[Y, tb["rr"]], writes=[pt])
                        a = nb % 2
                        nb += 1
                        gs, t1, t2, t3, t4 = GS[a], T1[a], T2[a], T3[a], T4[a]
                        S.op("act", lambda h: h.copy(gs[:], pt[:]), reads=[pt], writes=[gs])
                        g4 = gs[:].rearrange("p (b r k) -> p b r k", b=EBn, r=2)
                        gr, gi = g4[:, :, 0, :], g4[:, :, 1, :]
                        tcb = tcs["tc"][:, 0:N1].unsqueeze(1).to_broadcast([128, EBn, N1])
                        tsb = tcs["ts"][:, 0:N1].unsqueeze(1).to_broadcast([128, EBn, N1])
                        v = lambda t: t[:, 0:EBn * N1].rearrange("p (b k) -> p b k", b=EBn)
                        vt = lambda t: t[:, 0:EBn * N1].rearrange("p (b k) -> p k b", b=EBn)
                        S.op("dve", lambda h: h.tensor_tensor(v(t1), gr, tcb, op=ALU.mult),
                             reads=[gs, tcs["tc"]], writes=[t1])
                        S.op("dve", lambda h: h.tensor_tensor(v(t2), gi, tsb, op=ALU.mult),
                             reads=[gs, tcs["ts"]], writes=[t2])
                        S.op("pool", lambda h: h.tensor_tensor(v(t3), gi, tcb, op=ALU.mult),
                             reads=[gs, tcs["tc"]], writes=[t3])
                        S.op("pool", lambda h: h.tensor_tensor(v(t4), gr, tsb, op=ALU.mult),
                             reads=[gs, tcs["ts"]], writes=[t4])
                        S.op("dve", lambda h: h.tensor_tensor(GP[:, :, 0, c0:c0 + EBn], vt(t1), vt(t2), op=ALU.add),
                             reads=[t1, t2], wacc=[GP])
                        S.op("pool", lambda h: h.tensor_tensor(GP[:, :, 1, c0:c0 + EBn], vt(t3), vt(t4),
                                                               op=ALU.subtract), reads=[t3, t4], wacc=[GP])
                    KB = 512 // NK2
                    for k0 in range(0, N1, KB):
                        pt = P.ps[pbk % 8]
                        pbk += 1
                        for kk in range(KB):
                            k1 = k0 + kk
                            sl = slice(kk * NK2, (kk + 1) * NK2)
                            S.op("pe", lambda h: h.matmul(pt[:, sl], GP[:, k1, 0, :], tb["c2"][:, 0:NK2], start=True,
                                                          stop=False), reads=[GP, tb["c2"]], writes=[pt])
                            S.op("pe", lambda h: h.matmul(pt[:, sl], GP[:, k1, 1, :], tb["s2"][:, 0:NK2], start=False,
                                                          stop=True), reads=[GP, tb["s2"]], writes=[pt])
                        ov = OB[:, 0:n_own].rearrange("p (k2 k1) -> p k1 k2", k1=N1)[:, k0:k0 + KB, :]
                        P.copy(P.eng2(), ov, pt[:].rearrange("p (a b) -> p a b", a=KB), reads=[pt], wacc=[OB])
                    S.dma("pool", P.OT0[:, 6 + gp, ooff:ooff + n_own], OB[:, 0:n_own], reads=[OB])
                    S.barrier()
        S.barrier()


class RowBufs:
    def __init__(self, P, st):
        self.xr = [P.sb(st, "rb_xr%d" % i, [128, 8, TT], F32) for i in range(2)]
        self.r = P.sb(st, "rb_r", [128, 8, TT], F32)
        self.sq = P.sb(st, "rb_sq", [128, 8, TT], F32)
        self.rstd = P.sb(st, "rb_rstd", [128, TT], F32)
        self.of = [P.sb(st, "rb_of%d" % i, [128, 8, TT], F32) for i in range(1)]
        self.ob = [P.sb(st, "rb_ob%d" % i, [128, 8, TT], BF16) for i in range(1)]


def linear_resid_ln(P, RB, i, wb, kcin, src, xr, lnidx, outf_d, outb_d, pbase):
    S = P.S
    r, sq, rstd = RB.r, RB.sq, RB.rstd
    of, ob = RB.of[0], RB.ob[0]
    for oc in range(8):
        pt = P.ps[(pbase + oc) % 8]
        for c in range(kcin):
            S.op("pe", lambda h: h.matmul(pt[:], wb[:, c, oc * 128:(oc + 1) * 128], src[:, c, :], start=(c == 0),
                                          stop=(c == kcin - 1)), reads=[wb, src], writes=[pt])
        S.op("dve", lambda h: h.scalar_tensor_tensor(r[:, oc, :], xr[:, oc, :], DN_ALPHA, pt[:], op0=ALU.mult,
                                                     op1=ALU.add), reads=[xr, pt], wacc=[r])
    layer_norm_fm(P, r, sq, rstd, lnidx, of, ob, pbase)
    t0 = i * TT
    if outf_d is not None:
        S.dma("pool", outf_d[:, :, t0:t0 + TT], of[:], reads=[of])
    if outb_d is not None:
        S.dma("pool", outb_d[:, :, t0:t0 + TT], ob[:], reads=[ob])


def layer_norm_fm(P, r, sq, rstd, lnidx, of, ob, pbase):
    S = P.S
    pm = P.ps[(pbase + 0) % 8]
    pv = P.ps[(pbase + 1) % 8]
    for c in range(8):
        S.op("pe", lambda h: h.matmul(pm[:], P.c_onesd[:], r[:, c, :], start=(c == 0), stop=(c == 7)),
             reads=[P.c_onesd, r], writes=[pm])
    S.op("dve", lambda h: h.tensor_tensor(r[:], r[:], pm[:].unsqueeze(1).to_broadcast([128, 8, TT]),
                                          op=ALU.subtract), reads=[r, pm], wacc=[r])
    S.op("act", lambda h: h.activation(sq[:], r[:], AF.Square), reads=[r], writes=[sq])
    for c in range(8):
        S.op("pe", lambda h: h.matmul(pv[:], P.c_onesd[:], sq[:, c, :], start=(c == 0), stop=(c == 7)),
             reads=[P.c_onesd, sq], writes=[pv])
    P.rsqrt(rstd[:], pv[:], P.c_eps_ln, reads=[pv], wacc=[rstd])
    S.op("dve", lambda h: h.tensor_tensor(r[:], r[:], rstd[:].unsqueeze(1).to_broadcast([128, 8, TT]),
                                          op=ALU.mult), reads=[r, rstd], wacc=[r])
    for c in range(8):
        S.op("act", lambda h: h.activation(of[:, c, :], r[:, c, :], AF.Identity,
                                           bias=P.c_lnb[:, lnidx, c:c + 1], scale=P.c_lng[:, lnidx, c:c + 1]),
             reads=[r, P.c_lng, P.c_lnb], wacc=[of])
    S.op("pool", lambda h: h.tensor_copy(ob[:], of[:]), reads=[of], writes=[ob])


def phase_linear_ln(P, name, w_d, kcin, src_d, xres_fn, lnidx, outf_d, outb_d):
    S = P.S
    with ExitStack() as st:
        stg = [P.sb(st, "stg%d" % i, [128, 1024], F32) for i in range(3)]
        wb = P.load_w(st, "w_" + name, w_d, kcin * 128, D, stg)
        RB = RowBufs(P, st)
        srcs = [P.sb(st, "src%d" % i, [128, kcin, TT], BF16) for i in range(2)]
        for i in range(NTILE):
            sb_, xr = srcs[i % 2], RB.xr[i % 2]
            S.dma("sp", sb_[:], src_d[:, :, i * TT:(i + 1) * TT], writes=[sb_])
            S.dma("sp", xr[:], xres_fn(i), writes=[xr])
            linear_resid_ln(P, RB, i, wb, kcin, sb_, xr, lnidx, outf_d, outb_d, pbase=(i * 2) % 8)
        S.barrier()


def phase_xattn(P, layer, xf_d, xb_d, outf_d, outb_d):
    S = P.S
    wq_d = P.din("xa_w_q%d" % layer, [D, D])
    wkv_d = P.din("xa_w_kv%d" % layer, [D, 2 * D])
    wo_d = P.din("xa_w_o%d" % layer, [D, D])
    if not hasattr(P, "memT"):
        P.memT = {"p": P.din("memTp", [128, 8, 256]), "s": P.din("memTs", [128, 8, 256])}
    lnidx = layer * 3 + 1
    with ExitStack() as st:
        stg = [P.sb(st, "stg%d" % i, [128, 1024], F32) for i in range(3)]
        wq = P.load_w(st, "w_xq", wq_d, D, D, stg)
        wo = P.load_w(st, "w_xo", wo_d, D, D, stg)
        memK = {sg: P.sb(st, "memK" + sg, [128, 8, 256], BF16) for sg in ("p", "s")}
        memV = {sg: P.sb(st, "memV" + sg, [128, 2, D], BF16) for sg in ("p", "s")}
        with ExitStack() as st2:
            wkv = P.load_w(st2, "w_xkv", wkv_d, D, 2 * D, stg)
            mf = P.sb(st2, "memf", [128, 8, 256], F32)
            mb = P.sb(st2, "memb", [128, 8, 256], BF16)
            pb = 0
            for sg in ("p", "s"):
                S.dma("sp", mf[:], P.memT[sg][:, :, :], writes=[mf])
                S.op("dve", lambda h: h.tensor_copy(mb[:], mf[:]), reads=[mf], writes=[mb])
                for oc in range(8):
                    pt = P.ps[pb % 8]
                    pb += 1
                    for c in range(8):
                        S.op("pe", lambda h: h.matmul(pt[:, 0:256], wkv[:, c, oc * 128:(oc + 1) * 128], mb[:, c, :],
                                                      start=(c == 0), stop=(c == 7)), reads=[wkv, mb], writes=[pt])
                    P.copy(P.eng2(), memK[sg][:, oc, :], pt[:, 0:256], reads=[pt], wacc=[memK[sg]])
                for mc in range(2):
                    for n0 in range(2):
                        pt = P.ps[pb % 8]
                        pb += 1
                        for c in range(8):
                            S.op("pe", lambda h: h.matmul(pt[:], mb[:, c, mc * 128:(mc + 1) * 128],
                                                          wkv[:, c, D + n0 * 512:D + (n0 + 1) * 512],
                                                          start=(c == 0), stop=(c == 7)), reads=[wkv, mb], writes=[pt])
                        P.copy(P.eng2(), memV[sg][:, mc, n0 * 512:(n0 + 1) * 512], pt[:], reads=[pt],
                               wacc=[memV[sg]])
            S.barrier()
        RB = RowBufs(P, st)
        xbs = [P.sb(st, "xa_xb%d" % i, [128, 8, TT], BF16) for i in range(2)]
        qb = P.sb(st, "xa_q", [128, 8, TT], BF16)
        ob_ = P.sb(st, "xa_o", [128, 8, TT], BF16)
        ee = [P.sb(st, "xa_e%d" % i, [128, 2, TT], BF16) for i in range(2)]
        rden = [P.sb(st, "xa_rd%d" % i, [128, TT], F32) for i in range(2)]
        pb = 0
        for i in range(NTILE):
            sg = "p" if i < NP_OWN // TT else "s"
            xb, xr = xbs[i % 2], RB.xr[i % 2]
            S.dma("sp", xb[:], xb_d[:, :, i * TT:(i + 1) * TT], writes=[xb])
            S.dma("sp", xr[:], xf_d[:, :, i * TT:(i + 1) * TT], writes=[xr])
            for oc in range(8):
                pt = P.ps[pb % 8]
                pb += 1
                for c in range(8):
                    S.op("pe", lambda h: h.matmul(pt[:], wq[:, c, oc * 128:(oc + 1) * 128], xb[:, c, :],
                                                  start=(c == 0), stop=(c == 7)), reads=[wq, xb], writes=[pt])
                P.copy(P.eng2(), qb[:, oc, :], pt[:], reads=[pt], wacc=[qb], scale=1.0 / 16)
            for hh in range(4):
                E, RD = ee[hh % 2], rden[hh % 2]
                for mc in range(2):
                    pt = P.ps[pb % 8]
                    pb += 1
                    for cc in range(2):
                        S.op("pe", lambda h: h.matmul(pt[:], memK[sg][:, 2 * hh + cc, mc * 128:(mc + 1) * 128],
                                                      qb[:, 2 * hh + cc, :], start=(cc == 0), stop=(cc == 1)),
                             reads=[memK[sg], qb], writes=[pt])
                    S.op("act", lambda h: h.activation(E[:, mc, :], pt[:], AF.Exp), reads=[pt], wacc=[E])
                pd = P.ps[pb % 8]
                pb += 1
                for mc in range(2):
                    S.op("pe", lambda h: h.matmul(pd[:], P.c_onesb[:], E[:, mc, :], start=(mc == 0), stop=(mc == 1)),
                         reads=[P.c_onesb, E], writes=[pd])
                S.op("dve", lambda h: h.reciprocal(RD[:], pd[:]), reads=[pd], writes=[RD])
                for dvc in range(2):
                    po = P.ps[pb % 8]
                    pb += 1
                    for mc in range(2):
                        S.op("pe", lambda h: h.matmul(po[:], memV[sg][:, mc, hh * 256 + dvc * 128:hh * 256 + (dvc + 1) * 128],
                                                      E[:, mc, :], start=(mc == 0), stop=(mc == 1)),
                             reads=[memV[sg], E], writes=[po])
                    S.op("dve", lambda h: h.tensor_tensor(ob_[:, 2 * hh + dvc, :], po[:], RD[:], op=ALU.mult),
                         reads=[po, RD], wacc=[ob_])
            linear_resid_ln(P, RB, i, wo, 8, ob_, xr, lnidx, outf_d, outb_d, pbase=pb % 8)
            pb += 2
        S.barrier()


def phase_ffn1(P, layer, xb_d, h_d):
    S = P.S
    w_d = P.din("ffn_w_in%d" % layer, [D, 2 * FFN_H])
    with ExitStack() as st:
        stg = [P.sb(st, "stg%d" % i, [128, 1024], F32) for i in range(3)]
        wb = P.load_w(st, "w_ffn_in", w_d, D, 2 * FFN_H, stg)
        xbs = [P.sb(st, "f1_xb%d" % i, [128, 8, TT], BF16) for i in range(2)]
        hid = [P.sb(st, "f1_h%d" % i, [128, HC, TT], BF16) for i in range(2)]
        sgb = [P.sb(st, "f1_sg%d" % i, [128, TT], F32) for i in range(3)]
        pb = 0
        for i in range(NTILE):
            xb, hd = xbs[i % 2], hid[i % 2]
            S.dma("sp", xb[:], xb_d[:, :, i * TT:(i + 1) * TT], writes=[xb])
            for hc in range(HC):
                pg = P.ps[pb % 8]
                pu = P.ps[(pb + 1) % 8]
                pb += 2
                for c in range(8):
                    S.op("pe", lambda h: h.matmul(pg[:], wb[:, c, hc * 128:(hc + 1) * 128], xb[:, c, :],
                                                  start=(c == 0), stop=(c == 7)), reads=[wb, xb], writes=[pg])
                for c in range(8):
                    S.op("pe", lambda h: h.matmul(pu[:], wb[:, c, FFN_H + hc * 128:FFN_H + (hc + 1) * 128], xb[:, c, :],
                                                  start=(c == 0), stop=(c == 7)), reads=[wb, xb], writes=[pu])
                sgt = sgb[hc % 3]
                S.op("act", lambda h: h.activation(sgt[:], pg[:], AF.Silu), reads=[pg], writes=[sgt])
                S.op("dve", lambda h: h.tensor_tensor(hd[:, hc, :], sgt[:], pu[:], op=ALU.mult), reads=[sgt, pu],
                     wacc=[hd])
            S.dma("pool", h_d[:, :, i * TT:(i + 1) * TT], hd[:], reads=[hd])
        S.barrier()


def layer_tail(P, layer, mix_d, w_out_d, xres_fn):
    L = "L%d" % layer
    x1f = P.dscratch(L + "x1f", [128, 8, NOWN], F32)
    x1b = P.dscratch(L + "x1b", [128, 8, NOWN], BF16)
    phase_linear_ln(P, L + "mixout", w_out_d, 8, mix_d, xres_fn, layer * 3 + 0, x1f, x1b)
    x2f = P.dscratch(L + "x2f", [128, 8, NOWN], F32)
    x2b = P.dscratch(L + "x2b", [128, 8, NOWN], BF16)
    phase_xattn(P, layer, x1f, x1b, x2f, x2b)
    hd = P.dscratch(L + "hid", [128, HC, NOWN], BF16)
    phase_ffn1(P, layer, x2b, hd)
    if layer == 1:
        x3f = P.dout("yT", [128, 8, NOWN], F32)
        x3b = None
    else:
        x3f = P.dscratch(L + "x3f", [128, 8, NOWN], F32)
        x3b = P.dscratch(L + "x3b", [128, 8, NOWN], BF16)
    w2 = P.din("ffn_w_out%d" % layer, [FFN_H, D])
    phase_linear_ln(P, L + "ffnout", w2, HC, hd, lambda i: x2f[:, :, i * TT:(i + 1) * TT], layer * 3 + 2, x3f, x3b)
    return x3f, x3b


QPERM = [0, 3, 1, 4, 2, 5, 6, 9, 7, 10, 8, 11]


def phase_proj1(P, xb_d):
    nc, S = P.nc, P.S
    w_d = P.din("cd_w_in_p", [D, 1536])
    gains_d = P.din("qk_gain_r", [128, 16, 64])
    cos_d = P.din("rope_cos", [128, NOWN // 128, 32])
    sin_d = P.din("rope_sin", [128, NOWN // 128, 32])
    P.QT1 = P.dscratch("QT1", [128, 6, NOWN], BF16)
    P.KT1 = {"p": [P.dscratch("KT1p%d" % i, [256, 2048], BF16) for i in range(2)],
             "s": [P.dscratch("KT1s", [256, NS_OWN], BF16)]}
    P.V1 = {"p": [P.dscratch("V1p%d" % i, [1024, 260], BF16) for i in range(4)],
            "s": [P.dscratch("V1s%d" % i, [1024, 260], BF16) for i in range(2)]}
    P.U1 = P.dscratch("U1", [128, 2, NOWN], F32)
    with ExitStack() as st:
        stg = [P.sb(st, "stg%d" % i, [128, 1024], F32) for i in range(3)]
        wb = P.load_w(st, "w_cd_in", w_d, D, 1536, stg)
        gains = P.sb(st, "p1_gain", [128, 16, 64], F32)
        S.dma("sp", gains[:], gains_d[:, :, :], writes=[gains])
        S.op("dve", lambda h: h.tensor_scalar_mul(gains[:, 0:12, :], gains[:, 0:12, :], 0.125), reads=[gains],
             wacc=[gains])
        cs = P.sb(st, "p1_cos", [128, NOWN // 128, 32], F32)
        sn = P.sb(st, "p1_sin", [128, NOWN // 128, 32], F32)
        S.dma("sp", cs[:], cos_d[:, :, :], writes=[cs])
        S.dma("sp", sn[:], sin_d[:, :, :], writes=[sn])
        xbs = [P.sb(st, "p1_xb%d" % i, [128, 8, TT], BF16) for i in range(2)]
        sq = P.sb(st, "p1_sq", [128, 1024], F32)
        ss = P.sb(st, "p1_ss", [128, 16], F32)
        rs = P.sb(st, "p1_rs", [128, 16], F32)
        qn = P.sb(st, "p1_qn", [128, 16, 64], F32)
        tt = [P.sb(st, "p1_t%d" % i, [128, 16, 32], F32) for i in range(4)]
        qr = P.sb(st, "p1_qr", [128, 4, 16, 64], BF16)
        qts = [P.sb(st, "p1_qt%d" % i, [128, 8, TT], BF16) for i in range(2)]
        vs = [P.sb(st, "p1_vs%d" % i, [128, 4, 4, 65], BF16) for i in range(2)]
        us = [P.sb(st, "p1_us%d" % i, [128, 2, TT], F32) for i in range(2)]
        for v_ in vs:
            S.op("dve", lambda h: h.memset(v_[:], 1.0), writes=[v_])
        pb = 0
        for i in range(NTILE):
            sg = "p" if i < 8 else "s"
            il = i if sg == "p" else i - 8
            xb, QT, VS, US = xbs[i % 2], qts[i % 2], vs[i % 2], us[i % 2]
            S.dma("sp", xb[:], xb_d[:, :, i * TT:(i + 1) * TT], writes=[xb])
            for sub in range(4):
                pa, pbk = P.ps[pb % 8], P.ps[(pb + 1) % 8]
                pv = P.ps[(pb + 2) % 8]
                pb += 3
                for c in range(8):
                    S.op("pe", lambda h: h.matmul(pa[:], xb[:, c, sub * 128:(sub + 1) * 128], wb[:, c, 0:512],
                                                  start=(c == 0), stop=(c == 7)), reads=[wb, xb], writes=[pa])
                for c in range(8):
                    S.op("pe", lambda h: h.matmul(pbk[:], xb[:, c, sub * 128:(sub + 1) * 128], wb[:, c, 512:1024],
                                                  start=(c == 0), stop=(c == 7)), reads=[wb, xb], writes=[pbk])
                for c in range(8):
                    S.op("pe", lambda h: h.matmul(pv[:, 0:256], xb[:, c, sub * 128:(sub + 1) * 128],
                                                  wb[:, c, 1024:1280], start=(c == 0), stop=(c == 7)),
                         reads=[wb, xb], writes=[pv])
                S.op("act", lambda h: h.copy(VS[:, sub, :, 0:64], pv[:, 0:256].rearrange("p (h d) -> p h d", d=64)),
                     reads=[pv], wacc=[VS])
                S.op("act", lambda h: h.activation(sq[:, 0:512], pa[:], AF.Square), reads=[pa], wacc=[sq])
                S.op("act", lambda h: h.activation(sq[:, 512:1024], pbk[:], AF.Square), reads=[pbk], wacc=[sq])
                S.op("dve", lambda h: h.tensor_reduce(ss[:], sq[:].rearrange("p (h d) -> p h d", d=64), axis=AX.X,
                                                      op=ALU.add), reads=[sq], writes=[ss])
                S.op("dve", lambda h: h.tensor_scalar(ss[:], ss[:], 1.0 / 64, None, op0=ALU.mult), reads=[ss],
                     wacc=[ss])
                P.rsqrt(rs[:], ss[:], P.c_eps_rms, reads=[ss], wacc=[rs])
                rsb = rs[:].unsqueeze(2).to_broadcast([128, 16, 64])
                S.op("dve", lambda h: h.tensor_tensor(qn[:, 0:8, :], pa[:].rearrange("p (h d) -> p h d", d=64),
                                                      rsb[:, 0:8, :], op=ALU.mult), reads=[pa, rs], wacc=[qn])
                S.op("dve", lambda h: h.tensor_tensor(qn[:, 8:16, :], pbk[:].rearrange("p (h d) -> p h d", d=64),
                                                      rsb[:, 8:16, :], op=ALU.mult), reads=[pbk, rs], wacc=[qn])
                S.op("pool", lambda h: h.tensor_tensor(qn[:], qn[:], gains[:], op=ALU.mult), reads=[qn, gains],
                     wacc=[qn])
                q4 = qn[:].rearrange("p h (i two) -> p h i two", two=2)
                x1, x2 = q4[:, :, :, 0], q4[:, :, :, 1]
                gsub = i * 4 + sub
                cb = cs[:, gsub, :].unsqueeze(1).to_broadcast([128, 16, 32])
                sb_ = sn[:, gsub, :].unsqueeze(1).to_broadcast([128, 16, 32])
                o4 = qr[:, sub, :, :].rearrange("p h (i two) -> p h i two", two=2)
                S.op("dve", lambda h: h.tensor_tensor(tt[0][:], x1, cb, op=ALU.mult), reads=[qn, cs], writes=[tt[0]])
                S.op("pool", lambda h: h.tensor_tensor(tt[1][:], x2, sb_, op=ALU.mult), reads=[qn, sn],
                     writes=[tt[1]])
                S.op("pool", lambda h: h.tensor_tensor(tt[2][:], x1, sb_, op=ALU.mult), reads=[qn, sn],
                     writes=[tt[2]])
                S.op("dve", lambda h: h.tensor_tensor(tt[3][:], x2, cb, op=ALU.mult), reads=[qn, cs], writes=[tt[3]])
                S.op("dve", lambda h: h.tensor_tensor(o4[:, :, :, 0], tt[0][:], tt[1][:], op=ALU.subtract),
                     reads=[tt[0], tt[1]], wacc=[qr])
                S.op("pool", lambda h: h.tensor_tensor(o4[:, :, :, 1], tt[2][:], tt[3][:], op=ALU.add),
                     reads=[tt[2], tt[3]], wacc=[qr])
            for fc in range(8):
                pt = P.ps[pb % 8]
                pb += 1
                ptb = pt[:].bitcast(BF16)
                for sub in range(4):
                    S.op("pe", lambda h: h.transpose(ptb[:, sub * 128:(sub + 1) * 128],
                                                     qr[:, sub, 2 * fc:2 * fc + 2, :].rearrange("p h d -> p (h d)"),
                                                     P.c_identb[:]), reads=[qr, P.c_identb], writes=[pt])
                P.copy(P.eng2(), QT[:, fc, :], ptb[:, 0:512], reads=[pt], wacc=[QT])
            t0 = i * TT
            S.dma("pool", P.QT1[:, :, t0:t0 + TT], QT[:, 0:6, :], reads=[QT])
            tl = il * TT
            S.dma("pool", P.KT1[sg][tl // 2048][:, tl % 2048:tl % 2048 + TT].rearrange("(c p) t -> p c t", p=128),
                  QT[:, 6:8, :], reads=[QT])
            S.dma("pool", P.V1[sg][tl // 1024][tl % 1024:tl % 1024 + TT, :].rearrange("(s p) f -> p s f", p=128),
                  VS[:].rearrange("p s h d -> p s (h d)"), reads=[VS])
            for c2 in range(2):
                pt = P.ps[pb % 8]
                pb += 1
                for c in range(8):
                    S.op("pe", lambda h: h.matmul(pt[:], wb[:, c, 1280 + c2 * 128:1280 + (c2 + 1) * 128], xb[:, c, :],
                                                  start=(c == 0), stop=(c == 7)), reads=[wb, xb], writes=[pt])
                P.copy(P.eng2(), US[:, c2, :], pt[:], reads=[pt], wacc=[US])
            S.dma("pool", P.U1[:, :, t0:t0 + TT], US[:], reads=[US])
        S.barrier()


def phase_gqa(P):
    nc, S = P.nc, P.S
    P.OT1 = P.dscratch("OT1", [128, 8, NOWN], BF16)
    ktall = [P.dscratch("KT1all%d" % i, [1024, 2048], BF16) for i in range(2)]
    vall = [P.dscratch("V1all%d" % i, [4096, 260], BF16) for i in range(4)]
    G = Buf(None)
    for i in range(2):
        S.allgather(ktall[i], P.KT1["p"][i], reads=[], writes=[], groups=GROUPS4)
    for i in range(4):
        S.allgather(vall[i], P.V1["p"][i], reads=[], writes=[], groups=GROUPS4)
    S.cc_fence(GROUPS4)
    G.ws = [("cc", S.cc_cnt, "dma")]
    with ExitStack() as st:
        ones32 = P.sb(st, "g_ones32", [128, 64], F32)
        S.op("dve", lambda h: h.memset(ones32[:], 1.0), writes=[ones32])
        KT = P.sb(st, "gKT", [128, SEQ_P], BF16)
        VA = P.sb(st, "gVA", [128, SEQ_P // 128, 2, 65], BF16)
        QTs = [P.sb(st, "gQT%d" % i, [128, NP_OWN], BF16) for i in range(2)]
        PA = PairAttn(P, st, with_z=False)
        qi_ = 0
        for sg in ("p", "s"):
            nseq = SEQ_P if sg == "p" else SEQ_S
            n_own = NP_OWN if sg == "p" else NS_OWN
            ooff = 0 if sg == "p" else NP_OWN
            nch = nseq // 128
            nqt = n_own // TT
            for kc in range(2):
                if sg == "p":
                    for r in range(4):
                        for hf in range(2):
                            S.dma("sp", KT[:, r * 4096 + hf * 2048:r * 4096 + (hf + 1) * 2048],
                                  ktall[hf][r * 256 + kc * 128:r * 256 + (kc + 1) * 128, :], reads=[G], wacc=[KT])
                        for j in range(4):
                            c0 = (r * 4096 + j * 1024) // 128
                            S.dma("sp", VA[:, c0:c0 + 8, :, :].rearrange("p c h d -> p c (h d)"),
                                  vall[j][r * 1024:(r + 1) * 1024, kc * 130:(kc + 1) * 130].rearrange(
                                      "(c p) f -> p c f", p=128), reads=[G], wacc=[VA])
                else:
                    S.dma("sp", KT[:, 0:nseq], P.KT1["s"][0][kc * 128:(kc + 1) * 128, :], wacc=[KT])
                    for j in range(2):
                        S.dma("sp", VA[:, j * 8:(j + 1) * 8, :, :].rearrange("p c h d -> p c (h d)"),
                              P.V1["s"][j][:, kc * 130:(kc + 1) * 130].rearrange("(c p) f -> p c f", p=128),
                              wacc=[VA])
                for j in range(3):
                    qc = 3 * kc + j
                    QT = QTs[qi_ % 2]
                    qi_ += 1
                    S.dma("sp", QT[:, 0:n_own], P.QT1[:, qc, ooff:ooff + n_own], writes=[QT])
                    items = []
                    for qi in range(nqt):
                        for ch in range(nch):
                            items.append((qi, ch, 0, ch == 0, ch == nch - 1))

                    def out_fn(qi, out, qc=qc, ooff=ooff):
                        o0 = ooff + qi * TT
                        S.dma("pool", P.OT1[:, qc, o0:o0 + TT], out[:], reads=[out])

                    PA.run(items, KT, QT, VA, None, out_fn)
        S.barrier()


def phase_pool(P):
    nc, S = P.nc, P.S
    pw = P.din("cd_pool_w", [4, 64, 64])
    psc = P.din("pool_scale_r", [128, 2])
    rc_d = P.din("pool_rc", [128, 2, NOWN])
    sel_d = P.din("pool_sel", [128, 8])
    edge = P.dscratch("pool_edge", [256, 16], F32)
    edall = P.dscratch("pool_edall", [1024, 16], F32)
    EDG, EDA = Buf(edge), Buf(edall)
    with ExitStack() as st:
        psc_s = P.sb(st, "pl_sc", [128, 2], F32)
        sel = P.sb(st, "pl_sel", [128, 8], F32)
        S.dma("sp", psc_s[:], psc[:, :], writes=[psc_s])
        S.dma("sp", sel[:], sel_d[:, :], writes=[sel])
        eg = P.sb(st, "pl_eg", [128, 2, 16], F32)
        S.dma("sp", eg[:, :, 0:8], P.U1[:, :, 0:8], wacc=[eg])
        S.dma("sp", eg[:, :, 8:16], P.U1[:, :, NP_OWN - 8:NP_OWN], wacc=[eg])
        S.dma("pool", edge.ap().rearrange("(c p) t -> p c t", p=128), eg[:], reads=[eg], writes=[EDG])
        S.allgather(edall, edge, reads=[EDG], writes=[EDA], groups=GROUPS4)
        S.cc_fence(GROUPS4)
        EDA.ws = [("cc", S.cc_cnt, "dma")]
        ea = P.sb(st, "pl_ea", [128, 4, 2, 16], F32)
        for r in range(4):
            S.dma("sp", ea[:, r, :, :], edall[r * 256:(r + 1) * 256, :].rearrange("(c p) t -> p c t", p=128),
                  reads=[EDA], wacc=[ea])
        wbd = P.sb(st, "pl_wbd", [128, 128], F32)
        wbb = P.sb(st, "pl_wbb", [128, 128], BF16)
        NE = NP_OWN + 16
        ue = P.sb(st, "pl_ue", [128, NE], F32)
        sA = P.sb(st, "pl_sA", [128, NE], F32)
        sB = P.sb(st, "pl_sB", [128, NE], F32)
        rc = P.sb(st, "pl_rc", [128, NP_OWN], F32)
        mx = P.sb(st, "pl_mx", [128, NP_OWN], BF16)
        ot = [P.sb(st, "pl_ot%d" % i, [128, TT], BF16) for i in range(2)]
        pb = 0
        for sg in ("p", "s"):
            n = NP_OWN if sg == "p" else NS_OWN
            ooff = 0 if sg == "p" else NP_OWN
            for c2 in range(2):
                S.op("dve", lambda h: h.memset(wbd[:], 0.0), writes=[wbd])
                S.dma("sp", wbd[0:64, 0:64], pw[2 * c2, :, :], wacc=[wbd])
                S.dma("sp", wbd[64:128, 64:128], pw[2 * c2 + 1, :, :], wacc=[wbd])
                S.op("dve", lambda h: h.tensor_copy(wbb[:], wbd[:]), reads=[wbd], writes=[wbb])
                S.op("dve", lambda h: h.memset(ue[:, 0:8], 0.0), wacc=[ue])
                S.op("dve", lambda h: h.memset(ue[:, 8 + n:16 + n], 0.0), wacc=[ue])
                S.dma("sp", ue[:, 8:8 + n], P.U1[:, c2, ooff:ooff + n], wacc=[ue])
                S.dma("sp", rc[:, 0:n], rc_d[:, c2, ooff:ooff + n], writes=[rc])
                if sg == "p":
                    for r in range(4):
                        S.op("dve", lambda h: h.scalar_tensor_tensor(ue[:, 0:8], ea[:, r, c2, 8:16], sel[:, r:r + 1],
                                                                     ue[:, 0:8], op0=ALU.mult, op1=ALU.add),
                             reads=[ea, sel, ue], wacc=[ue])
                        S.op("dve", lambda h: h.scalar_tensor_tensor(ue[:, 8 + n:16 + n], ea[:, r, c2, 0:8],
                                                                     sel[:, 4 + r:5 + r], ue[:, 8 + n:16 + n],
                                                                     op0=ALU.mult, op1=ALU.add),
                             reads=[ea, sel, ue], wacc=[ue])
                S.op("dve", lambda h: h.tensor_tensor(sA[:, 1:16 + n], ue[:, 0:15 + n], ue[:, 1:16 + n], op=ALU.add),
                     reads=[ue], writes=[sA])
                S.op("dve", lambda h: h.tensor_tensor(sB[:, 2:15 + n], sA[:, 1:14 + n], sA[:, 3:16 + n], op=ALU.add),
                     reads=[sA], writes=[sB])
                if c2 == 1:
                    S.op("dve", lambda h: h.tensor_tensor(sA[:, 4:13 + n], sB[:, 2:11 + n], sB[:, 6:15 + n],
                                                          op=ALU.add), reads=[sB], writes=[sA])
                    S.op("dve", lambda h: h.tensor_tensor(sB[:, 8:8 + n], sA[:, 4:4 + n], sA[:, 12:12 + n],
                                                          op=ALU.add), reads=[sA], writes=[sB])
                for half, sbuf_ in ((0, sA), (1, sB)):
                    ps_ = slice(half * 64, (half + 1) * 64)
                    S.op("dve", lambda h: h.tensor_tensor(sbuf_[ps_, 8:8 + n], sbuf_[ps_, 8:8 + n], rc[ps_, 0:n],
                                                          op=ALU.mult), reads=[sbuf_, rc], wacc=[sbuf_])
                    S.op("dve", lambda h: h.tensor_tensor(mx[ps_, 0:n], sbuf_[ps_, 8:8 + n], ue[ps_, 8:8 + n],
                                                          op=ALU.subtract), reads=[sbuf_, ue], wacc=[mx])
                for ti in range(n // TT):
                    pt = P.ps[pb % 8]
                    pb += 1
                    o = ot[ti % 2]
                    S.op("pe", lambda h: h.matmul(pt[:], wbb[:], mx[:, ti * TT:(ti + 1) * TT], start=True, stop=True),
                         reads=[wbb, mx], writes=[pt])
                    S.op("act", lambda h: h.activation(o[:], pt[:], AF.Identity, scale=psc_s[:, c2:c2 + 1]),
                         reads=[pt, psc_s], writes=[o])
                    o0 = ooff + ti * TT
                    S.dma("pool", P.OT1[:, 6 + c2, o0:o0 + TT], o[:], reads=[o])
        S.barrier()


def host_inputs(inp, names):
    cc = _consts_common()
    f32 = lambda a: np.ascontiguousarray(a, dtype=np.float32)
    maps = []
    for c in range(NCORES):
        b, q = c // 4, c % 4
        o0 = q * NP_OWN
        m = {}
        for nm in names:
            if nm.startswith("k_"):
                m[nm] = cc[nm[2:]]
            elif nm == "xTp":
                ext = np.zeros((NP_EXT, D), np.float32)
                lo, hi = o0 - 1024, o0 + NP_OWN + 1024
                a, bb = max(lo, 0), min(hi, SEQ_P)
                ext[a - lo:bb - lo] = inp["x_prompt"][b, a:bb]
                m[nm] = _fm(ext.T)
            elif nm == "vldp":
                t = np.arange(o0 - 1024, o0 + NP_OWN + 1024)
                v = ((t >= 0) & (t < SEQ_P)).astype(np.float32)
                m[nm] = f32(v.reshape(NP_EXT // 128, 128).T)
            elif nm == "xTs":
                m[nm] = _fm(f32(inp["x_sample"][c]).T)
            elif nm == "vlds":
                m[nm] = np.ones((128, SEQ_S // 128), np.float32)
            elif nm == "ab_fnet_w":
                m[nm] = f32(inp["ab_fnet_w"][0])
            elif nm.startswith("f_"):
                kind, sg = nm[2:4], nm[5]
                if sg == "p":
                    tabs = _dft_tables(SEQ_P, 128, 128, list(range(32 * q, 32 * q + 32)))
                else:
                    tabs = _dft_tables(SEQ_S, 16, 128, list(range(128)))
                m[nm] = tabs[["rp", "rr", "tc", "ts", "c2", "s2"].index(kind)]
            elif nm[:-1] in ("xa_w_q", "xa_w_kv", "xa_w_o", "ffn_w_in", "ffn_w_out"):
                m[nm] = f32(inp[nm[:-1]][int(nm[-1])])
            elif nm == "ab_w_out":
                m[nm] = f32(inp["ab_w_out"][0])
            elif nm == "memTp":
                m[nm] = _fm(f32(inp["mem_prompt"][b]).T)
            elif nm == "memTs":
                m[nm] = _fm(f32(inp["mem_sample"][c]).T)
            elif nm == "cd_w_in_p":
                w = np.asarray(inp["cd_w_in"][0], np.float32)
                wq = w[:, :768].reshape(D, 12, 64)[:, QPERM, :].reshape(D, 768)
                m[nm] = f32(np.concatenate([wq, w[:, 768:]], 1))
            elif nm == "cd_w_out_p":
                w = np.asarray(inp["cd_w_out"][0], np.float32)
                wq = w[:768].reshape(12, 64, D)[QPERM].reshape(768, D)
                m[nm] = f32(np.concatenate([wq, w[768:]], 0))
            elif nm == "qk_gain_r":
                g = np.concatenate([np.tile(np.asarray(inp["cd_q_norm"][0], np.float32)[None], (12, 1)),
                                    np.tile(np.asarray(inp["cd_k_norm"][0], np.float32)[None], (4, 1))], 0)
                m[nm] = f32(np.broadcast_to(g[None], (128, 16, 64)))
            elif nm in ("rope_cos", "rope_sin"):
                pos = np.concatenate([o0 + np.arange(NP_OWN), np.arange(NS_OWN)])
                freqs = (np.float32(10000.0) ** (-np.arange(0, 32, 2, dtype=np.float32) / np.float32(32))).astype(np.float32)
                row = (pos // 64).astype(np.float32)
                col = (pos % 64).astype(np.float32)
                ang = np.concatenate([row[:, None] * freqs, col[:, None] * freqs], -1).astype(np.float32)
                t = np.cos(ang) if nm == "rope_cos" else np.sin(ang)
                m[nm] = f32(t.reshape(NOWN // 128, 128, 32).transpose(1, 0, 2))
            elif nm == "cd_pool_w":
                m[nm] = f32(inp["cd_pool_w"][0])
            elif nm == "pool_scale_r":
                m[nm] = f32(np.asarray(inp["cd_pool_scale"][0]).reshape(2, 128).T)
            elif nm == "pool_rc":
                pos = np.concatenate([o0 + np.arange(NP_OWN), np.arange(NS_OWN)])
                nn = np.concatenate([np.full(NP_OWN, SEQ_P), np.full(NS_OWN, SEQ_S)])
                rc = np.zeros((128, 2, NOWN), np.float32)
                for g_ in range(4):
                    w_ = (2, 4, 8, 16)[g_]
                    cnt = np.clip(pos + w_ // 2, 0, nn) - np.clip(pos - w_ // 2, 0, nn)
                    rc[(g_ % 2) * 64:(g_ % 2 + 1) * 64, g_ // 2, :] = (1.0 / cnt.astype(np.float32))[None]
                m[nm] = rc
            elif nm == "pool_sel":
                sel = np.zeros((128, 8), np.float32)
                if q > 0:
                    sel[:, q - 1] = 1.0
                if q < 3:
                    sel[:, 4 + q + 1] = 1.0
                m[nm] = sel
            elif nm == "rel_bias":
                m[nm] = f32(inp["rel_bias"])
            elif nm == "ab_w_in":
                m[nm] = f32(inp["ab_w_in"][0])
            elif nm == "fnet_g_r":
                m[nm] = f32(np.asarray(inp["ab_fnet_g"][0]).reshape(2, 128).T)
            elif nm in ("ln_g_r", "ln_b_r"):
                src = np.asarray(inp["ln_g" if nm == "ln_g_r" else "ln_b"], np.float32)
                m[nm] = f32(src.reshape(6, 8, 128).transpose(2, 0, 1))
            else:
                raise KeyError(nm)
        maps.append(m)
    return maps


def run_prog(P, inp):
    nc = P.finish()
    maps = host_inputs(inp, list(P.inputs.keys()))
    res = run_bass_kernel_spmd(nc, maps, core_ids=list(range(NCORES)))
    return res.results


def build_full(debug=()):
    P = Prog(debug=debug)
    P.setup()
    phase_proj0(P)
    phase_dil(P)
    phase_fnet(P)
    w_out0 = P.din("ab_w_out", [D, D])

    def xres0(i):
        if i < 8:
            return P.xT["p"][:, :, 1024 + i * TT:1024 + (i + 1) * TT]
        return P.xT["s"][:, :, (i - 8) * TT:(i - 7) * TT]

    x3f, x3b = layer_tail(P, 0, P.OT0, w_out0, xres0)
    phase_proj1(P, x3b)
    phase_gqa(P)
    phase_pool(P)
    w_out1 = P.din("cd_w_out_p", [D, D])
    layer_tail(P, 1, P.OT1, w_out1, lambda i: x3f[:, :, i * TT:(i + 1) * TT])
    return P


def kernel(**inputs):
    inp = {k: np.asarray(v) for k, v in inputs.items()}
    P = build_full()
    res = run_prog(P, inp)
    y_prompt = np.zeros((2, SEQ_P, D), np.float32)
    y_sample = np.zeros((8, SEQ_S, D), np.float32)
    for c in range(NCORES):
        b, q = c // 4, c % 4
        yT = np.asarray(res[c]["yT"], dtype=np.float32)
        y_prompt[b, q * NP_OWN:(q + 1) * NP_OWN] = _unfm(yT[:, :, :NP_OWN])
        y_sample[c] = _unfm(yT[:, :, NP_OWN:])
    return (y_prompt, y_sample)
```

```python
import math
import numpy as np
import ml_dtypes
import concourse.bass as bass
import concourse.mybir as mybir
from concourse.bass_utils import run_bass_kernel_spmd

F32 = mybir.dt.float32
BF16 = mybir.dt.bfloat16
AF = mybir.ActivationFunctionType
ALU = mybir.AluOpType
AX = mybir.AxisListType

NCORES = 8
D = 1024
KC = 8
TT = 512
NP_OWN = 4096
NS_OWN = 2048
NOWN = NP_OWN + NS_OWN
NTILE = NOWN // TT
NP_EXT = NP_OWN + 2048
SEQ_P = 16384
SEQ_S = 2048
FFN_H = 2816
HC = FFN_H // 128
DN_ALPHA = 4 ** 0.25
LN_EPS = 1e-5
RMS_EPS = 1e-6
ZW = 2944
ZC = 1408
FILLER = 1


class Buf:
    __slots__ = ("t", "ws", "r", "name", "wx")

    def __init__(self, t, name=""):
        self.t = t
        self.wx = None
        self.ws = []
        self.r = []
        self.name = name

    def __getitem__(self, idx):
        return self.t[idx]


def _compact(evs):
    best = {}
    for (k, v, s) in evs:
        if k not in best or best[k][1] < v:
            best[k] = (k, v, s)
    return list(best.values())


class Sch:
    def __init__(self, nc, ndma_sems=10):
        self.nc = nc
        self.eng = {"pe": nc.tensor, "act": nc.scalar, "dve": nc.vector,
                    "pool": nc.gpsimd, "sp": nc.sync}
        self.tick = {}
        self.seen = {}
        self.semh = {}
        self._ctx = []
        for e in self.eng:
            self._mksem("s_" + e)
            self.tick[e] = 0
            self.seen[e] = {}
        self.dq = {}
        for q in ("sp", "pool", "act"):
            names = []
            for i in range(ndma_sems):
                k = "d_%s_%d" % (q, i)
                self._mksem(k)
                names.append(k)
            self.dq[q] = {"sems": names, "n": 0, "cnt": {k: 0 for k in names}}
        self._mksem("cc")
        self.cc_cnt = 0
        self.ninst = 0

    def _mksem(self, key):
        cm = self.nc.semaphore(key)
        h = cm.__enter__()
        self._ctx.append(cm)
        self.semh[key] = h
        return h

    def close(self):
        for cm in reversed(self._ctx):
            cm.__exit__(None, None, None)
        self._ctx = []

    def _wait(self, e, ev):
        semkey, val, src = ev
        if src == e and e == "pe":
            return
        if self.seen[e].get(semkey, 0) >= val:
            return
        self.eng[e].wait_ge(self.semh[semkey], val)
        self.seen[e][semkey] = val
        self.ninst += 1

    def _deps(self, e, reads, writes, wacc):
        for b in reads:
            for ev in b.ws:
                self._wait(e, ev)
        for b in writes:
            for ev in b.ws:
                self._wait(e, ev)
            for ev in b.r:
                self._wait(e, ev)
        for b in wacc:
            if b.wx is not None:
                self._wait(e, b.wx)
            for ev in b.r:
                self._wait(e, ev)

    def _commit(self, ev, reads, writes, wacc):
        for b in reads:
            b.r.append(ev)
            if len(b.r) > 16:
                b.r = _compact(b.r)
        for b in writes:
            b.ws = [ev]
            b.wx = ev
            b.r = []
        for b in wacc:
            b.ws.append(ev)
            if len(b.ws) > 16:
                b.ws = _compact(b.ws)

    def op(self, e, fn, reads=(), writes=(), wacc=()):
        self._deps(e, reads, writes, wacc)
        ins = fn(self.eng[e])
        self.tick[e] += 1
        k = "s_" + e
        ins.then_inc(self.semh[k], 1)
        self._commit((k, self.tick[e], e), reads, writes, wacc)
        self.ninst += 1
        return ins

    def dma(self, q, out, in_, reads=(), writes=(), wacc=(), **kw):
        d = self.dq[q]
        k = d["sems"][d["n"] % len(d["sems"])]
        d["n"] += 1
        if d["cnt"][k] > 0:
            self._wait(q, (k, d["cnt"][k], "dma"))
        self._deps(q, reads, writes, wacc)
        ins = self.eng[q].dma_start(out=out, in_=in_, **kw)
        d["cnt"][k] += 16
        ins.then_inc(self.semh[k], 16)
        self._commit((k, d["cnt"][k], "dma"), reads, writes, wacc)
        self.ninst += 1

    def allgather(self, out_t, in_t, reads, writes, groups):
        self._deps("pool", reads, writes, ())
        if self.cc_cnt:
            self._wait("pool", ("cc", self.cc_cnt, "dma"))
        ins = self.nc.gpsimd.collective_compute("AllGather", ALU.bypass, replica_groups=groups,
                                                ins=[in_t.ap().opt()], outs=[out_t.ap().opt()])
        self.cc_cnt += 1
        ins.then_inc(self.semh["cc"], 1)
        self._commit(("cc", self.cc_cnt, "dma"), reads, writes, ())
        self.ninst += 1

    def cc_fence(self, groups):
        if not hasattr(self, "_fence_t"):
            self._fence_t = (self.nc.dram_tensor("cc_f_in", [16, 64], F32),
                             self.nc.dram_tensor("cc_f_out", [16 * len(groups[0]), 64], F32))
        fi, fo = self._fence_t
        self.allgather(fo, fi, reads=[], writes=[], groups=groups)

    def all_events(self):
        evs = []
        for e in self.eng:
            if self.tick[e] > 0:
                evs.append(("s_" + e, self.tick[e], e))
        for q, d in self.dq.items():
            for k, v in d["cnt"].items():
                if v > 0:
                    evs.append((k, v, "dma"))
        if self.cc_cnt:
            evs.append(("cc", self.cc_cnt, "dma"))
        return evs

    def barrier(self, engines=("pe", "act", "dve", "pool", "sp")):
        evs = self.all_events()
        for e in engines:
            for ev in evs:
                if ev[2] == e:
                    continue
                self._wait(e, ev)


def _t5_bucket_np(rel):
    nb = 16
    max_exact = 8
    ret = np.where(rel > 0, nb, 0)
    n = np.abs(rel)
    nf = np.maximum(n, 1).astype(np.float32)
    large = max_exact + (np.log(nf / np.float32(max_exact)) / np.float32(math.log(1024 / max_exact))
                         * np.float32(nb - max_exact)).astype(np.int32)
    large = np.minimum(large, nb - 1)
    return ret + np.where(n < max_exact, n, large)


def _consts_common():
    c = {}
    c["ident"] = np.eye(128, dtype=np.float32)
    c["antiI"] = np.eye(128, dtype=np.float32)[::-1].copy()
    c["onesd"] = np.full((128, 128), 1.0 / D, np.float32)
    blk = np.zeros((128, 128), np.float32)
    blk[:64, :64] = 1.0 / 64
    blk[64:, 64:] = 1.0 / 64
    c["blk64"] = blk
    i = np.arange(3072)
    delta = 1535 - i
    mult = ((np.abs(delta) <= 64).astype(np.int32)
            + ((delta % 4 == 0) & (np.abs(delta) <= 256)).astype(np.int32)
            + ((delta % 16 == 0) & (np.abs(delta) <= 1024)).astype(np.int32))
    mult[3071] = 0
    bk = _t5_bucket_np(delta)
    ohm = np.zeros((32, 3072), np.float32)
    ohm[bk, i] = mult
    c["ohm"] = ohm
    k = np.arange(64)
    ang = 2 * np.pi * np.outer(k, k) / 64
    c64 = (np.cos(ang) / 8).astype(np.float32)
    s64 = (np.sin(ang) / 8).astype(np.float32)
    cbd = np.zeros((128, 128), np.float32)
    sbd = np.zeros((128, 128), np.float32)
    cbd[:64, :64] = c64
    cbd[64:, 64:] = c64
    sbd[:64, :64] = s64
    sbd[64:, 64:] = s64
    c["c64bd"] = cbd
    c["s64bd"] = sbd
    return c


def _dft_tables(N, N1, N2, k2_list):
    sc = 1.0 / math.sqrt(N)
    n1 = np.arange(N1)
    k1 = np.arange(N1)
    n2 = np.arange(N2)
    a1 = 2 * np.pi * np.outer(n1, k1) / N1
    rp = np.concatenate([np.cos(a1), -np.sin(a1)], 1) * sc
    rr = np.concatenate([np.sin(a1), np.cos(a1)], 1) * sc
    at = 2 * np.pi * np.outer(n2, k1) / N
    tc = np.cos(at)
    ts = np.sin(at)
    a2 = 2 * np.pi * np.outer(n2, np.asarray(k2_list)) / N2
    c2 = np.cos(a2)
    s2 = np.sin(a2)
    f = lambda a: np.ascontiguousarray(a, dtype=np.float32)
    return f(rp), f(rr), f(tc), f(ts), f(c2), f(s2)


def _fm(a):
    F, T = a.shape
    return np.ascontiguousarray(a.reshape(F // 128, 128, T).transpose(1, 0, 2))


def _unfm(a):
    P_, C, T = a.shape
    return np.ascontiguousarray(a.transpose(2, 1, 0).reshape(T, C * P_))


from contextlib import ExitStack


class Prog:
    def __init__(self, debug=()):
        self.debug = set(debug)
        self.nc = bass.Bass("TRN2", target_bir_lowering=False)
        self.S = Sch(self.nc)
        self.inputs = {}
        self.outputs = {}
        self.gstack = ExitStack()
        self.rr = 0

    def din(self, name, shape, dtype=F32):
        t = self.nc.dram_tensor(name, list(shape), dtype, kind="ExternalInput")
        self.inputs[name] = t
        return t

    def dout(self, name, shape, dtype=F32):
        t = self.nc.dram_tensor(name, list(shape), dtype, kind="ExternalOutput")
        self.outputs[name] = t
        return t

    def dscratch(self, name, shape, dtype):
        if name in self.debug:
            return self.dout(name, shape, dtype)
        return self.nc.dram_tensor(name, list(shape), dtype)

    def sb(self, stack, name, shape, dtype):
        self._uid = getattr(self, "_uid", 0) + 1
        name = "%s_u%d" % (name, self._uid)
        t = stack.enter_context(self.nc.sbuf_tensor(name, list(shape), dtype))
        return Buf(t, name)

    def eng2(self):
        self.rr += 1
        return "act" if self.rr % 2 else "dve"

    def copy(self, e, out, in_, reads, writes=(), wacc=(), scale=None):
        S = self.S
        if e == "act":
            if scale is None:
                S.op("act", lambda h: h.copy(out, in_), reads, writes, wacc)
            else:
                S.op("act", lambda h: h.mul(out, in_, scale), reads, writes, wacc)
        else:
            if scale is None:
                S.op(e, lambda h: h.tensor_copy(out, in_), reads, writes, wacc)
            else:
                S.op(e, lambda h: h.tensor_scalar_mul(out, in_, scale), reads, writes, wacc)

    def rsqrt(self, out, in_, eps_tile, reads, wacc):
        S = self.S
        S.op("act", lambda h: h.activation(out, in_, AF.Sqrt, bias=eps_tile[:, 0:1], scale=1.0),
             reads=list(reads) + [eps_tile], wacc=wacc)
        S.op("dve", lambda h: h.reciprocal(out, out), reads=list(wacc), wacc=wacc)

    def setup(self):
        nc, S = self.nc, self.S
        g = self.gstack
        self.ps2 = [Buf(g.enter_context(nc.psum_tensor("psp%d" % i, [128, 1024], F32)), "psp%d" % i) for i in range(4)]
        self.ps = [Buf(self.ps2[i // 2].t[:, (i % 2) * 512:(i % 2 + 1) * 512], "ps%d" % i) for i in range(8)]
        self.c_onesA = self.sb(g, "c_onesA", [128, 128], F32)
        self.c_onesB = self.sb(g, "c_onesB", [128, 128], F32)
        S.op("dve", lambda h: h.memset(self.c_onesA[:], 0.0), writes=[self.c_onesA])
        S.op("dve", lambda h: h.memset(self.c_onesB[:], 0.0), writes=[self.c_onesB])
        S.op("dve", lambda h: h.memset(self.c_onesA[:, 0:64], 1.0), reads=[self.c_onesA], wacc=[self.c_onesA])
        S.op("dve", lambda h: h.memset(self.c_onesB[:, 64:128], 1.0), reads=[self.c_onesB], wacc=[self.c_onesB])
        self.c_ident = self.sb(g, "c_ident", [128, 128], F32)
        self.c_antiI = self.sb(g, "c_antiI", [128, 128], F32)
        self.c_onesd = self.sb(g, "c_onesd", [128, 128], F32)
        self.c_blk64 = self.sb(g, "c_blk64", [128, 128], F32)
        self.c_onesb = self.sb(g, "c_onesb", [128, 128], BF16)
        self.c_identb = self.sb(g, "c_identb", [128, 128], BF16)
        self.c_lng = self.sb(g, "c_lng", [128, 6, 8], F32)
        self.c_lnb = self.sb(g, "c_lnb", [128, 6, 8], F32)
        for nm, buf in (("ident", self.c_ident), ("antiI", self.c_antiI), ("onesd", self.c_onesd),
                        ("blk64", self.c_blk64)):
            t = self.din("k_" + nm, [128, 128])
            S.dma("sp", buf[:], t[:, :], writes=[buf])
        t = self.din("ln_g_r", [128, 6, 8])
        S.dma("sp", self.c_lng[:], t[:, :, :], writes=[self.c_lng])
        t = self.din("ln_b_r", [128, 6, 8])
        S.dma("sp", self.c_lnb[:], t[:, :, :], writes=[self.c_lnb])
        self.c_eps_ln = self.sb(g, "c_eps_ln", [128, 1], F32)
        self.c_eps_rms = self.sb(g, "c_eps_rms", [128, 1], F32)
        S.op("dve", lambda h: h.memset(self.c_eps_ln[:], LN_EPS), writes=[self.c_eps_ln])
        S.op("dve", lambda h: h.memset(self.c_eps_rms[:], RMS_EPS), writes=[self.c_eps_rms])
        S.op("dve", lambda h: h.memset(self.c_onesb[:], 1.0), writes=[self.c_onesb])
        S.op("dve", lambda h: h.tensor_copy(self.c_identb[:], self.c_ident[:]), reads=[self.c_ident],
             writes=[self.c_identb])

    def load_w(self, stack, name, wap, K, N, stg):
        S = self.S
        kc = K // 128
        wb = self.sb(stack, name, [128, kc, N], BF16)
        CH = stg[0].t.shape[1]
        i = 0
        for c in range(kc):
            for n0 in range(0, N, CH):
                n1 = min(N, n0 + CH)
                st = stg[i % len(stg)]
                i += 1
                S.dma(("sp", "act", "pool")[i % 3], st[:, 0:n1 - n0], wap[c * 128:(c + 1) * 128, n0:n1], writes=[st])
                self.copy(self.eng2(), wb[:, c, n0:n1], st[:, 0:n1 - n0], reads=[st], wacc=[wb])
        return wb

    def finish(self):
        S = self.S
        S.barrier()
        self.gstack.close()
        S.close()
        return self.nc


def phase_proj0(P):
    nc, S = P.nc, P.S
    w_in = P.din("ab_w_in", [D, 2560])
    fg = P.din("fnet_g_r", [128, 2])
    P.KT0 = {"p": P.dscratch("KT0p", [128, 6, NP_EXT], BF16), "s": P.dscratch("KT0s", [128, 6, SEQ_S], BF16)}
    P.V0 = {"p": P.dscratch("V0p", [NP_EXT // 128, 128, 780], BF16),
            "s": P.dscratch("V0s", [SEQ_S // 128, 128, 780], BF16)}
    P.QT0 = P.dscratch("QT0", [128, 6, NOWN], BF16)
    P.unT = {"p": [P.dscratch("unTp%d" % i, [256, 2048], BF16) for i in range(2)],
             "s": [P.dscratch("unTs", [256, NS_OWN], BF16)]}
    xT = {"p": P.din("xTp", [128, 8, NP_EXT]), "s": P.din("xTs", [128, 8, SEQ_S])}
    vld = {"p": P.din("vldp", [128, NP_EXT // 128]), "s": P.din("vlds", [128, SEQ_S // 128])}
    P.xT = xT
    P.vld = vld
    with ExitStack() as st:
        stg = [P.sb(st, "stg%d" % i, [128, 1024], F32) for i in range(3)]
        wb = P.load_w(st, "w_ab_in", w_in, D, 2560, stg)
        fgs = P.sb(st, "fgs", [128, 2], F32)
        S.dma("sp", fgs[:], fg[:, :], writes=[fgs])
        xf = [P.sb(st, "xf%d" % i, [128, 8, TT], F32) for i in range(2)]
        xb = [P.sb(st, "xb%d" % i, [128, 8, TT], BF16) for i in range(2)]
        kt = [P.sb(st, "kt%d" % i, [128, 6, TT], BF16) for i in range(2)]
        qt = [P.sb(st, "qt%d" % i, [128, 6, TT], BF16) for i in range(2)]
        vs = [P.sb(st, "vs%d" % i, [128, 4, 12, 65], BF16) for i in range(2)]
        uf = P.sb(st, "uf", [128, 2, TT], F32)
        usq = P.sb(st, "usq", [128, 2, TT], F32)
        urs = P.sb(st, "urs", [128, 2, TT], F32)
        un = [P.sb(st, "un%d" % i, [128, 2, TT], BF16) for i in range(2)]
        ones12 = P.sb(st, "ones12", [128, 12, 1], F32)
        S.op("dve", lambda h: h.memset(ones12[:], 1.0), writes=[ones12])
        vl = {}
        for sg in ("p", "s"):
            nch = (NP_EXT if sg == "p" else SEQ_S) // 128
            vl[sg] = P.sb(st, "vl" + sg, [128, nch], F32)
            S.dma("sp", vl[sg][:], vld[sg][:, :], writes=[vl[sg]])
        it = 0
        pb = 0
        for sg in ("p", "s"):
            n_ext = NP_EXT if sg == "p" else SEQ_S
            own0 = 2 if sg == "p" else 0
            nown_t = 8 if sg == "p" else 4
            ooff = 0 if sg == "p" else NP_OWN
            for i in range(n_ext // TT):
                a = it % 2
                it += 1
                X, XB, KT, QT, VS, UN = xf[a], xb[a], kt[a], qt[a], vs[a], un[a]
                S.dma("sp", X[:], xT[sg][:, :, i * TT:(i + 1) * TT], writes=[X])
                S.op("act", lambda h: h.copy(XB[:, 0:4, :], X[:, 0:4, :]), reads=[X], wacc=[XB])
                S.op("dve", lambda h: h.tensor_copy(XB[:, 4:8, :], X[:, 4:8, :]), reads=[X], wacc=[XB])
                for oc in range(6):
                    pt = P.ps[pb % 8]
                    pb += 1
                    for c in range(8):
                        S.op("pe", lambda h: h.matmul(pt[:], wb[:, c, 768 + oc * 128:768 + (oc + 1) * 128],
                                                      XB[:, c, :], start=(c == 0), stop=(c == 7)),
                             reads=[wb, XB], writes=[pt])
                    P.copy(P.eng2(), KT[:, oc, :], pt[:], reads=[pt], wacc=[KT])
                S.dma("pool", P.KT0[sg][:, :, i * TT:(i + 1) * TT], KT[:], reads=[KT])
                for sub in range(4):
                    for hf in range(2):
                        pt = P.ps[pb % 8]
                        pb += 1
                        for c in range(8):
                            S.op("pe", lambda h: h.matmul(pt[:, 0:384], XB[:, c, sub * 128:(sub + 1) * 128],
                                                          wb[:, c, 1536 + hf * 384:1536 + (hf + 1) * 384],
                                                          start=(c == 0), stop=(c == 7)),
                                 reads=[wb, XB], writes=[pt])
                        P.copy(P.eng2(), VS[:, sub, hf * 6:(hf + 1) * 6, 0:64],
                               pt[:, 0:384].rearrange("p (h d) -> p h d", d=64), reads=[pt], wacc=[VS])
                    ch = i * 4 + sub
                    S.op("dve", lambda h: h.tensor_scalar(VS[:, sub, :, 64:65], ones12[:], vl[sg][:, ch:ch + 1], None,
                                                          op0=ALU.mult), reads=[ones12, vl[sg]], wacc=[VS])
                S.dma("pool", P.V0[sg][i * 4:(i + 1) * 4].rearrange("c p f -> p c f"),
                      VS[:].rearrange("p s h d -> p s (h d)"), reads=[VS])
                if not (own0 <= i < own0 + nown_t):
                    continue
                o0 = ooff + (i - own0) * TT
                for oc in range(6):
                    pt = P.ps[pb % 8]
                    pb += 1
                    for c in range(8):
                        S.op("pe", lambda h: h.matmul(pt[:], wb[:, c, oc * 128:(oc + 1) * 128],
                                                      XB[:, c, :], start=(c == 0), stop=(c == 7)),
                             reads=[wb, XB], writes=[pt])
                    P.copy(P.eng2(), QT[:, oc, :], pt[:], reads=[pt], wacc=[QT], scale=0.125)
                S.dma("pool", P.QT0[:, :, o0:o0 + TT], QT[:], reads=[QT])
                for c2 in range(2):
                    pt = P.ps[pb % 8]
                    pb += 1
                    for c in range(8):
                        S.op("pe", lambda h: h.matmul(pt[:], wb[:, c, 2304 + c2 * 128:2304 + (c2 + 1) * 128],
                                                      XB[:, c, :], start=(c == 0), stop=(c == 7)),
                             reads=[wb, XB], writes=[pt])
                    S.op("act", lambda h: h.copy(uf[:, c2, :], pt[:]), reads=[pt], wacc=[uf])
                for c2 in range(2):
                    pm = P.ps[pb % 8]
                    pb += 1
                    S.op("pe", lambda h: h.matmul(pm[:], P.c_blk64[:], uf[:, c2, :], start=True, stop=True),
                         reads=[P.c_blk64, uf], writes=[pm])
                    S.op("dve", lambda h: h.tensor_tensor(uf[:, c2, :], uf[:, c2, :], pm[:], op=ALU.subtract),
                         reads=[pm, uf], wacc=[uf])
                    S.op("act", lambda h: h.activation(usq[:, c2, :], uf[:, c2, :], AF.Square),
                         reads=[uf], wacc=[usq])
                    pv = P.ps[pb % 8]
                    pb += 1
                    S.op("pe", lambda h: h.matmul(pv[:], P.c_blk64[:], usq[:, c2, :], start=True, stop=True),
                         reads=[P.c_blk64, usq], writes=[pv])
                    P.rsqrt(urs[:, c2, :], pv[:], P.c_eps_ln, reads=[pv], wacc=[urs])
                    S.op("dve", lambda h: h.tensor_tensor(uf[:, c2, :], uf[:, c2, :], urs[:, c2, :], op=ALU.mult),
                         reads=[urs, uf], wacc=[uf])
                    S.op("dve", lambda h: h.tensor_scalar(UN[:, c2, :], uf[:, c2, :], fgs[:, c2:c2 + 1], None,
                                                          op0=ALU.mult), reads=[uf, fgs], wacc=[UN])
                oo = (i - own0) * TT
                S.dma("pool", P.unT[sg][oo // 2048][:, oo % 2048:oo % 2048 + TT].rearrange("(c p) t -> p c t", p=128),
                      UN[:], reads=[UN])
        S.barrier()


def attn_pipeline(P, items, s_fn, e_fn, o_fn, look=2):
    n = len(items)
    for t in range(min(look, n)):
        s_fn(items[t], t)
    for t in range(n):
        if t + look < n:
            s_fn(items[t + look], t + look)
        e_fn(items[t], t)
        o_fn(items[t], t)


class PairAttn:
    def __init__(self, P, st, with_z):
        self.P = P
        self.with_z = with_z
        self.EB = [P.sb(st, "paEB%d" % i, [128, 1024], BF16) for i in range(4)]
        if with_z:
            self.EF = [P.sb(st, "paEF%d" % i, [128, 1024], F32) for i in range(3)]
        self.RD = [P.sb(st, "paRD%d" % i, [128, 512], F32) for i in range(2)]
        self.OUT = [P.sb(st, "paOUT%d" % i, [128, 512], BF16) for i in range(2)]

    def run(self, items, kt, qt, va, zz, out_fn, vl=None):
        P, S = self.P, self.P.S
        EB, RD, OUT = self.EB, self.RD, self.OUT

        def s_fn(itm, t):
            qi, ch, zoff, first, last = itm
            pp = P.ps2[t % 3]
            for hh in range(2):
                pb_ = hh * 64
                S.op("pe", lambda h: h.matmul(pp[:, hh * 512:(hh + 1) * 512], kt[pb_:pb_ + 64, ch * 128:(ch + 1) * 128],
                                              qt[pb_:pb_ + 64, qi * TT:(qi + 1) * TT], start=True, stop=True,
                                              tile_position=(pb_, 0)), reads=[kt, qt], writes=[pp])

        def e_fn(itm, t):
            qi, ch, zoff, first, last = itm
            pp = P.ps2[t % 3]
            eb = EB[t % 4]
            if self.with_z:
                ef = self.EF[t % 3]
                S.op("act", lambda h: h.activation(ef[:], pp[:], AF.Exp), reads=[pp], writes=[ef])
                S.op("dve", lambda h: h.tensor_tensor(eb[:, 0:512], ef[:, 0:512], zz[0][:, zoff:zoff + 512],
                                                      op=ALU.mult), reads=[ef, zz[0]], wacc=[eb])
                S.op("dve", lambda h: h.tensor_tensor(eb[:, 512:832], ef[:, 512:832], zz[1][:, zoff:zoff + 320],
                                                      op=ALU.mult), reads=[ef, zz[1]], wacc=[eb])
                S.op("pool", lambda h: h.tensor_tensor(eb[:, 832:1024], ef[:, 832:1024],
                                                       zz[1][:, zoff + 320:zoff + 512], op=ALU.mult),
                     reads=[ef, zz[1]], wacc=[eb])
            else:
                S.op("act", lambda h: h.activation(eb[:], pp[:], AF.Exp), reads=[pp], writes=[eb])

        def o_fn(itm, t):
            qi, ch, zoff, first, last = itm
            eb = EB[t % 4]
            po, pd = P.ps[6], P.ps[7]
            for hh in range(2):
                S.op("pe", lambda h: h.matmul(po[hh * 64:(hh + 1) * 64, :], va[:, ch, hh, 0:64],
                                              eb[:, hh * 512:(hh + 1) * 512], start=first, stop=last,
                                              tile_position=(0, hh * 64)), reads=[va, eb], writes=[po])
            for hh in range(2):
                lt = vl[:, ch, :] if vl is not None else P.c_onesb[:, 0:64]
                S.op("pe", lambda h: h.matmul(pd[hh * 64:(hh + 1) * 64, :], lt,
                                              eb[:, hh * 512:(hh + 1) * 512], start=first, stop=last,
                                              tile_position=(0, hh * 64)),
                     reads=[vl if vl is not None else P.c_onesb, eb], writes=[pd])
            if not last:
                return
            rd, out = RD[qi % 2], OUT[qi % 2]
            S.op("dve", lambda h: h.reciprocal(rd[:], pd[:]), reads=[pd], writes=[rd])
            S.op("dve", lambda h: h.tensor_tensor(out[:], po[:], rd[:], op=ALU.mult), reads=[po, rd], writes=[out])
            out_fn(qi, out)

        attn_pipeline(P, items, s_fn, e_fn, o_fn, look=2)


def phase_dil(P):
    nc, S = P.nc, P.S
    relb = P.din("rel_bias", [32, 12])
    ohm = P.din("k_ohm", [32, 3072])
    rev = P.dscratch("dil_rev", [12, 3200], F32)
    P.OT0 = P.dscratch("OT0", [128, 8, NOWN], BF16)
    REV = Buf(rev, "rev")
    with ExitStack() as st:
        ones32 = P.sb(st, "ones32", [128, 64], F32)
        S.op("dve", lambda h: h.memset(ones32[:], 1.0), writes=[ones32])
        rb = P.sb(st, "rb", [32, 12], F32)
        eb = P.sb(st, "eb", [32, 12], F32)
        oh = P.sb(st, "oh", [32, 3072], F32)
        wt = P.sb(st, "wt", [12, 3072], F32)
        S.dma("sp", rb[:], relb[:, :], writes=[rb])
        S.dma("sp", oh[:], ohm[:, :], writes=[oh])
        S.op("act", lambda h: h.activation(eb[:], rb[:], AF.Exp), reads=[rb], writes=[eb])
        for n0 in range(0, 3072, 512):
            pt = P.ps[(n0 // 512) % 8]
            S.op("pe", lambda h: h.matmul(pt[0:12, :], eb[:], oh[:, n0:n0 + 512], start=True, stop=True),
                 reads=[eb, oh], writes=[pt])
            S.op("dve", lambda h: h.tensor_copy(wt[:, n0:n0 + 512], pt[0:12, :]), reads=[pt], wacc=[wt])
        S.dma("pool", REV[:, 0:3072], wt[:], reads=[wt], writes=[REV])
        S.barrier()
        KT = [P.sb(st, "dKT%d" % i, [128, NP_EXT], BF16) for i in range(2)]
        QT = [P.sb(st, "dQT%d" % i, [128, NP_OWN], BF16) for i in range(2)]
        VA = [P.sb(st, "dVA%d" % i, [128, NP_EXT // 128, 2, 65], BF16) for i in range(2)]
        ZZ = [[P.sb(st, "dZ%d_%d" % (i, k), [128, ZW], BF16) for k in range(2)] for i in range(2)]
        HK = P.sb(st, "dHK", [128, ZW], F32)
        PA = PairAttn(P, st, with_z=True)
        VL = {}
        for sg_ in ("p", "s"):
            nch_ = (NP_EXT if sg_ == "p" else SEQ_S) // 128
            vlf = P.sb(st, "dvlf" + sg_, [128, nch_], F32)
            VL[sg_] = P.sb(st, "dvl" + sg_, [128, nch_, 64], BF16)
            S.dma("sp", vlf[:], P.vld[sg_][:, :], writes=[vlf])
            S.op("dve", lambda h: h.tensor_copy(VL[sg_][:], vlf[:].unsqueeze(2).to_broadcast([128, nch_, 64])),
                 reads=[vlf], writes=[VL[sg_]])
        it = 0
        for sg in ("p", "s"):
            n_ext = NP_EXT if sg == "p" else SEQ_S
            n_own = NP_OWN if sg == "p" else NS_OWN
            nqt = n_own // TT
            ooff = 0 if sg == "p" else NP_OWN
            nch = n_ext // 128
            for hp in range(6):
                a = it % 2
                it += 1
                kt, qt, va, zz = KT[a], QT[a], VA[a], ZZ[a]
                S.dma("sp", kt[:, 0:n_ext], P.KT0[sg][:, hp, :], writes=[kt])
                S.dma("sp", qt[:, 0:n_own], P.QT0[:, hp, ooff:ooff + n_own], writes=[qt])
                S.dma("sp", va[:, 0:nch, :, :].rearrange("p c h d -> p c (h d)"),
                      P.V0[sg][:, :, hp * 130:(hp + 1) * 130].rearrange("c p f -> p c f"), writes=[va])
                for hh in range(2):
                    h_ = 2 * hp + hh
                    src = bass.AP(tensor=rev, offset=h_ * 3200, ap=[[1, 128], [1, ZW]])
                    S.dma("sp", HK[:], src, reads=[REV], writes=[HK])
                    for n0 in range(0, ZW, 512):
                        w = min(512, ZW - n0)
                        pt = P.ps[6 + (n0 // 512) % 2]
                        S.op("pe", lambda h: h.matmul(pt[:, 0:w], P.c_antiI[:], HK[:, n0:n0 + w], start=True,
                                                      stop=True), reads=[P.c_antiI, HK], writes=[pt])
                        P.copy(P.eng2(), zz[hh][:, n0:n0 + w], pt[:, 0:w], reads=[pt], wacc=[zz[hh]])
                items = []
                for qi in range(nqt):
                    js = []
                    for j in range(20):
                        ch = 4 * qi + j - (0 if sg == "p" else 8)
                        if 0 <= ch < nch:
                            js.append((j, ch))
                    for idx, (j, ch) in enumerate(js):
                        items.append((qi, ch, 2432 - 128 * j, idx == 0, idx == len(js) - 1))

                def out_fn(qi, out, hp=hp, ooff=ooff):
                    o0 = ooff + qi * TT
                    S.dma("pool", P.OT0[:, hp, o0:o0 + TT], out[:], reads=[out])

                PA.run(items, kt, qt, va, zz, out_fn, vl=VL[sg])
        S.barrier()


GROUPS4 = [[0, 1, 2, 3], [4, 5, 6, 7]]


def phase_fnet(P):
    nc, S = P.nc, P.S
    fw = P.din("ab_fnet_w", [4, 64, 64])
    c64 = P.din("k_c64bd", [128, 128])
    s64 = P.din("k_s64bd", [128, 128])
    unall = [P.dscratch("unTall%d" % i, [1024, 2048], BF16) for i in range(2)]
    UNALL = Buf(unall)
    for i in range(2):
        S.allgather(unall[i], P.unT["p"][i], reads=[], writes=[UNALL] if i == 0 else [], groups=GROUPS4)
    S.cc_fence(GROUPS4)
    UNALL.ws = [("cc", S.cc_cnt, "dma")]
    if "unall_dbg" in P.debug:
        for i in range(2):
            dbg = P.dout("unall_dbg%d" % i, [1024, 2048], BF16)
            S.dma("sp", dbg[:, :], unall[i][:, :], reads=[UNALL])
            dbg2 = P.dout("unmine_dbg%d" % i, [256, 2048], BF16)
            S.dma("sp", dbg2[:, :], P.unT["p"][i][:, :], reads=[UNALL])
    tabs = {}
    for sg, n1 in (("p", 128), ("s", 16)):
        nk2 = 32 if sg == "p" else 128
        tabs[sg] = dict(rp=P.din("f_rp_" + sg, [n1, 2 * n1]), rr=P.din("f_rr_" + sg, [n1, 2 * n1]),
                        tc=P.din("f_tc_" + sg, [128, n1]), ts=P.din("f_ts_" + sg, [128, n1]),
                        c2=P.din("f_c2_" + sg, [128, nk2]), s2=P.din("f_s2_" + sg, [128, nk2]))
    with ExitStack() as st:
        cs = P.sb(st, "f_cs", [128, 2, 128], F32)
        S.dma("sp", cs[:, 0, :], c64[:, :], wacc=[cs])
        S.dma("sp", cs[:, 1, :], s64[:, :], wacc=[cs])
        wbd = P.sb(st, "f_wbd", [128, 128], F32)
        AB = P.sb(st, "f_AB", [128, 2, 128], BF16)
        stg = P.sb(st, "f_stg", [128, 2, 256], F32)
        tb = {k: P.sb(st, "f_t_" + k, [128, 256], BF16) for k in ("rp", "rr", "c2", "s2")}
        tcs = {k: P.sb(st, "f_t_" + k, [128, 128], F32) for k in ("tc", "ts")}
        Y = P.sb(st, "f_Y", [128, 128, 256], BF16)
        OB = P.sb(st, "f_OB", [128, NP_OWN], BF16)
        pbk = 0
        for sg in ("p", "s"):
            N1 = 128 if sg == "p" else 16
            NK2 = 32 if sg == "p" else 128
            n_own = NP_OWN if sg == "p" else NS_OWN
            nseq = SEQ_P if sg == "p" else SEQ_S
            ooff = 0 if sg == "p" else NP_OWN
            T = tabs[sg]
            for k, rows, cols in (("rp", N1, 2 * N1), ("rr", N1, 2 * N1), ("c2", 128, NK2), ("s2", 128, NK2)):
                S.dma("sp", stg[0:rows, 0, 0:cols], T[k][:, :], writes=[stg])
                S.op("dve", lambda h: h.tensor_copy(tb[k][0:rows, 0:cols], stg[0:rows, 0, 0:cols]), reads=[stg],
                     writes=[tb[k]])
            for k in ("tc", "ts"):
                S.dma("sp", tcs[k][:, 0:N1], T[k][:, :], writes=[tcs[k]])
            for gp in range(2):
                S.op("dve", lambda h: h.memset(wbd[:], 0.0), writes=[wbd])
                S.dma("sp", wbd[0:64, 0:64], fw[2 * gp, :, :], wacc=[wbd])
                S.dma("sp", wbd[64:128, 64:128], fw[2 * gp + 1, :, :], wacc=[wbd])
                for k in range(2):
                    pt = P.ps[pbk % 8]
                    pbk += 1
                    S.op("pe", lambda h: h.matmul(pt[:, 0:128], cs[:, k, :], wbd[:], start=True, stop=True),
                         reads=[cs, wbd], writes=[pt])
                    P.copy("dve", AB[:, k, :], pt[:, 0:128], reads=[pt], wacc=[AB], scale=(1.0 if k == 0 else -1.0))
                with ExitStack() as st2:
                    un = P.sb(st2, "f_un", [128, SEQ_P], BF16)
                    if sg == "p":
                        for r in range(4):
                            for hf in range(2):
                                S.dma("sp", un[:, r * 4096 + hf * 2048:r * 4096 + (hf + 1) * 2048],
                                      unall[hf][r * 256 + gp * 128:r * 256 + (gp + 1) * 128, :], reads=[UNALL],
                                      wacc=[un])
                    else:
                        S.dma("sp", un[:, 0:nseq], P.unT["s"][0][gp * 128:(gp + 1) * 128, :], wacc=[un])
                    for n2 in range(0, 128, 2):
                        pt = P.ps[pbk % 8]
                        pbk += 1
                        for d in range(2):
                            S.op("pe", lambda h: h.matmul(pt[0:N1, d * 256:(d + 1) * 256],
                                                          un[:, n2 + d:nseq:128], AB[:].rearrange("p a e -> p (a e)"),
                                                          start=True, stop=True), reads=[un, AB], writes=[pt])
                        P.copy(P.eng2(), Y[0:N1, n2:n2 + 2, :], pt[0:N1, :].rearrange("p (a c) -> p a c", a=2),
                               reads=[pt], wacc=[Y])
                    S.barrier()
                with ExitStack() as st3:
                    GP = P.sb(st3, "f_GP", [128, N1, 2, 128], BF16)
                    GS = [P.sb(st3, "f_GS%d" % i, [128, 512], F32) for i in range(2)]
                    T1 = [P.sb(st3, "f_T1%d" % i, [128, 256], F32) for i in range(2)]
                    T2 = [P.sb(st3, "f_T2%d" % i, [128, 256], F32) for i in range(2)]
                    T3 = [P.sb(st3, "f_T3%d" % i, [128, 256], F32) for i in range(2)]
                    T4 = [P.sb(st3, "f_T4%d" % i, [128, 256], F32) for i in range(2)]
                    EBn = 512 // (2 * N1)
                    nb = 0
                    for c0 in range(0, 128, EBn):
                        pt = P.ps[pbk % 8]
                        pbk += 1
                        for bi in range(EBn):
                            col = c0 + bi
                            sl = slice(bi * 2 * N1, (bi + 1) * 2 * N1)
                            S.op("pe", lambda h: h.matmul(pt[:, sl], Y[0:N1, :, col], tb["rp"][0:N1, 0:2 * N1],
                                                          start=True, stop=False), reads=[Y, tb["rp"]], writes=[pt])
                            S.op("pe", lambda h: h.matmul(pt[:, sl], Y[0:N1, :, 128 + col], tb["rr"][0:N1, 0:2 * N1],
                                                          start=False, stop=True), reads=[Y, tb["rr"]], writes=[pt])
                        a = nb % 2
                        nb += 1
                        gs, t1, t2, t3, t4 = GS[a], T1[a], T2[a], T3[a], T4[a]
                        S.op("act", lambda h: h.copy(gs[:], pt[:]), reads=[pt], writes=[gs])
                        g4 = gs[:].rearrange("p (b r k) -> p b r k", b=EBn, r=2)
                        gr, gi = g4[:, :, 0, :], g4[:, :, 1, :]
                        tcb = tcs["tc"][:, 0:N1].unsqueeze(1).to_broadcast([128, EBn, N1])
                        tsb = tcs["ts"][:, 0:N1].unsqueeze(1).to_broadcast([128, EBn, N1])
                        v = lambda t: t[:, 0:EBn * N1].rearrange("p (b k) -> p b k", b=EBn)
                        vt = lambda t: t[:, 0:EBn * N1].rearrange("p (b k) -> p k b", b=EBn)
                        S.op("dve", lambda h: h.tensor_tensor(v(t1), gr, tcb, op=ALU.mult),
                             reads=[gs, tcs["tc"]], writes=[t1])
                        S.op("dve", lambda h: h.tensor_tensor(v(t2), gi, tsb, op=ALU.mult),
                             reads=[gs, tcs["ts"]], writes=[t2])
                        S.op("pool", lambda h: h.tensor_tensor(v(t3), gi, tcb, op=ALU.mult),
                             reads=[gs, tcs["tc"]], writes=[t3])
                        S.op("pool", lambda h: h.tensor_tensor(v(t4), gr, tsb, op=ALU.mult),
                             reads=[gs, tcs["ts"]], writes=[t4])
                        S.op("dve", lambda h: h.tensor_tensor(GP[:, :, 0, c0:c0 + EBn], vt(t1), vt(t2), op=ALU.add),
                             reads=[t1, t2], wacc=[GP])
                        S.op("pool", lambda h: h.tensor_tensor(GP[:, :, 1, c0:c0 + EBn], vt(t3), vt(t4),
                                                               op=ALU.subtract), reads=[t3, t4], wacc=[GP])
                    KB = 512 // NK2
                    for k0 in range(0, N1, KB):
                        pt = P.ps[pbk % 8]
                        pbk += 1
                        for kk in range(KB):
                            k1 = k0 + kk
                            sl = slice(kk * NK2, (kk + 1) * NK2)
                            S.op("pe", lambda h: h.matmul(pt[:, sl], GP[:, k1, 0, :], tb["c2"][:, 0:NK2], start=True,
                                                          stop=False), reads=[GP, tb["c2"]], writes=[pt])
                            S.op("pe", lambda h: h.matmul(pt[:, sl], GP[:, k1, 1, :], tb["s2"][:, 0:NK2], start=False,
                                                          stop=True), reads=[GP, tb["s2"]], writes=[pt])
                        ov = OB[:, 0:n_own].rearrange("p (k2 k1) -> p k1 k2", k1=N1)[:, k0:k0 + KB, :]
                        P.copy(P.eng2(), ov, pt[:].rearrange("p (a b) -> p a b", a=KB), reads=[pt], wacc=[OB])
                    S.dma("pool", P.OT0[:, 6 + gp, ooff:ooff + n_own], OB[:, 0:n_own], reads=[OB])
                    S.barrier()
        S.barrier()


class RowBufs:
    def __init__(self, P, st):
        self.xr = [P.sb(st, "rb_xr%d" % i, [128, 8, TT], F32) for i in range(2)]
        self.r = P.sb(st, "rb_r", [128, 8, TT], F32)
        self.sq = P.sb(st, "rb_sq", [128, 8, TT], F32)
        self.rstd = P.sb(st, "rb_rstd", [128, TT], F32)
        self.s1 = P.sb(st, "rb_s1", [128, TT], F32)
        self.s2 = P.sb(st, "rb_s2", [128, TT], F32)
        self.mean = P.sb(st, "rb_mean", [128, TT], F32)
        self.m2 = P.sb(st, "rb_m2", [128, TT], F32)
        self.of = [P.sb(st, "rb_of%d" % i, [128, 8, TT], F32) for i in range(1)]
        self.ob = [P.sb(st, "rb_ob%d" % i, [128, 8, TT], BF16) for i in range(1)]


def linear_resid_ln(P, RB, i, wb, kcin, src, xr, lnidx, outf_d, outb_d, pbase):
    S = P.S
    r, sq, rstd = RB.r, RB.sq, RB.rstd
    of, ob = RB.of[0], RB.ob[0]
    for oc in range(8):
        pt = P.ps[(pbase + oc) % 8]
        for c in range(kcin):
            S.op("pe", lambda h: h.matmul(pt[:], wb[:, c, oc * 128:(oc + 1) * 128], src[:, c, :], start=(c == 0),
                                          stop=(c == kcin - 1)), reads=[wb, src], writes=[pt])
        S.op("dve", lambda h: h.scalar_tensor_tensor(r[:, oc, :], xr[:, oc, :], DN_ALPHA, pt[:], op0=ALU.mult,
                                                     op1=ALU.add), reads=[xr, pt], wacc=[r])
        S.op("act", lambda h: h.activation(sq[:, oc, :], r[:, oc, :], AF.Square), reads=[r], wacc=[sq])
    layer_norm_fm(P, RB, lnidx, of, ob, pbase)
    t0 = i * TT
    if outf_d is not None:
        S.dma("pool", outf_d[:, :, t0:t0 + TT], of[:], reads=[of])
    if outb_d is not None:
        S.dma("pool", outb_d[:, :, t0:t0 + TT], ob[:], reads=[ob])


def layer_norm_fm(P, RB, lnidx, of, ob, pbase):
    S = P.S
    r, sq, rstd, s1, s2, mean, m2 = RB.r, RB.sq, RB.rstd, RB.s1, RB.s2, RB.mean, RB.m2
    pm = P.ps[(pbase + 0) % 8]
    pv = P.ps[(pbase + 1) % 8]
    S.op("dve", lambda h: h.tensor_reduce(s1[:], r[:].rearrange("p c t -> p t c"), axis=AX.X, op=ALU.add),
         reads=[r], writes=[s1])
    S.op("dve", lambda h: h.tensor_reduce(s2[:], sq[:].rearrange("p c t -> p t c"), axis=AX.X, op=ALU.add),
         reads=[sq], writes=[s2])
    S.op("pe", lambda h: h.matmul(pm[:], P.c_onesd[:], s1[:], start=True, stop=True), reads=[P.c_onesd, s1],
         writes=[pm])
    S.op("pe", lambda h: h.matmul(pv[:], P.c_onesd[:], s2[:], start=True, stop=True), reads=[P.c_onesd, s2],
         writes=[pv])
    S.op("act", lambda h: h.copy(mean[:], pm[:]), reads=[pm], writes=[mean])
    S.op("dve", lambda h: h.tensor_tensor(m2[:], mean[:], mean[:], op=ALU.mult), reads=[mean], writes=[m2])
    S.op("dve", lambda h: h.tensor_tensor(m2[:], pv[:], m2[:], op=ALU.subtract), reads=[pv, m2], wacc=[m2])
    P.rsqrt(rstd[:], m2[:], P.c_eps_ln, reads=[m2], wacc=[rstd])
    S.op("dve", lambda h: h.tensor_tensor(r[:], r[:], mean[:].unsqueeze(1).to_broadcast([128, 8, TT]),
                                          op=ALU.subtract), reads=[r, mean], wacc=[r])
    S.op("dve", lambda h: h.tensor_tensor(r[:], r[:], rstd[:].unsqueeze(1).to_broadcast([128, 8, TT]),
                                          op=ALU.mult), reads=[r, rstd], wacc=[r])
    for c in range(8):
        S.op("act", lambda h: h.activation(of[:, c, :], r[:, c, :], AF.Identity,
                                           bias=P.c_lnb[:, lnidx, c:c + 1], scale=P.c_lng[:, lnidx, c:c + 1]),
             reads=[r, P.c_lng, P.c_lnb], wacc=[of])
    S.op("pool", lambda h: h.tensor_copy(ob[:], of[:]), reads=[of], writes=[ob])


def phase_linear_ln(P, name, w_d, kcin, src_d, xres_fn, lnidx, outf_d, outb_d):
    S = P.S
    with ExitStack() as st:
        stg = [P.sb(st, "stg%d" % i, [128, 1024], F32) for i in range(3)]
        wb = P.load_w(st, "w_" + name, w_d, kcin * 128, D, stg)
        RB = RowBufs(P, st)
        srcs = [P.sb(st, "src%d" % i, [128, kcin, TT], BF16) for i in range(2)]
        for i in range(NTILE):
            sb_, xr = srcs[i % 2], RB.xr[i % 2]
            S.dma("sp", sb_[:], src_d[:, :, i * TT:(i + 1) * TT], writes=[sb_])
            S.dma("sp", xr[:], xres_fn(i), writes=[xr])
            linear_resid_ln(P, RB, i, wb, kcin, sb_, xr, lnidx, outf_d, outb_d, pbase=(i * 2) % 8)
        S.barrier()


def phase_xattn(P, layer, xf_d, xb_d, outf_d, outb_d):
    S = P.S
    wq_d = P.din("xa_w_q%d" % layer, [D, D])
    wkv_d = P.din("xa_w_kv%d" % layer, [D, 2 * D])
    wo_d = P.din("xa_w_o%d" % layer, [D, D])
    if not hasattr(P, "memT"):
        P.memT = {"p": P.din("memTp", [128, 8, 256]), "s": P.din("memTs", [128, 8, 256])}
    lnidx = layer * 3 + 1
    with ExitStack() as st:
        stg = [P.sb(st, "stg%d" % i, [128, 1024], F32) for i in range(3)]
        wq = P.load_w(st, "w_xq", wq_d, D, D, stg)
        wo = P.load_w(st, "w_xo", wo_d, D, D, stg)
        memK = {sg: P.sb(st, "memK" + sg, [128, 8, 256], BF16) for sg in ("p", "s")}
        memV = {sg: P.sb(st, "memV" + sg, [128, 2, D], BF16) for sg in ("p", "s")}
        with ExitStack() as st2:
            wkv = P.load_w(st2, "w_xkv", wkv_d, D, 2 * D, stg)
            mf = P.sb(st2, "memf", [128, 8, 256], F32)
            mb = P.sb(st2, "memb", [128, 8, 256], BF16)
            pb = 0
            for sg in ("p", "s"):
                S.dma("sp", mf[:], P.memT[sg][:, :, :], writes=[mf])
                S.op("dve", lambda h: h.tensor_copy(mb[:], mf[:]), reads=[mf], writes=[mb])
                for oc in range(8):
                    pt = P.ps[pb % 8]
                    pb += 1
                    for c in range(8):
                        S.op("pe", lambda h: h.matmul(pt[:, 0:256], wkv[:, c, oc * 128:(oc + 1) * 128], mb[:, c, :],
                                                      start=(c == 0), stop=(c == 7)), reads=[wkv, mb], writes=[pt])
                    P.copy(P.eng2(), memK[sg][:, oc, :], pt[:, 0:256], reads=[pt], wacc=[memK[sg]])
                for mc in range(2):
                    for n0 in range(2):
                        pt = P.ps[pb % 8]
                        pb += 1
                        for c in range(8):
                            S.op("pe", lambda h: h.matmul(pt[:], mb[:, c, mc * 128:(mc + 1) * 128],
                                                          wkv[:, c, D + n0 * 512:D + (n0 + 1) * 512],
                                                          start=(c == 0), stop=(c == 7)), reads=[wkv, mb], writes=[pt])
                        P.copy(P.eng2(), memV[sg][:, mc, n0 * 512:(n0 + 1) * 512], pt[:], reads=[pt],
                               wacc=[memV[sg]])
            S.barrier()
        RB = RowBufs(P, st)
        xbs = [P.sb(st, "xa_xb%d" % i, [128, 8, TT], BF16) for i in range(2)]
        qb = P.sb(st, "xa_q", [128, 8, TT], BF16)
        ob_ = P.sb(st, "xa_o", [128, 8, TT], BF16)
        ee = [P.sb(st, "xa_e%d" % i, [128, 2, TT], BF16) for i in range(2)]
        rden = [P.sb(st, "xa_rd%d" % i, [128, TT], F32) for i in range(2)]
        pb = 0
        for i in range(NTILE):
            sg = "p" if i < NP_OWN // TT else "s"
            xb, xr = xbs[i % 2], RB.xr[i % 2]
            S.dma("sp", xb[:], xb_d[:, :, i * TT:(i + 1) * TT], writes=[xb])
            S.dma("sp", xr[:], xf_d[:, :, i * TT:(i + 1) * TT], writes=[xr])
            for oc in range(8):
                pt = P.ps[pb % 8]
                pb += 1
                for c in range(8):
                    S.op("pe", lambda h: h.matmul(pt[:], wq[:, c, oc * 128:(oc + 1) * 128], xb[:, c, :],
                                                  start=(c == 0), stop=(c == 7)), reads=[wq, xb], writes=[pt])
                P.copy(P.eng2(), qb[:, oc, :], pt[:], reads=[pt], wacc=[qb], scale=1.0 / 16)
            for hh in range(4):
                E, RD = ee[hh % 2], rden[hh % 2]
                for mc in range(2):
                    pt = P.ps[pb % 8]
                    pb += 1
                    for cc in range(2):
                        S.op("pe", lambda h: h.matmul(pt[:], memK[sg][:, 2 * hh + cc, mc * 128:(mc + 1) * 128],
                                                      qb[:, 2 * hh + cc, :], start=(cc == 0), stop=(cc == 1)),
                             reads=[memK[sg], qb], writes=[pt])
                    S.op("act", lambda h: h.activation(E[:, mc, :], pt[:], AF.Exp), reads=[pt], wacc=[E])
                pd = P.ps[pb % 8]
                pb += 1
                for mc in range(2):
                    S.op("pe", lambda h: h.matmul(pd[:], P.c_onesb[:], E[:, mc, :], start=(mc == 0), stop=(mc == 1)),
                         reads=[P.c_onesb, E], writes=[pd])
                S.op("dve", lambda h: h.reciprocal(RD[:], pd[:]), reads=[pd], writes=[RD])
                for dvc in range(2):
                    po = P.ps[pb % 8]
                    pb += 1
                    for mc in range(2):
                        S.op("pe", lambda h: h.matmul(po[:], memV[sg][:, mc, hh * 256 + dvc * 128:hh * 256 + (dvc + 1) * 128],
                                                      E[:, mc, :], start=(mc == 0), stop=(mc == 1)),
                             reads=[memV[sg], E], writes=[po])
                    S.op("dve", lambda h: h.tensor_tensor(ob_[:, 2 * hh + dvc, :], po[:], RD[:], op=ALU.mult),
                         reads=[po, RD], wacc=[ob_])
            linear_resid_ln(P, RB, i, wo, 8, ob_, xr, lnidx, outf_d, outb_d, pbase=pb % 8)
            pb += 2
        S.barrier()


def phase_ffn1(P, layer, xb_d, h_d):
    S = P.S
    w_d = P.din("ffn_w_in%d" % layer, [D, 2 * FFN_H])
    with ExitStack() as st:
        stg = [P.sb(st, "stg%d" % i, [128, 1024], F32) for i in range(3)]
        wb = P.load_w(st, "w_ffn_in", w_d, D, 2 * FFN_H, stg)
        xbs = [P.sb(st, "f1_xb%d" % i, [128, 8, TT], BF16) for i in range(2)]
        hid = [P.sb(st, "f1_h%d" % i, [128, HC, TT], BF16) for i in range(2)]
        sgb = [P.sb(st, "f1_sg%d" % i, [128, TT], F32) for i in range(3)]
        pb = 0
        for i in range(NTILE):
            xb, hd = xbs[i % 2], hid[i % 2]
            S.dma("sp", xb[:], xb_d[:, :, i * TT:(i + 1) * TT], writes=[xb])
            for hc in range(HC):
                pg = P.ps[pb % 8]
                pu = P.ps[(pb + 1) % 8]
                pb += 2
                for c in range(8):
                    S.op("pe", lambda h: h.matmul(pg[:], wb[:, c, hc * 128:(hc + 1) * 128], xb[:, c, :],
                                                  start=(c == 0), stop=(c == 7)), reads=[wb, xb], writes=[pg])
                for c in range(8):
                    S.op("pe", lambda h: h.matmul(pu[:], wb[:, c, FFN_H + hc * 128:FFN_H + (hc + 1) * 128], xb[:, c, :],
                                                  start=(c == 0), stop=(c == 7)), reads=[wb, xb], writes=[pu])
                sgt = sgb[hc % 3]
                S.op("act", lambda h: h.activation(sgt[:], pg[:], AF.Silu), reads=[pg], writes=[sgt])
                S.op("dve", lambda h: h.tensor_tensor(hd[:, hc, :], sgt[:], pu[:], op=ALU.mult), reads=[sgt, pu],
                     wacc=[hd])
            S.dma("pool", h_d[:, :, i * TT:(i + 1) * TT], hd[:], reads=[hd])
        S.barrier()


def layer_tail(P, layer, mix_d, w_out_d, xres_fn):
    L = "L%d" % layer
    x1f = P.dscratch(L + "x1f", [128, 8, NOWN], F32)
    x1b = P.dscratch(L + "x1b", [128, 8, NOWN], BF16)
    phase_linear_ln(P, L + "mixout", w_out_d, 8, mix_d, xres_fn, layer * 3 + 0, x1f, x1b)
    x2f = P.dscratch(L + "x2f", [128, 8, NOWN], F32)
    x2b = P.dscratch(L + "x2b", [128, 8, NOWN], BF16)
    phase_xattn(P, layer, x1f, x1b, x2f, x2b)
    hd = P.dscratch(L + "hid", [128, HC, NOWN], BF16)
    phase_ffn1(P, layer, x2b, hd)
    if layer == 1:
        x3f = P.dout("yT", [128, 8, NOWN], F32)
        x3b = None
    else:
        x3f = P.dscratch(L + "x3f", [128, 8, NOWN], F32)
        x3b = P.dscratch(L + "x3b", [128, 8, NOWN], BF16)
    w2 = P.din("ffn_w_out%d" % layer, [FFN_H, D])
    phase_linear_ln(P, L + "ffnout", w2, HC, hd, lambda i: x2f[:, :, i * TT:(i + 1) * TT], layer * 3 + 2, x3f, x3b)
    return x3f, x3b


QPERM = [0, 3, 1, 4, 2, 5, 6, 9, 7, 10, 8, 11]


def phase_proj1(P, xb_d):
    nc, S = P.nc, P.S
    w_d = P.din("cd_w_in_p", [D, 1536])
    gains_d = P.din("qk_gain_r", [128, 16, 64])
    cos_d = P.din("rope_cos", [128, NOWN // 128, 32])
    sin_d = P.din("rope_sin", [128, NOWN // 128, 32])
    P.QT1 = P.dscratch("QT1", [128, 6, NOWN], BF16)
    P.KT1 = {"p": [P.dscratch("KT1p%d" % i, [256, 2048], BF16) for i in range(2)],
             "s": [P.dscratch("KT1s", [256, NS_OWN], BF16)]}
    P.V1 = {"p": [P.dscratch("V1p%d" % i, [1024, 260], BF16) for i in range(4)],
            "s": [P.dscratch("V1s%d" % i, [1024, 260], BF16) for i in range(2)]}
    P.U1 = P.dscratch("U1", [128, 2, NOWN], F32)
    with ExitStack() as st:
        stg = [P.sb(st, "stg%d" % i, [128, 1024], F32) for i in range(3)]
        wb = P.load_w(st, "w_cd_in", w_d, D, 1536, stg)
        gains = P.sb(st, "p1_gain", [128, 16, 64], F32)
        S.dma("sp", gains[:], gains_d[:, :, :], writes=[gains])
        S.op("dve", lambda h: h.tensor_scalar_mul(gains[:, 0:12, :], gains[:, 0:12, :], 0.125), reads=[gains],
             wacc=[gains])
        cs = P.sb(st, "p1_cos", [128, NOWN // 128, 32], F32)
        sn = P.sb(st, "p1_sin", [128, NOWN // 128, 32], F32)
        S.dma("sp", cs[:], cos_d[:, :, :], writes=[cs])
        S.dma("sp", sn[:], sin_d[:, :, :], writes=[sn])
        xbs = [P.sb(st, "p1_xb%d" % i, [128, 8, TT], BF16) for i in range(2)]
        sq = P.sb(st, "p1_sq", [128, 1024], F32)
        ss = P.sb(st, "p1_ss", [128, 16], F32)
        rs = P.sb(st, "p1_rs", [128, 16], F32)
        qn = P.sb(st, "p1_qn", [128, 16, 64], F32)
        tt = [P.sb(st, "p1_t%d" % i, [128, 16, 32], F32) for i in range(4)]
        qr = P.sb(st, "p1_qr", [128, 4, 16, 64], BF16)
        qts = [P.sb(st, "p1_qt%d" % i, [128, 8, TT], BF16) for i in range(2)]
        vs = [P.sb(st, "p1_vs%d" % i, [128, 4, 4, 65], BF16) for i in range(2)]
        us = [P.sb(st, "p1_us%d" % i, [128, 2, TT], F32) for i in range(2)]
        for v_ in vs:
            S.op("dve", lambda h: h.memset(v_[:], 1.0), writes=[v_])
        pb = 0
        for i in range(NTILE):
            sg = "p" if i < 8 else "s"
            il = i if sg == "p" else i - 8
            xb, QT, VS, US = xbs[i % 2], qts[i % 2], vs[i % 2], us[i % 2]
            S.dma("sp", xb[:], xb_d[:, :, i * TT:(i + 1) * TT], writes=[xb])
            for sub in range(4):
                pa, pbk = P.ps[pb % 8], P.ps[(pb + 1) % 8]
                pv = P.ps[(pb + 2) % 8]
                pb += 3
                for c in range(8):
                    S.op("pe", lambda h: h.matmul(pa[:], xb[:, c, sub * 128:(sub + 1) * 128], wb[:, c, 0:512],
                                                  start=(c == 0), stop=(c == 7)), reads=[wb, xb], writes=[pa])
                for c in range(8):
                    S.op("pe", lambda h: h.matmul(pbk[:], xb[:, c, sub * 128:(sub + 1) * 128], wb[:, c, 512:1024],
                                                  start=(c == 0), stop=(c == 7)), reads=[wb, xb], writes=[pbk])
                for c in range(8):
                    S.op("pe", lambda h: h.matmul(pv[:, 0:256], xb[:, c, sub * 128:(sub + 1) * 128],
                                                  wb[:, c, 1024:1280], start=(c == 0), stop=(c == 7)),
                         reads=[wb, xb], writes=[pv])
                S.op("act", lambda h: h.copy(VS[:, sub, :, 0:64], pv[:, 0:256].rearrange("p (h d) -> p h d", d=64)),
                     reads=[pv], wacc=[VS])
                S.op("act", lambda h: h.activation(sq[:, 0:512], pa[:], AF.Square), reads=[pa], wacc=[sq])
                S.op("act", lambda h: h.activation(sq[:, 512:1024], pbk[:], AF.Square), reads=[pbk], wacc=[sq])
                S.op("dve", lambda h: h.tensor_reduce(ss[:], sq[:].rearrange("p (h d) -> p h d", d=64), axis=AX.X,
                                                      op=ALU.add), reads=[sq], writes=[ss])
                S.op("dve", lambda h: h.tensor_scalar(ss[:], ss[:], 1.0 / 64, None, op0=ALU.mult), reads=[ss],
                     wacc=[ss])
                P.rsqrt(rs[:], ss[:], P.c_eps_rms, reads=[ss], wacc=[rs])
                rsb = rs[:].unsqueeze(2).to_broadcast([128, 16, 64])
                S.op("dve", lambda h: h.tensor_tensor(qn[:, 0:8, :], pa[:].rearrange("p (h d) -> p h d", d=64),
                                                      rsb[:, 0:8, :], op=ALU.mult), reads=[pa, rs], wacc=[qn])
                S.op("dve", lambda h: h.tensor_tensor(qn[:, 8:16, :], pbk[:].rearrange("p (h d) -> p h d", d=64),
                                                      rsb[:, 8:16, :], op=ALU.mult), reads=[pbk, rs], wacc=[qn])
                S.op("pool", lambda h: h.tensor_tensor(qn[:], qn[:], gains[:], op=ALU.mult), reads=[qn, gains],
                     wacc=[qn])
                q4 = qn[:].rearrange("p h (i two) -> p h i two", two=2)
                x1, x2 = q4[:, :, :, 0], q4[:, :, :, 1]
                gsub = i * 4 + sub
                cb = cs[:, gsub, :].unsqueeze(1).to_broadcast([128, 16, 32])
                sb_ = sn[:, gsub, :].unsqueeze(1).to_broadcast([128, 16, 32])
                o4 = qr[:, sub, :, :].rearrange("p h (i two) -> p h i two", two=2)
                S.op("dve", lambda h: h.tensor_tensor(tt[0][:], x1, cb, op=ALU.mult), reads=[qn, cs], writes=[tt[0]])
                S.op("pool", lambda h: h.tensor_tensor(tt[1][:], x2, sb_, op=ALU.mult), reads=[qn, sn],
                     writes=[tt[1]])
                S.op("pool", lambda h: h.tensor_tensor(tt[2][:], x1, sb_, op=ALU.mult), reads=[qn, sn],
                     writes=[tt[2]])
                S.op("dve", lambda h: h.tensor_tensor(tt[3][:], x2, cb, op=ALU.mult), reads=[qn, cs], writes=[tt[3]])
                S.op("dve", lambda h: h.tensor_tensor(o4[:, :, :, 0], tt[0][:], tt[1][:], op=ALU.subtract),
                     reads=[tt[0], tt[1]], wacc=[qr])
                S.op("pool", lambda h: h.tensor_tensor(o4[:, :, :, 1], tt[2][:], tt[3][:], op=ALU.add),
                     reads=[tt[2], tt[3]], wacc=[qr])
            for fc in range(8):
                pt = P.ps[pb % 8]
                pb += 1
                ptb = pt[:].bitcast(BF16)
                for sub in range(4):
                    S.op("pe", lambda h: h.transpose(ptb[:, sub * 128:(sub + 1) * 128],
                                                     qr[:, sub, 2 * fc:2 * fc + 2, :].rearrange("p h d -> p (h d)"),
                                                     P.c_identb[:]), reads=[qr, P.c_identb], writes=[pt])
                P.copy(P.eng2(), QT[:, fc, :], ptb[:, 0:512], reads=[pt], wacc=[QT])
            t0 = i * TT
            S.dma("pool", P.QT1[:, :, t0:t0 + TT], QT[:, 0:6, :], reads=[QT])
            tl = il * TT
            S.dma("pool", P.KT1[sg][tl // 2048][:, tl % 2048:tl % 2048 + TT].rearrange("(c p) t -> p c t", p=128),
                  QT[:, 6:8, :], reads=[QT])
            S.dma("pool", P.V1[sg][tl // 1024][tl % 1024:tl % 1024 + TT, :].rearrange("(s p) f -> p s f", p=128),
                  VS[:].rearrange("p s h d -> p s (h d)"), reads=[VS])
            for c2 in range(2):
                pt = P.ps[pb % 8]
                pb += 1
                for c in range(8):
                    S.op("pe", lambda h: h.matmul(pt[:], wb[:, c, 1280 + c2 * 128:1280 + (c2 + 1) * 128], xb[:, c, :],
                                                  start=(c == 0), stop=(c == 7)), reads=[wb, xb], writes=[pt])
                P.copy(P.eng2(), US[:, c2, :], pt[:], reads=[pt], wacc=[US])
            S.dma("pool", P.U1[:, :, t0:t0 + TT], US[:], reads=[US])
        S.barrier()


def phase_gqa(P):
    nc, S = P.nc, P.S
    P.OT1 = P.dscratch("OT1", [128, 8, NOWN], BF16)
    ktall = [P.dscratch("KT1all%d" % i, [1024, 2048], BF16) for i in range(2)]
    vall = [P.dscratch("V1all%d" % i, [4096, 260], BF16) for i in range(4)]
    G = Buf(None)
    for i in range(2):
        S.allgather(ktall[i], P.KT1["p"][i], reads=[], writes=[], groups=GROUPS4)
    for i in range(4):
        S.allgather(vall[i], P.V1["p"][i], reads=[], writes=[], groups=GROUPS4)
    S.cc_fence(GROUPS4)
    G.ws = [("cc", S.cc_cnt, "dma")]
    with ExitStack() as st:
        ones32 = P.sb(st, "g_ones32", [128, 64], F32)
        S.op("dve", lambda h: h.memset(ones32[:], 1.0), writes=[ones32])
        KT = P.sb(st, "gKT", [128, SEQ_P], BF16)
        VA = P.sb(st, "gVA", [128, SEQ_P // 128, 2, 65], BF16)
        QTs = [P.sb(st, "gQT%d" % i, [128, NP_OWN], BF16) for i in range(2)]
        PA = PairAttn(P, st, with_z=False)
        qi_ = 0
        for sg in ("p", "s"):
            nseq = SEQ_P if sg == "p" else SEQ_S
            n_own = NP_OWN if sg == "p" else NS_OWN
            ooff = 0 if sg == "p" else NP_OWN
            nch = nseq // 128
            nqt = n_own // TT
            for kc in range(2):
                if sg == "p":
                    for r in range(4):
                        for hf in range(2):
                            S.dma("sp", KT[:, r * 4096 + hf * 2048:r * 4096 + (hf + 1) * 2048],
                                  ktall[hf][r * 256 + kc * 128:r * 256 + (kc + 1) * 128, :], reads=[G], wacc=[KT])
                        for j in range(4):
                            c0 = (r * 4096 + j * 1024) // 128
                            S.dma("sp", VA[:, c0:c0 + 8, :, :].rearrange("p c h d -> p c (h d)"),
                                  vall[j][r * 1024:(r + 1) * 1024, kc * 130:(kc + 1) * 130].rearrange(
                                      "(c p) f -> p c f", p=128), reads=[G], wacc=[VA])
                else:
                    S.dma("sp", KT[:, 0:nseq], P.KT1["s"][0][kc * 128:(kc + 1) * 128, :], wacc=[KT])
                    for j in range(2):
                        S.dma("sp", VA[:, j * 8:(j + 1) * 8, :, :].rearrange("p c h d -> p c (h d)"),
                              P.V1["s"][j][:, kc * 130:(kc + 1) * 130].rearrange("(c p) f -> p c f", p=128),
                              wacc=[VA])
                for j in range(3):
                    qc = 3 * kc + j
                    QT = QTs[qi_ % 2]
                    qi_ += 1
                    S.dma("sp", QT[:, 0:n_own], P.QT1[:, qc, ooff:ooff + n_own], writes=[QT])
                    items = []
                    for qi in range(nqt):
                        for ch in range(nch):
                            items.append((qi, ch, 0, ch == 0, ch == nch - 1))

                    def out_fn(qi, out, qc=qc, ooff=ooff):
                        o0 = ooff + qi * TT
                        S.dma("pool", P.OT1[:, qc, o0:o0 + TT], out[:], reads=[out])

                    PA.run(items, KT, QT, VA, None, out_fn)
        S.barrier()


def phase_pool(P):
    nc, S = P.nc, P.S
    pw = P.din("cd_pool_w", [4, 64, 64])
    psc = P.din("pool_scale_r", [128, 2])
    rc_d = P.din("pool_rc", [128, 2, NOWN])
    sel_d = P.din("pool_sel", [128, 8])
    edge = P.dscratch("pool_edge", [256, 16], F32)
    edall = P.dscratch("pool_edall", [1024, 16], F32)
    EDG, EDA = Buf(edge), Buf(edall)
    with ExitStack() as st:
        psc_s = P.sb(st, "pl_sc", [128, 2], F32)
        sel = P.sb(st, "pl_sel", [128, 8], F32)
        S.dma("sp", psc_s[:], psc[:, :], writes=[psc_s])
        S.dma("sp", sel[:], sel_d[:, :], writes=[sel])
        eg = P.sb(st, "pl_eg", [128, 2, 16], F32)
        S.dma("sp", eg[:, :, 0:8], P.U1[:, :, 0:8], wacc=[eg])
        S.dma("sp", eg[:, :, 8:16], P.U1[:, :, NP_OWN - 8:NP_OWN], wacc=[eg])
        S.dma("pool", edge.ap().rearrange("(c p) t -> p c t", p=128), eg[:], reads=[eg], writes=[EDG])
        S.allgather(edall, edge, reads=[EDG], writes=[EDA], groups=GROUPS4)
        S.cc_fence(GROUPS4)
        EDA.ws = [("cc", S.cc_cnt, "dma")]
        ea = P.sb(st, "pl_ea", [128, 4, 2, 16], F32)
        for r in range(4):
            S.dma("sp", ea[:, r, :, :], edall[r * 256:(r + 1) * 256, :].rearrange("(c p) t -> p c t", p=128),
                  reads=[EDA], wacc=[ea])
        wbd = P.sb(st, "pl_wbd", [128, 128], F32)
        wbb = P.sb(st, "pl_wbb", [128, 128], BF16)
        NE = NP_OWN + 16
        ue = P.sb(st, "pl_ue", [128, NE], F32)
        sA = P.sb(st, "pl_sA", [128, NE], F32)
        sB = P.sb(st, "pl_sB", [128, NE], F32)
        rc = P.sb(st, "pl_rc", [128, NP_OWN], F32)
        mx = P.sb(st, "pl_mx", [128, NP_OWN], BF16)
        ot = [P.sb(st, "pl_ot%d" % i, [128, TT], BF16) for i in range(2)]
        pb = 0
        for sg in ("p", "s"):
            n = NP_OWN if sg == "p" else NS_OWN
            ooff = 0 if sg == "p" else NP_OWN
            for c2 in range(2):
                S.op("dve", lambda h: h.memset(wbd[:], 0.0), writes=[wbd])
                S.dma("sp", wbd[0:64, 0:64], pw[2 * c2, :, :], wacc=[wbd])
                S.dma("sp", wbd[64:128, 64:128], pw[2 * c2 + 1, :, :], wacc=[wbd])
                S.op("dve", lambda h: h.tensor_copy(wbb[:], wbd[:]), reads=[wbd], writes=[wbb])
                S.op("dve", lambda h: h.memset(ue[:, 0:8], 0.0), wacc=[ue])
                S.op("dve", lambda h: h.memset(ue[:, 8 + n:16 + n], 0.0), wacc=[ue])
                S.dma("sp", ue[:, 8:8 + n], P.U1[:, c2, ooff:ooff + n], wacc=[ue])
                S.dma("sp", rc[:, 0:n], rc_d[:, c2, ooff:ooff + n], writes=[rc])
                if sg == "p":
                    for r in range(4):
                        S.op("dve", lambda h: h.scalar_tensor_tensor(ue[:, 0:8], ea[:, r, c2, 8:16], sel[:, r:r + 1],
                                                                     ue[:, 0:8], op0=ALU.mult, op1=ALU.add),
                             reads=[ea, sel, ue], wacc=[ue])
                        S.op("dve", lambda h: h.scalar_tensor_tensor(ue[:, 8 + n:16 + n], ea[:, r, c2, 0:8],
                                                                     sel[:, 4 + r:5 + r], ue[:, 8 + n:16 + n],
                                                                     op0=ALU.mult, op1=ALU.add),
                             reads=[ea, sel, ue], wacc=[ue])
                S.op("dve", lambda h: h.tensor_tensor(sA[:, 1:16 + n], ue[:, 0:15 + n], ue[:, 1:16 + n], op=ALU.add),
                     reads=[ue], writes=[sA])
                S.op("dve", lambda h: h.tensor_tensor(sB[:, 2:15 + n], sA[:, 1:14 + n], sA[:, 3:16 + n], op=ALU.add),
                     reads=[sA], writes=[sB])
                if c2 == 1:
                    S.op("dve", lambda h: h.tensor_tensor(sA[:, 4:13 + n], sB[:, 2:11 + n], sB[:, 6:15 + n],
                                                          op=ALU.add), reads=[sB], writes=[sA])
                    S.op("dve", lambda h: h.tensor_tensor(sB[:, 8:8 + n], sA[:, 4:4 + n], sA[:, 12:12 + n],
                                                          op=ALU.add), reads=[sA], writes=[sB])
                for half, sbuf_ in ((0, sA), (1, sB)):
                    ps_ = slice(half * 64, (half + 1) * 64)
                    S.op("dve", lambda h: h.tensor_tensor(sbuf_[ps_, 8:8 + n], sbuf_[ps_, 8:8 + n], rc[ps_, 0:n],
                                                          op=ALU.mult), reads=[sbuf_, rc], wacc=[sbuf_])
                    S.op("dve", lambda h: h.tensor_tensor(mx[ps_, 0:n], sbuf_[ps_, 8:8 + n], ue[ps_, 8:8 + n],
                                                          op=ALU.subtract), reads=[sbuf_, ue], wacc=[mx])
                for ti in range(n // TT):
                    pt = P.ps[pb % 8]
                    pb += 1
                    o = ot[ti % 2]
                    S.op("pe", lambda h: h.matmul(pt[:], wbb[:], mx[:, ti * TT:(ti + 1) * TT], start=True, stop=True),
                         reads=[wbb, mx], writes=[pt])
                    S.op("act", lambda h: h.activation(o[:], pt[:], AF.Identity, scale=psc_s[:, c2:c2 + 1]),
                         reads=[pt, psc_s], writes=[o])
                    o0 = ooff + ti * TT
                    S.dma("pool", P.OT1[:, 6 + c2, o0:o0 + TT], o[:], reads=[o])
        S.barrier()


def host_inputs(inp, names):
    cc = _consts_common()
    f32 = lambda a: np.ascontiguousarray(a, dtype=np.float32)
    maps = []
    for c in range(NCORES):
        b, q = c // 4, c % 4
        o0 = q * NP_OWN
        m = {}
        for nm in names:
            if nm.startswith("k_"):
                m[nm] = cc[nm[2:]]
            elif nm == "xTp":
                ext = np.zeros((NP_EXT, D), np.float32)
                lo, hi = o0 - 1024, o0 + NP_OWN + 1024
                a, bb = max(lo, 0), min(hi, SEQ_P)
                ext[a - lo:bb - lo] = inp["x_prompt"][b, a:bb]
                m[nm] = _fm(ext.T)
            elif nm == "vldp":
                t = np.arange(o0 - 1024, o0 + NP_OWN + 1024)
                v = ((t >= 0) & (t < SEQ_P)).astype(np.float32)
                m[nm] = f32(v.reshape(NP_EXT // 128, 128).T)
            elif nm == "xTs":
                m[nm] = _fm(f32(inp["x_sample"][c]).T)
            elif nm == "vlds":
                m[nm] = np.ones((128, SEQ_S // 128), np.float32)
            elif nm == "ab_fnet_w":
                m[nm] = f32(inp["ab_fnet_w"][0])
            elif nm.startswith("f_"):
                kind, sg = nm[2:4], nm[5]
                if sg == "p":
                    tabs = _dft_tables(SEQ_P, 128, 128, list(range(32 * q, 32 * q + 32)))
                else:
                    tabs = _dft_tables(SEQ_S, 16, 128, list(range(128)))
                m[nm] = tabs[["rp", "rr", "tc", "ts", "c2", "s2"].index(kind)]
            elif nm[:-1] in ("xa_w_q", "xa_w_kv", "xa_w_o", "ffn_w_in", "ffn_w_out"):
                m[nm] = f32(inp[nm[:-1]][int(nm[-1])])
            elif nm == "ab_w_out":
                m[nm] = f32(inp["ab_w_out"][0])
            elif nm == "memTp":
                m[nm] = _fm(f32(inp["mem_prompt"][b]).T)
            elif nm == "memTs":
                m[nm] = _fm(f32(inp["mem_sample"][c]).T)
            elif nm == "cd_w_in_p":
                w = np.asarray(inp["cd_w_in"][0], np.float32)
                wq = w[:, :768].reshape(D, 12, 64)[:, QPERM, :].reshape(D, 768)
                m[nm] = f32(np.concatenate([wq, w[:, 768:]], 1))
            elif nm == "cd_w_out_p":
                w = np.asarray(inp["cd_w_out"][0], np.float32)
                wq = w[:768].reshape(12, 64, D)[QPERM].reshape(768, D)
                m[nm] = f32(np.concatenate([wq, w[768:]], 0))
            elif nm == "qk_gain_r":
                g = np.concatenate([np.tile(np.asarray(inp["cd_q_norm"][0], np.float32)[None], (12, 1)),
                                    np.tile(np.asarray(inp["cd_k_norm"][0], np.float32)[None], (4, 1))], 0)
                m[nm] = f32(np.broadcast_to(g[None], (128, 16, 64)))
            elif nm in ("rope_cos", "rope_sin"):
                pos = np.concatenate([o0 + np.arange(NP_OWN), np.arange(NS_OWN)])
                freqs = (np.float32(10000.0) ** (-np.arange(0, 32, 2, dtype=np.float32) / np.float32(32))).astype(np.float32)
                row = (pos // 64).astype(np.float32)
                col = (pos % 64).astype(np.float32)
                ang = np.concatenate([row[:, None] * freqs, col[:, None] * freqs], -1).astype(np.float32)
                t = np.cos(ang) if nm == "rope_cos" else np.sin(ang)
                m[nm] = f32(t.reshape(NOWN // 128, 128, 32).transpose(1, 0, 2))
            elif nm == "cd_pool_w":
                m[nm] = f32(inp["cd_pool_w"][0])
            elif nm == "pool_scale_r":
                m[nm] = f32(np.asarray(inp["cd_pool_scale"][0]).reshape(2, 128).T)
            elif nm == "pool_rc":
                pos = np.concatenate([o0 + np.arange(NP_OWN), np.arange(NS_OWN)])
                nn = np.concatenate([np.full(NP_OWN, SEQ_P), np.full(NS_OWN, SEQ_S)])
                rc = np.zeros((128, 2, NOWN), np.float32)
                for g_ in range(4):
                    w_ = (2, 4, 8, 16)[g_]
                    cnt = np.clip(pos + w_ // 2, 0, nn) - np.clip(pos - w_ // 2, 0, nn)
                    rc[(g_ % 2) * 64:(g_ % 2 + 1) * 64, g_ // 2, :] = (1.0 / cnt.astype(np.float32))[None]
                m[nm] = rc
            elif nm == "pool_sel":
                sel = np.zeros((128, 8), np.float32)
                if q > 0:
                    sel[:, q - 1] = 1.0
                if q < 3:
                    sel[:, 4 + q + 1] = 1.0
                m[nm] = sel
            elif nm == "rel_bias":
                m[nm] = f32(inp["rel_bias"])
            elif nm == "ab_w_in":
                m[nm] = f32(inp["ab_w_in"][0])
            elif nm == "fnet_g_r":
                m[nm] = f32(np.asarray(inp["ab_fnet_g"][0]).reshape(2, 128).T)
            elif nm in ("ln_g_r", "ln_b_r"):
                src = np.asarray(inp["ln_g" if nm == "ln_g_r" else "ln_b"], np.float32)
                m[nm] = f32(src.reshape(6, 8, 128).transpose(2, 0, 1))
            else:
                raise KeyError(nm)
        maps.append(m)
    return maps


def run_prog(P, inp):
    nc = P.finish()
    maps = host_inputs(inp, list(P.inputs.keys()))
    res = run_bass_kernel_spmd(nc, maps, core_ids=list(range(NCORES)))
    return res.results


def build_full(debug=()):
    P = Prog(debug=debug)
    P.setup()
    phase_proj0(P)
    phase_dil(P)
    phase_fnet(P)
    w_out0 = P.din("ab_w_out", [D, D])

    def xres0(i):
        if i < 8:
            return P.xT["p"][:, :, 1024 + i * TT:1024 + (i + 1) * TT]
        return P.xT["s"][:, :, (i - 8) * TT:(i - 7) * TT]

    x3f, x3b = layer_tail(P, 0, P.OT0, w_out0, xres0)
    phase_proj1(P, x3b)
    phase_gqa(P)
    phase_pool(P)
    w_out1 = P.din("cd_w_out_p", [D, D])
    layer_tail(P, 1, P.OT1, w_out1, lambda i: x3f[:, :, i * TT:(i + 1) * TT])
    return P


def kernel(**inputs):
    inp = {k: np.asarray(v) for k, v in inputs.items()}
    P = build_full()
    res = run_prog(P, inp)
    y_prompt = np.zeros((2, SEQ_P, D), np.float32)
    y_sample = np.zeros((8, SEQ_S, D), np.float32)
    for c in range(NCORES):
        b, q = c // 4, c % 4
        yT = np.asarray(res[c]["yT"], dtype=np.float32)
        y_prompt[b, q * NP_OWN:(q + 1) * NP_OWN] = _unfm(yT[:, :, :NP_OWN])
        y_sample[c] = _unfm(yT[:, :, NP_OWN:])
    return (y_prompt, y_sample)
```
